# Optimizing a Trainium2 kernel written in Bass

```python
import math
import jax
import jax.numpy as jnp
from jax import lax
import numpy as np

D_MODEL = 1024
BATCH = 8
SEQ = 2048
DEPTH = 2
DEC_BATCH = 128
DEC_SEQ = 1
PAST_LEN = 16384
PAGE_SIZE = 128

DN_ALPHA = (2 * DEPTH) ** 0.25
DN_BETA = (8 * DEPTH) ** -0.25
LN_EPS = 1e-5
RMS_EPS = 1e-5
FFN_RES = 0.5
D_FF = 2752
N_MODS = 9

GLA_HEADS = 4
GLA_DK = 64
GLA_DV = 128
GLA_KEY = GLA_HEADS * GLA_DK
GLA_VAL = GLA_HEADS * GLA_DV
GLA_GATE_RANK = 16
GLA_GATE_NORM = 16.0
GLA_CHUNK = 32

S5_GROUP = 16
S5_WIDTH = 512
S5_GROUPS = S5_WIDTH // S5_GROUP
S5_STATE = 64
S5_DT_MIN = 1e-3
S5_DT_MAX = 1e-1

MIX_WIDTH = GLA_VAL + S5_WIDTH
IN_SPLITS = (GLA_KEY, 2 * GLA_KEY, 2 * GLA_KEY + GLA_VAL, 2 * GLA_KEY + 2 * GLA_VAL, 2 * GLA_KEY + 2 * GLA_VAL + GLA_GATE_RANK)
IN_WIDTH = 2 * GLA_KEY + 2 * GLA_VAL + GLA_GATE_RANK + S5_WIDTH

RWKV_HEAD = 64
RWKV_HEADS = D_MODEL // RWKV_HEAD
RWKV_DECAY_LORA = 64
RWKV_A_LORA = 64
RWKV_GATE_LORA = 160
RWKV_LNX_EPS = 64e-5
NORM_EPS = 1e-12

kernel_name = 'hybrid_gla_s5_rwkv7_adaln_deepnorm_step'


def layer_norm(x, g, b, eps=LN_EPS):
    xf = x.astype(jnp.float32)
    mu = jnp.mean(xf, -1, keepdims=True)
    var = jnp.mean(jnp.square(xf - mu), -1, keepdims=True)
    return ((xf - mu) * lax.rsqrt(var + eps) * g.astype(jnp.float32) + b.astype(jnp.float32)).astype(x.dtype)


def post_norm(x, r, g, b):
    return layer_norm(DN_ALPHA * x + r, g, b)


def swiglu(h, wg, wu, wd):
    return (jax.nn.silu(h @ wg) * (h @ wu)) @ wd


def ada_mods(c, w, b):
    m = jax.nn.silu(c) @ w + b
    return jnp.split(m[:, None, :], N_MODS, axis=-1)


def gla_chunked(q, k, v, gk, s0):
    f32 = jnp.float32
    B, T, H = q.shape[0], q.shape[1], q.shape[2]
    C = min(GLA_CHUNK, T)
    n = -(-T // C)
    pad = n * C - T

    def blk(t):
        t = jnp.pad(t.astype(f32), ((0, 0), (0, pad), (0, 0), (0, 0)))
        return t.reshape(B, n, C, t.shape[2], t.shape[3])

    q, k, v, gk = blk(q), blk(k), blk(v), blk(gk)
    b = jnp.cumsum(gk, axis=2)
    b_last = b[:, :, -1:]
    q_in = q * jnp.exp(b)
    k_in = k * jnp.exp(-b)
    k_st = k * jnp.exp(b_last - b)
    causal = jnp.tril(jnp.ones((C, C), dtype=bool))
    scores = jnp.where(causal, jnp.einsum('bnthd,bnshd->bnhts', q_in, k_in), 0.0)
    o_intra = jnp.einsum('bnhts,bnshv->bnthv', scores, v)
    kv = jnp.einsum('bnshd,bnshv->bnhdv', k_st, v)
    decay = jnp.exp(b_last[:, :, 0])

    def step(s, inp):
        dec, kv_n = inp
        return dec[..., None] * s + kv_n, s

    s_final, s_prev = lax.scan(step, s0.astype(f32), (jnp.moveaxis(decay, 1, 0), jnp.moveaxis(kv, 1, 0)))
    s_prev = jnp.moveaxis(s_prev, 0, 1)
    o = o_intra + jnp.einsum('bnthd,bnhdv->bnthv', q_in, s_prev)
    o = o.reshape(B, n * C, H, v.shape[-1])[:, :T]
    return o, s_final


def s5_scan(u, a_re, a_im, log_step, b_re, b_im, c_re, c_im, d, h0_re, h0_im):
    f32 = jnp.float32
    a_re, a_im = a_re.astype(f32), a_im.astype(f32)
    b_re, b_im = b_re.astype(f32), b_im.astype(f32)
    c_re, c_im = c_re.astype(f32), c_im.astype(f32)
    dt = jnp.exp(log_step.astype(f32))[:, None]
    mag = jnp.exp(a_re * dt)
    ab_re = mag * jnp.cos(a_im * dt)
    ab_im = mag * jnp.sin(a_im * dt)
    den = jnp.square(a_re) + jnp.square(a_im)
    nr = ab_re - 1.0
    z_re = (nr * a_re + ab_im * a_im) / den
    z_im = (ab_im * a_re - nr * a_im) / den
    bb_re = z_re[..., None] * b_re - z_im[..., None] * b_im
    bb_im = z_re[..., None] * b_im + z_im[..., None] * b_re
    bu_re = jnp.einsum('btgc,gpc->btgp', u, bb_re)
    bu_im = jnp.einsum('btgc,gpc->btgp', u, bb_im)
    h0_re, h0_im = h0_re.astype(f32), h0_im.astype(f32)
    bu_re = bu_re.at[:, 0].add(ab_re * h0_re - ab_im * h0_im)
    bu_im = bu_im.at[:, 0].add(ab_re * h0_im + ab_im * h0_re)
    T = u.shape[1]
    a_shape = (1, T) + ab_re.shape
    elems = (jnp.broadcast_to(ab_re, a_shape), jnp.broadcast_to(ab_im, a_shape), bu_re, bu_im)

    def combine(e1, e2):
        a1r, a1i, b1r, b1i = e1
        a2r, a2i, b2r, b2i = e2
        return (a2r * a1r - a2i * a1i, a2r * a1i + a2i * a1r,
                a2r * b1r - a2i * b1i + b2r, a2r * b1i + a2i * b1r + b2i)

    _, _, h_re, h_im = lax.associative_scan(combine, elems, axis=1)
    y = (jnp.einsum('gcp,btgp->btgc', c_re, h_re) - jnp.einsum('gcp,btgp->btgc', c_im, h_im)
         + d.astype(f32) * u)
    return y, h_re[:, -1], h_im[:, -1]


def gla_s5_mixer(h, s_gla, s_re, s_im, w_in, w_out, gla_w_gk, gla_b_gk, gla_norm_g,
                 s5_a_re, s5_a_im, s5_log_step, s5_b_re, s5_b_im, s5_c_re, s5_c_im, s5_d,
                 s5_w_glu, s5_b_glu):
    f32 = jnp.float32
    B, T, _ = h.shape
    q, k, v, g, gk_low, u = jnp.split(h @ w_in, IN_SPLITS, axis=-1)
    q = q.reshape(B, T, GLA_HEADS, GLA_DK) * (GLA_DK ** -0.5)
    k = k.reshape(B, T, GLA_HEADS, GLA_DK)
    v = v.reshape(B, T, GLA_HEADS, GLA_DV)
    gk = jax.nn.log_sigmoid((gk_low @ gla_w_gk + gla_b_gk).astype(f32)) / GLA_GATE_NORM
    o, s_gla = gla_chunked(q, k, v, gk.reshape(B, T, GLA_HEADS, GLA_DK), s_gla)
    o = o * lax.rsqrt(jnp.mean(jnp.square(o), -1, keepdims=True) + RMS_EPS) * gla_norm_g.astype(f32)
    o_gla = o.reshape(B, T, GLA_VAL).astype(h.dtype) * jax.nn.silu(g)
    y, s_re, s_im = s5_scan(u.reshape(B, T, S5_GROUPS, S5_GROUP).astype(f32), s5_a_re, s5_a_im,
                            s5_log_step, s5_b_re, s5_b_im, s5_c_re, s5_c_im, s5_d, s_re, s_im)
    z = jax.nn.gelu(y.reshape(B, T, S5_WIDTH)).astype(h.dtype)
    o_s5 = z * jax.nn.sigmoid(z @ s5_w_glu + s5_b_glu)
    out = jnp.concatenate([o_gla, o_s5], axis=-1) @ w_out
    return out, s_gla, s_re, s_im


def wkv7_scan(r, w, k, v, a, b, s0):
    def step(s, inp):
        r_t, w_t, k_t, v_t, a_t, b_t = inp
        sa = jnp.einsum('bhij,bhj->bhi', s, a_t)
        s = s * w_t[:, :, None, :] + sa[..., None] * b_t[:, :, None, :] + v_t[..., None] * k_t[:, :, None, :]
        return s, jnp.einsum('bhij,bhj->bhi', s, r_t)

    xs = tuple(jnp.moveaxis(t, 1, 0) for t in (r, w, k, v, a, b))
    s_final, y = lax.scan(step, s0, xs)
    return jnp.moveaxis(y, 0, 1), s_final


def rwkv7_mixer(h, s_shift, s_wkv, mu, w_r, w_k, w_v, w_o, w0, w1, w2, a0, a1, a2,
                g1, g2, k_k, k_a, r_k, lnx_g, lnx_b):
    f32 = jnp.float32
    B, T, D = h.shape
    H, N = RWKV_HEADS, RWKV_HEAD
    prev = jnp.concatenate([s_shift[:, None, :].astype(h.dtype), h[:, :-1]], axis=1)
    xx = prev - h
    xr, xw, xk, xv, xa, xg = [h + xx * mu[i] for i in range(6)]
    r = xr @ w_r
    k = xk @ w_k
    v = xv @ w_v
    w = -jax.nn.softplus(-(w0 + jnp.tanh(xw @ w1) @ w2).astype(f32)) - 0.5
    a = jax.nn.sigmoid((a0 + (xa @ a1) @ a2).astype(f32))
    gate = jax.nn.sigmoid(xg @ g1) @ g2

    def heads(t):
        return t.astype(f32).reshape(B, T, H, N)

    r, k, v, a, w = heads(r), heads(k), heads(v), heads(a), heads(w)
    kk = k * k_k.astype(f32).reshape(H, N)
    kk = kk / jnp.maximum(jnp.sqrt(jnp.sum(jnp.square(kk), -1, keepdims=True)), NORM_EPS)
    k = k * (1.0 + (a - 1.0) * k_a.astype(f32).reshape(H, N))
    y, s_wkv = wkv7_scan(r, jnp.exp(-jnp.exp(w)), k, v, -kk, kk * a, s_wkv.astype(f32))
    mu_y = jnp.mean(y, -1, keepdims=True)
    var_y = jnp.mean(jnp.square(y - mu_y), -1, keepdims=True)
    y = ((y - mu_y) * lax.rsqrt(var_y + RWKV_LNX_EPS) * lnx_g.astype(f32).reshape(H, N)
         + lnx_b.astype(f32).reshape(H, N))
    y = y + jnp.sum(r * k * r_k.astype(f32), -1, keepdims=True) * v
    out = (y.reshape(B, T, D).astype(h.dtype) * gate) @ w_o
    return out, h[:, -1], s_wkv


def setup_inputs(seed: int = 0) -> dict:
    key = jax.random.key(seed)
    ks = iter(jax.random.split(key, 96))
    f32 = jnp.float32

    def nrm(shape, s=1.0):
        return s * jax.random.normal(next(ks), shape, f32)

    def uni(shape, lo, hi):
        return jax.random.uniform(next(ks), shape, f32, lo, hi)

    D, P, G = D_MODEL, S5_STATE, S5_GROUPS
    inp = {}
    inp['x_prompt'] = nrm((BATCH, SEQ, D))
    inp['x_sample'] = nrm((DEC_BATCH, DEC_SEQ, D))
    inp['state_gla'] = nrm((DEC_BATCH, GLA_HEADS, GLA_DK, GLA_DV), 0.5)
    inp['state_s5_re'] = nrm((DEC_BATCH, G, P), 0.5)
    inp['state_s5_im'] = nrm((DEC_BATCH, G, P), 0.5)
    inp['state_rwkv_shift'] = nrm((DEC_BATCH, D))
    inp['state_rwkv_wkv'] = nrm((DEC_BATCH, RWKV_HEADS, RWKV_HEAD, RWKV_HEAD), 0.3)
    inp['c_prompt'] = nrm((BATCH, D))
    inp['c_sample'] = nrm((DEC_BATCH, D))
    inp['ada_w'] = nrm((DEPTH, D, N_MODS * D), 0.3 * D ** -0.5)
    inp['ada_b'] = nrm((DEPTH, N_MODS * D), 0.02)
    inp['ln_g'] = 1.0 + nrm((DEPTH, 3, D), 0.02)
    inp['ln_b'] = nrm((DEPTH, 3, D), 0.02)
    for name in ('ffn1', 'ffn2'):
        inp[name + '_wg'] = nrm((DEPTH, D, D_FF), D ** -0.5)
        inp[name + '_wu'] = nrm((DEPTH, D, D_FF), D ** -0.5)
        inp[name + '_wd'] = nrm((DEPTH, D_FF, D), DN_BETA * D_FF ** -0.5)
    inp['w_in'] = nrm((D, IN_WIDTH), D ** -0.5)
    inp['w_out'] = nrm((MIX_WIDTH, D), DN_BETA * MIX_WIDTH ** -0.5)
    inp['gla_w_gk'] = nrm((GLA_GATE_RANK, GLA_KEY), GLA_GATE_RANK ** -0.5)
    inp['gla_b_gk'] = nrm((GLA_KEY,), 0.1)
    inp['gla_norm_g'] = 1.0 + nrm((GLA_DV,), 0.02)
    inp['s5_a_re'] = -0.5 + nrm((G, P), 0.01)
    inp['s5_a_im'] = jnp.pi * jnp.arange(P, dtype=f32)[None, :] + nrm((G, P), 0.01)
    inp['s5_log_step'] = uni((G,), math.log(S5_DT_MIN), math.log(S5_DT_MAX))
    inp['s5_b_re'] = nrm((G, P, S5_GROUP), (2 * S5_GROUP) ** -0.5)
    inp['s5_b_im'] = nrm((G, P, S5_GROUP), (2 * S5_GROUP) ** -0.5)
    inp['s5_c_re'] = nrm((G, S5_GROUP, P), (2 * P) ** -0.5)
    inp['s5_c_im'] = nrm((G, S5_GROUP, P), (2 * P) ** -0.5)
    inp['s5_d'] = nrm((G, S5_GROUP))
    inp['s5_w_glu'] = nrm((S5_WIDTH, S5_WIDTH), S5_WIDTH ** -0.5)
    inp['s5_b_glu'] = nrm((S5_WIDTH,), 0.02)
    inp['rwkv_mu'] = uni((6, D), 0.0, 1.0)
    inp['rwkv_w_r'] = nrm((D, D), D ** -0.5)
    inp['rwkv_w_k'] = nrm((D, D), D ** -0.5)
    inp['rwkv_w_v'] = nrm((D, D), D ** -0.5)
    inp['rwkv_w_o'] = nrm((D, D), DN_BETA * D ** -0.5)
    inp['rwkv_w0'] = uni((D,), -6.5, -1.5)
    inp['rwkv_w1'] = nrm((D, RWKV_DECAY_LORA), D ** -0.5)
    inp['rwkv_w2'] = nrm((RWKV_DECAY_LORA, D), 0.1 * RWKV_DECAY_LORA ** -0.5)
    inp['rwkv_a0'] = nrm((D,), 0.1)
    inp['rwkv_a1'] = nrm((D, RWKV_A_LORA), D ** -0.5)
    inp['rwkv_a2'] = nrm((RWKV_A_LORA, D), 0.1 * RWKV_A_LORA ** -0.5)
    inp['rwkv_g1'] = nrm((D, RWKV_GATE_LORA), D ** -0.5)
    inp['rwkv_g2'] = nrm((RWKV_GATE_LORA, D), RWKV_GATE_LORA ** -0.5)
    inp['rwkv_k_k'] = 0.85 + nrm((D,), 0.02)
    inp['rwkv_k_a'] = 1.0 + nrm((D,), 0.02)
    inp['rwkv_r_k'] = nrm((RWKV_HEADS, RWKV_HEAD), 0.1)
    inp['rwkv_lnx_g'] = 1.0 + nrm((D,), 0.02)
    inp['rwkv_lnx_b'] = nrm((D,), 0.02)
    return inp


def reference(x_prompt, x_sample, state_gla, state_s5_re, state_s5_im, state_rwkv_shift, state_rwkv_wkv,
              c_prompt, c_sample, ada_w, ada_b, ln_g, ln_b,
              ffn1_wg, ffn1_wu, ffn1_wd, ffn2_wg, ffn2_wu, ffn2_wd,
              w_in, w_out, gla_w_gk, gla_b_gk, gla_norm_g,
              s5_a_re, s5_a_im, s5_log_step, s5_b_re, s5_b_im, s5_c_re, s5_c_im, s5_d, s5_w_glu, s5_b_glu,
              rwkv_mu, rwkv_w_r, rwkv_w_k, rwkv_w_v, rwkv_w_o, rwkv_w0, rwkv_w1, rwkv_w2,
              rwkv_a0, rwkv_a1, rwkv_a2, rwkv_g1, rwkv_g2, rwkv_k_k, rwkv_k_a, rwkv_r_k,
              rwkv_lnx_g, rwkv_lnx_b):
    f32 = jnp.float32

    def run_group(x, c, s_gla, s_re, s_im, s_shift, s_wkv):
        for layer in range(DEPTH):
            sh1, sc1, gt1, sh2, sc2, gt2, sh3, sc3, gt3 = ada_mods(c, ada_w[layer], ada_b[layer])
            y = swiglu(x * (1.0 + sc1) + sh1, ffn1_wg[layer], ffn1_wu[layer], ffn1_wd[layer])
            x = post_norm(x, FFN_RES * (1.0 + gt1) * y, ln_g[layer, 0], ln_b[layer, 0])
            h = x * (1.0 + sc2) + sh2
            if layer % 2 == 0:
                y, s_gla, s_re, s_im = gla_s5_mixer(
                    h, s_gla, s_re, s_im, w_in, w_out, gla_w_gk, gla_b_gk, gla_norm_g,
                    s5_a_re, s5_a_im, s5_log_step, s5_b_re, s5_b_im, s5_c_re, s5_c_im, s5_d,
                    s5_w_glu, s5_b_glu)
            else:
                y, s_shift, s_wkv = rwkv7_mixer(
                    h, s_shift, s_wkv, rwkv_mu, rwkv_w_r, rwkv_w_k, rwkv_w_v, rwkv_w_o,
                    rwkv_w0, rwkv_w1, rwkv_w2, rwkv_a0, rwkv_a1, rwkv_a2, rwkv_g1, rwkv_g2,
                    rwkv_k_k, rwkv_k_a, rwkv_r_k, rwkv_lnx_g, rwkv_lnx_b)
            x = post_norm(x, (1.0 + gt2) * y, ln_g[layer, 1], ln_b[layer, 1])
            y = swiglu(x * (1.0 + sc3) + sh3, ffn2_wg[layer], ffn2_wu[layer], ffn2_wd[layer])
            x = post_norm(x, FFN_RES * (1.0 + gt3) * y, ln_g[layer, 2], ln_b[layer, 2])
        return x, s_gla, s_re, s_im, s_shift, s_wkv

    bp = x_prompt.shape[0]
    y_prompt, gla_p, s5_re_p, s5_im_p, shift_p, wkv_p = run_group(
        x_prompt, c_prompt,
        jnp.zeros((bp, GLA_HEADS, GLA_DK, GLA_DV), f32),
        jnp.zeros((bp, S5_GROUPS, S5_STATE), f32),
        jnp.zeros((bp, S5_GROUPS, S5_STATE), f32),
        jnp.zeros((bp, D_MODEL), x_prompt.dtype),
        jnp.zeros((bp, RWKV_HEADS, RWKV_HEAD, RWKV_HEAD), f32))
    y_sample, gla_s, s5_re_s, s5_im_s, shift_s, wkv_s = run_group(
        x_sample, c_sample, state_gla, state_s5_re, state_s5_im, state_rwkv_shift, state_rwkv_wkv)
    return (y_prompt, y_sample, gla_p, s5_re_p, s5_im_p, shift_p, wkv_p,
            gla_s, s5_re_s, s5_im_s, shift_s, wkv_s)
```

```python
import numpy as np
from contextlib import ExitStack
import concourse.bass as bass
import concourse.mybir as mybir
from concourse.bass_utils import run_bass_kernel_spmd

F32 = mybir.dt.float32
F32R = mybir.dt.float32r
AF = mybir.ActivationFunctionType
ALU = mybir.AluOpType

D, DFF, DEPTH, NMOD = 1024, 2752, 2, 9
KC = D // 128
NP, NS = 2048, 16
NT = NP + NS
TW = 344
NTILE = NT // TW
GROUPS = [(0, 1), (2, 3), (4, 5)]
NCC = 18
FCH = [(f * 128, min(128, DFF - f * 128)) for f in range((DFF + 127) // 128)]
DN_ALPHA = (2 * DEPTH) ** 0.25
LN_EPS = 1e-5
GLA_SHAPE = (4, 64, 128)
S5_SHAPE = (32, 64)
WKV_SHAPE = (16, 64, 64)


class Sched:
    CAP, DCAP, MAX_SEMS = 4000, 240, 96

    def __init__(self, nc, es):
        self.nc, self.es = nc, es
        self.names = ('sp', 'act', 'dve', 'pool', 'pe')
        self.ops = {k: [] for k in self.names}
        self.nseq = {k: 0 for k in self.names}
        self.sems, self.dcount, self.res = {}, {}, {}
        self.waited = {k: {} for k in self.names}

    def _sem(self, kind, key, epoch):
        k = (kind, key, epoch)
        if k not in self.sems:
            assert len(self.sems) < self.MAX_SEMS, f"out of semaphores ({len(self.sems)})"
            self.sems[k] = self.es.enter_context(self.nc.semaphore(f"s_{kind}_{key}_{epoch}"))
        return self.sems[k]

    def _tok_wait(self, tok):
        if tok[0] == 'c':
            _, e, n = tok
            return self._sem('c', e, (n - 1) // self.CAP), (n - 1) % self.CAP + 1, ('c', e), n
        _, ch, n = tok
        ep = (n - 1) // self.DCAP
        in_ep = min(self.dcount[ch], (ep + 1) * self.DCAP) - ep * self.DCAP
        return self._sem('d', ch, ep), 16 * in_ep, ('d', ch, ep), in_ep

    def _need(self, e, tok, waits):
        if tok is None or (tok[0] == 'c' and tok[1] == e == 'pe'):
            return
        sem, v, src, order = self._tok_wait(tok)
        if self.waited[e].get(src, 0) >= order:
            return
        self.waited[e][src] = order
        waits.append((sem, v))

    def _deps(self, e, reads, writes):
        waits = []
        for r in reads:
            self._need(e, self.res.setdefault(r, {'w': None, 'r': {}})['w'], waits)
        for w in writes:
            st = self.res.setdefault(w, {'w': None, 'r': {}})
            self._need(e, st['w'], waits)
            for t in st['r'].values():
                self._need(e, t, waits)
        return waits

    def _commit(self, tok, rkey, reads, writes):
        for r in reads:
            self.res[r]['r'][rkey] = tok
        for w in writes:
            self.res[w] = {'w': tok, 'r': {}}

    def op(self, e, emit, reads=(), writes=()):
        waits = self._deps(e, reads, writes)
        self.nseq[e] += 1
        n = self.nseq[e]
        self.ops[e].append((waits, emit, self._sem('c', e, (n - 1) // self.CAP), 1))
        self._commit(('c', e, n), ('c', e), reads, writes)

    def dma(self, q, ch, emit, reads=(), writes=()):
        waits = self._deps(q, reads, writes)
        self.dcount[ch] = n = self.dcount.get(ch, 0) + 1
        self.ops[q].append((waits, emit, self._sem('d', ch, (n - 1) // self.DCAP), 16))
        self._commit(('d', ch, n), ('d', ch), reads, writes)

    def finish(self, e='sp'):
        waits = []
        for ch, tot in self.dcount.items():
            for ep in range((tot - 1) // self.DCAP + 1):
                self._need(e, ('d', ch, min(tot, (ep + 1) * self.DCAP)), waits)
        self.ops[e].append((waits, None, None, 0))

    def emit_all(self):
        with self.nc.Block() as block:
            for name, deco in (('sp', block.sync), ('act', block.scalar), ('dve', block.vector),
                               ('pool', block.gpsimd), ('pe', block.tensor)):
                def body(eng, lst=self.ops[name]):
                    for waits, emit, sem, amt in lst:
                        for s, v in waits:
                            eng.wait_ge(s, v)
                        if emit is not None:
                            emit(eng).then_inc(sem, amt)
                deco(body)


def build_nc(only=None):
    nc = bass.Bass("TRN2", target_bir_lowering=False)
    dr = lambda n, s, k: nc.dram_tensor(n, list(s), F32, kind=k).ap()
    xT_d = dr("xT", (128, KC, NT), "ExternalInput")
    cT_d = dr("cT", (128, KC, NCC), "ExternalInput")
    adaw_d = dr("ada_w", (DEPTH, D, NMOD * D), "ExternalInput")
    adab_d = dr("ada_bT", (DEPTH, 128, NMOD * KC), "ExternalInput")
    lng_d = dr("ln_gT", (DEPTH, 3, 128, KC), "ExternalInput")
    lnb_d = dr("ln_bT", (DEPTH, 3, 128, KC), "ExternalInput")
    ffw = {}
    for nm in ("ffn1", "ffn2"):
        ffw[nm] = (dr(nm + "_wguT", (DEPTH, len(FCH), 128, 2, KC, 128), "ExternalInput"),
                   dr(nm + "_wd", (DEPTH, DFF, D), "ExternalInput"))
    w_in_d = dr("w_inT", (13, 128, KC, 128), "ExternalInput")
    w_inu_d = dr("w_inuT", (4, 128, KC, 128), "ExternalInput")
    w_out_d = dr("w_outT", (KC, 128, KC, 128), "ExternalInput")
    wgk_d = dr("gla_w_gk", (16, 256), "ExternalInput")
    bgk_d = dr("gla_b_gkT", (128, 2), "ExternalInput")
    gng_d = dr("gla_norm_gT", (128, 1), "ExternalInput")
    rw = {n: dr("rwkv_" + n, sh, "ExternalInput") for n, sh in dict(
        w_rT=(KC, 128, KC, 128), w_kT=(KC, 128, KC, 128), w_vT=(KC, 128, KC, 128), w_oT=(KC, 128, KC, 128),
        w1T=(1, 128, KC, 128), w2=(64, D), a1T=(1, 128, KC, 128), a2=(64, D), g1T=(2, 128, KC, 128), g2=(160, D)).items()}
    rmu_d = dr("rwkv_muT", (128, 6, KC), "ExternalInput")
    rvec_d = dr("rwkv_vecT", (128, 7, KC), "ExternalInput")
    s5aT_d = dr("s5_aT", (128, 2, 32), "ExternalInput")
    s5ls_d = dr("s5_lsT", (128, 32), "ExternalInput")
    s5BA_d = dr("s5_BA", (128, 32, 16), "ExternalInput")
    s5BB_d = dr("s5_BB", (128, 32, 16), "ExternalInput")
    s5CT_d = dr("s5_CT", (128, 32, 16), "ExternalInput")
    s5d_d = dr("s5_dT", (128, 4), "ExternalInput")
    s5bg_d = dr("s5_b_gluT", (128, 4), "ExternalInput")
    s5wg_d = dr("s5_w_gluT", (4, 128, 4, 128), "ExternalInput")
    s5h0_d = dr("s5_h0T", (128, 32, NS), "ExternalInput")
    s5hp_d = dr("s5_hp", (128, 32), "ExternalOutput")
    s5hs_d = dr("s5_hs", (128, 32, NS), "ExternalOutput")
    glas0_d = dr("gla_s0T", (128, NS, 2, 128), "ExternalInput")
    glas_d = dr("gla_sT", (128, NS, 2, 128), "ExternalOutput")
    rshift_d = dr("rwkv_shiftT", (128, KC, NS), "ExternalInput")
    wkvs0_d = dr("wkv_s0T", (128, NS, KC, 64), "ExternalInput")
    wkvs_d = dr("wkv_sT", (128, NS, KC, 64), "ExternalOutput")
    shifts_d = dr("shift_sT", (128, KC, NS), "ExternalOutput")
    yT_d = dr("yT", (128, KC, NT), "ExternalOutput")
    st_out = {}
    for grp, nb in (("p", 1), ("s", NS)):
        if grp == "p":
            st_out["gla_" + grp] = dr("gla_" + grp, (nb, 4 * 64 * 128), "ExternalOutput")
        if grp == "p":
            st_out["shift_" + grp] = dr("shift_" + grp, (nb, D), "ExternalOutput")
            st_out["wkv_" + grp] = dr("wkv_" + grp, (nb, 16 * 64 * 64), "ExternalOutput")

    with ExitStack() as es:
        sb = lambda n, s: es.enter_context(nc.sbuf_tensor(n, list(s), F32))
        ps = lambda n, s: es.enter_context(nc.psum_tensor(n, list(s), F32))
        S = Sched(nc, es)

        xT = sb("xT_sb", (128, KC, NT))
        hin = sb("hin", (128, KC, 2 * TW))
        HT = sb("HT", (128, len(FCH), 2 * TW))
        wgu = [sb(f"wgu{i}", (128, 2, KC, 128)) for i in range(2)]
        cT = sb("cT_sb", (128, KC, NCC))
        sc = sb("silu_c", (128, KC, NCC))
        mods = [sb(f"mods{l}", (128, NMOD * KC, NCC)) for l in range(DEPTH)]
        adabT = sb("adabT", (128, DEPTH, NMOD * KC))
        lng = sb("lng", (128, DEPTH * 3, KC))
        lnb = sb("lnb", (128, DEPTH * 3, KC))
        ones = sb("ones", (128, 128))
        fence_t = sb('fence_t', (1, 2))
        zt = sb("zt", (128, KC, TW))
        zsq = sb("zsq", (128, KC, TW))
        mean = sb("mean", (128, TW)); rstd = sb("rstd", (128, TW))
        gsb = sb("g_sb", (128, TW))
        msq = gsb
        p_g = [ps(f"p_g{i}", (128, 512)) for i in range(2)]
        p_u = [ps(f"p_u{i}", (128, 512)) for i in range(2)]
        p_y = [ps(f"p_y{i}", (128, 512)) for i in range(2)]
        p_s = [ps(f"p_s{i}", (128, 512)) for i in range(2)]

        NQ = 256
        ident = sb('ident', (128, 128)); rstm = sb('rstm', (128, NQ))
        Wgk = sb('Wgk', (16, 256)); bgk = sb('bgk', (128, 2)); gng = sb('gng', (128, 1))
        dec = sb('dec', (128, 2, 2)); bl = sb('bl', (128, 2, 2))
        mask2 = sb('mask2', (128, 256))
        maskLT = sb('maskLT', (128, 128))
        bones = sb('bones', (128, 128))
        rmu = sb('rmu', (128, 6, KC)); rvec = sb('rvec', (128, 7, KC))
        Swk = sb('Swk', (128, KC, 64))
        Sst = [Swk[:, 2 * f:2 * f + 2, :].rearrange('p a b -> p (a b)') for f in range(2)]
        hprev = sb('hprev', (128, KC)); shp = sb('shp', (128, KC)); pcl = sb('pcl', (128, 2)); llast = sb('llast', (128, 2))
        st2 = sb('st2', (128, 8))
        blkm = sb('blkm', (128, 8)); sgn = sb('sgn', (128, 1))
        s5d = sb('s5d', (128, 4)); s5bg = sb('s5bg', (128, 4)); HS = sb('HS', (128, 32))
        r32 = lambda t: t.bitcast(F32R)
        rv = lambda ap: ap.bitcast(F32R)

        S.dma('sp', 'ld0', lambda q: q.dma_start(out=xT[:], in_=xT_d), writes=['xT'])
        S.dma('sp', 'ld0', lambda q: q.dma_start(out=cT[:], in_=cT_d), writes=['cT'])
        S.dma('sp', 'ld0', lambda q: q.dma_start(out=adabT[:], in_=adab_d.rearrange("l p m -> p l m")),
              writes=['adabT'])
        S.dma('sp', 'ld0', lambda q: q.dma_start(out=lng[:], in_=lng_d.rearrange("l s p k -> p (l s) k")),
              writes=['lng'])
        S.dma('sp', 'ld0', lambda q: q.dma_start(out=lnb[:], in_=lnb_d.rearrange("l s p k -> p (l s) k")),
              writes=['lnb'])
        S.op('pool', lambda g: g.memset(ones[:], 1.0), writes=['ones'])
        S.op('pool', lambda g: g.affine_select(out=ident[:], in_=ones[:], pattern=[[1, 128]], base=0,
                                               channel_multiplier=-1, compare_op=ALU.is_equal, fill=0.0),
             reads=['ones'], writes=['ident'])
        S.op('pool', lambda g: g.affine_select(out=mask2[:, 0:128], in_=ones[:], pattern=[[1, 128]], base=-1,
                                               channel_multiplier=-1, compare_op=ALU.is_ge, fill=0.0),
             reads=['ones'], writes=['mask2'])
        S.op('pool', lambda g: g.affine_select(out=mask2[:, 128:256], in_=ones[:], pattern=[[1, 128]], base=0,
                                               channel_multiplier=-1, compare_op=ALU.is_ge, fill=0.0),
             reads=['ones', 'mask2'], writes=['mask2'])
        S.op('pool', lambda g: g.affine_select(out=maskLT[:], in_=ones[:], pattern=[[-1, 128]], base=-1,
                                               channel_multiplier=1, compare_op=ALU.is_ge, fill=0.0),
             reads=['ones'], writes=['maskLT'])
        S.op('pool', lambda g: g.affine_select(out=blkm[:], in_=ones[:, 0:8], pattern=[[-16, 8]], base=0,
                                               channel_multiplier=1, compare_op=ALU.is_ge, fill=0.0), reads=['ones'], writes=['blkm'])
        S.op('pool', lambda g: g.affine_select(out=blkm[:], in_=blkm[:], pattern=[[16, 8]], base=15,
                                               channel_multiplier=-1, compare_op=ALU.is_ge, fill=0.0), reads=['blkm'], writes=['blkm'])
        S.op('pool', lambda g: g.memset(sgn[0:64, :], 1.0), writes=['sgn'])
        S.op('pool', lambda g: g.memset(sgn[64:128, :], -1.0), reads=['sgn'], writes=['sgn'])
        S.op('pool', lambda g: g.memset(bones[:], 0.0), writes=['bones'])
        for hb in range(2):
            S.op('pool', lambda g, hb=hb: g.memset(bones[hb * 64:(hb + 1) * 64, hb * 64:(hb + 1) * 64], 1.0),
                 reads=['bones'], writes=['bones'])
        S.dma('sp', 'ld0', lambda q: q.dma_start(out=rmu[:], in_=rmu_d), writes=['rmu'])
        S.dma('sp', 'ld0', lambda q: q.dma_start(out=rvec[:], in_=rvec_d), writes=['rvec'])
        S.op('pool', lambda g: g.memset(rstm[:], 1.0), writes=['rstm'])
        for c_ in range(NQ // 128):
            S.op('pool', lambda g, c_=c_: g.memset(rstm[:, c_ * 128:c_ * 128 + 1], 0.0), reads=['rstm'], writes=['rstm'])
        S.dma('pool', 'ldc', lambda q: q.dma_start(out=r32(Wgk)[:], in_=wgk_d), writes=['Wgk'])
        S.dma('sp', 'ld0', lambda q: q.dma_start(out=bgk[:], in_=bgk_d), writes=['bgk'])
        S.dma('sp', 'ld0', lambda q: q.dma_start(out=gng[:], in_=gng_d), writes=['gng'])
        S.op('act', lambda a: a.activation(out=r32(sc)[:], in_=cT[:], func=AF.Silu),
             reads=['cT'], writes=['sc'])

        ABUF = [(HT[:, 8 * (i // 2):8 * (i // 2) + 8, 256 * (i % 2):256 * (i % 2) + 256], f"adaw{i}") for i in range(4)] + \
               [(hin[:, :, 256 * i:256 * i + 256], f"adaw{4 + i}") for i in range(2)]
        nblk = 0
        for l in range(DEPTH):
            for cb in range(NMOD * D // 256):
                buf, key = ABUF[nblk % len(ABUF)]; nblk += 1
                S.dma('pool', key, lambda q, buf=buf, l=l, cb=cb: q.dma_start(
                    out=rv(buf), in_=adaw_d[l, :, cb * 256:(cb + 1) * 256].rearrange("(k p) c -> p k c", p=128)),
                    writes=[key])
                for j in range(2):
                    m = cb * 2 + j
                    pt = p_s[m % 2]; pk = f"p_s{m % 2}"
                    for k in range(KC):
                        S.op('pe', lambda p, pt=pt, buf=buf, k=k, j=j: p.matmul(
                            pt[:, 0:NCC], rv(buf[:, k, j * 128:(j + 1) * 128]), r32(sc)[:, k, :],
                            start=(k == 0), stop=(k == KC - 1)),
                            reads=[key, 'sc'], writes=[pk])
                    S.op('act', lambda a, pt=pt, l=l, m=m: a.activation(
                        out=mods[l][:, m, :], in_=pt[:, 0:NCC], func=AF.Identity,
                        bias=adabT[:, l, m:m + 1], scale=1.0),
                        reads=[pk, 'adabT'], writes=[f"mods{l}"])
            for which in (1, 2, 4, 5, 7, 8):
                S.op('dve', lambda v, l=l, which=which: v.tensor_scalar_add(
                    mods[l][:, which * KC:(which + 1) * KC, :], mods[l][:, which * KC:(which + 1) * KC, :], 1.0),
                    reads=[f"mods{l}"], writes=[f"mods{l}"])

        S.op('pool', lambda g: g.memset(fence_t[:], 0.0), writes=['hin', 'HT', 'fence_t'] + [f'adaw{i}' for i in range(6)])

        def split_cols(c0, w):
            npr = max(0, min(c0 + w, NP) - c0)
            return (0, c0, npr), (npr, c0 + npr, w - npr)

        def modulate(eng_p, out_t, ocol, l, sub, c0, w, rkeys, wkey):
            (po, pg, pn), (so, sg, sn) = split_cols(c0, w)
            for k in range(KC):
                if pn:
                    S.op('act', lambda a, k=k: a.activation(
                        out=r32(out_t)[:, k, ocol + po:ocol + po + pn], in_=xT[:, k, pg:pg + pn], func=AF.Identity,
                        scale=mods[l][:, (3 * sub + 1) * KC + k, 16:17], bias=mods[l][:, (3 * sub) * KC + k, 16:17]),
                        reads=['xT', f"mods{l}"] + rkeys, writes=[wkey])
                if sn:
                    s0 = sg - NP
                    S.op('dve', lambda v, k=k: v.tensor_tensor(
                        out=gsb[:, 0:sn], in0=xT[:, k, sg:sg + sn],
                        in1=mods[l][:, (3 * sub + 1) * KC + k, s0:s0 + sn], op=ALU.mult),
                        reads=['xT', f"mods{l}"], writes=['gsb'])
                    S.op('dve', lambda v, k=k: v.tensor_tensor(
                        out=r32(out_t)[:, k, ocol + so:ocol + so + sn], in0=gsb[:, 0:sn],
                        in1=mods[l][:, (3 * sub) * KC + k, s0:s0 + sn], op=ALU.add),
                        reads=['gsb', f"mods{l}"] + rkeys, writes=[wkey])

        wcnt = [0]

        def ffn_sublayer(l, sub, wts):
            wgu_d, wd_d = wts
            HTF = HT[:].rearrange("p a b -> p (a b)")
            hinF = hin[:].rearrange("p a b -> p (a b)")
            slotsA = [(wgu[0][:], 'wgu0'), (wgu[1][:], 'wgu1')] + \
                     [(HTF[:, (16 + 3 * i) * 688:(16 + 3 * i) * 688 + 2048].rearrange("p (a k c) -> p a k c", a=2, k=KC), f'hts{i}') for i in range(2)]
            slotsB = [(wgu[i][:].rearrange("p a k c -> p (a k c)")[:, 0:D], f'wgu{i}') for i in range(2)] + \
                     [(hinF[:, i * D:(i + 1) * D], f'hinw{i}') for i in range(5)]
            HW = [f'hinw{i}' for i in range(5)]
            for grp in GROUPS:
                c0g = grp[0] * TW
                modulate('act', hin, 0, l, sub, c0g, 2 * TW, [], 'hin')
                S.op('pool', lambda g: g.memset(fence_t[:], 0.0), writes=['HT', 'hts0', 'hts1', 'fence_t'])
                for fi, (f0, fw) in enumerate(FCH):
                    wb, wk = slotsA[fi % 4] if fi < 16 else slotsA[fi % 2]
                    S.dma('pool', wk, lambda q, wb=wb, fi=fi: q.dma_start(out=rv(wb), in_=wgu_d[l, fi]), writes=[wk])
                    for t in range(2):
                        for j, pp, pkn in ((0, p_g, 'p_g'), (1, p_u, 'p_u')):
                            for k in range(KC):
                                S.op('pe', lambda p, pp=pp, t=t, j=j, k=k, wb=wb: p.matmul(
                                    pp[t][0:128, 0:TW], rv(wb[:, j, k, :]), r32(hin)[:, k, t * TW:(t + 1) * TW],
                                    start=(k == 0), stop=(k == KC - 1)),
                                    reads=[wk, 'hin'], writes=[f"{pkn}{t}"])
                        S.op('act', lambda a, t=t: a.activation(out=gsb[:, 0:TW], in_=p_g[t][:, 0:TW], func=AF.Silu),
                             reads=[f"p_g{t}"], writes=['gsb'])
                        S.op('dve', lambda v, t=t, fi=fi: v.tensor_tensor(
                            out=r32(HT)[:, fi, t * TW:(t + 1) * TW], in0=gsb[:, 0:TW], in1=p_u[t][:, 0:TW],
                            op=ALU.mult), reads=['gsb', f"p_u{t}"],
                            writes=['HT', 'adaw0', 'adaw1'] + (['hts0', 'hts1'] if fi >= 16 else []))
                S.op('pool', lambda g: g.memset(fence_t[:], 0.0), writes=['hin', 'fence_t'] + HW)
                nb_ = 0
                for t in range(2):
                    c0 = c0g + t * TW
                    banks = [(p_g[0], 'p_g0'), (p_g[1], 'p_g1'), (p_u[0], 'p_u0'), (p_u[1], 'p_u1'),
                             (p_y[0], 'p_y0'), (p_y[1], 'p_y1'), (p_s[0], 'p_s0'), (p_s[1], 'p_s1')]
                    for fi, (f0, fw) in enumerate(FCH):
                        wbv, wk = slotsB[nb_ % len(slotsB)]; nb_ += 1
                        S.dma('pool', wk, lambda q, wbv=wbv, f0=f0, fw=fw: q.dma_start(
                            out=rv(wbv[0:fw, :]), in_=wd_d[l, f0:f0 + fw, :]), writes=[wk])
                        for dk in range(KC):
                            pt, pk = banks[dk]
                            S.op('pe', lambda p, pt=pt, wbv=wbv, dk=dk, fi=fi, fw=fw, t=t: p.matmul(
                                pt[:, 0:TW], rv(wbv[0:fw, dk * 128:(dk + 1) * 128]), r32(HT)[0:fw, fi, t * TW:(t + 1) * TW],
                                start=(fi == 0), stop=(fi == len(FCH) - 1)), reads=[wk, 'HT'], writes=[pk])
                    for dk in range(KC):
                        pt, pk = banks[dk]
                        if dk % 2:
                            S.op('act', lambda a, pt=pt, dk=dk: a.activation(out=zsq[:, dk, :], in_=pt[:, 0:TW], func=AF.Identity),
                                 reads=[pk], writes=['zsq_y', 'zsq'])
                        else:
                            S.op('dve', lambda v, pt=pt, dk=dk: v.tensor_copy(zsq[:, dk, :], pt[:, 0:TW]), reads=[pk], writes=['zsq_y', 'zsq'])
                    post_norm_tile_from_sbuf(l, sub, c0, 0.5)
                S.op('pool', lambda g: g.memset(fence_t[:], 0.0), writes=['hin', 'fence_t'] + HW)

        def post_norm_tile_from_sbuf(l, sub, c0, res_scale, W=TW):
            (po, pg, pn), (so, sg, sn) = split_cols(c0, W)
            gi = 3 * sub + 2
            for k in range(KC):
                if pn:
                    S.op('act', lambda a, k=k: a.activation(
                        out=gsb[:, po:po + pn], in_=zsq[:, k, po:po + pn], func=AF.Identity,
                        scale=mods[l][:, gi * KC + k, 16:17]), reads=['zsq_y', f"mods{l}"], writes=['gsb'])
                if sn:
                    s0 = sg - NP
                    S.op('dve', lambda v, k=k: v.tensor_tensor(
                        out=gsb[:, so:so + sn], in0=zsq[:, k, so:so + sn],
                        in1=mods[l][:, gi * KC + k, s0:s0 + sn], op=ALU.mult),
                        reads=['zsq_y', f"mods{l}"], writes=['gsb'])
                S.op('dve', lambda v, k=k: v.tensor_scalar_mul(zt[:, k, 0:W], xT[:, k, c0:c0 + W], DN_ALPHA),
                     reads=['xT'], writes=['zt'])
                S.op('dve', lambda v, k=k: v.scalar_tensor_tensor(
                    out=zt[:, k, 0:W], in0=gsb[:, 0:W], scalar=float(res_scale), in1=zt[:, k, 0:W],
                    op0=ALU.mult, op1=ALU.add), reads=['gsb', 'zt'], writes=['zt'])
            for k in range(KC):
                S.op('act', lambda a, k=k: a.activation(out=zsq[:, k, 0:W], in_=zt[:, k, 0:W], func=AF.Square),
                     reads=['zt', 'zsq_y'], writes=['zsq', 'zsq_y'])
            finish_norm(l, sub, c0, W)

        def finish_norm(l, sub, c0, W=TW):
            for k in range(KC):
                S.op('pe', lambda p, k=k: p.matmul(p_s[0][:, 0:W], ones[:], zt[:, k, 0:W],
                                                   start=(k == 0), stop=(k == KC - 1)),
                     reads=['ones', 'zt'], writes=['p_s0'])
            for k in range(KC):
                S.op('pe', lambda p, k=k: p.matmul(p_s[1][:, 0:W], ones[:], zsq[:, k, 0:W],
                                                   start=(k == 0), stop=(k == KC - 1)),
                     reads=['ones', 'zsq'], writes=['p_s1'])
            S.op('act', lambda a: a.activation(out=mean[:, 0:W], in_=p_s[0][:, 0:W], func=AF.Identity, scale=1.0 / D),
                 reads=['p_s0'], writes=['mean'])
            S.op('dve', lambda v: v.tensor_tensor(out=msq[:, 0:W], in0=mean[:, 0:W], in1=mean[:, 0:W], op=ALU.mult),
                 reads=['mean'], writes=['gsb'])
            S.op('dve', lambda v: v.scalar_tensor_tensor(out=rstd[:, 0:W], in0=p_s[1][:, 0:W], scalar=1.0 / D,
                                                         in1=msq[:, 0:W], op0=ALU.mult, op1=ALU.subtract),
                 reads=['p_s1', 'gsb'], writes=['rstd'])
            S.op('dve', lambda v: v.tensor_scalar_add(rstd[:, 0:W], rstd[:, 0:W], LN_EPS), reads=['rstd'], writes=['rstd'])
            S.op('act', lambda a: a.activation(out=rstd[:, 0:W], in_=rstd[:, 0:W], func=AF.Sqrt),
                 reads=['rstd'], writes=['rstd'])
            S.op('dve', lambda v: v.reciprocal(rstd[:, 0:W], rstd[:, 0:W]), reads=['rstd'], writes=['rstd'])
            for k in range(KC):
                S.op('dve', lambda v, k=k: v.tensor_tensor(out=zt[:, k, 0:W], in0=zt[:, k, 0:W], in1=mean[:, 0:W],
                                                           op=ALU.subtract), reads=['zt', 'mean'], writes=['zt'])
                S.op('dve', lambda v, k=k: v.tensor_tensor(out=zt[:, k, 0:W], in0=zt[:, k, 0:W], in1=rstd[:, 0:W],
                                                           op=ALU.mult), reads=['zt', 'rstd'], writes=['zt'])
                S.op('act', lambda a, k=k: a.activation(
                    out=xT[:, k, c0:c0 + W], in_=zt[:, k, 0:W], func=AF.Identity,
                    scale=lng[:, l * 3 + sub, k:k + 1], bias=lnb[:, l * 3 + sub, k:k + 1]),
                    reads=['zt', 'lng', 'lnb'], writes=['xT'])


        HTf = HT[:].rearrange("p a b -> p (a b)")
        ztf = zt[:].rearrange("p a b -> p (a b)")
        zsqf = zsq[:].rearrange("p a b -> p (a b)")
        TN_R = ['qin0', 'qin1', 'kin0', 'kin1', 'gkl', 'og0', 'og1', 'og2', 'og3']
        TN_A = ['qT0', 'qT1', 'kT0', 'kT1', 'L0', 'L1', 'tmp0', 'tmp1', 'kst0', 'kst1']
        TN_B = ['gT0', 'gT1', 'gT2', 'gT3', 'vT0', 'vT1', 'vT2', 'vT3']
        TN = TN_R + TN_A + TN_B
        T = {n: HTf[:, i * NQ:(i + 1) * NQ] for i, n in enumerate(TN_R)}
        T.update({n: ztf[:, i * NQ:(i + 1) * NQ] for i, n in enumerate(TN_A)})
        T.update({n: zsqf[:, i * NQ:(i + 1) * NQ] for i, n in enumerate(TN_B)})
        o0 = len(TN_R) * NQ
        Pm = HTf[:, o0:o0 + 128]
        vtok = HTf[:, o0 + 128:o0 + 1152].rearrange("p (c v) -> p c v", c=2)
        ksttok = HTf[:, o0 + 1152:o0 + 1664].rearrange("p (c v) -> p c v", c=2)
        sqt, rst_ = zsqf[:, 8 * NQ:8 * NQ + 128], zsqf[:, 8 * NQ + 128:8 * NQ + 256]
        ALIAS = ['HT', 'adaw0', 'adaw1', 'adaw2', 'adaw3', 'adaw4', 'adaw5', 'hin', 'zt', 'zsq', 'zsq_y', 'gsb', 'wgu0', 'wgu1', 'wd0', 'wd1',
                 'wslot0', 'wslot1', 'wslot2', 'wslot3', 'hts0', 'hts1', 'hinw0', 'hinw1', 'hinw2', 'hinw3', 'hinw4', 'Sst0', 'Sst1', 'Swk', 'Pm', 'sqt', 'rst_', 'Pm0', 'Pm1', 'sqt0', 'sqt1', 'rst0', 'rst1', 'vtok', 'ksttok'] + TN

        HEAVY = {'HT', 'hin', 'wgu0', 'wgu1', 'wd0', 'wd1', 'wslot0', 'wslot1', 'wslot2', 'wslot3', 'hts0', 'hts1'} | \
                {f'adaw{i}' for i in range(6)} | {f'hinw{i}' for i in range(5)}

        def fence(light=False):
            if light:
                ks = [k for k in ALIAS if k not in HEAVY]
                S.op('dve', lambda v: v.memset(fence_t[:], 0.0), reads=ks, writes=ks + ['fence_t'])
            else:
                S.op('pool', lambda g: g.memset(fence_t[:], 0.0), reads=ALIAS, writes=ALIAS + ['fence_t'])

        wslot, pslot = [0], [0]
        PBK = [(p_g[0], 'p_g0'), (p_g[1], 'p_g1'), (p_u[0], 'p_u0'), (p_u[1], 'p_u1')]

        def proj(wt, blk, rhs_fn, nk, N, rkeys):
            sl = wslot[0] % 4; wslot[0] += 1
            buf = wgu[sl // 2][:, sl % 2]
            key = f"wslot{sl}"
            S.dma('pool', key, lambda q: q.dma_start(out=rv(buf[:, 0:nk, :]), in_=wt[blk, :, 0:nk, :]), writes=[key])
            pt, pk = PBK[pslot[0] % 4]; pslot[0] += 1
            for k in range(nk):
                rhs = rhs_fn(k)
                S.op('pe', lambda p, k=k, rhs=rhs: p.matmul(pt[:, 0:N], rv(buf[:, k, :]), rhs,
                                                            start=(k == 0), stop=(k == nk - 1)),
                     reads=[key] + rkeys, writes=[pk])
            return pt[:, 0:N], pk

        def stub_norm(l, c0, W):
            for k in range(KC):
                S.op('act', lambda a, k=k: a.activation(out=zt[:, k, 0:W], in_=xT[:, k, c0:c0 + W],
                                                        func=AF.Identity, scale=DN_ALPHA), reads=['xT'], writes=['zt'])
                S.op('act', lambda a, k=k: a.activation(out=zsq[:, k, 0:W], in_=zt[:, k, 0:W], func=AF.Square),
                     reads=['zt', 'zsq_y'], writes=['zsq', 'zsq_y'])
            finish_norm(l, 1, c0, W)


        def s5_phase():
            U = HTf[:, 6944:6944 + 4 * NP].rearrange("p (q n) -> p q n", q=4)
            XA, XB = HTf[:, 0:2050], HTf[:, 2050:4100]
            ZP = HTf[:, 4100:5124].rearrange("p (g m) -> p g m", g=8)
            ZC = HTf[:, 5124:6148].rearrange("p (g m) -> p g m", g=8)
            Rt = HTf[:, 6148:6404].rearrange("p (a m) -> p a m", a=2)
            Us = HTf[:, 6404:6468].rearrange("p (q b) -> p q b", q=4)
            HnA, H0A = hin[:, :, 512:576], hin[:, :, 576:640]
            Hn = lambda g: HnA[:, g // 4, (g % 4) * 16:(g % 4) * 16 + 16]
            H0 = lambda g: H0A[:, g // 4, (g % 4) * 16:(g % 4) * 16 + 16]
            BbT = zsqf[:, 0:512].rearrange("p (g c) -> p g c", g=32)
            CTt = zsqf[:, 512:1024].rearrange("p (g c) -> p g c", g=32)
            ybuf = ztf[:, 0:2048].rearrange("p (q n) -> p q n", q=4)
            pwr = zsqf[:, 1024:1728].rearrange("p (a k g) -> p a k g", a=2, k=11)
            s5sm = zsqf[:, 1728:2112].rearrange("p (i g) -> p i g", i=12)
            Jm, Jt = zsqf[:, 2112:2240], zsqf[:, 2240:2368]
            Rtmp = zsqf[:, 2368:2496]
            identR = HTf[:, 6468:6596]
            K5 = ['Jm', 'Jt', 'Rtmp', 'identR', 'U', 'XA', 'XB', 'ZP', 'ZC', 'Rt', 'Us', 'Hn', 'H0', 'BbT', 'CTt', 'ybuf', 's5sm', 'pwr']
            ALIAS.extend(k for k in K5 if k not in ALIAS)
            fence()
            l = 0
            sm = lambda i: s5sm[:, i, :]
            S.op('pool', lambda g: g.affine_select(out=Jm, in_=ones[:], pattern=[[1, 128]], base=-64, channel_multiplier=-1,
                                                   compare_op=ALU.is_equal, fill=0.0), reads=['ones'], writes=['Jm'])
            S.op('pool', lambda g: g.affine_select(out=Jt, in_=ones[:], pattern=[[1, 128]], base=64, channel_multiplier=-1,
                                                   compare_op=ALU.is_equal, fill=0.0), reads=['ones'], writes=['Jt'])
            S.op('pool', lambda g: g.tensor_tensor(out=Jm, in0=Jm, in1=Jt, op=ALU.add), reads=['Jm', 'Jt'], writes=['Jm'])
            S.op('act', lambda a: a.activation(out=rv(identR), in_=ident[:], func=AF.Identity), reads=['ident'], writes=['identR'])
            V = lambda e, fn, r, w: S.op(e, fn, reads=r, writes=w)
            S.dma('sp', 'ld5', lambda q: q.dma_start(out=s5sm[:, 0:2, :], in_=s5aT_d), writes=['s5sm'])
            S.dma('sp', 'ld5', lambda q: q.dma_start(out=s5sm[:, 2, :], in_=s5ls_d), writes=['s5sm'])
            S.dma('sp', 'ld5', lambda q: q.dma_start(out=ybuf[:, 0, :].rearrange("p (g c) -> p g c", g=32), in_=s5BA_d), writes=['ybuf'])
            S.dma('sp', 'ld5', lambda q: q.dma_start(out=ybuf[:, 1, :].rearrange("p (g c) -> p g c", g=32), in_=s5BB_d), writes=['ybuf'])
            S.dma('sp', 'ld5', lambda q: q.dma_start(out=CTt, in_=s5CT_d), writes=['CTt'])
            S.dma('sp', 'ld5', lambda q: q.dma_start(out=s5d[:], in_=s5d_d), writes=['s5d'])
            S.dma('sp', 'ld5', lambda q: q.dma_start(out=s5bg[:], in_=s5bg_d), writes=['s5bg'])
            S.dma('pool', 'ld5p', lambda q: q.dma_start(out=rv(H0A), in_=s5h0_d.rearrange('p (k a) b -> p k (a b)', a=4)), writes=['H0'])
            k5 = ['s5sm']
            V('act', lambda a: a.activation(out=sm(2), in_=sm(2), func=AF.Exp), k5, k5)
            V('dve', lambda v: v.tensor_tensor(out=sm(3), in0=sm(1), in1=sm(2), op=ALU.mult), k5, k5)
            V('act', lambda a: a.activation(out=sm(4), in_=sm(3), func=AF.Sin, scale=1.0 / 16), k5, k5)
            V('act', lambda a: a.activation(out=sm(5), in_=sm(3), func=AF.Sin, scale=1.0 / 32), k5, k5)
            V('dve', lambda v: v.tensor_tensor(out=sm(5), in0=sm(5), in1=sm(5), op=ALU.mult), k5, k5)
            V('dve', lambda v: v.tensor_scalar(out=sm(5), in0=sm(5), scalar1=-2.0, scalar2=1.0, op0=ALU.mult, op1=ALU.add), k5, k5)
            for _ in range(4):
                V('dve', lambda v: v.tensor_tensor(out=sm(6), in0=sm(5), in1=sm(5), op=ALU.mult), k5, k5)
                V('dve', lambda v: v.tensor_tensor(out=sm(7), in0=sm(4), in1=sm(4), op=ALU.mult), k5, k5)
                V('dve', lambda v: v.scalar_tensor_tensor(out=sm(4), in0=sm(5), scalar=2.0, in1=sm(4), op0=ALU.mult, op1=ALU.mult), k5, k5)
                V('dve', lambda v: v.tensor_tensor(out=sm(5), in0=sm(6), in1=sm(7), op=ALU.subtract), k5, k5)
            V('dve', lambda v: v.tensor_tensor(out=sm(6), in0=sm(0), in1=sm(2), op=ALU.mult), k5, k5)
            V('act', lambda a: a.activation(out=sm(6), in_=sm(6), func=AF.Exp), k5, k5)
            V('dve', lambda v: v.tensor_tensor(out=sm(5), in0=sm(5), in1=sm(6), op=ALU.mult), k5, k5)
            V('dve', lambda v: v.tensor_tensor(out=sm(4), in0=sm(4), in1=sm(6), op=ALU.mult), k5, k5)
            V('dve', lambda v: v.tensor_tensor(out=sm(6), in0=sm(0), in1=sm(0), op=ALU.mult), k5, k5)
            V('dve', lambda v: v.tensor_tensor(out=sm(7), in0=sm(1), in1=sm(1), op=ALU.mult), k5, k5)
            V('dve', lambda v: v.tensor_tensor(out=sm(6), in0=sm(6), in1=sm(7), op=ALU.add), k5, k5)
            V('dve', lambda v: v.reciprocal(sm(6), sm(6)), k5, k5)
            V('dve', lambda v: v.tensor_scalar_add(sm(7), sm(5), -1.0), k5, k5)
            V('dve', lambda v: v.tensor_tensor(out=sm(8), in0=sm(7), in1=sm(0), op=ALU.mult), k5, k5)
            V('dve', lambda v: v.tensor_tensor(out=sm(9), in0=sm(4), in1=sm(1), op=ALU.mult), k5, k5)
            V('dve', lambda v: v.tensor_tensor(out=sm(8), in0=sm(8), in1=sm(9), op=ALU.add), k5, k5)
            V('dve', lambda v: v.tensor_tensor(out=sm(8), in0=sm(8), in1=sm(6), op=ALU.mult), k5, k5)
            V('dve', lambda v: v.tensor_tensor(out=sm(9), in0=sm(4), in1=sm(0), op=ALU.mult), k5, k5)
            V('dve', lambda v: v.tensor_tensor(out=sm(10), in0=sm(7), in1=sm(1), op=ALU.mult), k5, k5)
            V('dve', lambda v: v.tensor_tensor(out=sm(9), in0=sm(9), in1=sm(10), op=ALU.subtract), k5, k5)
            V('dve', lambda v: v.tensor_tensor(out=sm(9), in0=sm(9), in1=sm(6), op=ALU.mult), k5, k5)
            V('dve', lambda v: v.tensor_scalar_mul(sm(9), sm(9), sgn[:, 0:1]), k5 + ['sgn'], k5)
            V('dve', lambda v: v.tensor_scalar_mul(sm(9), sm(9), -1.0), k5, k5)
            zb = lambda i: sm(i).unsqueeze(2).to_broadcast([128, 32, 16])
            BAv, BBv = (ybuf[:, i, :].rearrange("p (g c) -> p g c", g=32) for i in range(2))
            V('dve', lambda v: v.tensor_tensor(out=BbT, in0=BAv, in1=zb(8), op=ALU.mult), k5 + ['ybuf'], ['BbT'])
            V('dve', lambda v: v.tensor_tensor(out=BBv, in0=BBv, in1=zb(9), op=ALU.mult), k5 + ['ybuf'], ['ybuf'])
            V('dve', lambda v: v.tensor_tensor(out=BbT, in0=BbT, in1=BBv, op=ALU.add), ['BbT', 'ybuf'], ['BbT'])
            V('dve', lambda v: v.tensor_scalar_mul(CTt[64:128], CTt[64:128], -1.0), ['CTt'], ['CTt'])
            V('dve', lambda v: v.tensor_copy(pwr[:, 0, 0, :], sm(5)), k5, ['pwr'])
            V('dve', lambda v: v.tensor_copy(pwr[:, 1, 0, :], sm(4)), k5, ['pwr'])
            for k in range(1, 11):
                cr, ci, nr_, ni_ = pwr[:, 0, k - 1, :], pwr[:, 1, k - 1, :], pwr[:, 0, k, :], pwr[:, 1, k, :]
                V('dve', lambda v, cr=cr: v.tensor_tensor(out=sm(6), in0=cr, in1=cr, op=ALU.mult), ['pwr'] + k5, k5)
                V('dve', lambda v, ci=ci: v.tensor_tensor(out=sm(7), in0=ci, in1=ci, op=ALU.mult), ['pwr'] + k5, k5)
                V('dve', lambda v, cr=cr, ci=ci, ni_=ni_: v.scalar_tensor_tensor(out=ni_, in0=cr, scalar=2.0, in1=ci, op0=ALU.mult, op1=ALU.mult),
                  ['pwr'], ['pwr'])
                V('dve', lambda v, nr_=nr_: v.tensor_tensor(out=nr_, in0=sm(6), in1=sm(7), op=ALU.subtract), k5, ['pwr'])
            V('dve', lambda v: v.tensor_scalar_mul(pwr[:, 1].rearrange("p k g -> p (k g)"), pwr[:, 1].rearrange("p k g -> p (k g)"), sgn[:, 0:1]),
              ['pwr', 'sgn'], ['pwr'])
            V('act', lambda a: a.activation(out=rv(HTf[:, 0:4100]), in_=ones[:, 0:1].to_broadcast([128, 4100]), func=AF.Identity, scale=0.0),
              ['ones'], ['XA', 'XB'])
            V('act', lambda a: a.activation(out=rv(HTf[:, 4100:6148]), in_=ones[:, 0:1].to_broadcast([128, 2048]), func=AF.Identity, scale=0.0),
              ['ones'], ['ZP', 'ZC'])
            for sc in list(range(NP // NQ)) + ['s']:
                t0, W = (NP, NS) if sc == 's' else (sc * NQ, NQ)
                modulate('act', hin, 0, l, 1, t0, W, [], 'hin')
                for q in range(4):
                    pa, pk = proj(w_inu_d, q, lambda k, W=W: r32(hin)[:, k, 0:W], KC, W, ['hin'])
                    dst = Us[:, q, :] if sc == 's' else U[:, q, t0:t0 + W]
                    S.op('act', lambda a, pa=pa, dst=dst: a.activation(out=rv(dst), in_=pa, func=AF.Identity), reads=[pk],
                         writes=['Us' if sc == 's' else 'U'])
            CTS = [(c0, 512) for c0 in range(0, NP, 512)]
            for q in range(4):
                S.op('pe', lambda p, q=q: p.transpose(p_s[0][:, 0:128], BbT[:, 8 * q:8 * q + 8, :].rearrange("p g c -> p (g c)"), ident[:]),
                     reads=['BbT', 'ident'], writes=['p_s0'])
                for gl in range(8):
                    S.op('dve', lambda v, gl=gl: v.tensor_scalar_mul(rv(ZP[:, gl, :]), p_s[0][:, 0:128], blkm[:, gl:gl + 1]),
                         reads=['p_s0', 'blkm'], writes=['ZP'])
                    S.op('act', lambda a, gl=gl, q=q: a.activation(out=rv(ZC[:, gl, gl * 16:(gl + 1) * 16]), in_=CTt[:, 8 * q + gl, :], func=AF.Identity),
                         reads=['CTt'], writes=['ZC'])
                for gl in range(8):
                    g = 8 * q + gl
                    for ci_, (c0, cw) in enumerate(CTS):
                        S.op('pe', lambda p, gl=gl, q=q, c0=c0, cw=cw: p.matmul(p_s[0][:, 0:cw], rv(ZP[:, gl, :]), rv(U[:, q, c0:c0 + cw]),
                                                                                start=True, stop=True), reads=['ZP', 'U'], writes=['p_s0'])
                        S.op('act', lambda a, c0=c0, cw=cw: a.activation(out=rv(XA[:, 2 + c0:2 + c0 + cw]), in_=p_s[0][:, 0:cw], func=AF.Identity),
                             reads=['p_s0'], writes=['XA'])
                    src, dst, sk, dk_ = XA, XB, 'XA', 'XB'
                    for k in range(11):
                        d = 1 << k
                        rt = Rt[:, k % 2, :]
                        S.op('dve', lambda v, k=k, g=g: v.tensor_scalar_mul(Rtmp, ident[:], pwr[:, 0, k, g:g + 1]),
                             reads=['ident', 'pwr'], writes=['Rtmp'])
                        S.op('dve', lambda v, rt=rt, k=k, g=g: v.scalar_tensor_tensor(out=rv(rt), in0=Jm, scalar=pwr[:, 1, k, g:g + 1], in1=Rtmp,
                                                                                     op0=ALU.mult, op1=ALU.add),
                             reads=['Jm', 'pwr', 'Rtmp'], writes=['Rt'])
                        if k == 0:
                            S.op('pe', lambda p, rt=rt, g=g: p.matmul(p_y[0][:, 0:NS], rv(rt), rv(H0(g)), start=True, stop=False),
                                 reads=['Rt', 'H0'], writes=['p_y0'])
                            S.op('pe', lambda p, gl=gl, q=q: p.matmul(p_y[0][:, 0:NS], rv(ZP[:, gl, :]), rv(Us[:, q, :]), start=False, stop=True),
                                 reads=['ZP', 'Us'], writes=['p_y0'])
                            S.op('act', lambda a, g=g: a.activation(out=rv(Hn(g)), in_=p_y[0][:, 0:NS], func=AF.Identity),
                                 reads=['p_y0'], writes=['Hn'])
                        for ci_, (c0, cw) in enumerate(CTS):
                            pt, pk = (p_y[ci_ % 2], f'p_y{ci_ % 2}')
                            lo = max(d - c0, 0)
                            lo -= lo % 2
                            has_r = lo < cw
                            S.op('pe', lambda p, pt=pt, src=src, c0=c0, cw=cw, has_r=has_r: p.matmul(
                                pt[:, 0:cw], rv(identR), rv(src[:, 2 + c0:2 + c0 + cw]), start=True, stop=not has_r),
                                reads=['identR', sk], writes=[pk])
                            if has_r:
                                S.op('pe', lambda p, pt=pt, rt=rt, src=src, c0=c0, cw=cw, lo=lo, d=d: p.matmul(
                                    pt[:, lo:cw], rv(rt), rv(src[:, 2 + c0 + lo - d:2 + c0 + cw - d]), start=False, stop=True),
                                    reads=['Rt', sk], writes=[pk])
                            S.op('act' if ci_ % 2 else 'dve', (lambda a, pt=pt, dst=dst, c0=c0, cw=cw: a.activation(
                                out=rv(dst[:, 2 + c0:2 + c0 + cw]), in_=pt[:, 0:cw], func=AF.Identity)) if ci_ % 2 else
                                (lambda v, pt=pt, dst=dst, c0=c0, cw=cw: v.tensor_copy(rv(dst[:, 2 + c0:2 + c0 + cw]), pt[:, 0:cw])),
                                reads=[pk], writes=[dk_])
                        src, dst, sk, dk_ = dst, src, dk_, sk
                    for ci_, (c0, cw) in enumerate(CTS):
                        S.op('pe', lambda p, ci_=ci_, gl=gl, src=src, c0=c0, cw=cw: p.matmul(
                            PBK[ci_][0][:, 0:cw], rv(ZC[:, gl, :]), rv(src[:, 2 + c0:2 + c0 + cw]), start=(gl == 0), stop=(gl == 7)),
                            reads=['ZC', sk], writes=[PBK[ci_][1]])
                    S.op('pe', lambda p, gl=gl, g=g: p.matmul(p_s[1][:, 0:NS], rv(ZC[:, gl, :]), rv(Hn(g)), start=(gl == 0), stop=(gl == 7)),
                         reads=['ZC', 'Hn'], writes=['p_s1'])
                    S.op('dve', lambda v, g=g, src=src: v.tensor_copy(HS[:, g:g + 1], src[:, 2 + NP - 1:2 + NP]), reads=[sk], writes=['HS'])
                GC = float(np.sqrt(2.0 / np.pi))
                for (pap, pk, uu, ukey, cw) in [(PBK[i][0], PBK[i][1], U[:, q, c0:c0 + cw_], 'U', cw_) for i, (c0, cw_) in enumerate(CTS)] + \
                                               [(p_s[1], 'p_s1', Us[:, q, :], 'Us', NS)]:
                    y_, t_ = ybuf[:, 0, 0:cw], ybuf[:, 1, 0:cw]
                    S.op('dve', lambda v, pap=pap, uu=uu, y_=y_, cw=cw, q=q: v.scalar_tensor_tensor(
                        out=y_, in0=uu, scalar=s5d[:, q:q + 1], in1=pap[:, 0:cw], op0=ALU.mult, op1=ALU.add),
                        reads=[pk, ukey, 's5d'], writes=['ybuf'])
                    S.op('dve', lambda v, y_=y_, t_=t_: v.tensor_tensor(out=t_, in0=y_, in1=y_, op=ALU.mult), reads=['ybuf'], writes=['ybuf'])
                    S.op('dve', lambda v, t_=t_: v.tensor_scalar(out=t_, in0=t_, scalar1=0.044715, scalar2=1.0, op0=ALU.mult, op1=ALU.add),
                         reads=['ybuf'], writes=['ybuf'])
                    S.op('dve', lambda v, y_=y_, t_=t_: v.tensor_tensor(out=t_, in0=t_, in1=y_, op=ALU.mult), reads=['ybuf'], writes=['ybuf'])
                    S.op('act', lambda a, t_=t_: a.activation(out=t_, in_=t_, func=AF.Tanh, scale=GC), reads=['ybuf'], writes=['ybuf'])
                    S.op('dve', lambda v, t_=t_: v.tensor_scalar(out=t_, in0=t_, scalar1=1.0, scalar2=0.5, op0=ALU.add, op1=ALU.mult),
                         reads=['ybuf'], writes=['ybuf'])
                    S.op('dve', lambda v, y_=y_, t_=t_, uu=uu: v.tensor_tensor(out=rv(uu), in0=t_, in1=y_, op=ALU.mult),
                         reads=['ybuf'], writes=[ukey])
            for (zsrc, zkey, c0, cw) in [(U, 'U', c0, cw) for (c0, cw) in CTS] + [(Us, 'Us', 0, NS)]:
                for q2 in range(4):
                    pa, pk = proj(s5wg_d, q2, lambda k, zsrc=zsrc, c0=c0, cw=cw: rv(zsrc[:, k, c0:c0 + cw]), 4, cw, [zkey])
                    S.op('act', lambda a, pa=pa, q2=q2, cw=cw: a.activation(out=ybuf[:, q2, 0:cw], in_=pa, func=AF.Sigmoid,
                                                                           bias=s5bg[:, q2:q2 + 1], scale=1.0),
                         reads=[pk, 's5bg'], writes=['ybuf'])
                for q2 in range(4):
                    S.op('dve', lambda v, zsrc=zsrc, q2=q2, c0=c0, cw=cw: v.tensor_tensor(
                        out=rv(zsrc[:, q2, c0:c0 + cw]), in0=zsrc[:, q2, c0:c0 + cw], in1=ybuf[:, q2, 0:cw], op=ALU.mult),
                        reads=['ybuf', zkey], writes=[zkey])
            S.dma('sp', 'st', lambda q: q.dma_start(out=s5hp_d, in_=HS[:]), reads=['HS'])
            S.dma('sp', 'st', lambda q: q.dma_start(out=s5hs_d.rearrange('p (k a) b -> p k (a b)', a=4), in_=HnA), reads=['Hn'])
            fence()
            return U, Us


        def gla_sample(Us5):
            l = 0
            fence()
            P16 = {n: ztf[:, i * NS:(i + 1) * NS] for i, n in enumerate(
                ['qS0', 'qS1', 'kS0', 'kS1', 'eg0', 'eg1', 'gS0', 'gS1', 'gS2', 'gS3', 'vS0', 'vS1', 'vS2', 'vS3', 'tS0', 'tS1'])}
            k_tok, v_tok, Km = ztf[0:NS, 256:512], ztf[0:NS, 512:1024], ztf[0:NS, 1024:1152]
            o_tok, sq_tok, st16 = ztf[0:NS, 1280:1792], ztf[0:NS, 1792:2304], ztf[0:NS, 2304:2312]
            Sg = zsqf[:, 0:2048].rearrange("p (b v) -> p b v", b=NS)
            Qm = zsqf[:, 2048:2304].rearrange("p (b c) -> p b c", b=NS)
            gkS, ogs = HTf[:, 0:NS], HTf[:, 64:64 + 4 * NS].rearrange("p (h b) -> p h b", h=4)
            KS = list(P16) + ['k_tok', 'v_tok', 'Km', 'o_tok', 'sq_tok', 'st16', 'Sg', 'Qm', 'gkS', 'ogs']
            ALIAS.extend(k for k in KS if k not in ALIAS)
            fence()
            modulate('act', hin, 0, l, 1, NP, NS, [], 'hin')
            rh = lambda k: r32(hin)[:, k, 0:NS]
            for f in range(2):
                pa, pk = proj(w_in_d, f, rh, KC, NS, ['hin'])
                S.op('act', lambda a, pa=pa, f=f: a.activation(out=P16[f'qS{f}'], in_=pa, func=AF.Identity, scale=64 ** -0.5),
                     reads=[pk], writes=[f'qS{f}'])
                pa, pk = proj(w_in_d, 2 + f, rh, KC, NS, ['hin'])
                S.op('dve', lambda v, pa=pa, f=f: v.tensor_copy(P16[f'kS{f}'], pa), reads=[pk], writes=[f'kS{f}'])
            for f in range(4):
                pa, pk = proj(w_in_d, 4 + f, rh, KC, NS, ['hin'])
                S.op('dve', lambda v, pa=pa, f=f: v.tensor_copy(P16[f'vS{f}'], pa), reads=[pk], writes=[f'vS{f}'])
                pa, pk = proj(w_in_d, 8 + f, rh, KC, NS, ['hin'])
                S.op('act', lambda a, pa=pa, f=f: a.activation(out=P16[f'gS{f}'], in_=pa, func=AF.Silu), reads=[pk], writes=[f'gS{f}'])
            pa, pk = proj(w_in_d, 12, rh, KC, NS, ['hin'])
            S.op('act', lambda a, pa=pa: a.activation(out=rv(gkS[0:16, :]), in_=pa[0:16, :], func=AF.Identity), reads=[pk], writes=['gkS'])
            for f in range(2):
                S.op('pe', lambda p, f=f: p.matmul(p_y[f][:, 0:NS], r32(Wgk)[0:16, f * 128:(f + 1) * 128], rv(gkS[0:16, :]),
                                                   start=True, stop=True), reads=['Wgk', 'gkS'], writes=[f'p_y{f}'])
                S.op('act', lambda a, f=f: a.activation(out=P16[f'tS{f}'], in_=p_y[f][:, 0:NS], func=AF.Sigmoid, bias=bgk[:, f:f + 1], scale=1.0),
                     reads=[f'p_y{f}', 'bgk'], writes=[f'tS{f}'])
                S.op('act', lambda a, f=f: a.activation(out=P16[f'tS{f}'], in_=P16[f'tS{f}'], func=AF.Ln), reads=[f'tS{f}'], writes=[f'tS{f}'])
                S.op('act', lambda a, f=f: a.activation(out=P16[f'eg{f}'], in_=P16[f'tS{f}'], func=AF.Exp, scale=1.0 / 16),
                     reads=[f'tS{f}'], writes=[f'eg{f}'])
            for nm, nf, dst, dkey in (('kS', 2, k_tok, 'k_tok'), ('vS', 4, v_tok, 'v_tok')):
                for f in range(nf):
                    S.op('pe', lambda p, nm=nm, f=f: p.transpose(p_s[f % 2][0:NS, 0:128], P16[f'{nm}{f}'], ident[:]),
                         reads=[f'{nm}{f}', 'ident'], writes=[f'p_s{f % 2}'])
                    S.op('act', lambda a, f=f, dst=dst: a.activation(out=dst[:, f * 128:(f + 1) * 128], in_=p_s[f % 2][0:NS, 0:128], func=AF.Identity),
                         reads=[f'p_s{f % 2}'], writes=[dkey])
            for f in range(2):
                S.dma('sp', 'ldg', lambda q, f=f: q.dma_start(out=Sg, in_=glas0_d[:, :, f, :]), writes=['Sg'])
                S.op('pool', lambda g: g.memset(Qm, 0.0), writes=['Qm'])
                for b in range(NS):
                    S.op('dve', lambda v, f=f, b=b: v.tensor_copy(Qm[:, b, b:b + 1], P16[f'qS{f}'][:, b:b + 1]), reads=[f'qS{f}', 'Qm'], writes=['Qm'])
                for hb in range(2):
                    h = 2 * f + hb
                    rows = slice(hb * 64, (hb + 1) * 64)
                    for b in range(NS):
                        S.op('dve', lambda v, f=f, b=b: v.tensor_scalar_mul(Km, k_tok[:, f * 128:(f + 1) * 128], ident[0:NS, b:b + 1]),
                             reads=['k_tok', 'ident'], writes=['Km'])
                        S.op('pe', lambda p, h=h: p.matmul(p_y[0][:, 0:128], Km, v_tok[:, h * 128:(h + 1) * 128], start=True, stop=True),
                             reads=['Km', 'v_tok'], writes=['p_y0'])
                        S.op('dve', lambda v, rows=rows, b=b, f=f: v.scalar_tensor_tensor(
                            out=Sg[rows, b, :], in0=Sg[rows, b, :], scalar=P16[f'eg{f}'][rows, b:b + 1], in1=p_y[0][rows, 0:128],
                            op0=ALU.mult, op1=ALU.add), reads=['Sg', f'eg{f}', 'p_y0'], writes=['Sg'])
                        S.op('pe', lambda p, rows=rows, b=b: p.matmul(p_y[1][0:NS, 0:128], Qm[rows, b, :], Sg[rows, b, :],
                                                                      start=(b == 0), stop=(b == NS - 1)), reads=['Qm', 'Sg'], writes=['p_y1'])
                    S.op('act', lambda a, h=h: a.activation(out=o_tok[:, h * 128:(h + 1) * 128], in_=p_y[1][0:NS, 0:128], func=AF.Identity),
                         reads=['p_y1'], writes=['o_tok'])
                S.dma('sp', 'st', lambda q, f=f: q.dma_start(out=glas_d[:, :, f, :], in_=Sg), reads=['Sg'])
            o3, q3 = o_tok.rearrange("p (h v) -> p h v", h=4), sq_tok.rearrange("p (h v) -> p h v", h=4)
            S.op('dve', lambda v: v.tensor_tensor(out=sq_tok, in0=o_tok, in1=o_tok, op=ALU.mult), reads=['o_tok'], writes=['sq_tok'])
            S.op('dve', lambda v: v.tensor_reduce(out=st16[:, 0:4], in_=q3, op=ALU.add, axis=mybir.AxisListType.X), reads=['sq_tok'], writes=['st16'])
            S.op('dve', lambda v: v.tensor_scalar(out=st16[:, 0:4], in0=st16[:, 0:4], scalar1=1.0 / 128, scalar2=1e-5, op0=ALU.mult, op1=ALU.add),
                 reads=['st16'], writes=['st16'])
            S.op('act', lambda a: a.activation(out=st16[:, 0:4], in_=st16[:, 0:4], func=AF.Sqrt), reads=['st16'], writes=['st16'])
            S.op('dve', lambda v: v.reciprocal(st16[:, 0:4], st16[:, 0:4]), reads=['st16'], writes=['st16'])
            S.op('dve', lambda v: v.tensor_tensor(out=o3, in0=o3, in1=st16[:, 0:4].unsqueeze(2).to_broadcast([NS, 4, 128]), op=ALU.mult),
                 reads=['o_tok', 'st16'], writes=['o_tok'])
            for h in range(4):
                S.op('pe', lambda p, h=h: p.transpose(p_s[h % 2][:, 0:NS], o_tok[:, h * 128:(h + 1) * 128], ident[0:NS, 0:NS]),
                     reads=['o_tok', 'ident'], writes=[f'p_s{h % 2}'])
                S.op('dve', lambda v, h=h: v.scalar_tensor_tensor(out=rv(ogs[:, h, :]), in0=p_s[h % 2][:, 0:NS], scalar=gng[:, 0:1],
                                                                  in1=P16[f'gS{h}'], op0=ALU.mult, op1=ALU.mult),
                     reads=[f'p_s{h % 2}', 'gng', f'gS{h}'], writes=['ogs'])
            fence()
            for dk in range(KC):
                pa, pk = proj(w_out_d, dk, lambda k: rv(ogs[:, k, :]) if k < 4 else rv(Us5[:, k - 4, :]), KC, NS, ['ogs', 'Us'])
                S.op('act', lambda a, pa=pa, dk=dk: a.activation(out=zsq[:, dk, 0:NS], in_=pa, func=AF.Identity), reads=[pk], writes=['zsq_y', 'zsq'])
            post_norm_tile_from_sbuf(l, 1, NP, 1.0, NS)
            fence()

        def mix0_sublayer():
            l = 0
            U5, Us5 = s5_phase()
            fence()
            for f in range(2):
                S.op('act', lambda a, f=f: a.activation(out=rv(Sst[f]), in_=ones[:], func=AF.Identity, scale=0.0),
                     reads=['ones'], writes=[f'Sst{f}'])
            for sc in range(NP // NQ):
                t0 = sc * NQ
                modulate('act', hin, 0, l, 1, t0, NQ, [], 'hin')
                rh = lambda k: r32(hin)[:, k, 0:NQ]
                for f in range(2):
                    pa, pk = proj(w_in_d, f, rh, KC, NQ, ['hin'])
                    S.op('act', lambda a, pa=pa, f=f: a.activation(out=T[f'qT{f}'], in_=pa, func=AF.Identity,
                                                                   scale=64 ** -0.5), reads=[pk], writes=[f'qT{f}'])
                    pa, pk = proj(w_in_d, 2 + f, rh, KC, NQ, ['hin'])
                    S.op('dve', lambda v, pa=pa, f=f: v.tensor_copy(T[f'kT{f}'], pa), reads=[pk], writes=[f'kT{f}'])
                for f in range(4):
                    pa, pk = proj(w_in_d, 4 + f, rh, KC, NQ, ['hin'])
                    S.op('dve', lambda v, pa=pa, f=f: v.tensor_copy(T[f'vT{f}'], pa), reads=[pk], writes=[f'vT{f}'])
                    pa, pk = proj(w_in_d, 8 + f, rh, KC, NQ, ['hin'])
                    S.op('act', lambda a, pa=pa, f=f: a.activation(out=T[f'gT{f}'], in_=pa, func=AF.Silu),
                         reads=[pk], writes=[f'gT{f}'])
                pa, pk = proj(w_in_d, 12, rh, KC, NQ, ['hin'])
                S.op('act', lambda a, pa=pa: a.activation(out=rv(T['gkl'][0:16, :]), in_=pa[0:16, :], func=AF.Identity),
                     reads=[pk], writes=['gkl'])
                for f in range(2):
                    S.op('pe', lambda p, f=f: p.matmul(p_y[f][:, 0:NQ], r32(Wgk)[0:16, f * 128:(f + 1) * 128],
                                                       rv(T['gkl'][0:16, :]), start=True, stop=True),
                         reads=['Wgk', 'gkl'], writes=[f'p_y{f}'])
                    S.op('act', lambda a, f=f: a.activation(out=T[f'tmp{f}'], in_=p_y[f][:, 0:NQ], func=AF.Sigmoid,
                                                            bias=bgk[:, f:f + 1], scale=1.0),
                         reads=[f'p_y{f}', 'bgk'], writes=[f'tmp{f}'])
                    S.op('act', lambda a, f=f: a.activation(out=T[f'tmp{f}'], in_=T[f'tmp{f}'], func=AF.Ln),
                         reads=[f'tmp{f}'], writes=[f'tmp{f}'])
                    S.op('dve', lambda v, f=f: v.tensor_tensor_scan(out=T[f'L{f}'], data0=rstm[:], data1=T[f'tmp{f}'],
                                                                    initial=0.0, op0=ALU.mult, op1=ALU.add),
                         reads=['rstm', f'tmp{f}'], writes=[f'L{f}'])
                    last = T[f'L{f}'].rearrange("p (c t) -> p c t", c=2)[:, :, 127]
                    S.op('dve', lambda v, f=f, last=last: v.tensor_scalar_mul(bl[:, f, :], last, 1.0 / 16),
                         reads=[f'L{f}'], writes=['bl'])
                    S.op('act', lambda a, f=f: a.activation(out=dec[:, f, :], in_=bl[:, f, :], func=AF.Exp),
                         reads=['bl'], writes=['dec'])
                    for nm, src, sgn in (('qin', 'qT', 1.0), ('kin', 'kT', -1.0)):
                        S.op('act', lambda a, f=f, sgn=sgn: a.activation(out=T[f'tmp{f}'], in_=T[f'L{f}'], func=AF.Exp,
                                                                         scale=sgn / 16), reads=[f'L{f}'], writes=[f'tmp{f}'])
                        S.op('dve', lambda v, f=f, nm=nm, src=src: v.tensor_tensor(
                            out=rv(T[f'{nm}{f}']), in0=T[f'{src}{f}'], in1=T[f'tmp{f}'], op=ALU.mult),
                            reads=[f'{src}{f}', f'tmp{f}'], writes=[f'{nm}{f}'])
                    for c in range(2):
                        S.op('act', lambda a, f=f, c=c: a.activation(
                            out=T[f'tmp{f}'][:, c * 128:(c + 1) * 128], in_=T[f'L{f}'][:, c * 128:(c + 1) * 128],
                            func=AF.Exp, scale=-1.0 / 16, bias=bl[:, f, c:c + 1]),
                            reads=[f'L{f}', 'bl'], writes=[f'tmp{f}'])
                    S.op('dve', lambda v, f=f: v.tensor_tensor(out=T[f'kst{f}'], in0=T[f'kT{f}'], in1=T[f'tmp{f}'],
                                                               op=ALU.mult), reads=[f'kT{f}', f'tmp{f}'], writes=[f'kst{f}'])
                for c in range(2):
                    cs = slice(c * 128, (c + 1) * 128)
                    for nm, nf, dst, dk_ in (('vT', 4, vtok, 'vtok'), ('kst', 2, ksttok, 'ksttok')):
                        for f in range(nf):
                            S.op('pe', lambda p, nm=nm, f=f, cs=cs: p.transpose(p_s[f % 2][:, 0:128], T[f'{nm}{f}'][:, cs], ident[:]),
                                 reads=[f'{nm}{f}', 'ident'], writes=[f'p_s{f % 2}'])
                            S.op('act', lambda a, f=f, c=c, dst=dst: a.activation(
                                out=rv(dst[:, c, f * 128:(f + 1) * 128]), in_=p_s[f % 2][:, 0:128], func=AF.Identity),
                                reads=[f'p_s{f % 2}'], writes=[dk_])
                def gla_chain(c, h):
                    cs = slice(c * 128, (c + 1) * 128)
                    f, r0 = h // 2, (h % 2) * 64
                    rows = slice(r0, r0 + 64)
                    w_ = h // 2
                    Pm_ = HTf[:, o0 + 1664 * w_:o0 + 1664 * w_ + 128] if w_ else Pm
                    sq_ = zsqf[:, 8 * NQ + 256 * w_:8 * NQ + 256 * w_ + 128]
                    rs_ = zsqf[:, 8 * NQ + 256 * w_ + 128:8 * NQ + 256 * w_ + 256]
                    kP, kS, kR = f'Pm{w_}', f'sqt{w_}', f'rst{w_}'
                    (bS, kbS), (bO, kbO), (bT, kbT), (bU, kbU) = ((p_y[0], 'p_y0'), (p_y[1], 'p_y1'), (p_s[0], 'p_s0'), (p_s[1], 'p_s1')) if w_ == 0 else \
                                                                 ((p_g[0], 'p_g0'), (p_g[1], 'p_g1'), (p_u[0], 'p_u0'), (p_u[1], 'p_u1'))
                    S.op('pe', lambda p: p.matmul(bS[:, 0:128], rv(T[f'kin{f}'][rows, cs]), rv(T[f'qin{f}'][rows, cs]), start=True, stop=True),
                         reads=[f'kin{f}', f'qin{f}'], writes=[kbS])
                    yield
                    S.op('dve', lambda v: v.tensor_tensor(out=rv(Pm_), in0=bS[:, 0:128], in1=mask2[:, 128:256], op=ALU.mult),
                         reads=[kbS, 'mask2'], writes=[kP])
                    yield
                    S.op('pe', lambda p: p.matmul(bO[:, 0:128], rv(vtok[:, c, h * 128:(h + 1) * 128]), rv(Pm_), start=True, stop=False),
                         reads=['vtok', kP], writes=[kbO])
                    S.op('pe', lambda p: p.matmul(bO[:, 0:128], rv(Sst[f][rows, :]), rv(T[f'qin{f}'][rows, cs]), start=False, stop=True),
                         reads=[f'Sst{f}', f'qin{f}'], writes=[kbO])
                    S.op('pe', lambda p: p.matmul(bU[:, 0:128], rv(ksttok[:, c, f * 128:(f + 1) * 128]), rv(vtok[:, c, h * 128:(h + 1) * 128]),
                                                  start=True, stop=True), reads=['ksttok', 'vtok'], writes=[kbU])
                    yield
                    S.op('act', lambda a: a.activation(out=sq_, in_=bO[:, 0:128], func=AF.Square), reads=[kbO], writes=[kS])
                    S.op('dve', lambda v: v.scalar_tensor_tensor(out=rv(Sst[f][rows, :]), in0=Sst[f][rows, :], scalar=dec[rows, f, c:c + 1],
                                                                 in1=bU[rows, 0:128], op0=ALU.mult, op1=ALU.add),
                         reads=[f'Sst{f}', 'dec', kbU, kbO], writes=[f'Sst{f}'])
                    yield
                    S.op('pe', lambda p: p.matmul(bT[:, 0:128], ones[:], sq_, start=True, stop=True), reads=['ones', kS], writes=[kbT])
                    yield
                    S.op('dve', lambda v: v.tensor_scalar(out=rs_, in0=bT[:, 0:128], scalar1=1.0 / 128, scalar2=1e-5, op0=ALU.mult, op1=ALU.add),
                         reads=[kbT], writes=[kR])
                    yield
                    S.op('act', lambda a: a.activation(out=rs_, in_=rs_, func=AF.Sqrt), reads=[kR], writes=[kR])
                    yield
                    S.op('dve', lambda v: v.reciprocal(rs_, rs_), reads=[kR], writes=[kR])
                    S.op('dve', lambda v: v.scalar_tensor_tensor(out=sq_, in0=bO[:, 0:128], scalar=gng[:, 0:1], in1=rs_, op0=ALU.mult, op1=ALU.mult),
                         reads=[kbO, 'gng', kR, kS], writes=[kS])
                    S.op('dve', lambda v: v.tensor_tensor(out=rv(T[f'og{h}'][:, cs]), in0=sq_, in1=T[f'gT{h}'][:, cs], op=ALU.mult),
                         reads=[kS, f'gT{h}'], writes=[f'og{h}'])
                    yield

                def run_il(gens):
                    gens = list(gens)
                    while gens:
                        for g_ in list(gens):
                            try:
                                next(g_)
                            except StopIteration:
                                gens.remove(g_)
                for c in range(2):
                    run_il([gla_chain(c, 0), gla_chain(c, 2)])
                    run_il([gla_chain(c, 1), gla_chain(c, 3)])
                fence(light=True)
                for dk in range(KC):
                    pa, pk = proj(w_out_d, dk, lambda k: rv(T[f'og{k}']) if k < 4 else rv(U5[:, k - 4, t0:t0 + NQ]), KC, NQ,
                                  [f'og{k}' for k in range(4)] + ['U'])
                    S.op('act', lambda a, pa=pa, dk=dk: a.activation(out=zsq[:, dk, 0:NQ], in_=pa, func=AF.Identity),
                         reads=[pk, 'vtok', 'ksttok'], writes=['zsq_y', 'zsq'])
                post_norm_tile_from_sbuf(l, 1, t0, 1.0, NQ)
                fence(light=True)
            gla_sample(Us5)
            for f in range(2):
                S.dma('sp', 'st', lambda q, f=f: q.dma_start(
                    out=st_out['gla_p'][0:1, f * 16384:(f + 1) * 16384].rearrange("o (r v) -> (o r) v", v=128),
                    in_=Sst[f]), reads=[f'Sst{f}'])
            fence()


        def projp(parts, N, rkeys):
            sl = wslot[0] % 4; wslot[0] += 1
            buf = wgu[sl // 2][:, sl % 2]
            key = f"wslot{sl}"
            for i, (src, _) in enumerate(parts):
                kr, nc_ = src.shape
                S.dma('pool', key, lambda q, i=i, src=src, kr=kr, nc_=nc_: q.dma_start(out=rv(buf[0:kr, i, 0:nc_]), in_=src),
                      writes=[key])
            pt, pk = PBK[pslot[0] % 4]; pslot[0] += 1
            for i, (src, rhs) in enumerate(parts):
                kr = src.shape[0]
                S.op('pe', lambda p, i=i, kr=kr, rhs=rhs: p.matmul(pt[:, 0:N], rv(buf[0:kr, i, :]), rhs,
                                                                   start=(i == 0), stop=(i == len(parts) - 1)),
                     reads=[key] + rkeys, writes=[pk])
            return pt[:, 0:N], pk

        def mix1_sublayer():
            l = 1
            hinf = hin[:].rearrange("p a b -> p (a b)")
            XV = HTf[:, 0:4 * KC * NQ].rearrange("p (v k n) -> p v k n", v=4, k=KC)
            o1 = 4 * KC * NQ
            R1 = ['lw', 'la', 'lg0', 'lg1', 'Bt', 'Kt']
            TR = {n: HTf[:, o1 + i * NQ:o1 + (i + 1) * NQ] for i, n in enumerate(R1)}
            o2 = o1 + len(R1) * NQ
            AR = HTf[:, o2:o2 + 2 * NQ].rearrange("p (a n) -> p a n", a=2)
            o3 = o2 + 2 * NQ
            Vtok, Bhat, Khat = (HTf[:, o3 + i * 128:o3 + (i + 1) * 128] for i in range(3))
            o4 = o3 + 384
            AabAbr, AakAkr = HTf[:, o4:o4 + 256], HTf[:, o4 + 256:o4 + 512]
            Xm, XTm, Tinv = (HTf[:, o4 + 512 + i * 128:o4 + 512 + (i + 1) * 128] for i in range(3))
            RH, UT = HTf[:, o4 + 896:o4 + 960], HTf[:, o4 + 960:o4 + 1024]
            assert o4 + 4096 <= HTf.shape[1]
            og = [hin[:, f, 256:512] for f in range(KC)]
            PA = ['rT', 'kT', 'vT', 'lgw', 'Lw', 'asg', 'kk', 'km', 'gt', 'tmp']
            TP = {n: ztf[:, i * NQ:(i + 1) * NQ] for i, n in enumerate(PA)}
            xx = zsqf[:, 0:KC * NQ].rearrange("p (k n) -> p k n", k=KC)
            tm2, ytok = zsqf[:, KC * NQ:KC * NQ + NQ], zsqf[:, KC * NQ + NQ:KC * NQ + 2 * NQ]
            KEYS1 = R1 + PA + ['XV', 'AR', 'Vtok', 'Bhat', 'Khat', 'og', 'xx', 'tm2', 'ytok', 'Swk0', 'Swk1'] + \
                    [f'{n}{c_}{hb}' for c_ in range(2) for hb in range(2) for n in ('Aab', 'Aak', 'X', 'XT', 'T', 'RH', 'UT')]
            ALIAS.extend(k for k in KEYS1 if k not in ALIAS)
            fence()
            for f in range(KC):
                S.op('act', lambda a, f=f: a.activation(out=r32(Swk)[:, f, :], in_=ones[:, 0:64], func=AF.Identity, scale=0.0),
                     reads=['ones'], writes=['Swk', 'Swk0', 'Swk1'])
            S.op('pool', lambda g: g.memset(hprev[:], 0.0), writes=['hprev'])
            for k in range(KC):
                S.op('act', lambda a, k=k: a.activation(out=shp[:, k:k + 1], in_=xT[:, k, NP - 1:NP], func=AF.Identity,
                                                        scale=mods[l][:, 4 * KC + k, 16:17], bias=mods[l][:, 3 * KC + k, 16:17]),
                     reads=['xT', f"mods{l}"], writes=['shp'])
            vec = lambda i, f: rvec[:, i, f:f + 1]
            EXPM05 = float(np.exp(-0.5))
            for sc in range(NP // NQ):
                t0 = sc * NQ
                modulate('act', hin, 0, l, 1, t0, NQ, [], 'hin')
                S.op('dve', lambda v: v.tensor_tensor(out=xx[:, :, 1:NQ], in0=hin[:, :, 0:NQ - 1], in1=hin[:, :, 1:NQ],
                                                      op=ALU.subtract), reads=['hin'], writes=['xx'])
                S.op('dve', lambda v: v.tensor_tensor(out=xx[:, :, 0:1], in0=hprev[:].unsqueeze(2), in1=hin[:, :, 0:1],
                                                      op=ALU.subtract), reads=['hin', 'hprev', 'xx'], writes=['xx'])
                S.op('dve', lambda v: v.tensor_copy(hprev[:].unsqueeze(2), hin[:, :, NQ - 1:NQ]), reads=['hin', 'xx'],
                     writes=['hprev'])

                def xvar(slot, i):
                    for k in range(KC):
                        S.op('dve', lambda v, k=k: v.scalar_tensor_tensor(
                            out=rv(XV[:, slot, k, :]), in0=xx[:, k, :], scalar=rmu[:, i, k:k + 1], in1=hin[:, k, 0:NQ],
                            op0=ALU.mult, op1=ALU.add), reads=['xx', 'hin', 'rmu'], writes=['XV'])
                xs = lambda slot: (lambda k: rv(XV[:, slot, k, :]))
                xvar(0, 0); xvar(1, 2); xvar(2, 3)
                xvar(3, 1)
                pa, pk = proj(rw['w1T'], 0, xs(3), KC, NQ, ['XV'])
                S.op('act', lambda a, pa=pa: a.activation(out=rv(TR['lw'][0:64, :]), in_=pa[0:64, :], func=AF.Tanh),
                     reads=[pk], writes=['lw'])
                xvar(3, 4)
                pa, pk = proj(rw['a1T'], 0, xs(3), KC, NQ, ['XV'])
                S.op('act', lambda a, pa=pa: a.activation(out=rv(TR['la'][0:64, :]), in_=pa[0:64, :], func=AF.Identity),
                     reads=[pk], writes=['la'])
                xvar(3, 5)
                pa, pk = proj(rw['g1T'], 0, xs(3), KC, NQ, ['XV'])
                S.op('act', lambda a, pa=pa: a.activation(out=rv(TR['lg0']), in_=pa, func=AF.Sigmoid), reads=[pk], writes=['lg0'])
                pa, pk = proj(rw['g1T'], 1, xs(3), KC, NQ, ['XV'])
                S.op('act', lambda a, pa=pa: a.activation(out=rv(TR['lg1'][0:32, :]), in_=pa[0:32, :], func=AF.Sigmoid),
                     reads=[pk], writes=['lg1'])
                for f in range(KC):
                    fc = f * 128
                    for nm, wn, sl_ in (('rT', 'w_r', 0), ('kT', 'w_k', 1), ('vT', 'w_v', 2)):
                        pa, pk = proj(rw[wn + 'T'], f, xs(sl_), KC, NQ, ['XV'])
                        S.op('act', lambda a, pa=pa, nm=nm: a.activation(out=TP[nm], in_=pa, func=AF.Identity),
                             reads=[pk], writes=[nm])
                    pa, pk = projp([(rw['w2'][0:64, fc:fc + 128], rv(TR['lw'][0:64, :]))], NQ, ['lw'])
                    S.op('act', lambda a, pa=pa, f=f: a.activation(out=TP['lgw'], in_=pa, func=AF.Sigmoid, bias=vec(0, f), scale=1.0),
                         reads=[pk, 'rvec'], writes=['lgw'])
                    S.op('dve', lambda v: v.tensor_scalar_mul(TP['lgw'], TP['lgw'], -EXPM05), reads=['lgw'], writes=['lgw'])
                    S.op('dve', lambda v: v.tensor_tensor_scan(out=TP['Lw'], data0=rstm[:], data1=TP['lgw'], initial=0.0,
                                                               op0=ALU.mult, op1=ALU.add), reads=['rstm', 'lgw'], writes=['Lw'])
                    S.op('dve', lambda v: v.tensor_copy(llast[:], TP['Lw'].rearrange("p (c t) -> p c t", c=2)[:, :, 127]),
                         reads=['Lw'], writes=['llast'])
                    S.op('act', lambda a: a.activation(out=pcl[:], in_=llast[:], func=AF.Exp), reads=['llast'], writes=['pcl'])
                    pa, pk = projp([(rw['a2'][0:64, fc:fc + 128], rv(TR['la'][0:64, :]))], NQ, ['la'])
                    S.op('act', lambda a, pa=pa, f=f: a.activation(out=TP['asg'], in_=pa, func=AF.Sigmoid, bias=vec(1, f), scale=1.0),
                         reads=[pk, 'rvec'], writes=['asg'])
                    pa, pk = projp([(rw['g2'][0:128, fc:fc + 128], rv(TR['lg0'])), (rw['g2'][128:160, fc:fc + 128], rv(TR['lg1'][0:32, :]))],
                                   NQ, ['lg0', 'lg1'])
                    S.op('act', lambda a, pa=pa: a.activation(out=TP['gt'], in_=pa, func=AF.Identity), reads=[pk], writes=['gt'])
                    S.op('dve', lambda v, f=f: v.tensor_scalar_mul(TP['kk'], TP['kT'], vec(2, f)), reads=['kT', 'rvec'], writes=['kk'])
                    S.op('act', lambda a: a.activation(out=TP['tmp'], in_=TP['kk'], func=AF.Square), reads=['kk'], writes=['tmp'])
                    S.op('pe', lambda p: p.matmul(p_s[0][:, 0:NQ], bones[:], TP['tmp'], start=True, stop=True),
                         reads=['bones', 'tmp'], writes=['p_s0'])
                    S.op('act', lambda a: a.activation(out=TP['tmp'], in_=p_s[0][:, 0:NQ], func=AF.Sqrt), reads=['p_s0'], writes=['tmp'])
                    S.op('dve', lambda v: v.tensor_scalar_max(TP['tmp'], TP['tmp'], 1e-12), reads=['tmp'], writes=['tmp'])
                    S.op('dve', lambda v: v.reciprocal(TP['tmp'], TP['tmp']), reads=['tmp'], writes=['tmp'])
                    S.op('dve', lambda v: v.tensor_tensor(out=TP['kk'], in0=TP['kk'], in1=TP['tmp'], op=ALU.mult),
                         reads=['kk', 'tmp'], writes=['kk'])
                    S.op('dve', lambda v, f=f: v.tensor_scalar(out=TP['km'], in0=TP['asg'], scalar1=-1.0, scalar2=vec(3, f),
                                                               op0=ALU.add, op1=ALU.mult), reads=['asg', 'rvec'], writes=['km'])
                    S.op('dve', lambda v: v.scalar_tensor_tensor(out=TP['km'], in0=TP['km'], scalar=1.0, in1=TP['kT'],
                                                                 op0=ALU.add, op1=ALU.mult), reads=['km', 'kT'], writes=['km'])
                    S.op('dve', lambda v: v.tensor_tensor(out=TP['tmp'], in0=TP['Lw'], in1=TP['lgw'], op=ALU.subtract),
                         reads=['Lw', 'lgw'], writes=['tmp'])
                    S.op('act', lambda a: a.activation(out=TP['tmp'], in_=TP['tmp'], func=AF.Exp), reads=['tmp'], writes=['tmp'])
                    S.op('dve', lambda v: v.scalar_tensor_tensor(out=rv(AR[:, 0, :]), in0=TP['kk'], scalar=-1.0, in1=TP['tmp'],
                                                                 op0=ALU.mult, op1=ALU.mult), reads=['kk', 'tmp'], writes=['AR'])
                    S.op('act', lambda a: a.activation(out=TP['tmp'], in_=TP['Lw'], func=AF.Exp), reads=['Lw', 'AR'], writes=['tmp'])
                    S.op('dve', lambda v: v.tensor_tensor(out=rv(AR[:, 1, :]), in0=TP['rT'], in1=TP['tmp'], op=ALU.mult),
                         reads=['rT', 'tmp'], writes=['AR'])
                    S.op('act', lambda a: a.activation(out=TP['tmp'], in_=TP['Lw'], func=AF.Exp, scale=-1.0), reads=['Lw', 'AR'], writes=['tmp'])
                    S.op('dve', lambda v: v.tensor_tensor(out=tm2, in0=TP['kk'], in1=TP['asg'], op=ALU.mult),
                         reads=['kk', 'asg'], writes=['tm2'])
                    S.op('dve', lambda v: v.tensor_tensor(out=rv(TR['Bt']), in0=tm2, in1=TP['tmp'], op=ALU.mult),
                         reads=['tm2', 'tmp'], writes=['Bt'])
                    S.op('dve', lambda v: v.tensor_tensor(out=rv(TR['Kt']), in0=TP['km'], in1=TP['tmp'], op=ALU.mult),
                         reads=['km', 'tmp'], writes=['Kt'])
                    S.op('dve', lambda v, f=f: v.scalar_tensor_tensor(out=TP['asg'], in0=TP['rT'], scalar=vec(4, f), in1=TP['km'],
                                                                      op0=ALU.mult, op1=ALU.mult),
                         reads=['rT', 'km', 'rvec', 'tm2', 'asg'], writes=['asg'])
                    S.op('pe', lambda p: p.matmul(p_s[1][:, 0:NQ], bones[:], TP['asg'], start=True, stop=True),
                         reads=['bones', 'asg'], writes=['p_s1'])
                    S.op('dve', lambda v: v.tensor_tensor(out=TP['rT'], in0=p_s[1][:, 0:NQ], in1=TP['vT'], op=ALU.mult),
                         reads=['p_s1', 'vT', 'AR', 'asg'], writes=['rT'])
                    INVB = {(0, 0): ((p_y[0], 'p_y0'), (p_y[1], 'p_y1')), (0, 1): ((p_g[0], 'p_g0'), (p_g[1], 'p_g1')),
                            (1, 0): ((p_u[0], 'p_u0'), (p_u[1], 'p_u1')), (1, 1): ((p_s[0], 'p_s0'), (p_s[1], 'p_s1'))}

                    def scratch(c, hb):
                        ob = o4 + (2 * c + hb) * 1024
                        d_ = dict(Aab=HTf[:, ob:ob + 256], Aak=HTf[:, ob + 256:ob + 512], RH=HTf[:, ob + 896:ob + 960], UT=HTf[:, ob + 960:ob + 1024])
                        d_['X'], d_['XT'], d_['T'] = (HTf[:, ob + 512 + i * 128:ob + 512 + (i + 1) * 128] for i in range(3))
                        d_['k'] = {n: f'{n}{c}{hb}' for n in ('Aab', 'Aak', 'X', 'XT', 'T', 'RH', 'UT')}
                        return d_

                    def inv_chain(c, hb, f=f):
                        cs = slice(c * 128, (c + 1) * 128)
                        rows = slice(hb * 64, (hb + 1) * 64)
                        arc = AR[rows, :, cs]
                        sc_ = scratch(c, hb); K_ = sc_['k']
                        (bA, kA), (bB, kB) = INVB[(c, hb)]
                        for lhs, lk, dst, dk_ in ((TR['Bt'], 'Bt', sc_['Aab'], K_['Aab']), (TR['Kt'], 'Kt', sc_['Aak'], K_['Aak'])):
                            S.op('pe', lambda p, lhs=lhs: p.matmul(bA[:, 0:256], rv(lhs[rows, cs]), rv(arc), start=True, stop=True),
                                 reads=[lk, 'AR'], writes=[kA])
                            S.op('dve', lambda v, dst=dst: v.tensor_tensor(out=rv(dst), in0=bA[:, 0:256], in1=mask2[:], op=ALU.mult),
                                 reads=[kA, 'mask2'], writes=[dk_])
                            yield
                        S.op('pe', lambda p: p.matmul(bB[:, 0:128], rv(AR[rows, 0, cs]), rv(TR['Bt'][rows, cs]), start=True, stop=True),
                             reads=['AR', 'Bt'], writes=[kB])
                        S.op('dve', lambda v: v.tensor_tensor(out=rv(sc_['XT']), in0=bB[:, 0:128], in1=maskLT[:], op=ALU.mult),
                             reads=[kB, 'maskLT'], writes=[K_['XT']])
                        S.op('act', lambda a: a.activation(out=rv(sc_['X']), in_=sc_['Aab'][:, 0:128], func=AF.Identity), reads=[K_['Aab']], writes=[K_['X']])
                        S.op('dve', lambda v: v.tensor_tensor(out=rv(sc_['T']), in0=sc_['Aab'][:, 0:128], in1=ident[:], op=ALU.add),
                             reads=[K_['Aab'], 'ident'], writes=[K_['T']])
                        yield
                        for lv in range(6):
                            S.op('pe', lambda p: p.matmul(bA[:, 0:128], rv(sc_['XT']), rv(sc_['X']), start=True, stop=True), reads=[K_['XT'], K_['X']], writes=[kA])
                            S.op('pe', lambda p: p.matmul(bB[:, 0:128], rv(sc_['X']), rv(sc_['XT']), start=True, stop=True), reads=[K_['XT'], K_['X']], writes=[kB])
                            yield
                            S.op('act', lambda a: a.activation(out=rv(sc_['X']), in_=bA[:, 0:128], func=AF.Identity), reads=[kA], writes=[K_['X']])
                            S.op('dve', lambda v: v.tensor_copy(rv(sc_['XT']), bB[:, 0:128]), reads=[kB], writes=[K_['XT']])
                            yield
                            S.op('pe', lambda p: p.matmul(bA[:, 0:128], rv(sc_['XT']), rv(sc_['T']), start=True, stop=True), reads=[K_['XT'], K_['T']], writes=[kA])
                            yield
                            S.op('dve', lambda v: v.tensor_tensor(out=rv(sc_['T']), in0=bA[:, 0:128], in1=sc_['T'], op=ALU.add), reads=[kA, K_['T']], writes=[K_['T']])
                            yield

                    def run_interleaved(gens):
                        gens = list(gens)
                        while gens:
                            for g_ in list(gens):
                                try:
                                    next(g_)
                                except StopIteration:
                                    gens.remove(g_)
                    run_interleaved([inv_chain(c_, hb_) for c_ in range(2) for hb_ in range(2)])
                    for c in range(2):
                        cs = slice(c * 128, (c + 1) * 128)
                        S.op('act', lambda a, c=c, cs=cs: a.activation(out=TP['tmp'][:, cs], in_=TP['Lw'][:, cs], func=AF.Exp,
                                                                       scale=-1.0, bias=llast[:, c:c + 1]),
                             reads=['Lw', 'llast', 'Bt', 'Kt'], writes=['tmp'])
                        S.op('dve', lambda v, cs=cs: v.tensor_tensor(out=tm2[:, cs], in0=tm2[:, cs], in1=TP['tmp'][:, cs], op=ALU.mult),
                             reads=['tm2', 'tmp'], writes=['tm2'])
                        S.op('dve', lambda v, cs=cs: v.tensor_tensor(out=TP['tmp'][:, cs], in0=TP['km'][:, cs], in1=TP['tmp'][:, cs],
                                                                     op=ALU.mult), reads=['km', 'tmp'], writes=['tmp'])
                        for src, skey, dst, dkey in ((TP['vT'], 'vT', Vtok, 'Vtok'), (tm2, 'tm2', Bhat, 'Bhat'), (TP['tmp'], 'tmp', Khat, 'Khat')):
                            S.op('pe', lambda p, src=src, cs=cs: p.transpose(p_s[0][:, 0:128], src[:, cs], ident[:]),
                                 reads=[skey, 'ident'], writes=['p_s0'])
                            S.op('act', lambda a, dst=dst: a.activation(out=rv(dst), in_=p_s[0][:, 0:128], func=AF.Identity),
                                 reads=['p_s0'], writes=[dkey])
                        def head_chain(hb, cs=cs, c=c, f=f):
                            rows = slice(hb * 64, (hb + 1) * 64)
                            sc_ = scratch(c, hb); K_ = sc_['k']
                            AabAbr_, AakAkr_, Tinv_, RH_, UT_ = sc_['Aab'], sc_['Aak'], sc_['T'], sc_['RH'], sc_['UT']
                            kAab, kAak, kT, kRH, kUT = K_['Aab'], K_['Aak'], K_['T'], K_['RH'], K_['UT']
                            (bA, kA), (bB, kB), (bC, kC) = ((p_y[0], 'p_y0'), (p_y[1], 'p_y1'), (p_s[1], 'p_s1')) if hb == 0 else \
                                                           ((p_g[0], 'p_g0'), (p_g[1], 'p_g1'), (p_u[0], 'p_u0'))
                            Sh = Swk[rows, f, :]
                            Vh = Vtok[:, hb * 64:(hb + 1) * 64]
                            S.op('pe', lambda p: p.matmul(bA[:, 0:64], rv(AR[rows, 0, cs]), rv(Sh), start=True, stop=False), reads=['AR', f'Swk{hb}'], writes=[kA])
                            S.op('pe', lambda p: p.matmul(bA[:, 0:64], rv(AakAkr_[:, 0:128]), rv(Vh), start=False, stop=True), reads=[kAak, 'Vtok'], writes=[kA])
                            yield
                            S.op('act', lambda a: a.activation(out=rv(RH_), in_=bA[:, 0:64], func=AF.Identity), reads=[kA], writes=[kRH])
                            yield
                            S.op('pe', lambda p: p.matmul(bB[:, 0:64], rv(Tinv_), rv(RH_), start=True, stop=True), reads=[kT, kRH], writes=[kB])
                            yield
                            S.op('act', lambda a: a.activation(out=rv(UT_), in_=bB[:, 0:64], func=AF.Identity), reads=[kB], writes=[kUT])
                            yield
                            S.op('pe', lambda p: p.matmul(bA[:, 0:64], rv(AR[rows, 1, cs]), rv(Sh), start=True, stop=False), reads=['AR', f'Swk{hb}'], writes=[kA])
                            S.op('pe', lambda p: p.matmul(bA[:, 0:64], rv(AabAbr_[:, 128:256]), rv(UT_), start=False, stop=False), reads=[kAab, kUT], writes=[kA])
                            S.op('pe', lambda p: p.matmul(bA[:, 0:64], rv(AakAkr_[:, 128:256]), rv(Vh), start=False, stop=True), reads=[kAak, 'Vtok'], writes=[kA])
                            S.op('pe', lambda p: p.matmul(bC[:, 0:64], rv(Bhat), rv(UT_), start=True, stop=False), reads=['Bhat', kUT], writes=[kC])
                            S.op('pe', lambda p: p.matmul(bC[:, 0:64], rv(Khat), rv(Vh), start=False, stop=True), reads=['Khat', 'Vtok'], writes=[kC])
                            yield
                            S.op('dve', lambda v: v.tensor_copy(ytok[:, c * 128 + hb * 64:c * 128 + (hb + 1) * 64], bA[:, 0:64]), reads=[kA], writes=['ytok'])
                            S.op('dve', lambda v: v.scalar_tensor_tensor(out=r32(Swk)[rows, f, :], in0=Swk[rows, f, :], scalar=pcl[rows, c:c + 1],
                                                                         in1=bC[rows, 0:64], op0=ALU.mult, op1=ALU.add),
                                 reads=[f'Swk{hb}', 'pcl', kC], writes=[f'Swk{hb}'])
                            yield
                        run_interleaved([head_chain(0), head_chain(1)])
                        yv = ytok[:, cs].rearrange("p (h i) -> p h i", h=2)
                        S.op('dve', lambda v, yv=yv: v.tensor_reduce(out=st2[:, 0:2], in_=yv, op=ALU.add, axis=mybir.AxisListType.X),
                             reads=['ytok'], writes=['st2'])
                        S.op('dve', lambda v: v.tensor_scalar_mul(st2[:, 0:2], st2[:, 0:2], 1.0 / 64), reads=['st2'], writes=['st2'])
                        S.op('dve', lambda v, yv=yv: v.tensor_tensor(out=yv, in0=yv, in1=st2[:, 0:2].unsqueeze(2).to_broadcast([128, 2, 64]),
                                                                     op=ALU.subtract), reads=['ytok', 'st2'], writes=['ytok'])
                        tv = TP['tmp'][:, cs].rearrange("p (h i) -> p h i", h=2)
                        S.op('dve', lambda v, yv=yv, tv=tv: v.tensor_tensor(out=tv, in0=yv, in1=yv, op=ALU.mult),
                             reads=['ytok', 'Khat'], writes=['tmp'])
                        S.op('dve', lambda v, tv=tv: v.tensor_reduce(out=st2[:, 2:4], in_=tv, op=ALU.add, axis=mybir.AxisListType.X),
                             reads=['tmp'], writes=['st2'])
                        S.op('dve', lambda v: v.tensor_scalar(out=st2[:, 2:4], in0=st2[:, 2:4], scalar1=1.0 / 64, scalar2=64e-5,
                                                              op0=ALU.mult, op1=ALU.add), reads=['st2'], writes=['st2'])
                        S.op('act', lambda a: a.activation(out=st2[:, 2:4], in_=st2[:, 2:4], func=AF.Sqrt), reads=['st2'], writes=['st2'])
                        S.op('dve', lambda v: v.reciprocal(st2[:, 2:4], st2[:, 2:4]), reads=['st2'], writes=['st2'])
                        S.op('dve', lambda v, yv=yv: v.tensor_tensor(out=yv, in0=yv, in1=st2[:, 2:4].unsqueeze(2).to_broadcast([128, 2, 64]),
                                                                     op=ALU.mult), reads=['ytok', 'st2'], writes=['ytok'])
                        S.op('pe', lambda p, cs=cs: p.transpose(p_s[0][:, 0:128], ytok[:, cs], ident[:]), reads=['ytok', 'ident'], writes=['p_s0'])
                        S.op('act', lambda a, f=f, cs=cs: a.activation(out=TP['kk'][:, cs], in_=p_s[0][:, 0:128], func=AF.Identity,
                                                                       scale=vec(5, f), bias=vec(6, f)), reads=['p_s0', 'rvec', 'Bt'], writes=['kk'])
                    S.op('dve', lambda v: v.tensor_tensor(out=TP['kk'], in0=TP['kk'], in1=TP['rT'], op=ALU.add), reads=['kk', 'rT'], writes=['kk'])
                    S.op('dve', lambda v, f=f: v.tensor_tensor(out=rv(og[f]), in0=TP['kk'], in1=TP['gt'], op=ALU.mult),
                         reads=['kk', 'gt'], writes=['og'])
                fence(light=True)
                for dk in range(KC):
                    pa, pk = proj(rw['w_oT'], dk, lambda k: rv(og[k]), KC, NQ, ['og'])
                    S.op('act', lambda a, pa=pa, dk=dk: a.activation(out=zsq[:, dk, 0:NQ], in_=pa, func=AF.Identity),
                         reads=[pk], writes=['zsq_y', 'zsq'])
                post_norm_tile_from_sbuf(l, 1, t0, 1.0, NQ)
                fence(light=True)
            rwkv_sample()
            S.dma('sp', 'st', lambda q: q.dma_start(out=st_out['shift_p'].rearrange("o (k p) -> p (o k)", p=128), in_=shp[:],
                                                    allow_slow_non_contiguous=True),
                  reads=['shp'])
            for f in range(KC):
                S.op('pe', lambda p, f=f: p.transpose(p_s[f % 2][0:64, 0:128], Swk[:, f, :], ident[:]),
                     reads=['Swk', 'ident'], writes=[f'p_s{f % 2}'])
                S.op('dve', lambda v, f=f: v.tensor_copy(ztf[0:64, f * 128:(f + 1) * 128], p_s[f % 2][0:64, 0:128]),
                     reads=[f'p_s{f % 2}'], writes=['zt'])
                for hb in range(2):
                    h = 2 * f + hb
                    S.dma('sp', 'st', lambda q, f=f, hb=hb, h=h: q.dma_start(
                        out=st_out['wkv_p'][0:1, h * 4096:(h + 1) * 4096].rearrange("o (i j) -> (o i) j", j=64),
                        in_=ztf[0:64, f * 128 + hb * 64:f * 128 + (hb + 1) * 64]), reads=['zt'])
            fence()


        def rwkv_sample():
            l = 1
            N16 = ['rS', 'kS', 'vS', 'lgw', 'asg', 'kk', 'km', 'gt', 'tmp', 'wS', 'aS', 'bS', 'bon', 'yf']
            P = {n: ztf[:, i * NS:(i + 1) * NS] for i, n in enumerate(N16)}
            TK = ['B_tok', 'K_tok', 'V_tok', 'SA_tok', 'Y_tok', 'Bm', 'Km2']
            Tk = {n: ztf[0:NS, 256 + i * 128:256 + (i + 1) * 128] for i, n in enumerate(TK)}
            s16 = ztf[0:NS, 1280:1288]
            xxS = ztf[:, 1296:1424].rearrange("p (k b) -> p k b", k=KC)
            shS = ztf[:, 1424:1552].rearrange("p (k b) -> p k b", k=KC)
            shT = ztf[:, 2064:2192].rearrange("p (k b) -> p k b", k=KC)
            Am = ztf[:, 1552:1808].rearrange("p (b c) -> p b c", b=NS)
            Rm = ztf[:, 1808:2064].rearrange("p (b c) -> p b c", b=NS)
            Sw = zsqf[:, 0:256].rearrange("p (b i) -> p b i", b=4)
            XVs = HTf[:, 0:512].rearrange("p (v k b) -> p v k b", v=4, k=KC)
            L16 = {n: HTf[:, 512 + i * NS:512 + (i + 1) * NS] for i, n in enumerate(['lwS', 'laS', 'lg0S', 'lg1S'])}
            ogS = HTf[:, 576:704].rearrange("p (k b) -> p k b", k=KC)
            KS = N16 + TK + ['s16', 'xxS', 'shS', 'shT', 'Am', 'Rm', 'Sw', 'XVs', 'ogS'] + list(L16) + \
                 [f'{n}{hb}' for hb in range(2) for n in ('Bm', 'Km2', 'SA_tok', 'Y_tok', 'Sw')]
            ALIAS.extend(k for k in KS if k not in ALIAS)
            fence()
            vec = lambda i, f: rvec[:, i, f:f + 1]
            EXPM05 = float(np.exp(-0.5))
            S.dma('sp', 'ldr', lambda q: q.dma_start(out=shT, in_=rshift_d), writes=['shT'])
            for k in range(KC):
                S.op('dve', lambda v, k=k: v.tensor_tensor(out=shS[:, k, :], in0=xT[:, k, NP:NT], in1=mods[l][:, 4 * KC + k, 0:NS], op=ALU.mult),
                     reads=['xT', f"mods{l}"], writes=['shS'])
                S.op('dve', lambda v, k=k: v.tensor_tensor(out=shS[:, k, :], in0=shS[:, k, :], in1=mods[l][:, 3 * KC + k, 0:NS], op=ALU.add),
                     reads=['shS', f"mods{l}"], writes=['shS'])
            S.dma('sp', 'st', lambda q: q.dma_start(out=shifts_d, in_=shS), reads=['shS'])
            S.op('dve', lambda v: v.tensor_tensor(out=xxS, in0=shT, in1=shS, op=ALU.subtract), reads=['shT', 'shS'], writes=['xxS'])

            def xvar(slot, i):
                for k in range(KC):
                    S.op('dve', lambda v, k=k: v.scalar_tensor_tensor(out=rv(XVs[:, slot, k, :]), in0=xxS[:, k, :], scalar=rmu[:, i, k:k + 1],
                                                                      in1=shS[:, k, :], op0=ALU.mult, op1=ALU.add),
                         reads=['xxS', 'shS', 'rmu'], writes=['XVs'])
            xs = lambda slot: (lambda k: rv(XVs[:, slot, k, :]))
            xvar(0, 0); xvar(1, 2); xvar(2, 3)
            xvar(3, 1)
            pa, pk = proj(rw['w1T'], 0, xs(3), KC, NS, ['XVs'])
            S.op('act', lambda a, pa=pa: a.activation(out=rv(L16['lwS'][0:64, :]), in_=pa[0:64, :], func=AF.Tanh), reads=[pk], writes=['lwS'])
            xvar(3, 4)
            pa, pk = proj(rw['a1T'], 0, xs(3), KC, NS, ['XVs'])
            S.op('act', lambda a, pa=pa: a.activation(out=rv(L16['laS'][0:64, :]), in_=pa[0:64, :], func=AF.Identity), reads=[pk], writes=['laS'])
            xvar(3, 5)
            pa, pk = proj(rw['g1T'], 0, xs(3), KC, NS, ['XVs'])
            S.op('act', lambda a, pa=pa: a.activation(out=rv(L16['lg0S']), in_=pa, func=AF.Sigmoid), reads=[pk], writes=['lg0S'])
            pa, pk = proj(rw['g1T'], 1, xs(3), KC, NS, ['XVs'])
            S.op('act', lambda a, pa=pa: a.activation(out=rv(L16['lg1S'][0:32, :]), in_=pa[0:32, :], func=AF.Sigmoid), reads=[pk], writes=['lg1S'])
            for f in range(KC):
                fc = f * 128
                for nm, wn, sl_ in (('rS', 'w_r', 0), ('kS', 'w_k', 1), ('vS', 'w_v', 2)):
                    pa, pk = proj(rw[wn + 'T'], f, xs(sl_), KC, NS, ['XVs'])
                    S.op('act', lambda a, pa=pa, nm=nm: a.activation(out=P[nm], in_=pa, func=AF.Identity), reads=[pk], writes=[nm])
                pa, pk = projp([(rw['w2'][0:64, fc:fc + 128], rv(L16['lwS'][0:64, :]))], NS, ['lwS'])
                S.op('act', lambda a, pa=pa, f=f: a.activation(out=P['lgw'], in_=pa, func=AF.Sigmoid, bias=vec(0, f), scale=1.0),
                     reads=[pk, 'rvec'], writes=['lgw'])
                S.op('act', lambda a: a.activation(out=P['wS'], in_=P['lgw'], func=AF.Exp, scale=-EXPM05), reads=['lgw'], writes=['wS'])
                pa, pk = projp([(rw['a2'][0:64, fc:fc + 128], rv(L16['laS'][0:64, :]))], NS, ['laS'])
                S.op('act', lambda a, pa=pa, f=f: a.activation(out=P['asg'], in_=pa, func=AF.Sigmoid, bias=vec(1, f), scale=1.0),
                     reads=[pk, 'rvec'], writes=['asg'])
                pa, pk = projp([(rw['g2'][0:128, fc:fc + 128], rv(L16['lg0S'])), (rw['g2'][128:160, fc:fc + 128], rv(L16['lg1S'][0:32, :]))],
                               NS, ['lg0S', 'lg1S'])
                S.op('act', lambda a, pa=pa: a.activation(out=P['gt'], in_=pa, func=AF.Identity), reads=[pk], writes=['gt'])
                S.op('dve', lambda v, f=f: v.tensor_scalar_mul(P['kk'], P['kS'], vec(2, f)), reads=['kS', 'rvec'], writes=['kk'])
                S.op('act', lambda a: a.activation(out=P['tmp'], in_=P['kk'], func=AF.Square), reads=['kk'], writes=['tmp'])
                S.op('pe', lambda p: p.matmul(p_s[0][:, 0:NS], bones[:], P['tmp'], start=True, stop=True), reads=['bones', 'tmp'], writes=['p_s0'])
                S.op('act', lambda a: a.activation(out=P['tmp'], in_=p_s[0][:, 0:NS], func=AF.Sqrt), reads=['p_s0'], writes=['tmp'])
                S.op('dve', lambda v: v.tensor_scalar_max(P['tmp'], P['tmp'], 1e-12), reads=['tmp'], writes=['tmp'])
                S.op('dve', lambda v: v.reciprocal(P['tmp'], P['tmp']), reads=['tmp'], writes=['tmp'])
                S.op('dve', lambda v: v.tensor_tensor(out=P['kk'], in0=P['kk'], in1=P['tmp'], op=ALU.mult), reads=['kk', 'tmp'], writes=['kk'])
                S.op('dve', lambda v, f=f: v.tensor_scalar(out=P['km'], in0=P['asg'], scalar1=-1.0, scalar2=vec(3, f), op0=ALU.add, op1=ALU.mult),
                     reads=['asg', 'rvec'], writes=['km'])
                S.op('dve', lambda v: v.scalar_tensor_tensor(out=P['km'], in0=P['km'], scalar=1.0, in1=P['kS'], op0=ALU.add, op1=ALU.mult),
                     reads=['km', 'kS'], writes=['km'])
                S.op('dve', lambda v: v.tensor_scalar_mul(P['aS'], P['kk'], -1.0), reads=['kk'], writes=['aS'])
                S.op('dve', lambda v: v.tensor_tensor(out=P['bS'], in0=P['kk'], in1=P['asg'], op=ALU.mult), reads=['kk', 'asg'], writes=['bS'])
                S.op('dve', lambda v, f=f: v.scalar_tensor_tensor(out=P['tmp'], in0=P['rS'], scalar=vec(4, f), in1=P['km'], op0=ALU.mult, op1=ALU.mult),
                     reads=['rS', 'km', 'rvec'], writes=['tmp'])
                S.op('pe', lambda p: p.matmul(p_s[1][:, 0:NS], bones[:], P['tmp'], start=True, stop=True), reads=['bones', 'tmp'], writes=['p_s1'])
                S.op('dve', lambda v: v.tensor_tensor(out=P['bon'], in0=p_s[1][:, 0:NS], in1=P['vS'], op=ALU.mult), reads=['p_s1', 'vS'], writes=['bon'])
                for src, dst in (('bS', 'B_tok'), ('km', 'K_tok'), ('vS', 'V_tok')):
                    S.op('pe', lambda p, src=src: p.transpose(p_s[0][0:NS, 0:128], P[src], ident[:]), reads=[src, 'ident'], writes=['p_s0'])
                    S.op('act', lambda a, dst=dst: a.activation(out=Tk[dst], in_=p_s[0][0:NS, 0:128], func=AF.Identity), reads=['p_s0'], writes=[dst])
                S.op('pool', lambda g: g.memset(Am, 0.0), writes=['Am'])
                S.op('pool', lambda g: g.memset(Rm, 0.0), writes=['Rm'])
                for b in range(NS):
                    S.op('dve', lambda v, b=b: v.tensor_copy(Am[:, b, b:b + 1], P['aS'][:, b:b + 1]), reads=['aS', 'Am'], writes=['Am'])
                    S.op('dve', lambda v, b=b: v.tensor_copy(Rm[:, b, b:b + 1], P['rS'][:, b:b + 1]), reads=['rS', 'Rm'], writes=['Rm'])
                def smp_chain(bt, hb, f=f):
                    rows = slice(hb * 64, (hb + 1) * 64)
                    hc = slice(hb * 64, (hb + 1) * 64)
                    Bm_ = Tk['Bm'] if hb == 0 else ztf[0:NS, 2192:2320]
                    Km_ = Tk['Km2'] if hb == 0 else ztf[0:NS, 2320:2448]
                    kBm, kKm, kSA, kY, kSw = f'Bm{hb}', f'Km2{hb}', f'SA_tok{hb}', f'Y_tok{hb}', f'Sw{hb}'
                    (bA, kA), (bB, kB), (bC, kC) = ((p_y[0], 'p_y0'), (p_y[1], 'p_y1'), (p_s[0], 'p_s0')) if hb == 0 else \
                                                   ((p_g[0], 'p_g0'), (p_g[1], 'p_g1'), (p_u[0], 'p_u0'))
                    for bi in range(4):
                        b = 4 * bt + bi
                        S.op('pe', lambda p, b=b, bi=bi: p.matmul(bA[0:NS, 0:64], Am[rows, b, :], Sw[rows, bi, :], start=(bi == 0), stop=(bi == 3)),
                             reads=['Am', kSw], writes=[kA])
                    yield
                    S.op('act', lambda a: a.activation(out=Tk['SA_tok'][:, hc], in_=bA[0:NS, 0:64], func=AF.Identity), reads=[kA], writes=[kSA])
                    yield
                    for bi in range(4):
                        b = 4 * bt + bi
                        S.op('dve', lambda v, b=b: v.tensor_scalar_mul(Bm_, Tk['B_tok'], ident[0:NS, b:b + 1]), reads=['B_tok', 'ident'], writes=[kBm])
                        S.op('dve', lambda v, b=b: v.tensor_scalar_mul(Km_, Tk['K_tok'], ident[0:NS, b:b + 1]), reads=['K_tok', 'ident'], writes=[kKm])
                        yield
                        S.op('pe', lambda p: p.matmul(bB[:, 0:64], Bm_, Tk['SA_tok'][:, hc], start=True, stop=False), reads=[kBm, kSA], writes=[kB])
                        S.op('pe', lambda p: p.matmul(bB[:, 0:64], Km_, Tk['V_tok'][:, hc], start=False, stop=True), reads=[kKm, 'V_tok'], writes=[kB])
                        yield
                        S.op('dve', lambda v, b=b, bi=bi: v.scalar_tensor_tensor(
                            out=Sw[rows, bi, :], in0=Sw[rows, bi, :], scalar=P['wS'][rows, b:b + 1], in1=bB[rows, 0:64],
                            op0=ALU.mult, op1=ALU.add), reads=[kSw, 'wS', kB], writes=[kSw])
                        yield
                        S.op('pe', lambda p, b=b, bi=bi: p.matmul(bC[0:NS, 0:64], Rm[rows, b, :], Sw[rows, bi, :], start=(bi == 0), stop=(bi == 3)),
                             reads=['Rm', kSw], writes=[kC])
                        yield
                    if bt == 0:
                        S.op('dve', lambda v: v.tensor_copy(Tk['Y_tok'][:, hc], bC[0:NS, 0:64]), reads=[kC], writes=[kY, 'Y_tok'])
                    else:
                        S.op('dve', lambda v: v.tensor_tensor(out=Tk['Y_tok'][:, hc], in0=Tk['Y_tok'][:, hc], in1=bC[0:NS, 0:64], op=ALU.add),
                             reads=[kC, kY], writes=[kY, 'Y_tok'])
                    yield

                def run_il2(gens):
                    gens = list(gens)
                    while gens:
                        for g_ in list(gens):
                            try:
                                next(g_)
                            except StopIteration:
                                gens.remove(g_)
                for bt in range(4):
                    S.dma('sp', 'ldw', lambda q, bt=bt, f=f: q.dma_start(out=Sw, in_=wkvs0_d[:, 4 * bt:4 * bt + 4, f, :]), writes=['Sw', 'Sw0', 'Sw1'])
                    run_il2([smp_chain(bt, 0), smp_chain(bt, 1)])
                    S.dma('sp', 'st', lambda q, bt=bt, f=f: q.dma_start(out=wkvs_d[:, 4 * bt:4 * bt + 4, f, :], in_=Sw), reads=['Sw', 'Sw0', 'Sw1'])
                yv = Tk['Y_tok'].rearrange("p (h i) -> p h i", h=2)
                tv = Tk['Bm'].rearrange("p (h i) -> p h i", h=2)
                S.op('dve', lambda v: v.tensor_reduce(out=s16[:, 0:2], in_=yv, op=ALU.add, axis=mybir.AxisListType.X),
                     reads=['Y_tok', 'Y_tok0', 'Y_tok1'], writes=['s16', 'Y_tok'])
                S.op('dve', lambda v: v.tensor_scalar_mul(s16[:, 0:2], s16[:, 0:2], 1.0 / 64), reads=['s16'], writes=['s16'])
                S.op('dve', lambda v: v.tensor_tensor(out=yv, in0=yv, in1=s16[:, 0:2].unsqueeze(2).to_broadcast([NS, 2, 64]), op=ALU.subtract),
                     reads=['Y_tok', 's16'], writes=['Y_tok'])
                S.op('dve', lambda v: v.tensor_tensor(out=tv, in0=yv, in1=yv, op=ALU.mult), reads=['Y_tok'], writes=['Bm', 'Bm0'])
                S.op('dve', lambda v: v.tensor_reduce(out=s16[:, 2:4], in_=tv, op=ALU.add, axis=mybir.AxisListType.X), reads=['Bm', 'Bm0'], writes=['s16'])
                S.op('dve', lambda v: v.tensor_scalar(out=s16[:, 2:4], in0=s16[:, 2:4], scalar1=1.0 / 64, scalar2=64e-5, op0=ALU.mult, op1=ALU.add),
                     reads=['s16'], writes=['s16'])
                S.op('act', lambda a: a.activation(out=s16[:, 2:4], in_=s16[:, 2:4], func=AF.Sqrt), reads=['s16'], writes=['s16'])
                S.op('dve', lambda v: v.reciprocal(s16[:, 2:4], s16[:, 2:4]), reads=['s16'], writes=['s16'])
                S.op('dve', lambda v: v.tensor_tensor(out=yv, in0=yv, in1=s16[:, 2:4].unsqueeze(2).to_broadcast([NS, 2, 64]), op=ALU.mult),
                     reads=['Y_tok', 's16'], writes=['Y_tok'])
                S.op('pe', lambda p: p.transpose(p_s[1][:, 0:NS], Tk['Y_tok'], ident[0:NS, 0:NS]), reads=['Y_tok', 'ident'], writes=['p_s1'])
                S.op('act', lambda a, f=f: a.activation(out=P['yf'], in_=p_s[1][:, 0:NS], func=AF.Identity, scale=vec(5, f), bias=vec(6, f)),
                     reads=['p_s1', 'rvec'], writes=['yf'])
                S.op('dve', lambda v: v.tensor_tensor(out=P['yf'], in0=P['yf'], in1=P['bon'], op=ALU.add), reads=['yf', 'bon'], writes=['yf'])
                S.op('dve', lambda v, f=f: v.tensor_tensor(out=rv(ogS[:, f, :]), in0=P['yf'], in1=P['gt'], op=ALU.mult), reads=['yf', 'gt'], writes=['ogS'])
            fence()
            for dk in range(KC):
                pa, pk = proj(rw['w_oT'], dk, lambda k: rv(ogS[:, k, :]), KC, NS, ['ogS'])
                S.op('act', lambda a, pa=pa, dk=dk: a.activation(out=zsq[:, dk, 0:NS], in_=pa, func=AF.Identity), reads=[pk], writes=['zsq_y', 'zsq'])
            post_norm_tile_from_sbuf(l, 1, NP, 1.0, NS)
            fence()

        def mixer_stub_sublayer(l):
            for t in range(NTILE):
                c0 = t * TW
                for k in range(KC):
                    S.op('act', lambda a, k=k, c0=c0: a.activation(out=zt[:, k, :], in_=xT[:, k, c0:c0 + TW],
                                                                   func=AF.Identity, scale=DN_ALPHA),
                         reads=['xT'], writes=['zt'])
                    S.op('act', lambda a, k=k: a.activation(out=zsq[:, k, :], in_=zt[:, k, :], func=AF.Square),
                         reads=['zt', 'zsq_y'], writes=['zsq', 'zsq_y'])
                finish_norm(l, 1, c0)

        for l in range(DEPTH):
            if only == 's5':
                if l == 0:
                    U5, Us5 = s5_phase()
                    S.dma('sp', 'st', lambda q: q.dma_start(out=yT_d[:, 0:4, 0:NP], in_=U5), reads=['U'])
                    S.dma('sp', 'st', lambda q: q.dma_start(out=yT_d[:, 0:4, NP:NT], in_=Us5), reads=['Us'])
                continue
            ffn_sublayer(l, 0, tuple(w for w in ffw["ffn1"]))
            mix0_sublayer() if l == 0 else mix1_sublayer()
            ffn_sublayer(l, 2, tuple(w for w in ffw["ffn2"]))

        if only is None:
            S.dma('sp', 'st', lambda q: q.dma_start(out=yT_d, in_=xT[:]), reads=['xT'])
        S.op('pool', lambda g: g.memset(zsq[:, 0, :], 0.0), writes=['zsq', 'zsq_y'])
        for name, ap in st_out.items():
            if name in ('gla_p', 'shift_p', 'wkv_p'):
                continue
            nb, width = ap.shape
            for c in range(0, width, TW):
                wdt = min(TW, width - c)
                S.dma('sp', 'st', lambda q, ap=ap, nb=nb, c=c, wdt=wdt: q.dma_start(
                    out=ap[:, c:c + wdt], in_=zsq[0:nb, 0, 0:wdt]), reads=['zsq'])
        S.finish('sp')
        S.emit_all()
    return nc


def _featmajor(v, ncols):
    return np.ascontiguousarray(np.swapaxes(v.reshape(v.shape[:-1] + (ncols, 128)), -1, -2))


def _tile_w(w, nk):
    C = w.shape[1]
    nb = (C + 127) // 128
    wp = np.zeros((nk * 128, nb * 128), np.float32)
    wp[:, :C] = w
    return np.ascontiguousarray(wp.reshape(nk, 128, nb, 128).transpose(2, 1, 0, 3))


def prep_inputs(inp):
    f = lambda k: np.ascontiguousarray(np.asarray(inp[k], dtype=np.float32))
    xp, xs, cp, cs = f("x_prompt"), f("x_sample"), f("c_prompt"), f("c_sample")
    shared = {
        "ada_w": f("ada_w"),
        "ada_bT": _featmajor(f("ada_b"), NMOD * KC),
        "ln_gT": _featmajor(f("ln_g"), KC),
        "ln_bT": _featmajor(f("ln_b"), KC),
        "w_inT": _tile_w(f("w_in")[:, 0:1664], KC), "w_inuT": _tile_w(f("w_in")[:, 1552:2064], KC),
        "w_outT": _tile_w(f("w_out"), KC), "gla_w_gk": f("gla_w_gk"),
        "gla_b_gkT": _featmajor(f("gla_b_gk"), 2),
        "gla_norm_gT": np.ascontiguousarray(f("gla_norm_g").reshape(128, 1)),
        "s5_aT": np.ascontiguousarray(np.tile(np.stack([f("s5_a_re").T, f("s5_a_im").T], 1), (2, 1, 1))),
        "s5_lsT": np.ascontiguousarray(np.tile(f("s5_log_step")[None, :], (128, 1))),
        "s5_BA": np.ascontiguousarray(np.concatenate([f("s5_b_re").transpose(1, 0, 2), f("s5_b_im").transpose(1, 0, 2)], 0)),
        "s5_BB": np.ascontiguousarray(np.concatenate([f("s5_b_im").transpose(1, 0, 2), f("s5_b_re").transpose(1, 0, 2)], 0)),
        "s5_CT": np.ascontiguousarray(np.concatenate([f("s5_c_re").transpose(2, 0, 1), f("s5_c_im").transpose(2, 0, 1)], 0)),
        "s5_dT": _featmajor(f("s5_d").reshape(-1), 4), "s5_b_gluT": _featmajor(f("s5_b_glu"), 4), "s5_w_gluT": _tile_w(f("s5_w_glu"), 4),
        "rwkv_muT": np.ascontiguousarray(_featmajor(f("rwkv_mu"), KC).transpose(1, 0, 2)),
        "rwkv_vecT": np.ascontiguousarray(_featmajor(np.stack([f("rwkv_w0"), f("rwkv_a0"), f("rwkv_k_k"), f("rwkv_k_a"),
                                                                f("rwkv_r_k").reshape(-1), f("rwkv_lnx_g"), f("rwkv_lnx_b")]), KC).transpose(1, 0, 2)),
    }
    for nm in ("ffn1", "ffn2"):
        shared[nm + "_wd"] = f(nm + "_wd")
        wg, wu = f(nm + "_wg"), f(nm + "_wu")
        shared[nm + "_wguT"] = np.ascontiguousarray(np.stack(
            [np.stack([_tile_w(wg[l], KC), _tile_w(wu[l], KC)], axis=2) for l in range(DEPTH)]))
    for nm in ("w2", "a2", "g2"):
        shared["rwkv_" + nm] = f("rwkv_" + nm)
    for nm in ("w_r", "w_k", "w_v", "w_o", "w1", "a1", "g1"):
        shared["rwkv_" + nm + "T"] = _tile_w(f("rwkv_" + nm), KC)
    in_maps = []
    for i in range(8):
        tok = np.concatenate([xp[i], xs[16 * i:16 * i + 16, 0, :]], axis=0)
        xT = np.ascontiguousarray(tok.T.reshape(KC, 128, NT).transpose(1, 0, 2))
        cc = np.concatenate([cs[16 * i:16 * i + 16], cp[i:i + 1], np.zeros((1, D), np.float32)], axis=0)
        cT = np.ascontiguousarray(cc.T.reshape(KC, 128, NCC).transpose(1, 0, 2))
        h0 = np.concatenate([f('state_s5_re')[16 * i:16 * i + 16].transpose(2, 1, 0), f('state_s5_im')[16 * i:16 * i + 16].transpose(2, 1, 0)], 0)
        g0 = f('state_gla')[16 * i:16 * i + 16].reshape(16, 2, 2, 64, 128).transpose(2, 3, 0, 1, 4).reshape(128, 16, 2, 128)
        sh = f('state_rwkv_shift')[16 * i:16 * i + 16]
        shT = sh.T.reshape(KC, 128, 16).transpose(1, 0, 2)
        w0 = f('state_rwkv_wkv')[16 * i:16 * i + 16].reshape(16, KC, 2, 64, 64).transpose(2, 4, 0, 1, 3).reshape(128, 16, KC, 64)
        in_maps.append(dict(shared, xT=xT, cT=cT, s5_h0T=np.ascontiguousarray(h0), gla_s0T=np.ascontiguousarray(g0),
                            rwkv_shiftT=np.ascontiguousarray(shT), wkv_s0T=np.ascontiguousarray(w0)))
    return in_maps


def kernel(**inp):
    in_maps = prep_inputs(inp)
    nc = build_nc()
    res = run_bass_kernel_spmd(nc, in_maps, core_ids=list(range(8))).results

    def tokens(r):
        return r["yT"].transpose(2, 1, 0).reshape(NT, D)
    y_prompt = np.stack([tokens(r)[:NP] for r in res]).astype(np.float32)
    y_sample = np.concatenate([tokens(r)[NP:] for r in res])[:, None, :].astype(np.float32)
    outs = [y_prompt, y_sample]
    for grp in ("p", "s"):
        cat = lambda k: np.concatenate([r[k + "_" + grp] for r in res], axis=0)
        if grp == "p":
            s5 = np.stack([r["s5_hp"] for r in res])
            s5re, s5im = s5[:, 0:64].transpose(0, 2, 1), s5[:, 64:128].transpose(0, 2, 1)
        else:
            s5 = np.concatenate([r["s5_hs"].transpose(2, 1, 0) for r in res], 0)
            s5re, s5im = s5[:, :, 0:64], s5[:, :, 64:128]
        if grp == "p":
            gla = cat("gla").reshape((-1,) + GLA_SHAPE)
        else:
            gla = np.concatenate([r["gla_sT"].reshape(2, 64, 16, 2, 128).transpose(2, 3, 0, 1, 4).reshape(16, 4, 64, 128) for r in res], 0)
        if grp == "p":
            shift, wkv = cat("shift"), cat("wkv").reshape((-1,) + WKV_SHAPE)
        else:
            shift = np.concatenate([r["shift_sT"].transpose(2, 1, 0).reshape(16, D) for r in res], 0)
            wkv = np.concatenate([r["wkv_sT"].reshape(2, 64, 16, KC, 64).transpose(2, 3, 0, 4, 1).reshape(16, 16, 64, 64) for r in res], 0)
        outs += [gla, s5re, s5im, shift, wkv]
    return tuple(np.ascontiguousarray(o, dtype=np.float32) for o in outs)
```

```python
import numpy as np
from contextlib import ExitStack
import concourse.bass as bass
import concourse.mybir as mybir
from concourse.bass_utils import run_bass_kernel_spmd

F32 = mybir.dt.float32
F32R = mybir.dt.float32r
AF = mybir.ActivationFunctionType
ALU = mybir.AluOpType

D, DFF, DEPTH, NMOD = 1024, 2752, 2, 9
KC = D // 128
NP, NS = 2048, 16
NT = NP + NS
TW = 344
NTILE = NT // TW
GROUPS = [(0, 1), (2, 3), (4, 5)]
NCC = 18
FCH = [(f * 128, min(128, DFF - f * 128)) for f in range((DFF + 127) // 128)]
DN_ALPHA = (2 * DEPTH) ** 0.25
LN_EPS = 1e-5
GLA_SHAPE = (4, 64, 128)
S5_SHAPE = (32, 64)
WKV_SHAPE = (16, 64, 64)


class Sched:
    CAP, DCAP, MAX_SEMS = 4000, 240, 96

    def __init__(self, nc, es):
        self.nc, self.es = nc, es
        self.names = ('sp', 'act', 'dve', 'pool', 'pe')
        self.ops = {k: [] for k in self.names}
        self.nseq = {k: 0 for k in self.names}
        self.sems, self.dcount, self.res = {}, {}, {}
        self.waited = {k: {} for k in self.names}

    def _sem(self, kind, key, epoch):
        k = (kind, key, epoch)
        if k not in self.sems:
            assert len(self.sems) < self.MAX_SEMS, f"out of semaphores ({len(self.sems)})"
            self.sems[k] = self.es.enter_context(self.nc.semaphore(f"s_{kind}_{key}_{epoch}"))
        return self.sems[k]

    def _tok_wait(self, tok):
        if tok[0] == 'c':
            _, e, n = tok
            return self._sem('c', e, (n - 1) // self.CAP), (n - 1) % self.CAP + 1, ('c', e), n
        _, ch, n = tok
        ep = (n - 1) // self.DCAP
        in_ep = min(self.dcount[ch], (ep + 1) * self.DCAP) - ep * self.DCAP
        return self._sem('d', ch, ep), 16 * in_ep, ('d', ch, ep), in_ep

    def _need(self, e, tok, waits):
        if tok is None or (tok[0] == 'c' and tok[1] == e == 'pe'):
            return
        sem, v, src, order = self._tok_wait(tok)
        if self.waited[e].get(src, 0) >= order:
            return
        self.waited[e][src] = order
        waits.append((sem, v))

    def _deps(self, e, reads, writes):
        waits = []
        for r in reads:
            self._need(e, self.res.setdefault(r, {'w': None, 'r': {}})['w'], waits)
        for w in writes:
            st = self.res.setdefault(w, {'w': None, 'r': {}})
            self._need(e, st['w'], waits)
            for t in st['r'].values():
                self._need(e, t, waits)
        return waits

    def _commit(self, tok, rkey, reads, writes):
        for r in reads:
            self.res[r]['r'][rkey] = tok
        for w in writes:
            self.res[w] = {'w': tok, 'r': {}}

    def op(self, e, emit, reads=(), writes=()):
        waits = self._deps(e, reads, writes)
        self.nseq[e] += 1
        n = self.nseq[e]
        self.ops[e].append((waits, emit, self._sem('c', e, (n - 1) // self.CAP), 1))
        self._commit(('c', e, n), ('c', e), reads, writes)

    def dma(self, q, ch, emit, reads=(), writes=()):
        waits = self._deps(q, reads, writes)
        self.dcount[ch] = n = self.dcount.get(ch, 0) + 1
        self.ops[q].append((waits, emit, self._sem('d', ch, (n - 1) // self.DCAP), 16))
        self._commit(('d', ch, n), ('d', ch), reads, writes)

    def finish(self, e='sp'):
        waits = []
        for ch, tot in self.dcount.items():
            for ep in range((tot - 1) // self.DCAP + 1):
                self._need(e, ('d', ch, min(tot, (ep + 1) * self.DCAP)), waits)
        self.ops[e].append((waits, None, None, 0))

    def emit_all(self):
        with self.nc.Block() as block:
            for name, deco in (('sp', block.sync), ('act', block.scalar), ('dve', block.vector),
                               ('pool', block.gpsimd), ('pe', block.tensor)):
                def body(eng, lst=self.ops[name]):
                    for waits, emit, sem, amt in lst:
                        for s, v in waits:
                            eng.wait_ge(s, v)
                        if emit is not None:
                            emit(eng).then_inc(sem, amt)
                deco(body)


def build_nc(only=None):
    nc = bass.Bass("TRN2", target_bir_lowering=False)
    dr = lambda n, s, k: nc.dram_tensor(n, list(s), F32, kind=k).ap()
    xT_d = dr("xT", (128, KC, NT), "ExternalInput")
    cT_d = dr("cT", (128, KC, NCC), "ExternalInput")
    adaw_d = dr("ada_w", (DEPTH, D, NMOD * D), "ExternalInput")
    adab_d = dr("ada_bT", (DEPTH, 128, NMOD * KC), "ExternalInput")
    lng_d = dr("ln_gT", (DEPTH, 3, 128, KC), "ExternalInput")
    lnb_d = dr("ln_bT", (DEPTH, 3, 128, KC), "ExternalInput")
    ffw = {}
    for nm in ("ffn1", "ffn2"):
        ffw[nm] = (dr(nm + "_wguT", (DEPTH, len(FCH), 128, 2, KC, 128), "ExternalInput"),
                   dr(nm + "_wd", (DEPTH, DFF, D), "ExternalInput"))
    w_in_d = dr("w_inT", (13, 128, KC, 128), "ExternalInput")
    w_inu_d = dr("w_inuT", (4, 128, KC, 128), "ExternalInput")
    w_out_d = dr("w_outT", (KC, 128, KC, 128), "ExternalInput")
    wgk_d = dr("gla_w_gk", (16, 256), "ExternalInput")
    bgk_d = dr("gla_b_gkT", (128, 2), "ExternalInput")
    gng_d = dr("gla_norm_gT", (128, 1), "ExternalInput")
    rw = {n: dr("rwkv_" + n, sh, "ExternalInput") for n, sh in dict(
        w_rT=(KC, 128, KC, 128), w_kT=(KC, 128, KC, 128), w_vT=(KC, 128, KC, 128), w_oT=(KC, 128, KC, 128),
        w1T=(1, 128, KC, 128), w2=(64, D), a1T=(1, 128, KC, 128), a2=(64, D), g1T=(2, 128, KC, 128), g2=(160, D)).items()}
    rmu_d = dr("rwkv_muT", (128, 6, KC), "ExternalInput")
    rvec_d = dr("rwkv_vecT", (128, 7, KC), "ExternalInput")
    s5aT_d = dr("s5_aT", (128, 2, 32), "ExternalInput")
    s5ls_d = dr("s5_lsT", (128, 32), "ExternalInput")
    s5BA_d = dr("s5_BA", (128, 32, 16), "ExternalInput")
    s5BB_d = dr("s5_BB", (128, 32, 16), "ExternalInput")
    s5CT_d = dr("s5_CT", (128, 32, 16), "ExternalInput")
    s5d_d = dr("s5_dT", (128, 4), "ExternalInput")
    s5bg_d = dr("s5_b_gluT", (128, 4), "ExternalInput")
    s5wg_d = dr("s5_w_gluT", (4, 128, 4, 128), "ExternalInput")
    s5h0_d = dr("s5_h0T", (128, 32, NS), "ExternalInput")
    s5hp_d = dr("s5_hp", (128, 32), "ExternalOutput")
    s5hs_d = dr("s5_hs", (128, 32, NS), "ExternalOutput")
    glas0_d = dr("gla_s0T", (128, NS, 2, 128), "ExternalInput")
    glas_d = dr("gla_sT", (128, NS, 2, 128), "ExternalOutput")
    rshift_d = dr("rwkv_shiftT", (128, KC, NS), "ExternalInput")
    wkvs0_d = dr("wkv_s0T", (128, NS, KC, 64), "ExternalInput")
    wkvs_d = dr("wkv_sT", (128, NS, KC, 64), "ExternalOutput")
    shifts_d = dr("shift_sT", (128, KC, NS), "ExternalOutput")
    yT_d = dr("yT", (128, KC, NT), "ExternalOutput")
    st_out = {}
    for grp, nb in (("p", 1), ("s", NS)):
        if grp == "p":
            st_out["gla_" + grp] = dr("gla_" + grp, (nb, 4 * 64 * 128), "ExternalOutput")
        if grp == "p":
            st_out["shift_" + grp] = dr("shift_" + grp, (nb, D), "ExternalOutput")
            st_out["wkv_" + grp] = dr("wkv_" + grp, (nb, 16 * 64 * 64), "ExternalOutput")

    with ExitStack() as es:
        sb = lambda n, s: es.enter_context(nc.sbuf_tensor(n, list(s), F32))
        ps = lambda n, s: es.enter_context(nc.psum_tensor(n, list(s), F32))
        S = Sched(nc, es)

        xT = sb("xT_sb", (128, KC, NT))
        hin = sb("hin", (128, KC, 2 * TW))
        HT = sb("HT", (128, len(FCH), 2 * TW))
        wgu = [sb(f"wgu{i}", (128, 2, KC, 128)) for i in range(2)]
        cT = sb("cT_sb", (128, KC, NCC))
        sc = sb("silu_c", (128, KC, NCC))
        mods = [sb(f"mods{l}", (128, NMOD * KC, NCC)) for l in range(DEPTH)]
        adabT = sb("adabT", (128, DEPTH, NMOD * KC))
        lng = sb("lng", (128, DEPTH * 3, KC))
        lnb = sb("lnb", (128, DEPTH * 3, KC))
        ones = sb("ones", (128, 128))
        fence_t = sb('fence_t', (1, 2))
        zt = sb("zt", (128, KC, TW))
        zsq = sb("zsq", (128, KC, TW))
        mean = sb("mean", (128, TW)); rstd = sb("rstd", (128, TW))
        gsb = sb("g_sb", (128, TW))
        msq = gsb
        p_g = [ps(f"p_g{i}", (128, 512)) for i in range(2)]
        p_u = [ps(f"p_u{i}", (128, 512)) for i in range(2)]
        p_y = [ps(f"p_y{i}", (128, 512)) for i in range(2)]
        p_s = [ps(f"p_s{i}", (128, 512)) for i in range(2)]

        NQ = 256
        ident = sb('ident', (128, 128)); rstm = sb('rstm', (128, NQ))
        Wgk = sb('Wgk', (16, 256)); bgk = sb('bgk', (128, 2)); gng = sb('gng', (128, 1))
        dec = sb('dec', (128, 2, 2)); bl = sb('bl', (128, 2, 2))
        mask2 = sb('mask2', (128, 256))
        maskLT = sb('maskLT', (128, 128))
        bones = sb('bones', (128, 128))
        rmu = sb('rmu', (128, 6, KC)); rvec = sb('rvec', (128, 7, KC))
        Swk = sb('Swk', (128, KC, 64))
        Sst = [Swk[:, 2 * f:2 * f + 2, :].rearrange('p a b -> p (a b)') for f in range(2)]
        hprev = sb('hprev', (128, KC)); shp = sb('shp', (128, KC)); pcl = sb('pcl', (128, 2)); llast = sb('llast', (128, 2))
        st2 = sb('st2', (128, 8))
        blkm = sb('blkm', (128, 8)); sgn = sb('sgn', (128, 1))
        s5d = sb('s5d', (128, 4)); s5bg = sb('s5bg', (128, 4)); HS = sb('HS', (128, 32))
        r32 = lambda t: t.bitcast(F32R)
        rv = lambda ap: ap.bitcast(F32R)

        S.dma('sp', 'ld0', lambda q: q.dma_start(out=xT[:], in_=xT_d), writes=['xT'])
        S.dma('sp', 'ld0', lambda q: q.dma_start(out=cT[:], in_=cT_d), writes=['cT'])
        S.dma('sp', 'ld0', lambda q: q.dma_start(out=adabT[:], in_=adab_d.rearrange("l p m -> p l m")),
              writes=['adabT'])
        S.dma('sp', 'ld0', lambda q: q.dma_start(out=lng[:], in_=lng_d.rearrange("l s p k -> p (l s) k")),
              writes=['lng'])
        S.dma('sp', 'ld0', lambda q: q.dma_start(out=lnb[:], in_=lnb_d.rearrange("l s p k -> p (l s) k")),
              writes=['lnb'])
        S.op('pool', lambda g: g.memset(ones[:], 1.0), writes=['ones'])
        S.op('pool', lambda g: g.affine_select(out=ident[:], in_=ones[:], pattern=[[1, 128]], base=0,
                                               channel_multiplier=-1, compare_op=ALU.is_equal, fill=0.0),
             reads=['ones'], writes=['ident'])
        S.op('pool', lambda g: g.affine_select(out=mask2[:, 0:128], in_=ones[:], pattern=[[1, 128]], base=-1,
                                               channel_multiplier=-1, compare_op=ALU.is_ge, fill=0.0),
             reads=['ones'], writes=['mask2'])
        S.op('pool', lambda g: g.affine_select(out=mask2[:, 128:256], in_=ones[:], pattern=[[1, 128]], base=0,
                                               channel_multiplier=-1, compare_op=ALU.is_ge, fill=0.0),
             reads=['ones', 'mask2'], writes=['mask2'])
        S.op('pool', lambda g: g.affine_select(out=maskLT[:], in_=ones[:], pattern=[[-1, 128]], base=-1,
                                               channel_multiplier=1, compare_op=ALU.is_ge, fill=0.0),
             reads=['ones'], writes=['maskLT'])
        S.op('pool', lambda g: g.affine_select(out=blkm[:], in_=ones[:, 0:8], pattern=[[-16, 8]], base=0,
                                               channel_multiplier=1, compare_op=ALU.is_ge, fill=0.0), reads=['ones'], writes=['blkm'])
        S.op('pool', lambda g: g.affine_select(out=blkm[:], in_=blkm[:], pattern=[[16, 8]], base=15,
                                               channel_multiplier=-1, compare_op=ALU.is_ge, fill=0.0), reads=['blkm'], writes=['blkm'])
        S.op('pool', lambda g: g.memset(sgn[0:64, :], 1.0), writes=['sgn'])
        S.op('pool', lambda g: g.memset(sgn[64:128, :], -1.0), reads=['sgn'], writes=['sgn'])
        S.op('pool', lambda g: g.memset(bones[:], 0.0), writes=['bones'])
        for hb in range(2):
            S.op('pool', lambda g, hb=hb: g.memset(bones[hb * 64:(hb + 1) * 64, hb * 64:(hb + 1) * 64], 1.0),
                 reads=['bones'], writes=['bones'])
        S.dma('sp', 'ld0', lambda q: q.dma_start(out=rmu[:], in_=rmu_d), writes=['rmu'])
        S.dma('sp', 'ld0', lambda q: q.dma_start(out=rvec[:], in_=rvec_d), writes=['rvec'])
        S.op('pool', lambda g: g.memset(rstm[:], 1.0), writes=['rstm'])
        for c_ in range(NQ // 128):
            S.op('pool', lambda g, c_=c_: g.memset(rstm[:, c_ * 128:c_ * 128 + 1], 0.0), reads=['rstm'], writes=['rstm'])
        S.dma('pool', 'ldc', lambda q: q.dma_start(out=r32(Wgk)[:], in_=wgk_d), writes=['Wgk'])
        S.dma('sp', 'ld0', lambda q: q.dma_start(out=bgk[:], in_=bgk_d), writes=['bgk'])
        S.dma('sp', 'ld0', lambda q: q.dma_start(out=gng[:], in_=gng_d), writes=['gng'])
        S.op('act', lambda a: a.activation(out=r32(sc)[:], in_=cT[:], func=AF.Silu),
             reads=['cT'], writes=['sc'])

        ABUF = [(HT[:, 8 * (i // 2):8 * (i // 2) + 8, 256 * (i % 2):256 * (i % 2) + 256], f"adaw{i}") for i in range(4)] + \
               [(hin[:, :, 256 * i:256 * i + 256], f"adaw{4 + i}") for i in range(2)]
        nblk = 0
        for l in range(DEPTH):
            for cb in range(NMOD * D // 256):
                buf, key = ABUF[nblk % len(ABUF)]; nblk += 1
                S.dma('pool', key, lambda q, buf=buf, l=l, cb=cb: q.dma_start(
                    out=rv(buf), in_=adaw_d[l, :, cb * 256:(cb + 1) * 256].rearrange("(k p) c -> p k c", p=128)),
                    writes=[key])
                for j in range(2):
                    m = cb * 2 + j
                    pt = p_s[m % 2]; pk = f"p_s{m % 2}"
                    for k in range(KC):
                        S.op('pe', lambda p, pt=pt, buf=buf, k=k, j=j: p.matmul(
                            pt[:, 0:NCC], rv(buf[:, k, j * 128:(j + 1) * 128]), r32(sc)[:, k, :],
                            start=(k == 0), stop=(k == KC - 1)),
                            reads=[key, 'sc'], writes=[pk])
                    S.op('act', lambda a, pt=pt, l=l, m=m: a.activation(
                        out=mods[l][:, m, :], in_=pt[:, 0:NCC], func=AF.Identity,
                        bias=adabT[:, l, m:m + 1], scale=1.0),
                        reads=[pk, 'adabT'], writes=[f"mods{l}"])
            for which in (1, 2, 4, 5, 7, 8):
                S.op('dve', lambda v, l=l, which=which: v.tensor_scalar_add(
                    mods[l][:, which * KC:(which + 1) * KC, :], mods[l][:, which * KC:(which + 1) * KC, :], 1.0),
                    reads=[f"mods{l}"], writes=[f"mods{l}"])

        S.op('pool', lambda g: g.memset(fence_t[:], 0.0), writes=['hin', 'HT', 'fence_t'] + [f'adaw{i}' for i in range(6)])

        def split_cols(c0, w):
            npr = max(0, min(c0 + w, NP) - c0)
            return (0, c0, npr), (npr, c0 + npr, w - npr)

        def modulate(eng_p, out_t, ocol, l, sub, c0, w, rkeys, wkey):
            (po, pg, pn), (so, sg, sn) = split_cols(c0, w)
            for k in range(KC):
                if pn:
                    S.op('act', lambda a, k=k: a.activation(
                        out=r32(out_t)[:, k, ocol + po:ocol + po + pn], in_=xT[:, k, pg:pg + pn], func=AF.Identity,
                        scale=mods[l][:, (3 * sub + 1) * KC + k, 16:17], bias=mods[l][:, (3 * sub) * KC + k, 16:17]),
                        reads=['xT', f"mods{l}"] + rkeys, writes=[wkey])
                if sn:
                    s0 = sg - NP
                    S.op('dve', lambda v, k=k: v.tensor_tensor(
                        out=gsb[:, 0:sn], in0=xT[:, k, sg:sg + sn],
                        in1=mods[l][:, (3 * sub + 1) * KC + k, s0:s0 + sn], op=ALU.mult),
                        reads=['xT', f"mods{l}"], writes=['gsb'])
                    S.op('dve', lambda v, k=k: v.tensor_tensor(
                        out=r32(out_t)[:, k, ocol + so:ocol + so + sn], in0=gsb[:, 0:sn],
                        in1=mods[l][:, (3 * sub) * KC + k, s0:s0 + sn], op=ALU.add),
                        reads=['gsb', f"mods{l}"] + rkeys, writes=[wkey])

        wcnt = [0]

        def ffn_sublayer(l, sub, wts):
            wgu_d, wd_d = wts
            HTF = HT[:].rearrange("p a b -> p (a b)")
            hinF = hin[:].rearrange("p a b -> p (a b)")
            slotsA = [(wgu[0][:], 'wgu0'), (wgu[1][:], 'wgu1')] + \
                     [(HTF[:, (16 + 3 * i) * 688:(16 + 3 * i) * 688 + 2048].rearrange("p (a k c) -> p a k c", a=2, k=KC), f'hts{i}') for i in range(2)]
            slotsB = [(wgu[i][:].rearrange("p a k c -> p (a k c)")[:, 0:D], f'wgu{i}') for i in range(2)] + \
                     [(hinF[:, i * D:(i + 1) * D], f'hinw{i}') for i in range(5)]
            HW = [f'hinw{i}' for i in range(5)]
            for grp in GROUPS:
                c0g = grp[0] * TW
                modulate('act', hin, 0, l, sub, c0g, 2 * TW, [], 'hin')
                S.op('dve', lambda g: g.memset(fence_t[:], 0.0), writes=['HT', 'hts0', 'hts1', 'fence_t'])
                for fi, (f0, fw) in enumerate(FCH):
                    wb, wk = slotsA[fi % 4] if fi < 16 else slotsA[fi % 2]
                    S.dma('pool', wk, lambda q, wb=wb, fi=fi: q.dma_start(out=rv(wb), in_=wgu_d[l, fi]), writes=[wk])
                    for t in range(2):
                        for j, pp, pkn in ((0, p_g, 'p_g'), (1, p_u, 'p_u')):
                            for k in range(KC):
                                S.op('pe', lambda p, pp=pp, t=t, j=j, k=k, wb=wb: p.matmul(
                                    pp[t][0:128, 0:TW], rv(wb[:, j, k, :]), r32(hin)[:, k, t * TW:(t + 1) * TW],
                                    start=(k == 0), stop=(k == KC - 1)),
                                    reads=[wk, 'hin'], writes=[f"{pkn}{t}"])
                        S.op('act', lambda a, t=t: a.activation(out=gsb[:, 0:TW], in_=p_g[t][:, 0:TW], func=AF.Silu),
                             reads=[f"p_g{t}"], writes=['gsb'])
                        S.op('dve', lambda v, t=t, fi=fi: v.tensor_tensor(
                            out=r32(HT)[:, fi, t * TW:(t + 1) * TW], in0=gsb[:, 0:TW], in1=p_u[t][:, 0:TW],
                            op=ALU.mult), reads=['gsb', f"p_u{t}"],
                            writes=['HT', 'adaw0', 'adaw1'] + (['hts0', 'hts1'] if fi >= 16 else []))
                S.op('dve', lambda g: g.memset(fence_t[:], 0.0), writes=['hin', 'fence_t'] + HW)
                nb_ = 0
                for t in range(2):
                    c0 = c0g + t * TW
                    banks = [(p_g[0], 'p_g0'), (p_g[1], 'p_g1'), (p_u[0], 'p_u0'), (p_u[1], 'p_u1'),
                             (p_y[0], 'p_y0'), (p_y[1], 'p_y1'), (p_s[0], 'p_s0'), (p_s[1], 'p_s1')]
                    for fi, (f0, fw) in enumerate(FCH):
                        wbv, wk = slotsB[nb_ % len(slotsB)]; nb_ += 1
                        S.dma('pool', wk, lambda q, wbv=wbv, f0=f0, fw=fw: q.dma_start(
                            out=rv(wbv[0:fw, :]), in_=wd_d[l, f0:f0 + fw, :]), writes=[wk])
                        for dk in range(KC):
                            pt, pk = banks[dk]
                            S.op('pe', lambda p, pt=pt, wbv=wbv, dk=dk, fi=fi, fw=fw, t=t: p.matmul(
                                pt[:, 0:TW], rv(wbv[0:fw, dk * 128:(dk + 1) * 128]), r32(HT)[0:fw, fi, t * TW:(t + 1) * TW],
                                start=(fi == 0), stop=(fi == len(FCH) - 1)), reads=[wk, 'HT'], writes=[pk])
                    for dk in range(KC):
                        pt, pk = banks[dk]
                        if dk % 2:
                            S.op('act', lambda a, pt=pt, dk=dk: a.activation(out=zsq[:, dk, :], in_=pt[:, 0:TW], func=AF.Identity),
                                 reads=[pk], writes=['zsq_y', 'zsq'])
                        else:
                            S.op('dve', lambda v, pt=pt, dk=dk: v.tensor_copy(zsq[:, dk, :], pt[:, 0:TW]), reads=[pk], writes=['zsq_y', 'zsq'])
                    post_norm_tile_from_sbuf(l, sub, c0, 0.5)
                S.op('dve', lambda g: g.memset(fence_t[:], 0.0), writes=['hin', 'fence_t'] + HW)

        def post_norm_tile_from_sbuf(l, sub, c0, res_scale, W=TW):
            (po, pg, pn), (so, sg, sn) = split_cols(c0, W)
            gi = 3 * sub + 2
            for k in range(KC):
                if pn:
                    S.op('act', lambda a, k=k: a.activation(
                        out=gsb[:, po:po + pn], in_=zsq[:, k, po:po + pn], func=AF.Identity,
                        scale=mods[l][:, gi * KC + k, 16:17]), reads=['zsq_y', f"mods{l}"], writes=['gsb'])
                if sn:
                    s0 = sg - NP
                    S.op('dve', lambda v, k=k: v.tensor_tensor(
                        out=gsb[:, so:so + sn], in0=zsq[:, k, so:so + sn],
                        in1=mods[l][:, gi * KC + k, s0:s0 + sn], op=ALU.mult),
                        reads=['zsq_y', f"mods{l}"], writes=['gsb'])
                S.op('dve', lambda v, k=k: v.tensor_scalar_mul(zt[:, k, 0:W], xT[:, k, c0:c0 + W], DN_ALPHA),
                     reads=['xT'], writes=['zt'])
                S.op('dve', lambda v, k=k: v.scalar_tensor_tensor(
                    out=zt[:, k, 0:W], in0=gsb[:, 0:W], scalar=float(res_scale), in1=zt[:, k, 0:W],
                    op0=ALU.mult, op1=ALU.add), reads=['gsb', 'zt'], writes=['zt'])
            for k in range(KC):
                S.op('act', lambda a, k=k: a.activation(out=zsq[:, k, 0:W], in_=zt[:, k, 0:W], func=AF.Square),
                     reads=['zt', 'zsq_y'], writes=['zsq', 'zsq_y'])
            finish_norm(l, sub, c0, W)

        def finish_norm(l, sub, c0, W=TW):
            for k in range(KC):
                S.op('pe', lambda p, k=k: p.matmul(p_s[0][:, 0:W], ones[:], zt[:, k, 0:W],
                                                   start=(k == 0), stop=(k == KC - 1)),
                     reads=['ones', 'zt'], writes=['p_s0'])
            for k in range(KC):
                S.op('pe', lambda p, k=k: p.matmul(p_s[1][:, 0:W], ones[:], zsq[:, k, 0:W],
                                                   start=(k == 0), stop=(k == KC - 1)),
                     reads=['ones', 'zsq'], writes=['p_s1'])
            S.op('act', lambda a: a.activation(out=mean[:, 0:W], in_=p_s[0][:, 0:W], func=AF.Identity, scale=1.0 / D),
                 reads=['p_s0'], writes=['mean'])
            S.op('dve', lambda v: v.tensor_tensor(out=msq[:, 0:W], in0=mean[:, 0:W], in1=mean[:, 0:W], op=ALU.mult),
                 reads=['mean'], writes=['gsb'])
            S.op('dve', lambda v: v.scalar_tensor_tensor(out=rstd[:, 0:W], in0=p_s[1][:, 0:W], scalar=1.0 / D,
                                                         in1=msq[:, 0:W], op0=ALU.mult, op1=ALU.subtract),
                 reads=['p_s1', 'gsb'], writes=['rstd'])
            S.op('dve', lambda v: v.tensor_scalar_add(rstd[:, 0:W], rstd[:, 0:W], LN_EPS), reads=['rstd'], writes=['rstd'])
            S.op('act', lambda a: a.activation(out=rstd[:, 0:W], in_=rstd[:, 0:W], func=AF.Sqrt),
                 reads=['rstd'], writes=['rstd'])
            S.op('dve', lambda v: v.reciprocal(rstd[:, 0:W], rstd[:, 0:W]), reads=['rstd'], writes=['rstd'])
            for k in range(KC):
                S.op('dve', lambda v, k=k: v.tensor_tensor(out=zt[:, k, 0:W], in0=zt[:, k, 0:W], in1=mean[:, 0:W],
                                                           op=ALU.subtract), reads=['zt', 'mean'], writes=['zt'])
                S.op('dve', lambda v, k=k: v.tensor_tensor(out=zt[:, k, 0:W], in0=zt[:, k, 0:W], in1=rstd[:, 0:W],
                                                           op=ALU.mult), reads=['zt', 'rstd'], writes=['zt'])
                S.op('act', lambda a, k=k: a.activation(
                    out=xT[:, k, c0:c0 + W], in_=zt[:, k, 0:W], func=AF.Identity,
                    scale=lng[:, l * 3 + sub, k:k + 1], bias=lnb[:, l * 3 + sub, k:k + 1]),
                    reads=['zt', 'lng', 'lnb'], writes=['xT'])


        HTf = HT[:].rearrange("p a b -> p (a b)")
        ztf = zt[:].rearrange("p a b -> p (a b)")
        zsqf = zsq[:].rearrange("p a b -> p (a b)")
        TN_R = ['qin0', 'qin1', 'kin0', 'kin1', 'gkl', 'og0', 'og1', 'og2', 'og3']
        TN_A = ['qT0', 'qT1', 'kT0', 'kT1', 'L0', 'L1', 'tmp0', 'tmp1', 'kst0', 'kst1']
        TN_B = ['gT0', 'gT1', 'gT2', 'gT3', 'vT0', 'vT1', 'vT2', 'vT3']
        TN = TN_R + TN_A + TN_B
        T = {n: HTf[:, i * NQ:(i + 1) * NQ] for i, n in enumerate(TN_R)}
        T.update({n: ztf[:, i * NQ:(i + 1) * NQ] for i, n in enumerate(TN_A)})
        T.update({n: zsqf[:, i * NQ:(i + 1) * NQ] for i, n in enumerate(TN_B)})
        o0 = len(TN_R) * NQ
        Pm = HTf[:, o0:o0 + 128]
        vtok = HTf[:, o0 + 128:o0 + 1152].rearrange("p (c v) -> p c v", c=2)
        ksttok = HTf[:, o0 + 1152:o0 + 1664].rearrange("p (c v) -> p c v", c=2)
        sqt, rst_ = zsqf[:, 8 * NQ:8 * NQ + 128], zsqf[:, 8 * NQ + 128:8 * NQ + 256]
        ALIAS = ['HT', 'adaw0', 'adaw1', 'adaw2', 'adaw3', 'adaw4', 'adaw5', 'hin', 'zt', 'zsq', 'zsq_y', 'gsb', 'wgu0', 'wgu1', 'wd0', 'wd1',
                 'wslot0', 'wslot1', 'wslot2', 'wslot3', 'hts0', 'hts1', 'hinw0', 'hinw1', 'hinw2', 'hinw3', 'hinw4', 'Sst0', 'Sst1', 'Swk', 'Pm', 'sqt', 'rst_', 'Pm0', 'Pm1', 'sqt0', 'sqt1', 'rst0', 'rst1', 'vtok', 'ksttok'] + TN

        HEAVY = {'HT', 'hin', 'wgu0', 'wgu1', 'wd0', 'wd1', 'wslot0', 'wslot1', 'wslot2', 'wslot3', 'hts0', 'hts1'} | \
                {f'adaw{i}' for i in range(6)} | {f'hinw{i}' for i in range(5)}

        def fence(light=False):
            if light:
                ks = [k for k in ALIAS if k not in HEAVY]
                S.op('dve', lambda v: v.memset(fence_t[:], 0.0), reads=ks, writes=ks + ['fence_t'])
            else:
                S.op('pool', lambda g: g.memset(fence_t[:], 0.0), reads=ALIAS, writes=ALIAS + ['fence_t'])

        wslot, pslot = [0], [0]
        PBK = [(p_g[0], 'p_g0'), (p_g[1], 'p_g1'), (p_u[0], 'p_u0'), (p_u[1], 'p_u1')]

        def proj(wt, blk, rhs_fn, nk, N, rkeys):
            sl = wslot[0] % 4; wslot[0] += 1
            buf = wgu[sl // 2][:, sl % 2]
            key = f"wslot{sl}"
            S.dma('pool', key, lambda q: q.dma_start(out=rv(buf[:, 0:nk, :]), in_=wt[blk, :, 0:nk, :]), writes=[key])
            pt, pk = PBK[pslot[0] % 4]; pslot[0] += 1
            for k in range(nk):
                rhs = rhs_fn(k)
                S.op('pe', lambda p, k=k, rhs=rhs: p.matmul(pt[:, 0:N], rv(buf[:, k, :]), rhs,
                                                            start=(k == 0), stop=(k == nk - 1)),
                     reads=[key] + rkeys, writes=[pk])
            return pt[:, 0:N], pk

        def stub_norm(l, c0, W):
            for k in range(KC):
                S.op('act', lambda a, k=k: a.activation(out=zt[:, k, 0:W], in_=xT[:, k, c0:c0 + W],
                                                        func=AF.Identity, scale=DN_ALPHA), reads=['xT'], writes=['zt'])
                S.op('act', lambda a, k=k: a.activation(out=zsq[:, k, 0:W], in_=zt[:, k, 0:W], func=AF.Square),
                     reads=['zt', 'zsq_y'], writes=['zsq', 'zsq_y'])
            finish_norm(l, 1, c0, W)


        def s5_phase():
            U = HTf[:, 6944:6944 + 4 * NP].rearrange("p (q n) -> p q n", q=4)
            XA, XB = HTf[:, 0:2050], HTf[:, 2050:4100]
            ZP = HTf[:, 4100:5124].rearrange("p (g m) -> p g m", g=8)
            ZC = HTf[:, 5124:6148].rearrange("p (g m) -> p g m", g=8)
            Rt = HTf[:, 6148:6404].rearrange("p (a m) -> p a m", a=2)
            Us = HTf[:, 6404:6468].rearrange("p (q b) -> p q b", q=4)
            HnA, H0A = hin[:, :, 512:576], hin[:, :, 576:640]
            Hn = lambda g: HnA[:, g // 4, (g % 4) * 16:(g % 4) * 16 + 16]
            H0 = lambda g: H0A[:, g // 4, (g % 4) * 16:(g % 4) * 16 + 16]
            BbT = zsqf[:, 0:512].rearrange("p (g c) -> p g c", g=32)
            CTt = zsqf[:, 512:1024].rearrange("p (g c) -> p g c", g=32)
            ybuf = ztf[:, 0:2048].rearrange("p (q n) -> p q n", q=4)
            pwr = zsqf[:, 1024:1728].rearrange("p (a k g) -> p a k g", a=2, k=11)
            s5sm = zsqf[:, 1728:2112].rearrange("p (i g) -> p i g", i=12)
            Jm, Jt = zsqf[:, 2112:2240], zsqf[:, 2240:2368]
            Rtmp = zsqf[:, 2368:2496]
            identR = HTf[:, 6468:6596]
            K5 = ['Jm', 'Jt', 'Rtmp', 'identR', 'U', 'XA', 'XB', 'ZP', 'ZC', 'Rt', 'Us', 'Hn', 'H0', 'BbT', 'CTt', 'ybuf', 's5sm', 'pwr']
            ALIAS.extend(k for k in K5 if k not in ALIAS)
            fence()
            l = 0
            sm = lambda i: s5sm[:, i, :]
            S.op('pool', lambda g: g.affine_select(out=Jm, in_=ones[:], pattern=[[1, 128]], base=-64, channel_multiplier=-1,
                                                   compare_op=ALU.is_equal, fill=0.0), reads=['ones'], writes=['Jm'])
            S.op('pool', lambda g: g.affine_select(out=Jt, in_=ones[:], pattern=[[1, 128]], base=64, channel_multiplier=-1,
                                                   compare_op=ALU.is_equal, fill=0.0), reads=['ones'], writes=['Jt'])
            S.op('pool', lambda g: g.tensor_tensor(out=Jm, in0=Jm, in1=Jt, op=ALU.add), reads=['Jm', 'Jt'], writes=['Jm'])
            S.op('act', lambda a: a.activation(out=rv(identR), in_=ident[:], func=AF.Identity), reads=['ident'], writes=['identR'])
            V = lambda e, fn, r, w: S.op(e, fn, reads=r, writes=w)
            S.dma('sp', 'ld5', lambda q: q.dma_start(out=s5sm[:, 0:2, :], in_=s5aT_d), writes=['s5sm'])
            S.dma('sp', 'ld5', lambda q: q.dma_start(out=s5sm[:, 2, :], in_=s5ls_d), writes=['s5sm'])
            S.dma('sp', 'ld5', lambda q: q.dma_start(out=ybuf[:, 0, :].rearrange("p (g c) -> p g c", g=32), in_=s5BA_d), writes=['ybuf'])
            S.dma('sp', 'ld5', lambda q: q.dma_start(out=ybuf[:, 1, :].rearrange("p (g c) -> p g c", g=32), in_=s5BB_d), writes=['ybuf'])
            S.dma('sp', 'ld5', lambda q: q.dma_start(out=CTt, in_=s5CT_d), writes=['CTt'])
            S.dma('sp', 'ld5', lambda q: q.dma_start(out=s5d[:], in_=s5d_d), writes=['s5d'])
            S.dma('sp', 'ld5', lambda q: q.dma_start(out=s5bg[:], in_=s5bg_d), writes=['s5bg'])
            S.dma('pool', 'ld5p', lambda q: q.dma_start(out=rv(H0A), in_=s5h0_d.rearrange('p (k a) b -> p k (a b)', a=4)), writes=['H0'])
            k5 = ['s5sm']
            V('act', lambda a: a.activation(out=sm(2), in_=sm(2), func=AF.Exp), k5, k5)
            V('dve', lambda v: v.tensor_tensor(out=sm(3), in0=sm(1), in1=sm(2), op=ALU.mult), k5, k5)
            V('act', lambda a: a.activation(out=sm(4), in_=sm(3), func=AF.Sin, scale=1.0 / 16), k5, k5)
            V('act', lambda a: a.activation(out=sm(5), in_=sm(3), func=AF.Sin, scale=1.0 / 32), k5, k5)
            V('dve', lambda v: v.tensor_tensor(out=sm(5), in0=sm(5), in1=sm(5), op=ALU.mult), k5, k5)
            V('dve', lambda v: v.tensor_scalar(out=sm(5), in0=sm(5), scalar1=-2.0, scalar2=1.0, op0=ALU.mult, op1=ALU.add), k5, k5)
            for _ in range(4):
                V('dve', lambda v: v.tensor_tensor(out=sm(6), in0=sm(5), in1=sm(5), op=ALU.mult), k5, k5)
                V('dve', lambda v: v.tensor_tensor(out=sm(7), in0=sm(4), in1=sm(4), op=ALU.mult), k5, k5)
                V('dve', lambda v: v.scalar_tensor_tensor(out=sm(4), in0=sm(5), scalar=2.0, in1=sm(4), op0=ALU.mult, op1=ALU.mult), k5, k5)
                V('dve', lambda v: v.tensor_tensor(out=sm(5), in0=sm(6), in1=sm(7), op=ALU.subtract), k5, k5)
            V('dve', lambda v: v.tensor_tensor(out=sm(6), in0=sm(0), in1=sm(2), op=ALU.mult), k5, k5)
            V('act', lambda a: a.activation(out=sm(6), in_=sm(6), func=AF.Exp), k5, k5)
            V('dve', lambda v: v.tensor_tensor(out=sm(5), in0=sm(5), in1=sm(6), op=ALU.mult), k5, k5)
            V('dve', lambda v: v.tensor_tensor(out=sm(4), in0=sm(4), in1=sm(6), op=ALU.mult), k5, k5)
            V('dve', lambda v: v.tensor_tensor(out=sm(6), in0=sm(0), in1=sm(0), op=ALU.mult), k5, k5)
            V('dve', lambda v: v.tensor_tensor(out=sm(7), in0=sm(1), in1=sm(1), op=ALU.mult), k5, k5)
            V('dve', lambda v: v.tensor_tensor(out=sm(6), in0=sm(6), in1=sm(7), op=ALU.add), k5, k5)
            V('dve', lambda v: v.reciprocal(sm(6), sm(6)), k5, k5)
            V('dve', lambda v: v.tensor_scalar_add(sm(7), sm(5), -1.0), k5, k5)
            V('dve', lambda v: v.tensor_tensor(out=sm(8), in0=sm(7), in1=sm(0), op=ALU.mult), k5, k5)
            V('dve', lambda v: v.tensor_tensor(out=sm(9), in0=sm(4), in1=sm(1), op=ALU.mult), k5, k5)
            V('dve', lambda v: v.tensor_tensor(out=sm(8), in0=sm(8), in1=sm(9), op=ALU.add), k5, k5)
            V('dve', lambda v: v.tensor_tensor(out=sm(8), in0=sm(8), in1=sm(6), op=ALU.mult), k5, k5)
            V('dve', lambda v: v.tensor_tensor(out=sm(9), in0=sm(4), in1=sm(0), op=ALU.mult), k5, k5)
            V('dve', lambda v: v.tensor_tensor(out=sm(10), in0=sm(7), in1=sm(1), op=ALU.mult), k5, k5)
            V('dve', lambda v: v.tensor_tensor(out=sm(9), in0=sm(9), in1=sm(10), op=ALU.subtract), k5, k5)
            V('dve', lambda v: v.tensor_tensor(out=sm(9), in0=sm(9), in1=sm(6), op=ALU.mult), k5, k5)
            V('dve', lambda v: v.tensor_scalar_mul(sm(9), sm(9), sgn[:, 0:1]), k5 + ['sgn'], k5)
            V('dve', lambda v: v.tensor_scalar_mul(sm(9), sm(9), -1.0), k5, k5)
            zb = lambda i: sm(i).unsqueeze(2).to_broadcast([128, 32, 16])
            BAv, BBv = (ybuf[:, i, :].rearrange("p (g c) -> p g c", g=32) for i in range(2))
            V('dve', lambda v: v.tensor_tensor(out=BbT, in0=BAv, in1=zb(8), op=ALU.mult), k5 + ['ybuf'], ['BbT'])
            V('dve', lambda v: v.tensor_tensor(out=BBv, in0=BBv, in1=zb(9), op=ALU.mult), k5 + ['ybuf'], ['ybuf'])
            V('dve', lambda v: v.tensor_tensor(out=BbT, in0=BbT, in1=BBv, op=ALU.add), ['BbT', 'ybuf'], ['BbT'])
            V('dve', lambda v: v.tensor_scalar_mul(CTt[64:128], CTt[64:128], -1.0), ['CTt'], ['CTt'])
            V('dve', lambda v: v.tensor_copy(pwr[:, 0, 0, :], sm(5)), k5, ['pwr'])
            V('dve', lambda v: v.tensor_copy(pwr[:, 1, 0, :], sm(4)), k5, ['pwr'])
            for k in range(1, 11):
                cr, ci, nr_, ni_ = pwr[:, 0, k - 1, :], pwr[:, 1, k - 1, :], pwr[:, 0, k, :], pwr[:, 1, k, :]
                V('dve', lambda v, cr=cr: v.tensor_tensor(out=sm(6), in0=cr, in1=cr, op=ALU.mult), ['pwr'] + k5, k5)
                V('dve', lambda v, ci=ci: v.tensor_tensor(out=sm(7), in0=ci, in1=ci, op=ALU.mult), ['pwr'] + k5, k5)
                V('dve', lambda v, cr=cr, ci=ci, ni_=ni_: v.scalar_tensor_tensor(out=ni_, in0=cr, scalar=2.0, in1=ci, op0=ALU.mult, op1=ALU.mult),
                  ['pwr'], ['pwr'])
                V('dve', lambda v, nr_=nr_: v.tensor_tensor(out=nr_, in0=sm(6), in1=sm(7), op=ALU.subtract), k5, ['pwr'])
            V('dve', lambda v: v.tensor_scalar_mul(pwr[:, 1].rearrange("p k g -> p (k g)"), pwr[:, 1].rearrange("p k g -> p (k g)"), sgn[:, 0:1]),
              ['pwr', 'sgn'], ['pwr'])
            V('act', lambda a: a.activation(out=rv(HTf[:, 0:4100]), in_=ones[:, 0:1].to_broadcast([128, 4100]), func=AF.Identity, scale=0.0),
              ['ones'], ['XA', 'XB'])
            V('act', lambda a: a.activation(out=rv(HTf[:, 4100:6148]), in_=ones[:, 0:1].to_broadcast([128, 2048]), func=AF.Identity, scale=0.0),
              ['ones'], ['ZP', 'ZC'])
            for sc in list(range(NP // NQ)) + ['s']:
                t0, W = (NP, NS) if sc == 's' else (sc * NQ, NQ)
                modulate('act', hin, 0, l, 1, t0, W, [], 'hin')
                for q in range(4):
                    pa, pk = proj(w_inu_d, q, lambda k, W=W: r32(hin)[:, k, 0:W], KC, W, ['hin'])
                    dst = Us[:, q, :] if sc == 's' else U[:, q, t0:t0 + W]
                    S.op('act', lambda a, pa=pa, dst=dst: a.activation(out=rv(dst), in_=pa, func=AF.Identity), reads=[pk],
                         writes=['Us' if sc == 's' else 'U'])
            CTS = [(c0, 512) for c0 in range(0, NP, 512)]
            for q in range(4):
                S.op('pe', lambda p, q=q: p.transpose(p_s[0][:, 0:128], BbT[:, 8 * q:8 * q + 8, :].rearrange("p g c -> p (g c)"), ident[:]),
                     reads=['BbT', 'ident'], writes=['p_s0'])
                for gl in range(8):
                    S.op('dve', lambda v, gl=gl: v.tensor_scalar_mul(rv(ZP[:, gl, :]), p_s[0][:, 0:128], blkm[:, gl:gl + 1]),
                         reads=['p_s0', 'blkm'], writes=['ZP'])
                    S.op('act', lambda a, gl=gl, q=q: a.activation(out=rv(ZC[:, gl, gl * 16:(gl + 1) * 16]), in_=CTt[:, 8 * q + gl, :], func=AF.Identity),
                         reads=['CTt'], writes=['ZC'])
                for gl in range(8):
                    g = 8 * q + gl
                    for ci_, (c0, cw) in enumerate(CTS):
                        S.op('pe', lambda p, gl=gl, q=q, c0=c0, cw=cw: p.matmul(p_s[0][:, 0:cw], rv(ZP[:, gl, :]), rv(U[:, q, c0:c0 + cw]),
                                                                                start=True, stop=True), reads=['ZP', 'U'], writes=['p_s0'])
                        S.op('act', lambda a, c0=c0, cw=cw: a.activation(out=rv(XA[:, 2 + c0:2 + c0 + cw]), in_=p_s[0][:, 0:cw], func=AF.Identity),
                             reads=['p_s0'], writes=['XA'])
                    src, dst, sk, dk_ = XA, XB, 'XA', 'XB'
                    for k in range(11):
                        d = 1 << k
                        rt = Rt[:, k % 2, :]
                        S.op('dve', lambda v, k=k, g=g: v.tensor_scalar_mul(Rtmp, ident[:], pwr[:, 0, k, g:g + 1]),
                             reads=['ident', 'pwr'], writes=['Rtmp'])
                        S.op('dve', lambda v, rt=rt, k=k, g=g: v.scalar_tensor_tensor(out=rv(rt), in0=Jm, scalar=pwr[:, 1, k, g:g + 1], in1=Rtmp,
                                                                                     op0=ALU.mult, op1=ALU.add),
                             reads=['Jm', 'pwr', 'Rtmp'], writes=['Rt'])
                        if k == 0:
                            S.op('pe', lambda p, rt=rt, g=g: p.matmul(p_y[0][:, 0:NS], rv(rt), rv(H0(g)), start=True, stop=False),
                                 reads=['Rt', 'H0'], writes=['p_y0'])
                            S.op('pe', lambda p, gl=gl, q=q: p.matmul(p_y[0][:, 0:NS], rv(ZP[:, gl, :]), rv(Us[:, q, :]), start=False, stop=True),
                                 reads=['ZP', 'Us'], writes=['p_y0'])
                            S.op('act', lambda a, g=g: a.activation(out=rv(Hn(g)), in_=p_y[0][:, 0:NS], func=AF.Identity),
                                 reads=['p_y0'], writes=['Hn'])
                        for ci_, (c0, cw) in enumerate(CTS):
                            pt, pk = (p_y[ci_ % 2], f'p_y{ci_ % 2}')
                            lo = max(d - c0, 0)
                            lo -= lo % 2
                            has_r = lo < cw
                            S.op('pe', lambda p, pt=pt, src=src, c0=c0, cw=cw, has_r=has_r: p.matmul(
                                pt[:, 0:cw], rv(identR), rv(src[:, 2 + c0:2 + c0 + cw]), start=True, stop=not has_r),
                                reads=['identR', sk], writes=[pk])
                            if has_r:
                                S.op('pe', lambda p, pt=pt, rt=rt, src=src, c0=c0, cw=cw, lo=lo, d=d: p.matmul(
                                    pt[:, lo:cw], rv(rt), rv(src[:, 2 + c0 + lo - d:2 + c0 + cw - d]), start=False, stop=True),
                                    reads=['Rt', sk], writes=[pk])
                            S.op('act' if ci_ % 2 else 'dve', (lambda a, pt=pt, dst=dst, c0=c0, cw=cw: a.activation(
                                out=rv(dst[:, 2 + c0:2 + c0 + cw]), in_=pt[:, 0:cw], func=AF.Identity)) if ci_ % 2 else
                                (lambda v, pt=pt, dst=dst, c0=c0, cw=cw: v.tensor_copy(rv(dst[:, 2 + c0:2 + c0 + cw]), pt[:, 0:cw])),
                                reads=[pk], writes=[dk_])
                        src, dst, sk, dk_ = dst, src, dk_, sk
                    for ci_, (c0, cw) in enumerate(CTS):
                        S.op('pe', lambda p, ci_=ci_, gl=gl, src=src, c0=c0, cw=cw: p.matmul(
                            PBK[ci_][0][:, 0:cw], rv(ZC[:, gl, :]), rv(src[:, 2 + c0:2 + c0 + cw]), start=(gl == 0), stop=(gl == 7)),
                            reads=['ZC', sk], writes=[PBK[ci_][1]])
                    S.op('pe', lambda p, gl=gl, g=g: p.matmul(p_s[1][:, 0:NS], rv(ZC[:, gl, :]), rv(Hn(g)), start=(gl == 0), stop=(gl == 7)),
                         reads=['ZC', 'Hn'], writes=['p_s1'])
                    S.op('dve', lambda v, g=g, src=src: v.tensor_copy(HS[:, g:g + 1], src[:, 2 + NP - 1:2 + NP]), reads=[sk], writes=['HS'])
                GC = float(np.sqrt(2.0 / np.pi))
                for (pap, pk, uu, ukey, cw) in [(PBK[i][0], PBK[i][1], U[:, q, c0:c0 + cw_], 'U', cw_) for i, (c0, cw_) in enumerate(CTS)] + \
                                               [(p_s[1], 'p_s1', Us[:, q, :], 'Us', NS)]:
                    y_, t_ = ybuf[:, 0, 0:cw], ybuf[:, 1, 0:cw]
                    S.op('dve', lambda v, pap=pap, uu=uu, y_=y_, cw=cw, q=q: v.scalar_tensor_tensor(
                        out=y_, in0=uu, scalar=s5d[:, q:q + 1], in1=pap[:, 0:cw], op0=ALU.mult, op1=ALU.add),
                        reads=[pk, ukey, 's5d'], writes=['ybuf'])
                    S.op('dve', lambda v, y_=y_, t_=t_: v.tensor_tensor(out=t_, in0=y_, in1=y_, op=ALU.mult), reads=['ybuf'], writes=['ybuf'])
                    S.op('dve', lambda v, t_=t_: v.tensor_scalar(out=t_, in0=t_, scalar1=0.044715, scalar2=1.0, op0=ALU.mult, op1=ALU.add),
                         reads=['ybuf'], writes=['ybuf'])
                    S.op('dve', lambda v, y_=y_, t_=t_: v.tensor_tensor(out=t_, in0=t_, in1=y_, op=ALU.mult), reads=['ybuf'], writes=['ybuf'])
                    S.op('act', lambda a, t_=t_: a.activation(out=t_, in_=t_, func=AF.Tanh, scale=GC), reads=['ybuf'], writes=['ybuf'])
                    S.op('dve', lambda v, t_=t_: v.tensor_scalar(out=t_, in0=t_, scalar1=1.0, scalar2=0.5, op0=ALU.add, op1=ALU.mult),
                         reads=['ybuf'], writes=['ybuf'])
                    S.op('dve', lambda v, y_=y_, t_=t_, uu=uu: v.tensor_tensor(out=rv(uu), in0=t_, in1=y_, op=ALU.mult),
                         reads=['ybuf'], writes=[ukey])
            for (zsrc, zkey, c0, cw) in [(U, 'U', c0, cw) for (c0, cw) in CTS] + [(Us, 'Us', 0, NS)]:
                for q2 in range(4):
                    pa, pk = proj(s5wg_d, q2, lambda k, zsrc=zsrc, c0=c0, cw=cw: rv(zsrc[:, k, c0:c0 + cw]), 4, cw, [zkey])
                    S.op('act', lambda a, pa=pa, q2=q2, cw=cw: a.activation(out=ybuf[:, q2, 0:cw], in_=pa, func=AF.Sigmoid,
                                                                           bias=s5bg[:, q2:q2 + 1], scale=1.0),
                         reads=[pk, 's5bg'], writes=['ybuf'])
                for q2 in range(4):
                    S.op('dve', lambda v, zsrc=zsrc, q2=q2, c0=c0, cw=cw: v.tensor_tensor(
                        out=rv(zsrc[:, q2, c0:c0 + cw]), in0=zsrc[:, q2, c0:c0 + cw], in1=ybuf[:, q2, 0:cw], op=ALU.mult),
                        reads=['ybuf', zkey], writes=[zkey])
            S.dma('sp', 'st', lambda q: q.dma_start(out=s5hp_d, in_=HS[:]), reads=['HS'])
            S.dma('sp', 'st', lambda q: q.dma_start(out=s5hs_d.rearrange('p (k a) b -> p k (a b)', a=4), in_=HnA), reads=['Hn'])
            fence()
            return U, Us


        def gla_sample(Us5):
            l = 0
            fence()
            P16 = {n: ztf[:, i * NS:(i + 1) * NS] for i, n in enumerate(
                ['qS0', 'qS1', 'kS0', 'kS1', 'eg0', 'eg1', 'gS0', 'gS1', 'gS2', 'gS3', 'vS0', 'vS1', 'vS2', 'vS3', 'tS0', 'tS1'])}
            k_tok, v_tok, Km = ztf[0:NS, 256:512], ztf[0:NS, 512:1024], ztf[0:NS, 1024:1152]
            o_tok, sq_tok, st16 = ztf[0:NS, 1280:1792], ztf[0:NS, 1792:2304], ztf[0:NS, 2304:2312]
            Sg = zsqf[:, 0:2048].rearrange("p (b v) -> p b v", b=NS)
            Qm = zsqf[:, 2048:2304].rearrange("p (b c) -> p b c", b=NS)
            gkS, ogs = HTf[:, 0:NS], HTf[:, 64:64 + 4 * NS].rearrange("p (h b) -> p h b", h=4)
            KS = list(P16) + ['k_tok', 'v_tok', 'Km', 'o_tok', 'sq_tok', 'st16', 'Sg', 'Qm', 'gkS', 'ogs']
            ALIAS.extend(k for k in KS if k not in ALIAS)
            fence()
            modulate('act', hin, 0, l, 1, NP, NS, [], 'hin')
            rh = lambda k: r32(hin)[:, k, 0:NS]
            for f in range(2):
                pa, pk = proj(w_in_d, f, rh, KC, NS, ['hin'])
                S.op('act', lambda a, pa=pa, f=f: a.activation(out=P16[f'qS{f}'], in_=pa, func=AF.Identity, scale=64 ** -0.5),
                     reads=[pk], writes=[f'qS{f}'])
                pa, pk = proj(w_in_d, 2 + f, rh, KC, NS, ['hin'])
                S.op('dve', lambda v, pa=pa, f=f: v.tensor_copy(P16[f'kS{f}'], pa), reads=[pk], writes=[f'kS{f}'])
            for f in range(4):
                pa, pk = proj(w_in_d, 4 + f, rh, KC, NS, ['hin'])
                S.op('dve', lambda v, pa=pa, f=f: v.tensor_copy(P16[f'vS{f}'], pa), reads=[pk], writes=[f'vS{f}'])
                pa, pk = proj(w_in_d, 8 + f, rh, KC, NS, ['hin'])
                S.op('act', lambda a, pa=pa, f=f: a.activation(out=P16[f'gS{f}'], in_=pa, func=AF.Silu), reads=[pk], writes=[f'gS{f}'])
            pa, pk = proj(w_in_d, 12, rh, KC, NS, ['hin'])
            S.op('act', lambda a, pa=pa: a.activation(out=rv(gkS[0:16, :]), in_=pa[0:16, :], func=AF.Identity), reads=[pk], writes=['gkS'])
            for f in range(2):
                S.op('pe', lambda p, f=f: p.matmul(p_y[f][:, 0:NS], r32(Wgk)[0:16, f * 128:(f + 1) * 128], rv(gkS[0:16, :]),
                                                   start=True, stop=True), reads=['Wgk', 'gkS'], writes=[f'p_y{f}'])
                S.op('act', lambda a, f=f: a.activation(out=P16[f'tS{f}'], in_=p_y[f][:, 0:NS], func=AF.Sigmoid, bias=bgk[:, f:f + 1], scale=1.0),
                     reads=[f'p_y{f}', 'bgk'], writes=[f'tS{f}'])
                S.op('act', lambda a, f=f: a.activation(out=P16[f'tS{f}'], in_=P16[f'tS{f}'], func=AF.Ln), reads=[f'tS{f}'], writes=[f'tS{f}'])
                S.op('act', lambda a, f=f: a.activation(out=P16[f'eg{f}'], in_=P16[f'tS{f}'], func=AF.Exp, scale=1.0 / 16),
                     reads=[f'tS{f}'], writes=[f'eg{f}'])
            for nm, nf, dst, dkey in (('kS', 2, k_tok, 'k_tok'), ('vS', 4, v_tok, 'v_tok')):
                for f in range(nf):
                    S.op('pe', lambda p, nm=nm, f=f: p.transpose(p_s[f % 2][0:NS, 0:128], P16[f'{nm}{f}'], ident[:]),
                         reads=[f'{nm}{f}', 'ident'], writes=[f'p_s{f % 2}'])
                    S.op('act', lambda a, f=f, dst=dst: a.activation(out=dst[:, f * 128:(f + 1) * 128], in_=p_s[f % 2][0:NS, 0:128], func=AF.Identity),
                         reads=[f'p_s{f % 2}'], writes=[dkey])
            for f in range(2):
                S.dma('sp', 'ldg', lambda q, f=f: q.dma_start(out=Sg, in_=glas0_d[:, :, f, :]), writes=['Sg'])
                S.op('pool', lambda g: g.memset(Qm, 0.0), writes=['Qm'])
                for b in range(NS):
                    S.op('dve', lambda v, f=f, b=b: v.tensor_copy(Qm[:, b, b:b + 1], P16[f'qS{f}'][:, b:b + 1]), reads=[f'qS{f}', 'Qm'], writes=['Qm'])
                for hb in range(2):
                    h = 2 * f + hb
                    rows = slice(hb * 64, (hb + 1) * 64)
                    for b in range(NS):
                        S.op('dve', lambda v, f=f, b=b: v.tensor_scalar_mul(Km, k_tok[:, f * 128:(f + 1) * 128], ident[0:NS, b:b + 1]),
                             reads=['k_tok', 'ident'], writes=['Km'])
                        S.op('pe', lambda p, h=h: p.matmul(p_y[0][:, 0:128], Km, v_tok[:, h * 128:(h + 1) * 128], start=True, stop=True),
                             reads=['Km', 'v_tok'], writes=['p_y0'])
                        S.op('dve', lambda v, rows=rows, b=b, f=f: v.scalar_tensor_tensor(
                            out=Sg[rows, b, :], in0=Sg[rows, b, :], scalar=P16[f'eg{f}'][rows, b:b + 1], in1=p_y[0][rows, 0:128],
                            op0=ALU.mult, op1=ALU.add), reads=['Sg', f'eg{f}', 'p_y0'], writes=['Sg'])
                        S.op('pe', lambda p, rows=rows, b=b: p.matmul(p_y[1][0:NS, 0:128], Qm[rows, b, :], Sg[rows, b, :],
                                                                      start=(b == 0), stop=(b == NS - 1)), reads=['Qm', 'Sg'], writes=['p_y1'])
                    S.op('act', lambda a, h=h: a.activation(out=o_tok[:, h * 128:(h + 1) * 128], in_=p_y[1][0:NS, 0:128], func=AF.Identity),
                         reads=['p_y1'], writes=['o_tok'])
                S.dma('sp', 'st', lambda q, f=f: q.dma_start(out=glas_d[:, :, f, :], in_=Sg), reads=['Sg'])
            o3, q3 = o_tok.rearrange("p (h v) -> p h v", h=4), sq_tok.rearrange("p (h v) -> p h v", h=4)
            S.op('dve', lambda v: v.tensor_tensor(out=sq_tok, in0=o_tok, in1=o_tok, op=ALU.mult), reads=['o_tok'], writes=['sq_tok'])
            S.op('dve', lambda v: v.tensor_reduce(out=st16[:, 0:4], in_=q3, op=ALU.add, axis=mybir.AxisListType.X), reads=['sq_tok'], writes=['st16'])
            S.op('dve', lambda v: v.tensor_scalar(out=st16[:, 0:4], in0=st16[:, 0:4], scalar1=1.0 / 128, scalar2=1e-5, op0=ALU.mult, op1=ALU.add),
                 reads=['st16'], writes=['st16'])
            S.op('act', lambda a: a.activation(out=st16[:, 0:4], in_=st16[:, 0:4], func=AF.Sqrt), reads=['st16'], writes=['st16'])
            S.op('dve', lambda v: v.reciprocal(st16[:, 0:4], st16[:, 0:4]), reads=['st16'], writes=['st16'])
            S.op('dve', lambda v: v.tensor_tensor(out=o3, in0=o3, in1=st16[:, 0:4].unsqueeze(2).to_broadcast([NS, 4, 128]), op=ALU.mult),
                 reads=['o_tok', 'st16'], writes=['o_tok'])
            for h in range(4):
                S.op('pe', lambda p, h=h: p.transpose(p_s[h % 2][:, 0:NS], o_tok[:, h * 128:(h + 1) * 128], ident[0:NS, 0:NS]),
                     reads=['o_tok', 'ident'], writes=[f'p_s{h % 2}'])
                S.op('dve', lambda v, h=h: v.scalar_tensor_tensor(out=rv(ogs[:, h, :]), in0=p_s[h % 2][:, 0:NS], scalar=gng[:, 0:1],
                                                                  in1=P16[f'gS{h}'], op0=ALU.mult, op1=ALU.mult),
                     reads=[f'p_s{h % 2}', 'gng', f'gS{h}'], writes=['ogs'])
            fence()
            for dk in range(KC):
                pa, pk = proj(w_out_d, dk, lambda k: rv(ogs[:, k, :]) if k < 4 else rv(Us5[:, k - 4, :]), KC, NS, ['ogs', 'Us'])
                S.op('act', lambda a, pa=pa, dk=dk: a.activation(out=zsq[:, dk, 0:NS], in_=pa, func=AF.Identity), reads=[pk], writes=['zsq_y', 'zsq'])
            post_norm_tile_from_sbuf(l, 1, NP, 1.0, NS)
            fence()

        def mix0_sublayer():
            l = 0
            U5, Us5 = s5_phase()
            fence()
            for f in range(2):
                S.op('act', lambda a, f=f: a.activation(out=rv(Sst[f]), in_=ones[:], func=AF.Identity, scale=0.0),
                     reads=['ones'], writes=[f'Sst{f}'])
            for sc in range(NP // NQ):
                t0 = sc * NQ
                modulate('act', hin, 0, l, 1, t0, NQ, [], 'hin')
                rh = lambda k: r32(hin)[:, k, 0:NQ]
                for f in range(2):
                    pa, pk = proj(w_in_d, f, rh, KC, NQ, ['hin'])
                    S.op('act', lambda a, pa=pa, f=f: a.activation(out=T[f'qT{f}'], in_=pa, func=AF.Identity,
                                                                   scale=64 ** -0.5), reads=[pk], writes=[f'qT{f}'])
                    pa, pk = proj(w_in_d, 2 + f, rh, KC, NQ, ['hin'])
                    S.op('dve', lambda v, pa=pa, f=f: v.tensor_copy(T[f'kT{f}'], pa), reads=[pk], writes=[f'kT{f}'])
                for f in range(4):
                    pa, pk = proj(w_in_d, 4 + f, rh, KC, NQ, ['hin'])
                    S.op('dve', lambda v, pa=pa, f=f: v.tensor_copy(T[f'vT{f}'], pa), reads=[pk], writes=[f'vT{f}'])
                    pa, pk = proj(w_in_d, 8 + f, rh, KC, NQ, ['hin'])
                    S.op('act', lambda a, pa=pa, f=f: a.activation(out=T[f'gT{f}'], in_=pa, func=AF.Silu),
                         reads=[pk], writes=[f'gT{f}'])
                pa, pk = proj(w_in_d, 12, rh, KC, NQ, ['hin'])
                S.op('act', lambda a, pa=pa: a.activation(out=rv(T['gkl'][0:16, :]), in_=pa[0:16, :], func=AF.Identity),
                     reads=[pk], writes=['gkl'])
                for f in range(2):
                    S.op('pe', lambda p, f=f: p.matmul(p_y[f][:, 0:NQ], r32(Wgk)[0:16, f * 128:(f + 1) * 128],
                                                       rv(T['gkl'][0:16, :]), start=True, stop=True),
                         reads=['Wgk', 'gkl'], writes=[f'p_y{f}'])
                    S.op('act', lambda a, f=f: a.activation(out=T[f'tmp{f}'], in_=p_y[f][:, 0:NQ], func=AF.Sigmoid,
                                                            bias=bgk[:, f:f + 1], scale=1.0),
                         reads=[f'p_y{f}', 'bgk'], writes=[f'tmp{f}'])
                    S.op('act', lambda a, f=f: a.activation(out=T[f'tmp{f}'], in_=T[f'tmp{f}'], func=AF.Ln),
                         reads=[f'tmp{f}'], writes=[f'tmp{f}'])
                    S.op('dve', lambda v, f=f: v.tensor_tensor_scan(out=T[f'L{f}'], data0=rstm[:], data1=T[f'tmp{f}'],
                                                                    initial=0.0, op0=ALU.mult, op1=ALU.add),
                         reads=['rstm', f'tmp{f}'], writes=[f'L{f}'])
                    last = T[f'L{f}'].rearrange("p (c t) -> p c t", c=2)[:, :, 127]
                    S.op('dve', lambda v, f=f, last=last: v.tensor_scalar_mul(bl[:, f, :], last, 1.0 / 16),
                         reads=[f'L{f}'], writes=['bl'])
                    S.op('act', lambda a, f=f: a.activation(out=dec[:, f, :], in_=bl[:, f, :], func=AF.Exp),
                         reads=['bl'], writes=['dec'])
                    for nm, src, sgn in (('qin', 'qT', 1.0), ('kin', 'kT', -1.0)):
                        S.op('act', lambda a, f=f, sgn=sgn: a.activation(out=T[f'tmp{f}'], in_=T[f'L{f}'], func=AF.Exp,
                                                                         scale=sgn / 16), reads=[f'L{f}'], writes=[f'tmp{f}'])
                        S.op('dve', lambda v, f=f, nm=nm, src=src: v.tensor_tensor(
                            out=rv(T[f'{nm}{f}']), in0=T[f'{src}{f}'], in1=T[f'tmp{f}'], op=ALU.mult),
                            reads=[f'{src}{f}', f'tmp{f}'], writes=[f'{nm}{f}'])
                    for c in range(2):
                        S.op('act', lambda a, f=f, c=c: a.activation(
                            out=T[f'tmp{f}'][:, c * 128:(c + 1) * 128], in_=T[f'L{f}'][:, c * 128:(c + 1) * 128],
                            func=AF.Exp, scale=-1.0 / 16, bias=bl[:, f, c:c + 1]),
                            reads=[f'L{f}', 'bl'], writes=[f'tmp{f}'])
                    S.op('dve', lambda v, f=f: v.tensor_tensor(out=T[f'kst{f}'], in0=T[f'kT{f}'], in1=T[f'tmp{f}'],
                                                               op=ALU.mult), reads=[f'kT{f}', f'tmp{f}'], writes=[f'kst{f}'])
                for c in range(2):
                    cs = slice(c * 128, (c + 1) * 128)
                    for nm, nf, dst, dk_ in (('vT', 4, vtok, 'vtok'), ('kst', 2, ksttok, 'ksttok')):
                        for f in range(nf):
                            S.op('pe', lambda p, nm=nm, f=f, cs=cs: p.transpose(p_s[f % 2][:, 0:128], T[f'{nm}{f}'][:, cs], ident[:]),
                                 reads=[f'{nm}{f}', 'ident'], writes=[f'p_s{f % 2}'])
                            S.op('act', lambda a, f=f, c=c, dst=dst: a.activation(
                                out=rv(dst[:, c, f * 128:(f + 1) * 128]), in_=p_s[f % 2][:, 0:128], func=AF.Identity),
                                reads=[f'p_s{f % 2}'], writes=[dk_])
                def gla_chain(c, h):
                    cs = slice(c * 128, (c + 1) * 128)
                    f, r0 = h // 2, (h % 2) * 64
                    rows = slice(r0, r0 + 64)
                    w_ = h // 2
                    Pm_ = HTf[:, o0 + 1664 * w_:o0 + 1664 * w_ + 128] if w_ else Pm
                    sq_ = zsqf[:, 8 * NQ + 256 * w_:8 * NQ + 256 * w_ + 128]
                    rs_ = zsqf[:, 8 * NQ + 256 * w_ + 128:8 * NQ + 256 * w_ + 256]
                    kP, kS, kR = f'Pm{w_}', f'sqt{w_}', f'rst{w_}'
                    (bS, kbS), (bO, kbO), (bT, kbT), (bU, kbU) = ((p_y[0], 'p_y0'), (p_y[1], 'p_y1'), (p_s[0], 'p_s0'), (p_s[1], 'p_s1')) if w_ == 0 else \
                                                                 ((p_g[0], 'p_g0'), (p_g[1], 'p_g1'), (p_u[0], 'p_u0'), (p_u[1], 'p_u1'))
                    S.op('pe', lambda p: p.matmul(bS[:, 0:128], rv(T[f'kin{f}'][rows, cs]), rv(T[f'qin{f}'][rows, cs]), start=True, stop=True),
                         reads=[f'kin{f}', f'qin{f}'], writes=[kbS])
                    yield
                    S.op('dve', lambda v: v.tensor_tensor(out=rv(Pm_), in0=bS[:, 0:128], in1=mask2[:, 128:256], op=ALU.mult),
                         reads=[kbS, 'mask2'], writes=[kP])
                    yield
                    S.op('pe', lambda p: p.matmul(bO[:, 0:128], rv(vtok[:, c, h * 128:(h + 1) * 128]), rv(Pm_), start=True, stop=False),
                         reads=['vtok', kP], writes=[kbO])
                    S.op('pe', lambda p: p.matmul(bO[:, 0:128], rv(Sst[f][rows, :]), rv(T[f'qin{f}'][rows, cs]), start=False, stop=True),
                         reads=[f'Sst{f}', f'qin{f}'], writes=[kbO])
                    S.op('pe', lambda p: p.matmul(bU[:, 0:128], rv(ksttok[:, c, f * 128:(f + 1) * 128]), rv(vtok[:, c, h * 128:(h + 1) * 128]),
                                                  start=True, stop=True), reads=['ksttok', 'vtok'], writes=[kbU])
                    yield
                    S.op('act', lambda a: a.activation(out=sq_, in_=bO[:, 0:128], func=AF.Square), reads=[kbO], writes=[kS])
                    S.op('dve', lambda v: v.scalar_tensor_tensor(out=rv(Sst[f][rows, :]), in0=Sst[f][rows, :], scalar=dec[rows, f, c:c + 1],
                                                                 in1=bU[rows, 0:128], op0=ALU.mult, op1=ALU.add),
                         reads=[f'Sst{f}', 'dec', kbU, kbO], writes=[f'Sst{f}'])
                    yield
                    S.op('pe', lambda p: p.matmul(bT[:, 0:128], ones[:], sq_, start=True, stop=True), reads=['ones', kS], writes=[kbT])
                    yield
                    S.op('dve', lambda v: v.tensor_scalar(out=rs_, in0=bT[:, 0:128], scalar1=1.0 / 128, scalar2=1e-5, op0=ALU.mult, op1=ALU.add),
                         reads=[kbT], writes=[kR])
                    yield
                    S.op('act', lambda a: a.activation(out=rs_, in_=rs_, func=AF.Sqrt), reads=[kR], writes=[kR])
                    yield
                    S.op('dve', lambda v: v.reciprocal(rs_, rs_), reads=[kR], writes=[kR])
                    S.op('dve', lambda v: v.scalar_tensor_tensor(out=sq_, in0=bO[:, 0:128], scalar=gng[:, 0:1], in1=rs_, op0=ALU.mult, op1=ALU.mult),
                         reads=[kbO, 'gng', kR, kS], writes=[kS])
                    S.op('dve', lambda v: v.tensor_tensor(out=rv(T[f'og{h}'][:, cs]), in0=sq_, in1=T[f'gT{h}'][:, cs], op=ALU.mult),
                         reads=[kS, f'gT{h}'], writes=[f'og{h}'])
                    yield

                def run_il(gens):
                    gens = list(gens)
                    while gens:
                        for g_ in list(gens):
                            try:
                                next(g_)
                            except StopIteration:
                                gens.remove(g_)
                for c in range(2):
                    run_il([gla_chain(c, 0), gla_chain(c, 2)])
                    run_il([gla_chain(c, 1), gla_chain(c, 3)])
                fence(light=True)
                for dk in range(KC):
                    pa, pk = proj(w_out_d, dk, lambda k: rv(T[f'og{k}']) if k < 4 else rv(U5[:, k - 4, t0:t0 + NQ]), KC, NQ,
                                  [f'og{k}' for k in range(4)] + ['U'])
                    S.op('act', lambda a, pa=pa, dk=dk: a.activation(out=zsq[:, dk, 0:NQ], in_=pa, func=AF.Identity),
                         reads=[pk, 'vtok', 'ksttok'], writes=['zsq_y', 'zsq'])
                post_norm_tile_from_sbuf(l, 1, t0, 1.0, NQ)
                fence(light=True)
            gla_sample(Us5)
            for f in range(2):
                S.dma('sp', 'st', lambda q, f=f: q.dma_start(
                    out=st_out['gla_p'][0:1, f * 16384:(f + 1) * 16384].rearrange("o (r v) -> (o r) v", v=128),
                    in_=Sst[f]), reads=[f'Sst{f}'])
            fence()


        def projp(parts, N, rkeys):
            sl = wslot[0] % 4; wslot[0] += 1
            buf = wgu[sl // 2][:, sl % 2]
            key = f"wslot{sl}"
            for i, (src, _) in enumerate(parts):
                kr, nc_ = src.shape
                S.dma('pool', key, lambda q, i=i, src=src, kr=kr, nc_=nc_: q.dma_start(out=rv(buf[0:kr, i, 0:nc_]), in_=src),
                      writes=[key])
            pt, pk = PBK[pslot[0] % 4]; pslot[0] += 1
            for i, (src, rhs) in enumerate(parts):
                kr = src.shape[0]
                S.op('pe', lambda p, i=i, kr=kr, rhs=rhs: p.matmul(pt[:, 0:N], rv(buf[0:kr, i, :]), rhs,
                                                                   start=(i == 0), stop=(i == len(parts) - 1)),
                     reads=[key] + rkeys, writes=[pk])
            return pt[:, 0:N], pk

        def mix1_sublayer():
            l = 1
            hinf = hin[:].rearrange("p a b -> p (a b)")
            XV = HTf[:, 0:4 * KC * NQ].rearrange("p (v k n) -> p v k n", v=4, k=KC)
            o1 = 4 * KC * NQ
            R1 = ['lw', 'la', 'lg0', 'lg1', 'Bt', 'Kt']
            TR = {n: HTf[:, o1 + i * NQ:o1 + (i + 1) * NQ] for i, n in enumerate(R1)}
            o2 = o1 + len(R1) * NQ
            AR = HTf[:, o2:o2 + 2 * NQ].rearrange("p (a n) -> p a n", a=2)
            o3 = o2 + 2 * NQ
            Vtok, Bhat, Khat = (HTf[:, o3 + i * 128:o3 + (i + 1) * 128] for i in range(3))
            o4 = o3 + 384
            AabAbr, AakAkr = HTf[:, o4:o4 + 256], HTf[:, o4 + 256:o4 + 512]
            Xm, XTm, Tinv = (HTf[:, o4 + 512 + i * 128:o4 + 512 + (i + 1) * 128] for i in range(3))
            RH, UT = HTf[:, o4 + 896:o4 + 960], HTf[:, o4 + 960:o4 + 1024]
            assert o4 + 4096 <= HTf.shape[1]
            og = [hin[:, f, 256:512] for f in range(KC)]
            PA = ['rT', 'kT', 'vT', 'lgw', 'Lw', 'asg', 'kk', 'km', 'gt', 'tmp']
            TP = {n: ztf[:, i * NQ:(i + 1) * NQ] for i, n in enumerate(PA)}
            xx = zsqf[:, 0:KC * NQ].rearrange("p (k n) -> p k n", k=KC)
            tm2, ytok = zsqf[:, KC * NQ:KC * NQ + NQ], zsqf[:, KC * NQ + NQ:KC * NQ + 2 * NQ]
            KEYS1 = R1 + PA + ['XV', 'AR', 'Vtok', 'Bhat', 'Khat', 'og', 'xx', 'tm2', 'ytok', 'Swk0', 'Swk1'] + \
                    [f'{n}{c_}{hb}' for c_ in range(2) for hb in range(2) for n in ('Aab', 'Aak', 'X', 'XT', 'T', 'RH', 'UT')]
            ALIAS.extend(k for k in KEYS1 if k not in ALIAS)
            fence()
            for f in range(KC):
                S.op('act', lambda a, f=f: a.activation(out=r32(Swk)[:, f, :], in_=ones[:, 0:64], func=AF.Identity, scale=0.0),
                     reads=['ones'], writes=['Swk', 'Swk0', 'Swk1'])
            S.op('pool', lambda g: g.memset(hprev[:], 0.0), writes=['hprev'])
            for k in range(KC):
                S.op('act', lambda a, k=k: a.activation(out=shp[:, k:k + 1], in_=xT[:, k, NP - 1:NP], func=AF.Identity,
                                                        scale=mods[l][:, 4 * KC + k, 16:17], bias=mods[l][:, 3 * KC + k, 16:17]),
                     reads=['xT', f"mods{l}"], writes=['shp'])
            vec = lambda i, f: rvec[:, i, f:f + 1]
            EXPM05 = float(np.exp(-0.5))
            for sc in range(NP // NQ):
                t0 = sc * NQ
                modulate('act', hin, 0, l, 1, t0, NQ, [], 'hin')
                S.op('dve', lambda v: v.tensor_tensor(out=xx[:, :, 1:NQ], in0=hin[:, :, 0:NQ - 1], in1=hin[:, :, 1:NQ],
                                                      op=ALU.subtract), reads=['hin'], writes=['xx'])
                S.op('dve', lambda v: v.tensor_tensor(out=xx[:, :, 0:1], in0=hprev[:].unsqueeze(2), in1=hin[:, :, 0:1],
                                                      op=ALU.subtract), reads=['hin', 'hprev', 'xx'], writes=['xx'])
                S.op('dve', lambda v: v.tensor_copy(hprev[:].unsqueeze(2), hin[:, :, NQ - 1:NQ]), reads=['hin', 'xx'],
                     writes=['hprev'])

                def xvar(slot, i):
                    for k in range(KC):
                        S.op('dve', lambda v, k=k: v.scalar_tensor_tensor(
                            out=rv(XV[:, slot, k, :]), in0=xx[:, k, :], scalar=rmu[:, i, k:k + 1], in1=hin[:, k, 0:NQ],
                            op0=ALU.mult, op1=ALU.add), reads=['xx', 'hin', 'rmu'], writes=['XV'])
                xs = lambda slot: (lambda k: rv(XV[:, slot, k, :]))
                xvar(0, 0); xvar(1, 2); xvar(2, 3)
                xvar(3, 1)
                pa, pk = proj(rw['w1T'], 0, xs(3), KC, NQ, ['XV'])
                S.op('act', lambda a, pa=pa: a.activation(out=rv(TR['lw'][0:64, :]), in_=pa[0:64, :], func=AF.Tanh),
                     reads=[pk], writes=['lw'])
                xvar(3, 4)
                pa, pk = proj(rw['a1T'], 0, xs(3), KC, NQ, ['XV'])
                S.op('act', lambda a, pa=pa: a.activation(out=rv(TR['la'][0:64, :]), in_=pa[0:64, :], func=AF.Identity),
                     reads=[pk], writes=['la'])
                xvar(3, 5)
                pa, pk = proj(rw['g1T'], 0, xs(3), KC, NQ, ['XV'])
                S.op('act', lambda a, pa=pa: a.activation(out=rv(TR['lg0']), in_=pa, func=AF.Sigmoid), reads=[pk], writes=['lg0'])
                pa, pk = proj(rw['g1T'], 1, xs(3), KC, NQ, ['XV'])
                S.op('act', lambda a, pa=pa: a.activation(out=rv(TR['lg1'][0:32, :]), in_=pa[0:32, :], func=AF.Sigmoid),
                     reads=[pk], writes=['lg1'])
                for f in range(KC):
                    fc = f * 128
                    for nm, wn, sl_ in (('rT', 'w_r', 0), ('kT', 'w_k', 1), ('vT', 'w_v', 2)):
                        pa, pk = proj(rw[wn + 'T'], f, xs(sl_), KC, NQ, ['XV'])
                        S.op('act', lambda a, pa=pa, nm=nm: a.activation(out=TP[nm], in_=pa, func=AF.Identity),
                             reads=[pk], writes=[nm])
                    pa, pk = projp([(rw['w2'][0:64, fc:fc + 128], rv(TR['lw'][0:64, :]))], NQ, ['lw'])
                    S.op('act', lambda a, pa=pa, f=f: a.activation(out=TP['lgw'], in_=pa, func=AF.Sigmoid, bias=vec(0, f), scale=1.0),
                         reads=[pk, 'rvec'], writes=['lgw'])
                    S.op('dve', lambda v: v.tensor_scalar_mul(TP['lgw'], TP['lgw'], -EXPM05), reads=['lgw'], writes=['lgw'])
                    S.op('dve', lambda v: v.tensor_tensor_scan(out=TP['Lw'], data0=rstm[:], data1=TP['lgw'], initial=0.0,
                                                               op0=ALU.mult, op1=ALU.add), reads=['rstm', 'lgw'], writes=['Lw'])
                    S.op('dve', lambda v: v.tensor_copy(llast[:], TP['Lw'].rearrange("p (c t) -> p c t", c=2)[:, :, 127]),
                         reads=['Lw'], writes=['llast'])
                    S.op('act', lambda a: a.activation(out=pcl[:], in_=llast[:], func=AF.Exp), reads=['llast'], writes=['pcl'])
                    pa, pk = projp([(rw['a2'][0:64, fc:fc + 128], rv(TR['la'][0:64, :]))], NQ, ['la'])
                    S.op('act', lambda a, pa=pa, f=f: a.activation(out=TP['asg'], in_=pa, func=AF.Sigmoid, bias=vec(1, f), scale=1.0),
                         reads=[pk, 'rvec'], writes=['asg'])
                    pa, pk = projp([(rw['g2'][0:128, fc:fc + 128], rv(TR['lg0'])), (rw['g2'][128:160, fc:fc + 128], rv(TR['lg1'][0:32, :]))],
                                   NQ, ['lg0', 'lg1'])
                    S.op('act', lambda a, pa=pa: a.activation(out=TP['gt'], in_=pa, func=AF.Identity), reads=[pk], writes=['gt'])
                    S.op('dve', lambda v, f=f: v.tensor_scalar_mul(TP['kk'], TP['kT'], vec(2, f)), reads=['kT', 'rvec'], writes=['kk'])
                    S.op('act', lambda a: a.activation(out=TP['tmp'], in_=TP['kk'], func=AF.Square), reads=['kk'], writes=['tmp'])
                    S.op('pe', lambda p: p.matmul(p_s[0][:, 0:NQ], bones[:], TP['tmp'], start=True, stop=True),
                         reads=['bones', 'tmp'], writes=['p_s0'])
                    S.op('act', lambda a: a.activation(out=TP['tmp'], in_=p_s[0][:, 0:NQ], func=AF.Sqrt), reads=['p_s0'], writes=['tmp'])
                    S.op('dve', lambda v: v.tensor_scalar_max(TP['tmp'], TP['tmp'], 1e-12), reads=['tmp'], writes=['tmp'])
                    S.op('dve', lambda v: v.reciprocal(TP['tmp'], TP['tmp']), reads=['tmp'], writes=['tmp'])
                    S.op('dve', lambda v: v.tensor_tensor(out=TP['kk'], in0=TP['kk'], in1=TP['tmp'], op=ALU.mult),
                         reads=['kk', 'tmp'], writes=['kk'])
                    S.op('dve', lambda v, f=f: v.tensor_scalar(out=TP['km'], in0=TP['asg'], scalar1=-1.0, scalar2=vec(3, f),
                                                               op0=ALU.add, op1=ALU.mult), reads=['asg', 'rvec'], writes=['km'])
                    S.op('dve', lambda v: v.scalar_tensor_tensor(out=TP['km'], in0=TP['km'], scalar=1.0, in1=TP['kT'],
                                                                 op0=ALU.add, op1=ALU.mult), reads=['km', 'kT'], writes=['km'])
                    S.op('dve', lambda v: v.tensor_tensor(out=TP['tmp'], in0=TP['Lw'], in1=TP['lgw'], op=ALU.subtract),
                         reads=['Lw', 'lgw'], writes=['tmp'])
                    S.op('act', lambda a: a.activation(out=TP['tmp'], in_=TP['tmp'], func=AF.Exp), reads=['tmp'], writes=['tmp'])
                    S.op('dve', lambda v: v.scalar_tensor_tensor(out=rv(AR[:, 0, :]), in0=TP['kk'], scalar=-1.0, in1=TP['tmp'],
                                                                 op0=ALU.mult, op1=ALU.mult), reads=['kk', 'tmp'], writes=['AR'])
                    S.op('act', lambda a: a.activation(out=TP['tmp'], in_=TP['Lw'], func=AF.Exp), reads=['Lw', 'AR'], writes=['tmp'])
                    S.op('dve', lambda v: v.tensor_tensor(out=rv(AR[:, 1, :]), in0=TP['rT'], in1=TP['tmp'], op=ALU.mult),
                         reads=['rT', 'tmp'], writes=['AR'])
                    S.op('act', lambda a: a.activation(out=TP['tmp'], in_=TP['Lw'], func=AF.Exp, scale=-1.0), reads=['Lw', 'AR'], writes=['tmp'])
                    S.op('dve', lambda v: v.tensor_tensor(out=tm2, in0=TP['kk'], in1=TP['asg'], op=ALU.mult),
                         reads=['kk', 'asg'], writes=['tm2'])
                    S.op('dve', lambda v: v.tensor_tensor(out=rv(TR['Bt']), in0=tm2, in1=TP['tmp'], op=ALU.mult),
                         reads=['tm2', 'tmp'], writes=['Bt'])
                    S.op('dve', lambda v: v.tensor_tensor(out=rv(TR['Kt']), in0=TP['km'], in1=TP['tmp'], op=ALU.mult),
                         reads=['km', 'tmp'], writes=['Kt'])
                    S.op('dve', lambda v, f=f: v.scalar_tensor_tensor(out=TP['asg'], in0=TP['rT'], scalar=vec(4, f), in1=TP['km'],
                                                                      op0=ALU.mult, op1=ALU.mult),
                         reads=['rT', 'km', 'rvec', 'tm2', 'asg'], writes=['asg'])
                    S.op('pe', lambda p: p.matmul(p_s[1][:, 0:NQ], bones[:], TP['asg'], start=True, stop=True),
                         reads=['bones', 'asg'], writes=['p_s1'])
                    S.op('dve', lambda v: v.tensor_tensor(out=TP['rT'], in0=p_s[1][:, 0:NQ], in1=TP['vT'], op=ALU.mult),
                         reads=['p_s1', 'vT', 'AR', 'asg'], writes=['rT'])
                    INVB = {(0, 0): ((p_y[0], 'p_y0'), (p_y[1], 'p_y1')), (0, 1): ((p_g[0], 'p_g0'), (p_g[1], 'p_g1')),
                            (1, 0): ((p_u[0], 'p_u0'), (p_u[1], 'p_u1')), (1, 1): ((p_s[0], 'p_s0'), (p_s[1], 'p_s1'))}

                    def scratch(c, hb):
                        ob = o4 + (2 * c + hb) * 1024
                        d_ = dict(Aab=HTf[:, ob:ob + 256], Aak=HTf[:, ob + 256:ob + 512], RH=HTf[:, ob + 896:ob + 960], UT=HTf[:, ob + 960:ob + 1024])
                        d_['X'], d_['XT'], d_['T'] = (HTf[:, ob + 512 + i * 128:ob + 512 + (i + 1) * 128] for i in range(3))
                        d_['k'] = {n: f'{n}{c}{hb}' for n in ('Aab', 'Aak', 'X', 'XT', 'T', 'RH', 'UT')}
                        return d_

                    def inv_chain(c, hb, f=f):
                        cs = slice(c * 128, (c + 1) * 128)
                        rows = slice(hb * 64, (hb + 1) * 64)
                        arc = AR[rows, :, cs]
                        sc_ = scratch(c, hb); K_ = sc_['k']
                        (bA, kA), (bB, kB) = INVB[(c, hb)]
                        for lhs, lk, dst, dk_ in ((TR['Bt'], 'Bt', sc_['Aab'], K_['Aab']), (TR['Kt'], 'Kt', sc_['Aak'], K_['Aak'])):
                            S.op('pe', lambda p, lhs=lhs: p.matmul(bA[:, 0:256], rv(lhs[rows, cs]), rv(arc), start=True, stop=True),
                                 reads=[lk, 'AR'], writes=[kA])
                            S.op('dve', lambda v, dst=dst: v.tensor_tensor(out=rv(dst), in0=bA[:, 0:256], in1=mask2[:], op=ALU.mult),
                                 reads=[kA, 'mask2'], writes=[dk_])
                            yield
                        S.op('pe', lambda p: p.matmul(bB[:, 0:128], rv(AR[rows, 0, cs]), rv(TR['Bt'][rows, cs]), start=True, stop=True),
                             reads=['AR', 'Bt'], writes=[kB])
                        S.op('dve', lambda v: v.tensor_tensor(out=rv(sc_['XT']), in0=bB[:, 0:128], in1=maskLT[:], op=ALU.mult),
                             reads=[kB, 'maskLT'], writes=[K_['XT']])
                        S.op('act', lambda a: a.activation(out=rv(sc_['X']), in_=sc_['Aab'][:, 0:128], func=AF.Identity), reads=[K_['Aab']], writes=[K_['X']])
                        S.op('dve', lambda v: v.tensor_tensor(out=rv(sc_['T']), in0=sc_['Aab'][:, 0:128], in1=ident[:], op=ALU.add),
                             reads=[K_['Aab'], 'ident'], writes=[K_['T']])
                        yield
                        for lv in range(6):
                            S.op('pe', lambda p: p.matmul(bA[:, 0:128], rv(sc_['XT']), rv(sc_['X']), start=True, stop=True), reads=[K_['XT'], K_['X']], writes=[kA])
                            S.op('pe', lambda p: p.matmul(bB[:, 0:128], rv(sc_['X']), rv(sc_['XT']), start=True, stop=True), reads=[K_['XT'], K_['X']], writes=[kB])
                            yield
                            S.op('act', lambda a: a.activation(out=rv(sc_['X']), in_=bA[:, 0:128], func=AF.Identity), reads=[kA], writes=[K_['X']])
                            S.op('dve', lambda v: v.tensor_copy(rv(sc_['XT']), bB[:, 0:128]), reads=[kB], writes=[K_['XT']])
                            yield
                            S.op('pe', lambda p: p.matmul(bA[:, 0:128], rv(sc_['XT']), rv(sc_['T']), start=True, stop=True), reads=[K_['XT'], K_['T']], writes=[kA])
                            yield
                            S.op('dve', lambda v: v.tensor_tensor(out=rv(sc_['T']), in0=bA[:, 0:128], in1=sc_['T'], op=ALU.add), reads=[kA, K_['T']], writes=[K_['T']])
                            yield

                    def run_interleaved(gens):
                        gens = list(gens)
                        while gens:
                            for g_ in list(gens):
                                try:
                                    next(g_)
                                except StopIteration:
                                    gens.remove(g_)
                    run_interleaved([inv_chain(c_, hb_) for c_ in range(2) for hb_ in range(2)])
                    for c in range(2):
                        cs = slice(c * 128, (c + 1) * 128)
                        S.op('act', lambda a, c=c, cs=cs: a.activation(out=TP['tmp'][:, cs], in_=TP['Lw'][:, cs], func=AF.Exp,
                                                                       scale=-1.0, bias=llast[:, c:c + 1]),
                             reads=['Lw', 'llast', 'Bt', 'Kt'], writes=['tmp'])
                        S.op('dve', lambda v, cs=cs: v.tensor_tensor(out=tm2[:, cs], in0=tm2[:, cs], in1=TP['tmp'][:, cs], op=ALU.mult),
                             reads=['tm2', 'tmp'], writes=['tm2'])
                        S.op('dve', lambda v, cs=cs: v.tensor_tensor(out=TP['tmp'][:, cs], in0=TP['km'][:, cs], in1=TP['tmp'][:, cs],
                                                                     op=ALU.mult), reads=['km', 'tmp'], writes=['tmp'])
                        for src, skey, dst, dkey in ((TP['vT'], 'vT', Vtok, 'Vtok'), (tm2, 'tm2', Bhat, 'Bhat'), (TP['tmp'], 'tmp', Khat, 'Khat')):
                            S.op('pe', lambda p, src=src, cs=cs: p.transpose(p_s[0][:, 0:128], src[:, cs], ident[:]),
                                 reads=[skey, 'ident'], writes=['p_s0'])
                            S.op('act', lambda a, dst=dst: a.activation(out=rv(dst), in_=p_s[0][:, 0:128], func=AF.Identity),
                                 reads=['p_s0'], writes=[dkey])
                        def head_chain(hb, cs=cs, c=c, f=f):
                            rows = slice(hb * 64, (hb + 1) * 64)
                            sc_ = scratch(c, hb); K_ = sc_['k']
                            AabAbr_, AakAkr_, Tinv_, RH_, UT_ = sc_['Aab'], sc_['Aak'], sc_['T'], sc_['RH'], sc_['UT']
                            kAab, kAak, kT, kRH, kUT = K_['Aab'], K_['Aak'], K_['T'], K_['RH'], K_['UT']
                            (bA, kA), (bB, kB), (bC, kC) = ((p_y[0], 'p_y0'), (p_y[1], 'p_y1'), (p_s[1], 'p_s1')) if hb == 0 else \
                                                           ((p_g[0], 'p_g0'), (p_g[1], 'p_g1'), (p_u[0], 'p_u0'))
                            Sh = Swk[rows, f, :]
                            Vh = Vtok[:, hb * 64:(hb + 1) * 64]
                            S.op('pe', lambda p: p.matmul(bA[:, 0:64], rv(AR[rows, 0, cs]), rv(Sh), start=True, stop=False), reads=['AR', f'Swk{hb}'], writes=[kA])
                            S.op('pe', lambda p: p.matmul(bA[:, 0:64], rv(AakAkr_[:, 0:128]), rv(Vh), start=False, stop=True), reads=[kAak, 'Vtok'], writes=[kA])
                            yield
                            S.op('act', lambda a: a.activation(out=rv(RH_), in_=bA[:, 0:64], func=AF.Identity), reads=[kA], writes=[kRH])
                            yield
                            S.op('pe', lambda p: p.matmul(bB[:, 0:64], rv(Tinv_), rv(RH_), start=True, stop=True), reads=[kT, kRH], writes=[kB])
                            yield
                            S.op('act', lambda a: a.activation(out=rv(UT_), in_=bB[:, 0:64], func=AF.Identity), reads=[kB], writes=[kUT])
                            yield
                            S.op('pe', lambda p: p.matmul(bA[:, 0:64], rv(AR[rows, 1, cs]), rv(Sh), start=True, stop=False), reads=['AR', f'Swk{hb}'], writes=[kA])
                            S.op('pe', lambda p: p.matmul(bA[:, 0:64], rv(AabAbr_[:, 128:256]), rv(UT_), start=False, stop=False), reads=[kAab, kUT], writes=[kA])
                            S.op('pe', lambda p: p.matmul(bA[:, 0:64], rv(AakAkr_[:, 128:256]), rv(Vh), start=False, stop=True), reads=[kAak, 'Vtok'], writes=[kA])
                            S.op('pe', lambda p: p.matmul(bC[:, 0:64], rv(Bhat), rv(UT_), start=True, stop=False), reads=['Bhat', kUT], writes=[kC])
                            S.op('pe', lambda p: p.matmul(bC[:, 0:64], rv(Khat), rv(Vh), start=False, stop=True), reads=['Khat', 'Vtok'], writes=[kC])
                            yield
                            S.op('dve', lambda v: v.tensor_copy(ytok[:, c * 128 + hb * 64:c * 128 + (hb + 1) * 64], bA[:, 0:64]), reads=[kA], writes=['ytok'])
                            S.op('dve', lambda v: v.scalar_tensor_tensor(out=r32(Swk)[rows, f, :], in0=Swk[rows, f, :], scalar=pcl[rows, c:c + 1],
                                                                         in1=bC[rows, 0:64], op0=ALU.mult, op1=ALU.add),
                                 reads=[f'Swk{hb}', 'pcl', kC], writes=[f'Swk{hb}'])
                            yield
                        run_interleaved([head_chain(0), head_chain(1)])
                        yv = ytok[:, cs].rearrange("p (h i) -> p h i", h=2)
                        S.op('dve', lambda v, yv=yv: v.tensor_reduce(out=st2[:, 0:2], in_=yv, op=ALU.add, axis=mybir.AxisListType.X),
                             reads=['ytok'], writes=['st2'])
                        S.op('dve', lambda v: v.tensor_scalar_mul(st2[:, 0:2], st2[:, 0:2], 1.0 / 64), reads=['st2'], writes=['st2'])
                        S.op('dve', lambda v, yv=yv: v.tensor_tensor(out=yv, in0=yv, in1=st2[:, 0:2].unsqueeze(2).to_broadcast([128, 2, 64]),
                                                                     op=ALU.subtract), reads=['ytok', 'st2'], writes=['ytok'])
                        tv = TP['tmp'][:, cs].rearrange("p (h i) -> p h i", h=2)
                        S.op('dve', lambda v, yv=yv, tv=tv: v.tensor_tensor(out=tv, in0=yv, in1=yv, op=ALU.mult),
                             reads=['ytok', 'Khat'], writes=['tmp'])
                        S.op('dve', lambda v, tv=tv: v.tensor_reduce(out=st2[:, 2:4], in_=tv, op=ALU.add, axis=mybir.AxisListType.X),
                             reads=['tmp'], writes=['st2'])
                        S.op('dve', lambda v: v.tensor_scalar(out=st2[:, 2:4], in0=st2[:, 2:4], scalar1=1.0 / 64, scalar2=64e-5,
                                                              op0=ALU.mult, op1=ALU.add), reads=['st2'], writes=['st2'])
                        S.op('act', lambda a: a.activation(out=st2[:, 2:4], in_=st2[:, 2:4], func=AF.Sqrt), reads=['st2'], writes=['st2'])
                        S.op('dve', lambda v: v.reciprocal(st2[:, 2:4], st2[:, 2:4]), reads=['st2'], writes=['st2'])
                        S.op('dve', lambda v, yv=yv: v.tensor_tensor(out=yv, in0=yv, in1=st2[:, 2:4].unsqueeze(2).to_broadcast([128, 2, 64]),
                                                                     op=ALU.mult), reads=['ytok', 'st2'], writes=['ytok'])
                        S.op('pe', lambda p, cs=cs: p.transpose(p_s[0][:, 0:128], ytok[:, cs], ident[:]), reads=['ytok', 'ident'], writes=['p_s0'])
                        S.op('act', lambda a, f=f, cs=cs: a.activation(out=TP['kk'][:, cs], in_=p_s[0][:, 0:128], func=AF.Identity,
                                                                       scale=vec(5, f), bias=vec(6, f)), reads=['p_s0', 'rvec', 'Bt'], writes=['kk'])
                    S.op('dve', lambda v: v.tensor_tensor(out=TP['kk'], in0=TP['kk'], in1=TP['rT'], op=ALU.add), reads=['kk', 'rT'], writes=['kk'])
                    S.op('dve', lambda v, f=f: v.tensor_tensor(out=rv(og[f]), in0=TP['kk'], in1=TP['gt'], op=ALU.mult),
                         reads=['kk', 'gt'], writes=['og'])
                fence(light=True)
                for dk in range(KC):
                    pa, pk = proj(rw['w_oT'], dk, lambda k: rv(og[k]), KC, NQ, ['og'])
                    S.op('act', lambda a, pa=pa, dk=dk: a.activation(out=zsq[:, dk, 0:NQ], in_=pa, func=AF.Identity),
                         reads=[pk], writes=['zsq_y', 'zsq'])
                post_norm_tile_from_sbuf(l, 1, t0, 1.0, NQ)
                fence(light=True)
            rwkv_sample()
            S.dma('sp', 'st', lambda q: q.dma_start(out=st_out['shift_p'].rearrange("o (k p) -> p (o k)", p=128), in_=shp[:],
                                                    allow_slow_non_contiguous=True),
                  reads=['shp'])
            for f in range(KC):
                S.op('pe', lambda p, f=f: p.transpose(p_s[f % 2][0:64, 0:128], Swk[:, f, :], ident[:]),
                     reads=['Swk', 'ident'], writes=[f'p_s{f % 2}'])
                S.op('dve', lambda v, f=f: v.tensor_copy(ztf[0:64, f * 128:(f + 1) * 128], p_s[f % 2][0:64, 0:128]),
                     reads=[f'p_s{f % 2}'], writes=['zt'])
                for hb in range(2):
                    h = 2 * f + hb
                    S.dma('sp', 'st', lambda q, f=f, hb=hb, h=h: q.dma_start(
                        out=st_out['wkv_p'][0:1, h * 4096:(h + 1) * 4096].rearrange("o (i j) -> (o i) j", j=64),
                        in_=ztf[0:64, f * 128 + hb * 64:f * 128 + (hb + 1) * 64]), reads=['zt'])
            fence()


        def rwkv_sample():
            l = 1
            N16 = ['rS', 'kS', 'vS', 'lgw', 'asg', 'kk', 'km', 'gt', 'tmp', 'wS', 'aS', 'bS', 'bon', 'yf']
            P = {n: ztf[:, i * NS:(i + 1) * NS] for i, n in enumerate(N16)}
            TK = ['B_tok', 'K_tok', 'V_tok', 'SA_tok', 'Y_tok', 'Bm', 'Km2']
            Tk = {n: ztf[0:NS, 256 + i * 128:256 + (i + 1) * 128] for i, n in enumerate(TK)}
            s16 = ztf[0:NS, 1280:1288]
            xxS = ztf[:, 1296:1424].rearrange("p (k b) -> p k b", k=KC)
            shS = ztf[:, 1424:1552].rearrange("p (k b) -> p k b", k=KC)
            shT = ztf[:, 2064:2192].rearrange("p (k b) -> p k b", k=KC)
            Am = ztf[:, 1552:1808].rearrange("p (b c) -> p b c", b=NS)
            Rm = ztf[:, 1808:2064].rearrange("p (b c) -> p b c", b=NS)
            Sw = zsqf[:, 0:256].rearrange("p (b i) -> p b i", b=4)
            XVs = HTf[:, 0:512].rearrange("p (v k b) -> p v k b", v=4, k=KC)
            L16 = {n: HTf[:, 512 + i * NS:512 + (i + 1) * NS] for i, n in enumerate(['lwS', 'laS', 'lg0S', 'lg1S'])}
            ogS = HTf[:, 576:704].rearrange("p (k b) -> p k b", k=KC)
            KS = N16 + TK + ['s16', 'xxS', 'shS', 'shT', 'Am', 'Rm', 'Sw', 'XVs', 'ogS'] + list(L16) + \
                 [f'{n}{hb}' for hb in range(2) for n in ('Bm', 'Km2', 'SA_tok', 'Y_tok', 'Sw')]
            ALIAS.extend(k for k in KS if k not in ALIAS)
            fence()
            vec = lambda i, f: rvec[:, i, f:f + 1]
            EXPM05 = float(np.exp(-0.5))
            S.dma('sp', 'ldr', lambda q: q.dma_start(out=shT, in_=rshift_d), writes=['shT'])
            for k in range(KC):
                S.op('dve', lambda v, k=k: v.tensor_tensor(out=shS[:, k, :], in0=xT[:, k, NP:NT], in1=mods[l][:, 4 * KC + k, 0:NS], op=ALU.mult),
                     reads=['xT', f"mods{l}"], writes=['shS'])
                S.op('dve', lambda v, k=k: v.tensor_tensor(out=shS[:, k, :], in0=shS[:, k, :], in1=mods[l][:, 3 * KC + k, 0:NS], op=ALU.add),
                     reads=['shS', f"mods{l}"], writes=['shS'])
            S.dma('sp', 'st', lambda q: q.dma_start(out=shifts_d, in_=shS), reads=['shS'])
            S.op('dve', lambda v: v.tensor_tensor(out=xxS, in0=shT, in1=shS, op=ALU.subtract), reads=['shT', 'shS'], writes=['xxS'])

            def xvar(slot, i):
                for k in range(KC):
                    S.op('dve', lambda v, k=k: v.scalar_tensor_tensor(out=rv(XVs[:, slot, k, :]), in0=xxS[:, k, :], scalar=rmu[:, i, k:k + 1],
                                                                      in1=shS[:, k, :], op0=ALU.mult, op1=ALU.add),
                         reads=['xxS', 'shS', 'rmu'], writes=['XVs'])
            xs = lambda slot: (lambda k: rv(XVs[:, slot, k, :]))
            xvar(0, 0); xvar(1, 2); xvar(2, 3)
            xvar(3, 1)
            pa, pk = proj(rw['w1T'], 0, xs(3), KC, NS, ['XVs'])
            S.op('act', lambda a, pa=pa: a.activation(out=rv(L16['lwS'][0:64, :]), in_=pa[0:64, :], func=AF.Tanh), reads=[pk], writes=['lwS'])
            xvar(3, 4)
            pa, pk = proj(rw['a1T'], 0, xs(3), KC, NS, ['XVs'])
            S.op('act', lambda a, pa=pa: a.activation(out=rv(L16['laS'][0:64, :]), in_=pa[0:64, :], func=AF.Identity), reads=[pk], writes=['laS'])
            xvar(3, 5)
            pa, pk = proj(rw['g1T'], 0, xs(3), KC, NS, ['XVs'])
            S.op('act', lambda a, pa=pa: a.activation(out=rv(L16['lg0S']), in_=pa, func=AF.Sigmoid), reads=[pk], writes=['lg0S'])
            pa, pk = proj(rw['g1T'], 1, xs(3), KC, NS, ['XVs'])
            S.op('act', lambda a, pa=pa: a.activation(out=rv(L16['lg1S'][0:32, :]), in_=pa[0:32, :], func=AF.Sigmoid), reads=[pk], writes=['lg1S'])
            for f in range(KC):
                fc = f * 128
                for nm, wn, sl_ in (('rS', 'w_r', 0), ('kS', 'w_k', 1), ('vS', 'w_v', 2)):
                    pa, pk = proj(rw[wn + 'T'], f, xs(sl_), KC, NS, ['XVs'])
                    S.op('act', lambda a, pa=pa, nm=nm: a.activation(out=P[nm], in_=pa, func=AF.Identity), reads=[pk], writes=[nm])
                pa, pk = projp([(rw['w2'][0:64, fc:fc + 128], rv(L16['lwS'][0:64, :]))], NS, ['lwS'])
                S.op('act', lambda a, pa=pa, f=f: a.activation(out=P['lgw'], in_=pa, func=AF.Sigmoid, bias=vec(0, f), scale=1.0),
                     reads=[pk, 'rvec'], writes=['lgw'])
                S.op('act', lambda a: a.activation(out=P['wS'], in_=P['lgw'], func=AF.Exp, scale=-EXPM05), reads=['lgw'], writes=['wS'])
                pa, pk = projp([(rw['a2'][0:64, fc:fc + 128], rv(L16['laS'][0:64, :]))], NS, ['laS'])
                S.op('act', lambda a, pa=pa, f=f: a.activation(out=P['asg'], in_=pa, func=AF.Sigmoid, bias=vec(1, f), scale=1.0),
                     reads=[pk, 'rvec'], writes=['asg'])
                pa, pk = projp([(rw['g2'][0:128, fc:fc + 128], rv(L16['lg0S'])), (rw['g2'][128:160, fc:fc + 128], rv(L16['lg1S'][0:32, :]))],
                               NS, ['lg0S', 'lg1S'])
                S.op('act', lambda a, pa=pa: a.activation(out=P['gt'], in_=pa, func=AF.Identity), reads=[pk], writes=['gt'])
                S.op('dve', lambda v, f=f: v.tensor_scalar_mul(P['kk'], P['kS'], vec(2, f)), reads=['kS', 'rvec'], writes=['kk'])
                S.op('act', lambda a: a.activation(out=P['tmp'], in_=P['kk'], func=AF.Square), reads=['kk'], writes=['tmp'])
                S.op('pe', lambda p: p.matmul(p_s[0][:, 0:NS], bones[:], P['tmp'], start=True, stop=True), reads=['bones', 'tmp'], writes=['p_s0'])
                S.op('act', lambda a: a.activation(out=P['tmp'], in_=p_s[0][:, 0:NS], func=AF.Sqrt), reads=['p_s0'], writes=['tmp'])
                S.op('dve', lambda v: v.tensor_scalar_max(P['tmp'], P['tmp'], 1e-12), reads=['tmp'], writes=['tmp'])
                S.op('dve', lambda v: v.reciprocal(P['tmp'], P['tmp']), reads=['tmp'], writes=['tmp'])
                S.op('dve', lambda v: v.tensor_tensor(out=P['kk'], in0=P['kk'], in1=P['tmp'], op=ALU.mult), reads=['kk', 'tmp'], writes=['kk'])
                S.op('dve', lambda v, f=f: v.tensor_scalar(out=P['km'], in0=P['asg'], scalar1=-1.0, scalar2=vec(3, f), op0=ALU.add, op1=ALU.mult),
                     reads=['asg', 'rvec'], writes=['km'])
                S.op('dve', lambda v: v.scalar_tensor_tensor(out=P['km'], in0=P['km'], scalar=1.0, in1=P['kS'], op0=ALU.add, op1=ALU.mult),
                     reads=['km', 'kS'], writes=['km'])
                S.op('dve', lambda v: v.tensor_scalar_mul(P['aS'], P['kk'], -1.0), reads=['kk'], writes=['aS'])
                S.op('dve', lambda v: v.tensor_tensor(out=P['bS'], in0=P['kk'], in1=P['asg'], op=ALU.mult), reads=['kk', 'asg'], writes=['bS'])
                S.op('dve', lambda v, f=f: v.scalar_tensor_tensor(out=P['tmp'], in0=P['rS'], scalar=vec(4, f), in1=P['km'], op0=ALU.mult, op1=ALU.mult),
                     reads=['rS', 'km', 'rvec'], writes=['tmp'])
                S.op('pe', lambda p: p.matmul(p_s[1][:, 0:NS], bones[:], P['tmp'], start=True, stop=True), reads=['bones', 'tmp'], writes=['p_s1'])
                S.op('dve', lambda v: v.tensor_tensor(out=P['bon'], in0=p_s[1][:, 0:NS], in1=P['vS'], op=ALU.mult), reads=['p_s1', 'vS'], writes=['bon'])
                for src, dst in (('bS', 'B_tok'), ('km', 'K_tok'), ('vS', 'V_tok')):
                    S.op('pe', lambda p, src=src: p.transpose(p_s[0][0:NS, 0:128], P[src], ident[:]), reads=[src, 'ident'], writes=['p_s0'])
                    S.op('act', lambda a, dst=dst: a.activation(out=Tk[dst], in_=p_s[0][0:NS, 0:128], func=AF.Identity), reads=['p_s0'], writes=[dst])
                S.op('pool', lambda g: g.memset(Am, 0.0), writes=['Am'])
                S.op('pool', lambda g: g.memset(Rm, 0.0), writes=['Rm'])
                for b in range(NS):
                    S.op('dve', lambda v, b=b: v.tensor_copy(Am[:, b, b:b + 1], P['aS'][:, b:b + 1]), reads=['aS', 'Am'], writes=['Am'])
                    S.op('dve', lambda v, b=b: v.tensor_copy(Rm[:, b, b:b + 1], P['rS'][:, b:b + 1]), reads=['rS', 'Rm'], writes=['Rm'])
                def smp_chain(bt, hb, f=f):
                    rows = slice(hb * 64, (hb + 1) * 64)
                    hc = slice(hb * 64, (hb + 1) * 64)
                    Bm_ = Tk['Bm'] if hb == 0 else ztf[0:NS, 2192:2320]
                    Km_ = Tk['Km2'] if hb == 0 else ztf[0:NS, 2320:2448]
                    kBm, kKm, kSA, kY, kSw = f'Bm{hb}', f'Km2{hb}', f'SA_tok{hb}', f'Y_tok{hb}', f'Sw{hb}'
                    (bA, kA), (bB, kB), (bC, kC) = ((p_y[0], 'p_y0'), (p_y[1], 'p_y1'), (p_s[0], 'p_s0')) if hb == 0 else \
                                                   ((p_g[0], 'p_g0'), (p_g[1], 'p_g1'), (p_u[0], 'p_u0'))
                    for bi in range(4):
                        b = 4 * bt + bi
                        S.op('pe', lambda p, b=b, bi=bi: p.matmul(bA[0:NS, 0:64], Am[rows, b, :], Sw[rows, bi, :], start=(bi == 0), stop=(bi == 3)),
                             reads=['Am', kSw], writes=[kA])
                    yield
                    S.op('act', lambda a: a.activation(out=Tk['SA_tok'][:, hc], in_=bA[0:NS, 0:64], func=AF.Identity), reads=[kA], writes=[kSA])
                    yield
                    for bi in range(4):
                        b = 4 * bt + bi
                        S.op('dve', lambda v, b=b: v.tensor_scalar_mul(Bm_, Tk['B_tok'], ident[0:NS, b:b + 1]), reads=['B_tok', 'ident'], writes=[kBm])
                        S.op('dve', lambda v, b=b: v.tensor_scalar_mul(Km_, Tk['K_tok'], ident[0:NS, b:b + 1]), reads=['K_tok', 'ident'], writes=[kKm])
                        yield
                        S.op('pe', lambda p: p.matmul(bB[:, 0:64], Bm_, Tk['SA_tok'][:, hc], start=True, stop=False), reads=[kBm, kSA], writes=[kB])
                        S.op('pe', lambda p: p.matmul(bB[:, 0:64], Km_, Tk['V_tok'][:, hc], start=False, stop=True), reads=[kKm, 'V_tok'], writes=[kB])
                        yield
                        S.op('dve', lambda v, b=b, bi=bi: v.scalar_tensor_tensor(
                            out=Sw[rows, bi, :], in0=Sw[rows, bi, :], scalar=P['wS'][rows, b:b + 1], in1=bB[rows, 0:64],
                            op0=ALU.mult, op1=ALU.add), reads=[kSw, 'wS', kB], writes=[kSw])
                        yield
                        S.op('pe', lambda p, b=b, bi=bi: p.matmul(bC[0:NS, 0:64], Rm[rows, b, :], Sw[rows, bi, :], start=(bi == 0), stop=(bi == 3)),
                             reads=['Rm', kSw], writes=[kC])
                        yield
                    if bt == 0:
                        S.op('dve', lambda v: v.tensor_copy(Tk['Y_tok'][:, hc], bC[0:NS, 0:64]), reads=[kC], writes=[kY, 'Y_tok'])
                    else:
                        S.op('dve', lambda v: v.tensor_tensor(out=Tk['Y_tok'][:, hc], in0=Tk['Y_tok'][:, hc], in1=bC[0:NS, 0:64], op=ALU.add),
                             reads=[kC, kY], writes=[kY, 'Y_tok'])
                    yield

                def run_il2(gens):
                    gens = list(gens)
                    while gens:
                        for g_ in list(gens):
                            try:
                                next(g_)
                            except StopIteration:
                                gens.remove(g_)
                for bt in range(4):
                    S.dma('sp', 'ldw', lambda q, bt=bt, f=f: q.dma_start(out=Sw, in_=wkvs0_d[:, 4 * bt:4 * bt + 4, f, :]), writes=['Sw', 'Sw0', 'Sw1'])
                    run_il2([smp_chain(bt, 0), smp_chain(bt, 1)])
                    S.dma('sp', 'st', lambda q, bt=bt, f=f: q.dma_start(out=wkvs_d[:, 4 * bt:4 * bt + 4, f, :], in_=Sw), reads=['Sw', 'Sw0', 'Sw1'])
                yv = Tk['Y_tok'].rearrange("p (h i) -> p h i", h=2)
                tv = Tk['Bm'].rearrange("p (h i) -> p h i", h=2)
                S.op('dve', lambda v: v.tensor_reduce(out=s16[:, 0:2], in_=yv, op=ALU.add, axis=mybir.AxisListType.X),
                     reads=['Y_tok', 'Y_tok0', 'Y_tok1'], writes=['s16', 'Y_tok'])
                S.op('dve', lambda v: v.tensor_scalar_mul(s16[:, 0:2], s16[:, 0:2], 1.0 / 64), reads=['s16'], writes=['s16'])
                S.op('dve', lambda v: v.tensor_tensor(out=yv, in0=yv, in1=s16[:, 0:2].unsqueeze(2).to_broadcast([NS, 2, 64]), op=ALU.subtract),
                     reads=['Y_tok', 's16'], writes=['Y_tok'])
                S.op('dve', lambda v: v.tensor_tensor(out=tv, in0=yv, in1=yv, op=ALU.mult), reads=['Y_tok'], writes=['Bm', 'Bm0'])
                S.op('dve', lambda v: v.tensor_reduce(out=s16[:, 2:4], in_=tv, op=ALU.add, axis=mybir.AxisListType.X), reads=['Bm', 'Bm0'], writes=['s16'])
                S.op('dve', lambda v: v.tensor_scalar(out=s16[:, 2:4], in0=s16[:, 2:4], scalar1=1.0 / 64, scalar2=64e-5, op0=ALU.mult, op1=ALU.add),
                     reads=['s16'], writes=['s16'])
                S.op('act', lambda a: a.activation(out=s16[:, 2:4], in_=s16[:, 2:4], func=AF.Sqrt), reads=['s16'], writes=['s16'])
                S.op('dve', lambda v: v.reciprocal(s16[:, 2:4], s16[:, 2:4]), reads=['s16'], writes=['s16'])
                S.op('dve', lambda v: v.tensor_tensor(out=yv, in0=yv, in1=s16[:, 2:4].unsqueeze(2).to_broadcast([NS, 2, 64]), op=ALU.mult),
                     reads=['Y_tok', 's16'], writes=['Y_tok'])
                S.op('pe', lambda p: p.transpose(p_s[1][:, 0:NS], Tk['Y_tok'], ident[0:NS, 0:NS]), reads=['Y_tok', 'ident'], writes=['p_s1'])
                S.op('act', lambda a, f=f: a.activation(out=P['yf'], in_=p_s[1][:, 0:NS], func=AF.Identity, scale=vec(5, f), bias=vec(6, f)),
                     reads=['p_s1', 'rvec'], writes=['yf'])
                S.op('dve', lambda v: v.tensor_tensor(out=P['yf'], in0=P['yf'], in1=P['bon'], op=ALU.add), reads=['yf', 'bon'], writes=['yf'])
                S.op('dve', lambda v, f=f: v.tensor_tensor(out=rv(ogS[:, f, :]), in0=P['yf'], in1=P['gt'], op=ALU.mult), reads=['yf', 'gt'], writes=['ogS'])
            fence()
            for dk in range(KC):
                pa, pk = proj(rw['w_oT'], dk, lambda k: rv(ogS[:, k, :]), KC, NS, ['ogS'])
                S.op('act', lambda a, pa=pa, dk=dk: a.activation(out=zsq[:, dk, 0:NS], in_=pa, func=AF.Identity), reads=[pk], writes=['zsq_y', 'zsq'])
            post_norm_tile_from_sbuf(l, 1, NP, 1.0, NS)
            fence()

        def mixer_stub_sublayer(l):
            for t in range(NTILE):
                c0 = t * TW
                for k in range(KC):
                    S.op('act', lambda a, k=k, c0=c0: a.activation(out=zt[:, k, :], in_=xT[:, k, c0:c0 + TW],
                                                                   func=AF.Identity, scale=DN_ALPHA),
                         reads=['xT'], writes=['zt'])
                    S.op('act', lambda a, k=k: a.activation(out=zsq[:, k, :], in_=zt[:, k, :], func=AF.Square),
                         reads=['zt', 'zsq_y'], writes=['zsq', 'zsq_y'])
                finish_norm(l, 1, c0)

        for l in range(DEPTH):
            if only == 's5':
                if l == 0:
                    U5, Us5 = s5_phase()
                    S.dma('sp', 'st', lambda q: q.dma_start(out=yT_d[:, 0:4, 0:NP], in_=U5), reads=['U'])
                    S.dma('sp', 'st', lambda q: q.dma_start(out=yT_d[:, 0:4, NP:NT], in_=Us5), reads=['Us'])
                continue
            ffn_sublayer(l, 0, tuple(w for w in ffw["ffn1"]))
            mix0_sublayer() if l == 0 else mix1_sublayer()
            ffn_sublayer(l, 2, tuple(w for w in ffw["ffn2"]))

        if only is None:
            S.dma('sp', 'st', lambda q: q.dma_start(out=yT_d, in_=xT[:]), reads=['xT'])
        S.op('pool', lambda g: g.memset(zsq[:, 0, :], 0.0), writes=['zsq', 'zsq_y'])
        for name, ap in st_out.items():
            if name in ('gla_p', 'shift_p', 'wkv_p'):
                continue
            nb, width = ap.shape
            for c in range(0, width, TW):
                wdt = min(TW, width - c)
                S.dma('sp', 'st', lambda q, ap=ap, nb=nb, c=c, wdt=wdt: q.dma_start(
                    out=ap[:, c:c + wdt], in_=zsq[0:nb, 0, 0:wdt]), reads=['zsq'])
        S.finish('sp')
        S.emit_all()
    return nc


def _featmajor(v, ncols):
    return np.ascontiguousarray(np.swapaxes(v.reshape(v.shape[:-1] + (ncols, 128)), -1, -2))


def _tile_w(w, nk):
    C = w.shape[1]
    nb = (C + 127) // 128
    wp = np.zeros((nk * 128, nb * 128), np.float32)
    wp[:, :C] = w
    return np.ascontiguousarray(wp.reshape(nk, 128, nb, 128).transpose(2, 1, 0, 3))


def prep_inputs(inp):
    f = lambda k: np.ascontiguousarray(np.asarray(inp[k], dtype=np.float32))
    xp, xs, cp, cs = f("x_prompt"), f("x_sample"), f("c_prompt"), f("c_sample")
    shared = {
        "ada_w": f("ada_w"),
        "ada_bT": _featmajor(f("ada_b"), NMOD * KC),
        "ln_gT": _featmajor(f("ln_g"), KC),
        "ln_bT": _featmajor(f("ln_b"), KC),
        "w_inT": _tile_w(f("w_in")[:, 0:1664], KC), "w_inuT": _tile_w(f("w_in")[:, 1552:2064], KC),
        "w_outT": _tile_w(f("w_out"), KC), "gla_w_gk": f("gla_w_gk"),
        "gla_b_gkT": _featmajor(f("gla_b_gk"), 2),
        "gla_norm_gT": np.ascontiguousarray(f("gla_norm_g").reshape(128, 1)),
        "s5_aT": np.ascontiguousarray(np.tile(np.stack([f("s5_a_re").T, f("s5_a_im").T], 1), (2, 1, 1))),
        "s5_lsT": np.ascontiguousarray(np.tile(f("s5_log_step")[None, :], (128, 1))),
        "s5_BA": np.ascontiguousarray(np.concatenate([f("s5_b_re").transpose(1, 0, 2), f("s5_b_im").transpose(1, 0, 2)], 0)),
        "s5_BB": np.ascontiguousarray(np.concatenate([f("s5_b_im").transpose(1, 0, 2), f("s5_b_re").transpose(1, 0, 2)], 0)),
        "s5_CT": np.ascontiguousarray(np.concatenate([f("s5_c_re").transpose(2, 0, 1), f("s5_c_im").transpose(2, 0, 1)], 0)),
        "s5_dT": _featmajor(f("s5_d").reshape(-1), 4), "s5_b_gluT": _featmajor(f("s5_b_glu"), 4), "s5_w_gluT": _tile_w(f("s5_w_glu"), 4),
        "rwkv_muT": np.ascontiguousarray(_featmajor(f("rwkv_mu"), KC).transpose(1, 0, 2)),
        "rwkv_vecT": np.ascontiguousarray(_featmajor(np.stack([f("rwkv_w0"), f("rwkv_a0"), f("rwkv_k_k"), f("rwkv_k_a"),
                                                                f("rwkv_r_k").reshape(-1), f("rwkv_lnx_g"), f("rwkv_lnx_b")]), KC).transpose(1, 0, 2)),
    }
    for nm in ("ffn1", "ffn2"):
        shared[nm + "_wd"] = f(nm + "_wd")
        wg, wu = f(nm + "_wg"), f(nm + "_wu")
        shared[nm + "_wguT"] = np.ascontiguousarray(np.stack(
            [np.stack([_tile_w(wg[l], KC), _tile_w(wu[l], KC)], axis=2) for l in range(DEPTH)]))
    for nm in ("w2", "a2", "g2"):
        shared["rwkv_" + nm] = f("rwkv_" + nm)
    for nm in ("w_r", "w_k", "w_v", "w_o", "w1", "a1", "g1"):
        shared["rwkv_" + nm + "T"] = _tile_w(f("rwkv_" + nm), KC)
    in_maps = []
    for i in range(8):
        tok = np.concatenate([xp[i], xs[16 * i:16 * i + 16, 0, :]], axis=0)
        xT = np.ascontiguousarray(tok.T.reshape(KC, 128, NT).transpose(1, 0, 2))
        cc = np.concatenate([cs[16 * i:16 * i + 16], cp[i:i + 1], np.zeros((1, D), np.float32)], axis=0)
        cT = np.ascontiguousarray(cc.T.reshape(KC, 128, NCC).transpose(1, 0, 2))
        h0 = np.concatenate([f('state_s5_re')[16 * i:16 * i + 16].transpose(2, 1, 0), f('state_s5_im')[16 * i:16 * i + 16].transpose(2, 1, 0)], 0)
        g0 = f('state_gla')[16 * i:16 * i + 16].reshape(16, 2, 2, 64, 128).transpose(2, 3, 0, 1, 4).reshape(128, 16, 2, 128)
        sh = f('state_rwkv_shift')[16 * i:16 * i + 16]
        shT = sh.T.reshape(KC, 128, 16).transpose(1, 0, 2)
        w0 = f('state_rwkv_wkv')[16 * i:16 * i + 16].reshape(16, KC, 2, 64, 64).transpose(2, 4, 0, 1, 3).reshape(128, 16, KC, 64)
        in_maps.append(dict(shared, xT=xT, cT=cT, s5_h0T=np.ascontiguousarray(h0), gla_s0T=np.ascontiguousarray(g0),
                            rwkv_shiftT=np.ascontiguousarray(shT), wkv_s0T=np.ascontiguousarray(w0)))
    return in_maps


def kernel(**inp):
    in_maps = prep_inputs(inp)
    nc = build_nc()
    res = run_bass_kernel_spmd(nc, in_maps, core_ids=list(range(8))).results

    def tokens(r):
        return r["yT"].transpose(2, 1, 0).reshape(NT, D)
    y_prompt = np.stack([tokens(r)[:NP] for r in res]).astype(np.float32)
    y_sample = np.concatenate([tokens(r)[NP:] for r in res])[:, None, :].astype(np.float32)
    outs = [y_prompt, y_sample]
    for grp in ("p", "s"):
        cat = lambda k: np.concatenate([r[k + "_" + grp] for r in res], axis=0)
        if grp == "p":
            s5 = np.stack([r["s5_hp"] for r in res])
            s5re, s5im = s5[:, 0:64].transpose(0, 2, 1), s5[:, 64:128].transpose(0, 2, 1)
        else:
            s5 = np.concatenate([r["s5_hs"].transpose(2, 1, 0) for r in res], 0)
            s5re, s5im = s5[:, :, 0:64], s5[:, :, 64:128]
        if grp == "p":
            gla = cat("gla").reshape((-1,) + GLA_SHAPE)
        else:
            gla = np.concatenate([r["gla_sT"].reshape(2, 64, 16, 2, 128).transpose(2, 3, 0, 1, 4).reshape(16, 4, 64, 128) for r in res], 0)
        if grp == "p":
            shift, wkv = cat("shift"), cat("wkv").reshape((-1,) + WKV_SHAPE)
        else:
            shift = np.concatenate([r["shift_sT"].transpose(2, 1, 0).reshape(16, D) for r in res], 0)
            wkv = np.concatenate([r["wkv_sT"].reshape(2, 64, 16, KC, 64).transpose(2, 3, 0, 4, 1).reshape(16, 16, 64, 64) for r in res], 0)
        outs += [gla, s5re, s5im, shift, wkv]
    return tuple(np.ascontiguousarray(o, dtype=np.float32) for o in outs)
```

```python
import numpy as np
from contextlib import ExitStack
import concourse.bass as bass
import concourse.mybir as mybir
from concourse.bass_utils import run_bass_kernel_spmd

F32 = mybir.dt.float32
F32R = mybir.dt.float32r
AF = mybir.ActivationFunctionType
ALU = mybir.AluOpType

D, DFF, DEPTH, NMOD = 1024, 2752, 2, 9
KC = D // 128
NP, NS = 2048, 16
NT = NP + NS
TW = 344
NTILE = NT // TW
GROUPS = [(0, 1), (2, 3), (4, 5)]
NCC = 18
FCH = [(f * 128, min(128, DFF - f * 128)) for f in range((DFF + 127) // 128)]
DN_ALPHA = (2 * DEPTH) ** 0.25
LN_EPS = 1e-5
GLA_SHAPE = (4, 64, 128)
S5_SHAPE = (32, 64)
WKV_SHAPE = (16, 64, 64)


class Sched:
    CAP, DCAP, MAX_SEMS = 4000, 240, 96

    def __init__(self, nc, es):
        self.nc, self.es = nc, es
        self.names = ('sp', 'act', 'dve', 'pool', 'pe')
        self.ops = {k: [] for k in self.names}
        self.nseq = {k: 0 for k in self.names}
        self.sems, self.dcount, self.res = {}, {}, {}
        self.waited = {k: {} for k in self.names}

    def _sem(self, kind, key, epoch):
        k = (kind, key, epoch)
        if k not in self.sems:
            assert len(self.sems) < self.MAX_SEMS, f"out of semaphores ({len(self.sems)})"
            self.sems[k] = self.es.enter_context(self.nc.semaphore(f"s_{kind}_{key}_{epoch}"))
        return self.sems[k]

    def _tok_wait(self, tok):
        if tok[0] == 'c':
            _, e, n = tok
            return self._sem('c', e, (n - 1) // self.CAP), (n - 1) % self.CAP + 1, ('c', e), n
        _, ch, n = tok
        ep = (n - 1) // self.DCAP
        in_ep = min(self.dcount[ch], (ep + 1) * self.DCAP) - ep * self.DCAP
        return self._sem('d', ch, ep), 16 * in_ep, ('d', ch, ep), in_ep

    def _need(self, e, tok, waits):
        if tok is None or (tok[0] == 'c' and tok[1] == e == 'pe'):
            return
        sem, v, src, order = self._tok_wait(tok)
        if self.waited[e].get(src, 0) >= order:
            return
        self.waited[e][src] = order
        waits.append((sem, v))

    def _deps(self, e, reads, writes):
        waits = []
        for r in reads:
            self._need(e, self.res.setdefault(r, {'w': None, 'r': {}})['w'], waits)
        for w in writes:
            st = self.res.setdefault(w, {'w': None, 'r': {}})
            self._need(e, st['w'], waits)
            for t in st['r'].values():
                self._need(e, t, waits)
        return waits

    def _commit(self, tok, rkey, reads, writes):
        for r in reads:
            self.res[r]['r'][rkey] = tok
        for w in writes:
            self.res[w] = {'w': tok, 'r': {}}

    def op(self, e, emit, reads=(), writes=()):
        waits = self._deps(e, reads, writes)
        self.nseq[e] += 1
        n = self.nseq[e]
        self.ops[e].append((waits, emit, self._sem('c', e, (n - 1) // self.CAP), 1))
        self._commit(('c', e, n), ('c', e), reads, writes)

    def dma(self, q, ch, emit, reads=(), writes=()):
        waits = self._deps(q, reads, writes)
        self.dcount[ch] = n = self.dcount.get(ch, 0) + 1
        self.ops[q].append((waits, emit, self._sem('d', ch, (n - 1) // self.DCAP), 16))
        self._commit(('d', ch, n), ('d', ch), reads, writes)

    def finish(self, e='sp'):
        waits = []
        for ch, tot in self.dcount.items():
            for ep in range((tot - 1) // self.DCAP + 1):
                self._need(e, ('d', ch, min(tot, (ep + 1) * self.DCAP)), waits)
        self.ops[e].append((waits, None, None, 0))

    def emit_all(self):
        with self.nc.Block() as block:
            for name, deco in (('sp', block.sync), ('act', block.scalar), ('dve', block.vector),
                               ('pool', block.gpsimd), ('pe', block.tensor)):
                def body(eng, lst=self.ops[name]):
                    for waits, emit, sem, amt in lst:
                        for s, v in waits:
                            eng.wait_ge(s, v)
                        if emit is not None:
                            emit(eng).then_inc(sem, amt)
                deco(body)


def build_nc(only=None):
    nc = bass.Bass("TRN2", target_bir_lowering=False)
    dr = lambda n, s, k: nc.dram_tensor(n, list(s), F32, kind=k).ap()
    xT_d = dr("xT", (128, KC, NT), "ExternalInput")
    cT_d = dr("cT", (128, KC, NCC), "ExternalInput")
    adaw_d = dr("ada_w", (DEPTH, D, NMOD * D), "ExternalInput")
    adab_d = dr("ada_bT", (DEPTH, 128, NMOD * KC), "ExternalInput")
    lng_d = dr("ln_gT", (DEPTH, 3, 128, KC), "ExternalInput")
    lnb_d = dr("ln_bT", (DEPTH, 3, 128, KC), "ExternalInput")
    ffw = {}
    for nm in ("ffn1", "ffn2"):
        ffw[nm] = (dr(nm + "_wguT", (DEPTH, len(FCH), 128, 2, KC, 128), "ExternalInput"),
                   dr(nm + "_wd", (DEPTH, DFF, D), "ExternalInput"))
    w_in_d = dr("w_inT", (13, 128, KC, 128), "ExternalInput")
    w_inu_d = dr("w_inuT", (4, 128, KC, 128), "ExternalInput")
    w_out_d = dr("w_outT", (KC, 128, KC, 128), "ExternalInput")
    wgk_d = dr("gla_w_gk", (16, 256), "ExternalInput")
    bgk_d = dr("gla_b_gkT", (128, 2), "ExternalInput")
    gng_d = dr("gla_norm_gT", (128, 1), "ExternalInput")
    rw = {n: dr("rwkv_" + n, sh, "ExternalInput") for n, sh in dict(
        w_rT=(KC, 128, KC, 128), w_kT=(KC, 128, KC, 128), w_vT=(KC, 128, KC, 128), w_oT=(KC, 128, KC, 128),
        w1T=(1, 128, KC, 128), w2=(64, D), a1T=(1, 128, KC, 128), a2=(64, D), g1T=(2, 128, KC, 128), g2=(160, D)).items()}
    rmu_d = dr("rwkv_muT", (128, 6, KC), "ExternalInput")
    rvec_d = dr("rwkv_vecT", (128, 7, KC), "ExternalInput")
    s5aT_d = dr("s5_aT", (128, 2, 32), "ExternalInput")
    s5ls_d = dr("s5_lsT", (128, 32), "ExternalInput")
    s5BA_d = dr("s5_BA", (128, 32, 16), "ExternalInput")
    s5BB_d = dr("s5_BB", (128, 32, 16), "ExternalInput")
    s5CT_d = dr("s5_CT", (128, 32, 16), "ExternalInput")
    s5d_d = dr("s5_dT", (128, 4), "ExternalInput")
    s5bg_d = dr("s5_b_gluT", (128, 4), "ExternalInput")
    s5wg_d = dr("s5_w_gluT", (4, 128, 4, 128), "ExternalInput")
    s5h0_d = dr("s5_h0T", (128, 32, NS), "ExternalInput")
    s5hp_d = dr("s5_hp", (128, 32), "ExternalOutput")
    s5hs_d = dr("s5_hs", (128, 32, NS), "ExternalOutput")
    glas0_d = dr("gla_s0T", (128, NS, 2, 128), "ExternalInput")
    glas_d = dr("gla_sT", (128, NS, 2, 128), "ExternalOutput")
    rshift_d = dr("rwkv_shiftT", (128, KC, NS), "ExternalInput")
    wkvs0_d = dr("wkv_s0T", (128, NS, KC, 64), "ExternalInput")
    wkvs_d = dr("wkv_sT", (128, NS, KC, 64), "ExternalOutput")
    shifts_d = dr("shift_sT", (128, KC, NS), "ExternalOutput")
    yT_d = dr("yT", (128, KC, NT), "ExternalOutput")
    st_out = {}
    for grp, nb in (("p", 1), ("s", NS)):
        if grp == "p":
            st_out["gla_" + grp] = dr("gla_" + grp, (nb, 4 * 64 * 128), "ExternalOutput")
        if grp == "p":
            st_out["shift_" + grp] = dr("shift_" + grp, (nb, D), "ExternalOutput")
            st_out["wkv_" + grp] = dr("wkv_" + grp, (nb, 16 * 64 * 64), "ExternalOutput")

    with ExitStack() as es:
        sb = lambda n, s: es.enter_context(nc.sbuf_tensor(n, list(s), F32))
        ps = lambda n, s: es.enter_context(nc.psum_tensor(n, list(s), F32))
        S = Sched(nc, es)

        xT = sb("xT_sb", (128, KC, NT))
        hin = sb("hin", (128, KC, 2 * TW))
        HT = sb("HT", (128, len(FCH), 2 * TW))
        wgu = [sb(f"wgu{i}", (128, 2, KC, 128)) for i in range(2)]
        cT = sb("cT_sb", (128, KC, NCC))
        sc = sb("silu_c", (128, KC, NCC))
        mods = [sb(f"mods{l}", (128, NMOD * KC, NCC)) for l in range(DEPTH)]
        adabT = sb("adabT", (128, DEPTH, NMOD * KC))
        lng = sb("lng", (128, DEPTH * 3, KC))
        lnb = sb("lnb", (128, DEPTH * 3, KC))
        ones = sb("ones", (128, 128))
        fence_t = sb('fence_t', (1, 2))
        zt = sb("zt", (128, KC, TW))
        zsq = sb("zsq", (128, KC, TW))
        mean = sb("mean", (128, TW)); rstd = sb("rstd", (128, TW))
        gsb = sb("g_sb", (128, TW))
        msq = gsb
        p_g = [ps(f"p_g{i}", (128, 512)) for i in range(2)]
        p_u = [ps(f"p_u{i}", (128, 512)) for i in range(2)]
        p_y = [ps(f"p_y{i}", (128, 512)) for i in range(2)]
        p_s = [ps(f"p_s{i}", (128, 512)) for i in range(2)]

        NQ = 256
        ident = sb('ident', (128, 128)); rstm = sb('rstm', (128, NQ))
        Wgk = sb('Wgk', (16, 256)); bgk = sb('bgk', (128, 2)); gng = sb('gng', (128, 1))
        dec = sb('dec', (128, 2, 2)); bl = sb('bl', (128, 2, 2))
        mask2 = sb('mask2', (128, 256))
        maskLT = sb('maskLT', (128, 128))
        bones = sb('bones', (128, 128))
        rmu = sb('rmu', (128, 6, KC)); rvec = sb('rvec', (128, 7, KC))
        Swk = sb('Swk', (128, KC, 64))
        Sst = [Swk[:, 2 * f:2 * f + 2, :].rearrange('p a b -> p (a b)') for f in range(2)]
        hprev = sb('hprev', (128, KC)); shp = sb('shp', (128, KC)); pcl = sb('pcl', (128, 2)); llast = sb('llast', (128, 2))
        st2 = sb('st2', (128, 8))
        blkm = sb('blkm', (128, 8)); sgn = sb('sgn', (128, 1))
        s5d = sb('s5d', (128, 4)); s5bg = sb('s5bg', (128, 4)); HS = sb('HS', (128, 32))
        r32 = lambda t: t.bitcast(F32R)
        rv = lambda ap: ap.bitcast(F32R)

        S.dma('sp', 'ld0', lambda q: q.dma_start(out=xT[:], in_=xT_d), writes=['xT'])
        S.dma('sp', 'ld0', lambda q: q.dma_start(out=cT[:], in_=cT_d), writes=['cT'])
        S.dma('sp', 'ld0', lambda q: q.dma_start(out=adabT[:], in_=adab_d.rearrange("l p m -> p l m")),
              writes=['adabT'])
        S.dma('sp', 'ld0', lambda q: q.dma_start(out=lng[:], in_=lng_d.rearrange("l s p k -> p (l s) k")),
              writes=['lng'])
        S.dma('sp', 'ld0', lambda q: q.dma_start(out=lnb[:], in_=lnb_d.rearrange("l s p k -> p (l s) k")),
              writes=['lnb'])
        S.op('pool', lambda g: g.memset(ones[:], 1.0), writes=['ones'])
        S.op('pool', lambda g: g.affine_select(out=ident[:], in_=ones[:], pattern=[[1, 128]], base=0,
                                               channel_multiplier=-1, compare_op=ALU.is_equal, fill=0.0),
             reads=['ones'], writes=['ident'])
        S.op('pool', lambda g: g.affine_select(out=mask2[:, 0:128], in_=ones[:], pattern=[[1, 128]], base=-1,
                                               channel_multiplier=-1, compare_op=ALU.is_ge, fill=0.0),
             reads=['ones'], writes=['mask2'])
        S.op('pool', lambda g: g.affine_select(out=mask2[:, 128:256], in_=ones[:], pattern=[[1, 128]], base=0,
                                               channel_multiplier=-1, compare_op=ALU.is_ge, fill=0.0),
             reads=['ones', 'mask2'], writes=['mask2'])
        S.op('pool', lambda g: g.affine_select(out=maskLT[:], in_=ones[:], pattern=[[-1, 128]], base=-1,
                                               channel_multiplier=1, compare_op=ALU.is_ge, fill=0.0),
             reads=['ones'], writes=['maskLT'])
        S.op('pool', lambda g: g.affine_select(out=blkm[:], in_=ones[:, 0:8], pattern=[[-16, 8]], base=0,
                                               channel_multiplier=1, compare_op=ALU.is_ge, fill=0.0), reads=['ones'], writes=['blkm'])
        S.op('pool', lambda g: g.affine_select(out=blkm[:], in_=blkm[:], pattern=[[16, 8]], base=15,
                                               channel_multiplier=-1, compare_op=ALU.is_ge, fill=0.0), reads=['blkm'], writes=['blkm'])
        S.op('pool', lambda g: g.memset(sgn[0:64, :], 1.0), writes=['sgn'])
        S.op('pool', lambda g: g.memset(sgn[64:128, :], -1.0), reads=['sgn'], writes=['sgn'])
        S.op('pool', lambda g: g.memset(bones[:], 0.0), writes=['bones'])
        for hb in range(2):
            S.op('pool', lambda g, hb=hb: g.memset(bones[hb * 64:(hb + 1) * 64, hb * 64:(hb + 1) * 64], 1.0),
                 reads=['bones'], writes=['bones'])
        S.dma('sp', 'ld0', lambda q: q.dma_start(out=rmu[:], in_=rmu_d), writes=['rmu'])
        S.dma('sp', 'ld0', lambda q: q.dma_start(out=rvec[:], in_=rvec_d), writes=['rvec'])
        S.op('pool', lambda g: g.memset(rstm[:], 1.0), writes=['rstm'])
        for c_ in range(NQ // 128):
            S.op('pool', lambda g, c_=c_: g.memset(rstm[:, c_ * 128:c_ * 128 + 1], 0.0), reads=['rstm'], writes=['rstm'])
        S.dma('pool', 'ldc', lambda q: q.dma_start(out=r32(Wgk)[:], in_=wgk_d), writes=['Wgk'])
        S.dma('sp', 'ld0', lambda q: q.dma_start(out=bgk[:], in_=bgk_d), writes=['bgk'])
        S.dma('sp', 'ld0', lambda q: q.dma_start(out=gng[:], in_=gng_d), writes=['gng'])
        S.op('act', lambda a: a.activation(out=r32(sc)[:], in_=cT[:], func=AF.Silu),
             reads=['cT'], writes=['sc'])

        ABUF = [(HT[:, 8 * (i // 2):8 * (i // 2) + 8, 256 * (i % 2):256 * (i % 2) + 256], f"adaw{i}") for i in range(4)] + \
               [(hin[:, :, 256 * i:256 * i + 256], f"adaw{4 + i}") for i in range(2)]
        nblk = 0
        for l in range(DEPTH):
            for cb in range(NMOD * D // 256):
                buf, key = ABUF[nblk % len(ABUF)]; nblk += 1
                S.dma('pool', key, lambda q, buf=buf, l=l, cb=cb: q.dma_start(
                    out=rv(buf), in_=adaw_d[l, :, cb * 256:(cb + 1) * 256].rearrange("(k p) c -> p k c", p=128)),
                    writes=[key])
                for j in range(2):
                    m = cb * 2 + j
                    pt = p_s[m % 2]; pk = f"p_s{m % 2}"
                    for k in range(KC):
                        S.op('pe', lambda p, pt=pt, buf=buf, k=k, j=j: p.matmul(
                            pt[:, 0:NCC], rv(buf[:, k, j * 128:(j + 1) * 128]), r32(sc)[:, k, :],
                            start=(k == 0), stop=(k == KC - 1)),
                            reads=[key, 'sc'], writes=[pk])
                    S.op('act', lambda a, pt=pt, l=l, m=m: a.activation(
                        out=mods[l][:, m, :], in_=pt[:, 0:NCC], func=AF.Identity,
                        bias=adabT[:, l, m:m + 1], scale=1.0),
                        reads=[pk, 'adabT'], writes=[f"mods{l}"])
            for which in (1, 2, 4, 5, 7, 8):
                S.op('dve', lambda v, l=l, which=which: v.tensor_scalar_add(
                    mods[l][:, which * KC:(which + 1) * KC, :], mods[l][:, which * KC:(which + 1) * KC, :], 1.0),
                    reads=[f"mods{l}"], writes=[f"mods{l}"])

        S.op('pool', lambda g: g.memset(fence_t[:], 0.0), writes=['hin', 'HT', 'fence_t'] + [f'adaw{i}' for i in range(6)])

        def split_cols(c0, w):
            npr = max(0, min(c0 + w, NP) - c0)
            return (0, c0, npr), (npr, c0 + npr, w - npr)

        def modulate(eng_p, out_t, ocol, l, sub, c0, w, rkeys, wkey):
            (po, pg, pn), (so, sg, sn) = split_cols(c0, w)
            for k in range(KC):
                if pn:
                    S.op('act', lambda a, k=k: a.activation(
                        out=r32(out_t)[:, k, ocol + po:ocol + po + pn], in_=xT[:, k, pg:pg + pn], func=AF.Identity,
                        scale=mods[l][:, (3 * sub + 1) * KC + k, 16:17], bias=mods[l][:, (3 * sub) * KC + k, 16:17]),
                        reads=['xT', f"mods{l}"] + rkeys, writes=[wkey])
                if sn:
                    s0 = sg - NP
                    S.op('dve', lambda v, k=k: v.tensor_tensor(
                        out=gsb[:, 0:sn], in0=xT[:, k, sg:sg + sn],
                        in1=mods[l][:, (3 * sub + 1) * KC + k, s0:s0 + sn], op=ALU.mult),
                        reads=['xT', f"mods{l}"], writes=['gsb'])
                    S.op('dve', lambda v, k=k: v.tensor_tensor(
                        out=r32(out_t)[:, k, ocol + so:ocol + so + sn], in0=gsb[:, 0:sn],
                        in1=mods[l][:, (3 * sub) * KC + k, s0:s0 + sn], op=ALU.add),
                        reads=['gsb', f"mods{l}"] + rkeys, writes=[wkey])

        wcnt = [0]

        def ffn_sublayer(l, sub, wts):
            wgu_d, wd_d = wts
            HTF = HT[:].rearrange("p a b -> p (a b)")
            hinF = hin[:].rearrange("p a b -> p (a b)")
            slotsA = [(wgu[0][:], 'wgu0'), (wgu[1][:], 'wgu1')] + \
                     [(HTF[:, (16 + 3 * i) * 688:(16 + 3 * i) * 688 + 2048].rearrange("p (a k c) -> p a k c", a=2, k=KC), f'hts{i}') for i in range(2)]
            slotsB = [(wgu[i][:].rearrange("p a k c -> p (a k c)")[:, 0:D], f'wgu{i}') for i in range(2)] + \
                     [(hinF[:, i * D:(i + 1) * D], f'hinw{i}') for i in range(5)]
            HW = [f'hinw{i}' for i in range(5)]
            for grp in GROUPS:
                c0g = grp[0] * TW
                modulate('act', hin, 0, l, sub, c0g, 2 * TW, [], 'hin')
                S.op('pool', lambda g: g.memset(fence_t[:], 0.0), writes=['HT', 'hts0', 'hts1', 'fence_t'])
                for fi, (f0, fw) in enumerate(FCH):
                    wb, wk = slotsA[fi % 4] if fi < 16 else slotsA[fi % 2]
                    S.dma('pool', wk, lambda q, wb=wb, fi=fi: q.dma_start(out=rv(wb), in_=wgu_d[l, fi]), writes=[wk])
                    for t in range(2):
                        for j, pp, pkn in ((0, p_g, 'p_g'), (1, p_u, 'p_u')):
                            for k in range(KC):
                                S.op('pe', lambda p, pp=pp, t=t, j=j, k=k, wb=wb: p.matmul(
                                    pp[t][0:128, 0:TW], rv(wb[:, j, k, :]), r32(hin)[:, k, t * TW:(t + 1) * TW],
                                    start=(k == 0), stop=(k == KC - 1)),
                                    reads=[wk, 'hin'], writes=[f"{pkn}{t}"])
                        S.op('act', lambda a, t=t: a.activation(out=gsb[:, 0:TW], in_=p_g[t][:, 0:TW], func=AF.Silu),
                             reads=[f"p_g{t}"], writes=['gsb'])
                        S.op('dve', lambda v, t=t, fi=fi: v.tensor_tensor(
                            out=r32(HT)[:, fi, t * TW:(t + 1) * TW], in0=gsb[:, 0:TW], in1=p_u[t][:, 0:TW],
                            op=ALU.mult), reads=['gsb', f"p_u{t}"],
                            writes=['HT', 'adaw0', 'adaw1'] + (['hts0', 'hts1'] if fi >= 16 else []))
                S.op('pool', lambda g: g.memset(fence_t[:], 0.0), writes=['hin', 'fence_t'] + HW)
                nb_ = 0
                for t in range(2):
                    c0 = c0g + t * TW
                    banks = [(p_g[0], 'p_g0'), (p_g[1], 'p_g1'), (p_u[0], 'p_u0'), (p_u[1], 'p_u1'),
                             (p_y[0], 'p_y0'), (p_y[1], 'p_y1'), (p_s[0], 'p_s0'), (p_s[1], 'p_s1')]
                    for fi, (f0, fw) in enumerate(FCH):
                        wbv, wk = slotsB[nb_ % len(slotsB)]; nb_ += 1
                        S.dma('pool', wk, lambda q, wbv=wbv, f0=f0, fw=fw: q.dma_start(
                            out=rv(wbv[0:fw, :]), in_=wd_d[l, f0:f0 + fw, :]), writes=[wk])
                        for dk in range(KC):
                            pt, pk = banks[dk]
                            S.op('pe', lambda p, pt=pt, wbv=wbv, dk=dk, fi=fi, fw=fw, t=t: p.matmul(
                                pt[:, 0:TW], rv(wbv[0:fw, dk * 128:(dk + 1) * 128]), r32(HT)[0:fw, fi, t * TW:(t + 1) * TW],
                                start=(fi == 0), stop=(fi == len(FCH) - 1)), reads=[wk, 'HT'], writes=[pk])
                    for dk in range(KC):
                        pt, pk = banks[dk]
                        if dk % 2:
                            S.op('act', lambda a, pt=pt, dk=dk: a.activation(out=zsq[:, dk, :], in_=pt[:, 0:TW], func=AF.Identity),
                                 reads=[pk], writes=['zsq_y', 'zsq'])
                        else:
                            S.op('dve', lambda v, pt=pt, dk=dk: v.tensor_copy(zsq[:, dk, :], pt[:, 0:TW]), reads=[pk], writes=['zsq_y', 'zsq'])
                    post_norm_tile_from_sbuf(l, sub, c0, 0.5)
                S.op('pool', lambda g: g.memset(fence_t[:], 0.0), writes=['hin', 'fence_t'] + HW)

        def post_norm_tile_from_sbuf(l, sub, c0, res_scale, W=TW):
            (po, pg, pn), (so, sg, sn) = split_cols(c0, W)
            gi = 3 * sub + 2
            for k in range(KC):
                if pn:
                    S.op('act', lambda a, k=k: a.activation(
                        out=gsb[:, po:po + pn], in_=zsq[:, k, po:po + pn], func=AF.Identity,
                        scale=mods[l][:, gi * KC + k, 16:17]), reads=['zsq_y', f"mods{l}"], writes=['gsb'])
                if sn:
                    s0 = sg - NP
                    S.op('dve', lambda v, k=k: v.tensor_tensor(
                        out=gsb[:, so:so + sn], in0=zsq[:, k, so:so + sn],
                        in1=mods[l][:, gi * KC + k, s0:s0 + sn], op=ALU.mult),
                        reads=['zsq_y', f"mods{l}"], writes=['gsb'])
                S.op('dve', lambda v, k=k: v.tensor_scalar_mul(zt[:, k, 0:W], xT[:, k, c0:c0 + W], DN_ALPHA),
                     reads=['xT'], writes=['zt'])
                S.op('dve', lambda v, k=k: v.scalar_tensor_tensor(
                    out=zt[:, k, 0:W], in0=gsb[:, 0:W], scalar=float(res_scale), in1=zt[:, k, 0:W],
                    op0=ALU.mult, op1=ALU.add), reads=['gsb', 'zt'], writes=['zt'])
            for k in range(KC):
                S.op('act', lambda a, k=k: a.activation(out=zsq[:, k, 0:W], in_=zt[:, k, 0:W], func=AF.Square),
                     reads=['zt', 'zsq_y'], writes=['zsq', 'zsq_y'])
            finish_norm(l, sub, c0, W)

        def finish_norm(l, sub, c0, W=TW):
            for k in range(KC):
                S.op('pe', lambda p, k=k: p.matmul(p_s[0][:, 0:W], ones[:], zt[:, k, 0:W],
                                                   start=(k == 0), stop=(k == KC - 1)),
                     reads=['ones', 'zt'], writes=['p_s0'])
            for k in range(KC):
                S.op('pe', lambda p, k=k: p.matmul(p_s[1][:, 0:W], ones[:], zsq[:, k, 0:W],
                                                   start=(k == 0), stop=(k == KC - 1)),
                     reads=['ones', 'zsq'], writes=['p_s1'])
            S.op('act', lambda a: a.activation(out=mean[:, 0:W], in_=p_s[0][:, 0:W], func=AF.Identity, scale=1.0 / D),
                 reads=['p_s0'], writes=['mean'])
            S.op('dve', lambda v: v.tensor_tensor(out=msq[:, 0:W], in0=mean[:, 0:W], in1=mean[:, 0:W], op=ALU.mult),
                 reads=['mean'], writes=['gsb'])
            S.op('dve', lambda v: v.scalar_tensor_tensor(out=rstd[:, 0:W], in0=p_s[1][:, 0:W], scalar=1.0 / D,
                                                         in1=msq[:, 0:W], op0=ALU.mult, op1=ALU.subtract),
                 reads=['p_s1', 'gsb'], writes=['rstd'])
            S.op('dve', lambda v: v.tensor_scalar_add(rstd[:, 0:W], rstd[:, 0:W], LN_EPS), reads=['rstd'], writes=['rstd'])
            S.op('act', lambda a: a.activation(out=rstd[:, 0:W], in_=rstd[:, 0:W], func=AF.Sqrt),
                 reads=['rstd'], writes=['rstd'])
            S.op('dve', lambda v: v.reciprocal(rstd[:, 0:W], rstd[:, 0:W]), reads=['rstd'], writes=['rstd'])
            for k in range(KC):
                S.op('dve', lambda v, k=k: v.tensor_tensor(out=zt[:, k, 0:W], in0=zt[:, k, 0:W], in1=mean[:, 0:W],
                                                           op=ALU.subtract), reads=['zt', 'mean'], writes=['zt'])
                S.op('dve', lambda v, k=k: v.tensor_tensor(out=zt[:, k, 0:W], in0=zt[:, k, 0:W], in1=rstd[:, 0:W],
                                                           op=ALU.mult), reads=['zt', 'rstd'], writes=['zt'])
                S.op('act', lambda a, k=k: a.activation(
                    out=xT[:, k, c0:c0 + W], in_=zt[:, k, 0:W], func=AF.Identity,
                    scale=lng[:, l * 3 + sub, k:k + 1], bias=lnb[:, l * 3 + sub, k:k + 1]),
                    reads=['zt', 'lng', 'lnb'], writes=['xT'])


        HTf = HT[:].rearrange("p a b -> p (a b)")
        ztf = zt[:].rearrange("p a b -> p (a b)")
        zsqf = zsq[:].rearrange("p a b -> p (a b)")
        TN_R = ['qin0', 'qin1', 'kin0', 'kin1', 'gkl', 'og0', 'og1', 'og2', 'og3']
        TN_A = ['qT0', 'qT1', 'kT0', 'kT1', 'L0', 'L1', 'tmp0', 'tmp1', 'kst0', 'kst1']
        TN_B = ['gT0', 'gT1', 'gT2', 'gT3', 'vT0', 'vT1', 'vT2', 'vT3']
        TN = TN_R + TN_A + TN_B
        T = {n: HTf[:, i * NQ:(i + 1) * NQ] for i, n in enumerate(TN_R)}
        T.update({n: ztf[:, i * NQ:(i + 1) * NQ] for i, n in enumerate(TN_A)})
        T.update({n: zsqf[:, i * NQ:(i + 1) * NQ] for i, n in enumerate(TN_B)})
        o0 = len(TN_R) * NQ
        Pm = HTf[:, o0:o0 + 128]
        vtok = HTf[:, o0 + 128:o0 + 1152].rearrange("p (c v) -> p c v", c=2)
        ksttok = HTf[:, o0 + 1152:o0 + 1664].rearrange("p (c v) -> p c v", c=2)
        sqt, rst_ = zsqf[:, 8 * NQ:8 * NQ + 128], zsqf[:, 8 * NQ + 128:8 * NQ + 256]
        ALIAS = ['HT', 'adaw0', 'adaw1', 'adaw2', 'adaw3', 'adaw4', 'adaw5', 'hin', 'zt', 'zsq', 'zsq_y', 'gsb', 'wgu0', 'wgu1', 'wd0', 'wd1',
                 'wslot0', 'wslot1', 'wslot2', 'wslot3', 'hts0', 'hts1', 'hinw0', 'hinw1', 'hinw2', 'hinw3', 'hinw4', 'Sst0', 'Sst1', 'Swk', 'Pm', 'sqt', 'rst_', 'Pm0', 'Pm1', 'sqt0', 'sqt1', 'rst0', 'rst1', 'vtok', 'ksttok'] + TN

        HEAVY = {'HT', 'hin', 'wgu0', 'wgu1', 'wd0', 'wd1', 'wslot0', 'wslot1', 'wslot2', 'wslot3', 'hts0', 'hts1'} | \
                {f'adaw{i}' for i in range(6)} | {f'hinw{i}' for i in range(5)}

        def fence(light=False):
            if light:
                ks = [k for k in ALIAS if k not in HEAVY]
                S.op('dve', lambda v: v.memset(fence_t[:], 0.0), reads=ks, writes=ks + ['fence_t'])
            else:
                S.op('pool', lambda g: g.memset(fence_t[:], 0.0), reads=ALIAS, writes=ALIAS + ['fence_t'])

        wslot, pslot = [0], [0]
        PBK = [(p_g[0], 'p_g0'), (p_g[1], 'p_g1'), (p_u[0], 'p_u0'), (p_u[1], 'p_u1')]

        def proj(wt, blk, rhs_fn, nk, N, rkeys):
            sl = wslot[0] % 4; wslot[0] += 1
            buf = wgu[sl // 2][:, sl % 2]
            key = f"wslot{sl}"
            S.dma('pool', key, lambda q: q.dma_start(out=rv(buf[:, 0:nk, :]), in_=wt[blk, :, 0:nk, :]), writes=[key])
            pt, pk = PBK[pslot[0] % 4]; pslot[0] += 1
            for k in range(nk):
                rhs = rhs_fn(k)
                S.op('pe', lambda p, k=k, rhs=rhs: p.matmul(pt[:, 0:N], rv(buf[:, k, :]), rhs,
                                                            start=(k == 0), stop=(k == nk - 1)),
                     reads=[key] + rkeys, writes=[pk])
            return pt[:, 0:N], pk

        def stub_norm(l, c0, W):
            for k in range(KC):
                S.op('act', lambda a, k=k: a.activation(out=zt[:, k, 0:W], in_=xT[:, k, c0:c0 + W],
                                                        func=AF.Identity, scale=DN_ALPHA), reads=['xT'], writes=['zt'])
                S.op('act', lambda a, k=k: a.activation(out=zsq[:, k, 0:W], in_=zt[:, k, 0:W], func=AF.Square),
                     reads=['zt', 'zsq_y'], writes=['zsq', 'zsq_y'])
            finish_norm(l, 1, c0, W)


        def s5_phase():
            U = HTf[:, 6944:6944 + 4 * NP].rearrange("p (q n) -> p q n", q=4)
            XA, XB = HTf[:, 0:2050], HTf[:, 2050:4100]
            ZP = HTf[:, 4100:5124].rearrange("p (g m) -> p g m", g=8)
            ZC = HTf[:, 5124:6148].rearrange("p (g m) -> p g m", g=8)
            Rt = HTf[:, 6148:6404].rearrange("p (a m) -> p a m", a=2)
            Us = HTf[:, 6404:6468].rearrange("p (q b) -> p q b", q=4)
            HnA, H0A = hin[:, :, 512:576], hin[:, :, 576:640]
            Hn = lambda g: HnA[:, g // 4, (g % 4) * 16:(g % 4) * 16 + 16]
            H0 = lambda g: H0A[:, g // 4, (g % 4) * 16:(g % 4) * 16 + 16]
            BbT = zsqf[:, 0:512].rearrange("p (g c) -> p g c", g=32)
            CTt = zsqf[:, 512:1024].rearrange("p (g c) -> p g c", g=32)
            ybuf = ztf[:, 0:2048].rearrange("p (q n) -> p q n", q=4)
            pwr = zsqf[:, 1024:1728].rearrange("p (a k g) -> p a k g", a=2, k=11)
            s5sm = zsqf[:, 1728:2112].rearrange("p (i g) -> p i g", i=12)
            Jm, Jt = zsqf[:, 2112:2240], zsqf[:, 2240:2368]
            Rtmp = zsqf[:, 2368:2496]
            identR = HTf[:, 6468:6596]
            K5 = ['Jm', 'Jt', 'Rtmp', 'identR', 'U', 'XA', 'XB', 'ZP', 'ZC', 'Rt', 'Us', 'Hn', 'H0', 'BbT', 'CTt', 'ybuf', 's5sm', 'pwr']
            ALIAS.extend(k for k in K5 if k not in ALIAS)
            fence()
            l = 0
            sm = lambda i: s5sm[:, i, :]
            S.op('pool', lambda g: g.affine_select(out=Jm, in_=ones[:], pattern=[[1, 128]], base=-64, channel_multiplier=-1,
                                                   compare_op=ALU.is_equal, fill=0.0), reads=['ones'], writes=['Jm'])
            S.op('pool', lambda g: g.affine_select(out=Jt, in_=ones[:], pattern=[[1, 128]], base=64, channel_multiplier=-1,
                                                   compare_op=ALU.is_equal, fill=0.0), reads=['ones'], writes=['Jt'])
            S.op('pool', lambda g: g.tensor_tensor(out=Jm, in0=Jm, in1=Jt, op=ALU.add), reads=['Jm', 'Jt'], writes=['Jm'])
            S.op('act', lambda a: a.activation(out=rv(identR), in_=ident[:], func=AF.Identity), reads=['ident'], writes=['identR'])
            V = lambda e, fn, r, w: S.op(e, fn, reads=r, writes=w)
            S.dma('sp', 'ld5', lambda q: q.dma_start(out=s5sm[:, 0:2, :], in_=s5aT_d), writes=['s5sm'])
            S.dma('sp', 'ld5', lambda q: q.dma_start(out=s5sm[:, 2, :], in_=s5ls_d), writes=['s5sm'])
            S.dma('sp', 'ld5', lambda q: q.dma_start(out=ybuf[:, 0, :].rearrange("p (g c) -> p g c", g=32), in_=s5BA_d), writes=['ybuf'])
            S.dma('sp', 'ld5', lambda q: q.dma_start(out=ybuf[:, 1, :].rearrange("p (g c) -> p g c", g=32), in_=s5BB_d), writes=['ybuf'])
            S.dma('sp', 'ld5', lambda q: q.dma_start(out=CTt, in_=s5CT_d), writes=['CTt'])
            S.dma('sp', 'ld5', lambda q: q.dma_start(out=s5d[:], in_=s5d_d), writes=['s5d'])
            S.dma('sp', 'ld5', lambda q: q.dma_start(out=s5bg[:], in_=s5bg_d), writes=['s5bg'])
            S.dma('pool', 'ld5p', lambda q: q.dma_start(out=rv(H0A), in_=s5h0_d.rearrange('p (k a) b -> p k (a b)', a=4)), writes=['H0'])
            k5 = ['s5sm']
            V('act', lambda a: a.activation(out=sm(2), in_=sm(2), func=AF.Exp), k5, k5)
            V('dve', lambda v: v.tensor_tensor(out=sm(3), in0=sm(1), in1=sm(2), op=ALU.mult), k5, k5)
            V('act', lambda a: a.activation(out=sm(4), in_=sm(3), func=AF.Sin, scale=1.0 / 16), k5, k5)
            V('act', lambda a: a.activation(out=sm(5), in_=sm(3), func=AF.Sin, scale=1.0 / 32), k5, k5)
            V('dve', lambda v: v.tensor_tensor(out=sm(5), in0=sm(5), in1=sm(5), op=ALU.mult), k5, k5)
            V('dve', lambda v: v.tensor_scalar(out=sm(5), in0=sm(5), scalar1=-2.0, scalar2=1.0, op0=ALU.mult, op1=ALU.add), k5, k5)
            for _ in range(4):
                V('dve', lambda v: v.tensor_tensor(out=sm(6), in0=sm(5), in1=sm(5), op=ALU.mult), k5, k5)
                V('dve', lambda v: v.tensor_tensor(out=sm(7), in0=sm(4), in1=sm(4), op=ALU.mult), k5, k5)
                V('dve', lambda v: v.scalar_tensor_tensor(out=sm(4), in0=sm(5), scalar=2.0, in1=sm(4), op0=ALU.mult, op1=ALU.mult), k5, k5)
                V('dve', lambda v: v.tensor_tensor(out=sm(5), in0=sm(6), in1=sm(7), op=ALU.subtract), k5, k5)
            V('dve', lambda v: v.tensor_tensor(out=sm(6), in0=sm(0), in1=sm(2), op=ALU.mult), k5, k5)
            V('act', lambda a: a.activation(out=sm(6), in_=sm(6), func=AF.Exp), k5, k5)
            V('dve', lambda v: v.tensor_tensor(out=sm(5), in0=sm(5), in1=sm(6), op=ALU.mult), k5, k5)
            V('dve', lambda v: v.tensor_tensor(out=sm(4), in0=sm(4), in1=sm(6), op=ALU.mult), k5, k5)
            V('dve', lambda v: v.tensor_tensor(out=sm(6), in0=sm(0), in1=sm(0), op=ALU.mult), k5, k5)
            V('dve', lambda v: v.tensor_tensor(out=sm(7), in0=sm(1), in1=sm(1), op=ALU.mult), k5, k5)
            V('dve', lambda v: v.tensor_tensor(out=sm(6), in0=sm(6), in1=sm(7), op=ALU.add), k5, k5)
            V('dve', lambda v: v.reciprocal(sm(6), sm(6)), k5, k5)
            V('dve', lambda v: v.tensor_scalar_add(sm(7), sm(5), -1.0), k5, k5)
            V('dve', lambda v: v.tensor_tensor(out=sm(8), in0=sm(7), in1=sm(0), op=ALU.mult), k5, k5)
            V('dve', lambda v: v.tensor_tensor(out=sm(9), in0=sm(4), in1=sm(1), op=ALU.mult), k5, k5)
            V('dve', lambda v: v.tensor_tensor(out=sm(8), in0=sm(8), in1=sm(9), op=ALU.add), k5, k5)
            V('dve', lambda v: v.tensor_tensor(out=sm(8), in0=sm(8), in1=sm(6), op=ALU.mult), k5, k5)
            V('dve', lambda v: v.tensor_tensor(out=sm(9), in0=sm(4), in1=sm(0), op=ALU.mult), k5, k5)
            V('dve', lambda v: v.tensor_tensor(out=sm(10), in0=sm(7), in1=sm(1), op=ALU.mult), k5, k5)
            V('dve', lambda v: v.tensor_tensor(out=sm(9), in0=sm(9), in1=sm(10), op=ALU.subtract), k5, k5)
            V('dve', lambda v: v.tensor_tensor(out=sm(9), in0=sm(9), in1=sm(6), op=ALU.mult), k5, k5)
            V('dve', lambda v: v.tensor_scalar_mul(sm(9), sm(9), sgn[:, 0:1]), k5 + ['sgn'], k5)
            V('dve', lambda v: v.tensor_scalar_mul(sm(9), sm(9), -1.0), k5, k5)
            zb = lambda i: sm(i).unsqueeze(2).to_broadcast([128, 32, 16])
            BAv, BBv = (ybuf[:, i, :].rearrange("p (g c) -> p g c", g=32) for i in range(2))
            V('dve', lambda v: v.tensor_tensor(out=BbT, in0=BAv, in1=zb(8), op=ALU.mult), k5 + ['ybuf'], ['BbT'])
            V('dve', lambda v: v.tensor_tensor(out=BBv, in0=BBv, in1=zb(9), op=ALU.mult), k5 + ['ybuf'], ['ybuf'])
            V('dve', lambda v: v.tensor_tensor(out=BbT, in0=BbT, in1=BBv, op=ALU.add), ['BbT', 'ybuf'], ['BbT'])
            V('dve', lambda v: v.tensor_scalar_mul(CTt[64:128], CTt[64:128], -1.0), ['CTt'], ['CTt'])
            V('dve', lambda v: v.tensor_copy(pwr[:, 0, 0, :], sm(5)), k5, ['pwr'])
            V('dve', lambda v: v.tensor_copy(pwr[:, 1, 0, :], sm(4)), k5, ['pwr'])
            for k in range(1, 11):
                cr, ci, nr_, ni_ = pwr[:, 0, k - 1, :], pwr[:, 1, k - 1, :], pwr[:, 0, k, :], pwr[:, 1, k, :]
                V('dve', lambda v, cr=cr: v.tensor_tensor(out=sm(6), in0=cr, in1=cr, op=ALU.mult), ['pwr'] + k5, k5)
                V('dve', lambda v, ci=ci: v.tensor_tensor(out=sm(7), in0=ci, in1=ci, op=ALU.mult), ['pwr'] + k5, k5)
                V('dve', lambda v, cr=cr, ci=ci, ni_=ni_: v.scalar_tensor_tensor(out=ni_, in0=cr, scalar=2.0, in1=ci, op0=ALU.mult, op1=ALU.mult),
                  ['pwr'], ['pwr'])
                V('dve', lambda v, nr_=nr_: v.tensor_tensor(out=nr_, in0=sm(6), in1=sm(7), op=ALU.subtract), k5, ['pwr'])
            V('dve', lambda v: v.tensor_scalar_mul(pwr[:, 1].rearrange("p k g -> p (k g)"), pwr[:, 1].rearrange("p k g -> p (k g)"), sgn[:, 0:1]),
              ['pwr', 'sgn'], ['pwr'])
            V('act', lambda a: a.activation(out=rv(HTf[:, 0:4100]), in_=ones[:, 0:1].to_broadcast([128, 4100]), func=AF.Identity, scale=0.0),
              ['ones'], ['XA', 'XB'])
            V('act', lambda a: a.activation(out=rv(HTf[:, 4100:6148]), in_=ones[:, 0:1].to_broadcast([128, 2048]), func=AF.Identity, scale=0.0),
              ['ones'], ['ZP', 'ZC'])
            for sc in list(range(NP // NQ)) + ['s']:
                t0, W = (NP, NS) if sc == 's' else (sc * NQ, NQ)
                modulate('act', hin, 0, l, 1, t0, W, [], 'hin')
                for q in range(4):
                    pa, pk = proj(w_inu_d, q, lambda k, W=W: r32(hin)[:, k, 0:W], KC, W, ['hin'])
                    dst = Us[:, q, :] if sc == 's' else U[:, q, t0:t0 + W]
                    S.op('act', lambda a, pa=pa, dst=dst: a.activation(out=rv(dst), in_=pa, func=AF.Identity), reads=[pk],
                         writes=['Us' if sc == 's' else 'U'])
            CTS = [(c0, 512) for c0 in range(0, NP, 512)]
            for q in range(4):
                S.op('pe', lambda p, q=q: p.transpose(p_s[0][:, 0:128], BbT[:, 8 * q:8 * q + 8, :].rearrange("p g c -> p (g c)"), ident[:]),
                     reads=['BbT', 'ident'], writes=['p_s0'])
                for gl in range(8):
                    S.op('dve', lambda v, gl=gl: v.tensor_scalar_mul(rv(ZP[:, gl, :]), p_s[0][:, 0:128], blkm[:, gl:gl + 1]),
                         reads=['p_s0', 'blkm'], writes=['ZP'])
                    S.op('act', lambda a, gl=gl, q=q: a.activation(out=rv(ZC[:, gl, gl * 16:(gl + 1) * 16]), in_=CTt[:, 8 * q + gl, :], func=AF.Identity),
                         reads=['CTt'], writes=['ZC'])
                for gl in range(8):
                    g = 8 * q + gl
                    for ci_, (c0, cw) in enumerate(CTS):
                        S.op('pe', lambda p, gl=gl, q=q, c0=c0, cw=cw: p.matmul(p_s[0][:, 0:cw], rv(ZP[:, gl, :]), rv(U[:, q, c0:c0 + cw]),
                                                                                start=True, stop=True), reads=['ZP', 'U'], writes=['p_s0'])
                        S.op('act', lambda a, c0=c0, cw=cw: a.activation(out=rv(XA[:, 2 + c0:2 + c0 + cw]), in_=p_s[0][:, 0:cw], func=AF.Identity),
                             reads=['p_s0'], writes=['XA'])
                    src, dst, sk, dk_ = XA, XB, 'XA', 'XB'
                    for k in range(11):
                        d = 1 << k
                        rt = Rt[:, k % 2, :]
                        S.op('dve', lambda v, k=k, g=g: v.tensor_scalar_mul(Rtmp, ident[:], pwr[:, 0, k, g:g + 1]),
                             reads=['ident', 'pwr'], writes=['Rtmp'])
                        S.op('dve', lambda v, rt=rt, k=k, g=g: v.scalar_tensor_tensor(out=rv(rt), in0=Jm, scalar=pwr[:, 1, k, g:g + 1], in1=Rtmp,
                                                                                     op0=ALU.mult, op1=ALU.add),
                             reads=['Jm', 'pwr', 'Rtmp'], writes=['Rt'])
                        if k == 0:
                            S.op('pe', lambda p, rt=rt, g=g: p.matmul(p_y[0][:, 0:NS], rv(rt), rv(H0(g)), start=True, stop=False),
                                 reads=['Rt', 'H0'], writes=['p_y0'])
                            S.op('pe', lambda p, gl=gl, q=q: p.matmul(p_y[0][:, 0:NS], rv(ZP[:, gl, :]), rv(Us[:, q, :]), start=False, stop=True),
                                 reads=['ZP', 'Us'], writes=['p_y0'])
                            S.op('act', lambda a, g=g: a.activation(out=rv(Hn(g)), in_=p_y[0][:, 0:NS], func=AF.Identity),
                                 reads=['p_y0'], writes=['Hn'])
                        for ci_, (c0, cw) in enumerate(CTS):
                            pt, pk = (p_y[ci_ % 2], f'p_y{ci_ % 2}')
                            lo = max(d - c0, 0)
                            lo -= lo % 2
                            has_r = lo < cw
                            S.op('pe', lambda p, pt=pt, src=src, c0=c0, cw=cw, has_r=has_r: p.matmul(
                                pt[:, 0:cw], rv(identR), rv(src[:, 2 + c0:2 + c0 + cw]), start=True, stop=not has_r),
                                reads=['identR', sk], writes=[pk])
                            if has_r:
                                S.op('pe', lambda p, pt=pt, rt=rt, src=src, c0=c0, cw=cw, lo=lo, d=d: p.matmul(
                                    pt[:, lo:cw], rv(rt), rv(src[:, 2 + c0 + lo - d:2 + c0 + cw - d]), start=False, stop=True),
                                    reads=['Rt', sk], writes=[pk])
                            S.op('act' if ci_ % 2 else 'dve', (lambda a, pt=pt, dst=dst, c0=c0, cw=cw: a.activation(
                                out=rv(dst[:, 2 + c0:2 + c0 + cw]), in_=pt[:, 0:cw], func=AF.Identity)) if ci_ % 2 else
                                (lambda v, pt=pt, dst=dst, c0=c0, cw=cw: v.tensor_copy(rv(dst[:, 2 + c0:2 + c0 + cw]), pt[:, 0:cw])),
                                reads=[pk], writes=[dk_])
                        src, dst, sk, dk_ = dst, src, dk_, sk
                    for ci_, (c0, cw) in enumerate(CTS):
                        S.op('pe', lambda p, ci_=ci_, gl=gl, src=src, c0=c0, cw=cw: p.matmul(
                            PBK[ci_][0][:, 0:cw], rv(ZC[:, gl, :]), rv(src[:, 2 + c0:2 + c0 + cw]), start=(gl == 0), stop=(gl == 7)),
                            reads=['ZC', sk], writes=[PBK[ci_][1]])
                    S.op('pe', lambda p, gl=gl, g=g: p.matmul(p_s[1][:, 0:NS], rv(ZC[:, gl, :]), rv(Hn(g)), start=(gl == 0), stop=(gl == 7)),
                         reads=['ZC', 'Hn'], writes=['p_s1'])
                    S.op('dve', lambda v, g=g, src=src: v.tensor_copy(HS[:, g:g + 1], src[:, 2 + NP - 1:2 + NP]), reads=[sk], writes=['HS'])
                GC = float(np.sqrt(2.0 / np.pi))
                for (pap, pk, uu, ukey, cw) in [(PBK[i][0], PBK[i][1], U[:, q, c0:c0 + cw_], 'U', cw_) for i, (c0, cw_) in enumerate(CTS)] + \
                                               [(p_s[1], 'p_s1', Us[:, q, :], 'Us', NS)]:
                    y_, t_ = ybuf[:, 0, 0:cw], ybuf[:, 1, 0:cw]
                    S.op('dve', lambda v, pap=pap, uu=uu, y_=y_, cw=cw, q=q: v.scalar_tensor_tensor(
                        out=y_, in0=uu, scalar=s5d[:, q:q + 1], in1=pap[:, 0:cw], op0=ALU.mult, op1=ALU.add),
                        reads=[pk, ukey, 's5d'], writes=['ybuf'])
                    S.op('dve', lambda v, y_=y_, t_=t_: v.tensor_tensor(out=t_, in0=y_, in1=y_, op=ALU.mult), reads=['ybuf'], writes=['ybuf'])
                    S.op('dve', lambda v, t_=t_: v.tensor_scalar(out=t_, in0=t_, scalar1=0.044715, scalar2=1.0, op0=ALU.mult, op1=ALU.add),
                         reads=['ybuf'], writes=['ybuf'])
                    S.op('dve', lambda v, y_=y_, t_=t_: v.tensor_tensor(out=t_, in0=t_, in1=y_, op=ALU.mult), reads=['ybuf'], writes=['ybuf'])
                    S.op('act', lambda a, t_=t_: a.activation(out=t_, in_=t_, func=AF.Tanh, scale=GC), reads=['ybuf'], writes=['ybuf'])
                    S.op('dve', lambda v, t_=t_: v.tensor_scalar(out=t_, in0=t_, scalar1=1.0, scalar2=0.5, op0=ALU.add, op1=ALU.mult),
                         reads=['ybuf'], writes=['ybuf'])
                    S.op('dve', lambda v, y_=y_, t_=t_, uu=uu: v.tensor_tensor(out=rv(uu), in0=t_, in1=y_, op=ALU.mult),
                         reads=['ybuf'], writes=[ukey])
            for (zsrc, zkey, c0, cw) in [(U, 'U', c0, cw) for (c0, cw) in CTS] + [(Us, 'Us', 0, NS)]:
                for q2 in range(4):
                    pa, pk = proj(s5wg_d, q2, lambda k, zsrc=zsrc, c0=c0, cw=cw: rv(zsrc[:, k, c0:c0 + cw]), 4, cw, [zkey])
                    S.op('act', lambda a, pa=pa, q2=q2, cw=cw: a.activation(out=ybuf[:, q2, 0:cw], in_=pa, func=AF.Sigmoid,
                                                                           bias=s5bg[:, q2:q2 + 1], scale=1.0),
                         reads=[pk, 's5bg'], writes=['ybuf'])
                for q2 in range(4):
                    S.op('dve', lambda v, zsrc=zsrc, q2=q2, c0=c0, cw=cw: v.tensor_tensor(
                        out=rv(zsrc[:, q2, c0:c0 + cw]), in0=zsrc[:, q2, c0:c0 + cw], in1=ybuf[:, q2, 0:cw], op=ALU.mult),
                        reads=['ybuf', zkey], writes=[zkey])
            S.dma('sp', 'st', lambda q: q.dma_start(out=s5hp_d, in_=HS[:]), reads=['HS'])
            S.dma('sp', 'st', lambda q: q.dma_start(out=s5hs_d.rearrange('p (k a) b -> p k (a b)', a=4), in_=HnA), reads=['Hn'])
            fence()
            return U, Us


        def gla_sample(Us5):
            l = 0
            fence()
            P16 = {n: ztf[:, i * NS:(i + 1) * NS] for i, n in enumerate(
                ['qS0', 'qS1', 'kS0', 'kS1', 'eg0', 'eg1', 'gS0', 'gS1', 'gS2', 'gS3', 'vS0', 'vS1', 'vS2', 'vS3', 'tS0', 'tS1'])}
            k_tok, v_tok, Km = ztf[0:NS, 256:512], ztf[0:NS, 512:1024], ztf[0:NS, 1024:1152]
            o_tok, sq_tok, st16 = ztf[0:NS, 1280:1792], ztf[0:NS, 1792:2304], ztf[0:NS, 2304:2312]
            Sg = zsqf[:, 0:2048].rearrange("p (b v) -> p b v", b=NS)
            Qm = zsqf[:, 2048:2304].rearrange("p (b c) -> p b c", b=NS)
            gkS, ogs = HTf[:, 0:NS], HTf[:, 64:64 + 4 * NS].rearrange("p (h b) -> p h b", h=4)
            KS = list(P16) + ['k_tok', 'v_tok', 'Km', 'o_tok', 'sq_tok', 'st16', 'Sg', 'Qm', 'gkS', 'ogs']
            ALIAS.extend(k for k in KS if k not in ALIAS)
            fence()
            modulate('act', hin, 0, l, 1, NP, NS, [], 'hin')
            rh = lambda k: r32(hin)[:, k, 0:NS]
            for f in range(2):
                pa, pk = proj(w_in_d, f, rh, KC, NS, ['hin'])
                S.op('act', lambda a, pa=pa, f=f: a.activation(out=P16[f'qS{f}'], in_=pa, func=AF.Identity, scale=64 ** -0.5),
                     reads=[pk], writes=[f'qS{f}'])
                pa, pk = proj(w_in_d, 2 + f, rh, KC, NS, ['hin'])
                S.op('dve', lambda v, pa=pa, f=f: v.tensor_copy(P16[f'kS{f}'], pa), reads=[pk], writes=[f'kS{f}'])
            for f in range(4):
                pa, pk = proj(w_in_d, 4 + f, rh, KC, NS, ['hin'])
                S.op('dve', lambda v, pa=pa, f=f: v.tensor_copy(P16[f'vS{f}'], pa), reads=[pk], writes=[f'vS{f}'])
                pa, pk = proj(w_in_d, 8 + f, rh, KC, NS, ['hin'])
                S.op('act', lambda a, pa=pa, f=f: a.activation(out=P16[f'gS{f}'], in_=pa, func=AF.Silu), reads=[pk], writes=[f'gS{f}'])
            pa, pk = proj(w_in_d, 12, rh, KC, NS, ['hin'])
            S.op('act', lambda a, pa=pa: a.activation(out=rv(gkS[0:16, :]), in_=pa[0:16, :], func=AF.Identity), reads=[pk], writes=['gkS'])
            for f in range(2):
                S.op('pe', lambda p, f=f: p.matmul(p_y[f][:, 0:NS], r32(Wgk)[0:16, f * 128:(f + 1) * 128], rv(gkS[0:16, :]),
                                                   start=True, stop=True), reads=['Wgk', 'gkS'], writes=[f'p_y{f}'])
                S.op('act', lambda a, f=f: a.activation(out=P16[f'tS{f}'], in_=p_y[f][:, 0:NS], func=AF.Sigmoid, bias=bgk[:, f:f + 1], scale=1.0),
                     reads=[f'p_y{f}', 'bgk'], writes=[f'tS{f}'])
                S.op('act', lambda a, f=f: a.activation(out=P16[f'tS{f}'], in_=P16[f'tS{f}'], func=AF.Ln), reads=[f'tS{f}'], writes=[f'tS{f}'])
                S.op('act', lambda a, f=f: a.activation(out=P16[f'eg{f}'], in_=P16[f'tS{f}'], func=AF.Exp, scale=1.0 / 16),
                     reads=[f'tS{f}'], writes=[f'eg{f}'])
            for nm, nf, dst, dkey in (('kS', 2, k_tok, 'k_tok'), ('vS', 4, v_tok, 'v_tok')):
                for f in range(nf):
                    S.op('pe', lambda p, nm=nm, f=f: p.transpose(p_s[f % 2][0:NS, 0:128], P16[f'{nm}{f}'], ident[:]),
                         reads=[f'{nm}{f}', 'ident'], writes=[f'p_s{f % 2}'])
                    S.op('act', lambda a, f=f, dst=dst: a.activation(out=dst[:, f * 128:(f + 1) * 128], in_=p_s[f % 2][0:NS, 0:128], func=AF.Identity),
                         reads=[f'p_s{f % 2}'], writes=[dkey])
            for f in range(2):
                S.dma('sp', 'ldg', lambda q, f=f: q.dma_start(out=Sg, in_=glas0_d[:, :, f, :]), writes=['Sg'])
                S.op('pool', lambda g: g.memset(Qm, 0.0), writes=['Qm'])
                for b in range(NS):
                    S.op('dve', lambda v, f=f, b=b: v.tensor_copy(Qm[:, b, b:b + 1], P16[f'qS{f}'][:, b:b + 1]), reads=[f'qS{f}', 'Qm'], writes=['Qm'])
                for hb in range(2):
                    h = 2 * f + hb
                    rows = slice(hb * 64, (hb + 1) * 64)
                    for b in range(NS):
                        S.op('dve', lambda v, f=f, b=b: v.tensor_scalar_mul(Km, k_tok[:, f * 128:(f + 1) * 128], ident[0:NS, b:b + 1]),
                             reads=['k_tok', 'ident'], writes=['Km'])
                        S.op('pe', lambda p, h=h: p.matmul(p_y[0][:, 0:128], Km, v_tok[:, h * 128:(h + 1) * 128], start=True, stop=True),
                             reads=['Km', 'v_tok'], writes=['p_y0'])
                        S.op('dve', lambda v, rows=rows, b=b, f=f: v.scalar_tensor_tensor(
                            out=Sg[rows, b, :], in0=Sg[rows, b, :], scalar=P16[f'eg{f}'][rows, b:b + 1], in1=p_y[0][rows, 0:128],
                            op0=ALU.mult, op1=ALU.add), reads=['Sg', f'eg{f}', 'p_y0'], writes=['Sg'])
                        S.op('pe', lambda p, rows=rows, b=b: p.matmul(p_y[1][0:NS, 0:128], Qm[rows, b, :], Sg[rows, b, :],
                                                                      start=(b == 0), stop=(b == NS - 1)), reads=['Qm', 'Sg'], writes=['p_y1'])
                    S.op('act', lambda a, h=h: a.activation(out=o_tok[:, h * 128:(h + 1) * 128], in_=p_y[1][0:NS, 0:128], func=AF.Identity),
                         reads=['p_y1'], writes=['o_tok'])
                S.dma('sp', 'st', lambda q, f=f: q.dma_start(out=glas_d[:, :, f, :], in_=Sg), reads=['Sg'])
            o3, q3 = o_tok.rearrange("p (h v) -> p h v", h=4), sq_tok.rearrange("p (h v) -> p h v", h=4)
            S.op('dve', lambda v: v.tensor_tensor(out=sq_tok, in0=o_tok, in1=o_tok, op=ALU.mult), reads=['o_tok'], writes=['sq_tok'])
            S.op('dve', lambda v: v.tensor_reduce(out=st16[:, 0:4], in_=q3, op=ALU.add, axis=mybir.AxisListType.X), reads=['sq_tok'], writes=['st16'])
            S.op('dve', lambda v: v.tensor_scalar(out=st16[:, 0:4], in0=st16[:, 0:4], scalar1=1.0 / 128, scalar2=1e-5, op0=ALU.mult, op1=ALU.add),
                 reads=['st16'], writes=['st16'])
            S.op('act', lambda a: a.activation(out=st16[:, 0:4], in_=st16[:, 0:4], func=AF.Sqrt), reads=['st16'], writes=['st16'])
            S.op('dve', lambda v: v.reciprocal(st16[:, 0:4], st16[:, 0:4]), reads=['st16'], writes=['st16'])
            S.op('dve', lambda v: v.tensor_tensor(out=o3, in0=o3, in1=st16[:, 0:4].unsqueeze(2).to_broadcast([NS, 4, 128]), op=ALU.mult),
                 reads=['o_tok', 'st16'], writes=['o_tok'])
            for h in range(4):
                S.op('pe', lambda p, h=h: p.transpose(p_s[h % 2][:, 0:NS], o_tok[:, h * 128:(h + 1) * 128], ident[0:NS, 0:NS]),
                     reads=['o_tok', 'ident'], writes=[f'p_s{h % 2}'])
                S.op('dve', lambda v, h=h: v.scalar_tensor_tensor(out=rv(ogs[:, h, :]), in0=p_s[h % 2][:, 0:NS], scalar=gng[:, 0:1],
                                                                  in1=P16[f'gS{h}'], op0=ALU.mult, op1=ALU.mult),
                     reads=[f'p_s{h % 2}', 'gng', f'gS{h}'], writes=['ogs'])
            fence()
            for dk in range(KC):
                pa, pk = proj(w_out_d, dk, lambda k: rv(ogs[:, k, :]) if k < 4 else rv(Us5[:, k - 4, :]), KC, NS, ['ogs', 'Us'])
                S.op('act', lambda a, pa=pa, dk=dk: a.activation(out=zsq[:, dk, 0:NS], in_=pa, func=AF.Identity), reads=[pk], writes=['zsq_y', 'zsq'])
            post_norm_tile_from_sbuf(l, 1, NP, 1.0, NS)
            fence()

        def mix0_sublayer():
            l = 0
            U5, Us5 = s5_phase()
            fence()
            for f in range(2):
                S.op('act', lambda a, f=f: a.activation(out=rv(Sst[f]), in_=ones[:], func=AF.Identity, scale=0.0),
                     reads=['ones'], writes=[f'Sst{f}'])
            for sc in range(NP // NQ):
                t0 = sc * NQ
                modulate('act', hin, 0, l, 1, t0, NQ, [], 'hin')
                rh = lambda k: r32(hin)[:, k, 0:NQ]
                for f in range(2):
                    pa, pk = proj(w_in_d, f, rh, KC, NQ, ['hin'])
                    S.op('act', lambda a, pa=pa, f=f: a.activation(out=T[f'qT{f}'], in_=pa, func=AF.Identity,
                                                                   scale=64 ** -0.5), reads=[pk], writes=[f'qT{f}'])
                    pa, pk = proj(w_in_d, 2 + f, rh, KC, NQ, ['hin'])
                    S.op('dve', lambda v, pa=pa, f=f: v.tensor_copy(T[f'kT{f}'], pa), reads=[pk], writes=[f'kT{f}'])
                for f in range(4):
                    pa, pk = proj(w_in_d, 4 + f, rh, KC, NQ, ['hin'])
                    S.op('dve', lambda v, pa=pa, f=f: v.tensor_copy(T[f'vT{f}'], pa), reads=[pk], writes=[f'vT{f}'])
                    pa, pk = proj(w_in_d, 8 + f, rh, KC, NQ, ['hin'])
                    S.op('act', lambda a, pa=pa, f=f: a.activation(out=T[f'gT{f}'], in_=pa, func=AF.Silu),
                         reads=[pk], writes=[f'gT{f}'])
                pa, pk = proj(w_in_d, 12, rh, KC, NQ, ['hin'])
                S.op('act', lambda a, pa=pa: a.activation(out=rv(T['gkl'][0:16, :]), in_=pa[0:16, :], func=AF.Identity),
                     reads=[pk], writes=['gkl'])
                for f in range(2):
                    S.op('pe', lambda p, f=f: p.matmul(p_y[f][:, 0:NQ], r32(Wgk)[0:16, f * 128:(f + 1) * 128],
                                                       rv(T['gkl'][0:16, :]), start=True, stop=True),
                         reads=['Wgk', 'gkl'], writes=[f'p_y{f}'])
                    S.op('act', lambda a, f=f: a.activation(out=T[f'tmp{f}'], in_=p_y[f][:, 0:NQ], func=AF.Sigmoid,
                                                            bias=bgk[:, f:f + 1], scale=1.0),
                         reads=[f'p_y{f}', 'bgk'], writes=[f'tmp{f}'])
                    S.op('act', lambda a, f=f: a.activation(out=T[f'tmp{f}'], in_=T[f'tmp{f}'], func=AF.Ln),
                         reads=[f'tmp{f}'], writes=[f'tmp{f}'])
                    S.op('dve', lambda v, f=f: v.tensor_tensor_scan(out=T[f'L{f}'], data0=rstm[:], data1=T[f'tmp{f}'],
                                                                    initial=0.0, op0=ALU.mult, op1=ALU.add),
                         reads=['rstm', f'tmp{f}'], writes=[f'L{f}'])
                    last = T[f'L{f}'].rearrange("p (c t) -> p c t", c=2)[:, :, 127]
                    S.op('dve', lambda v, f=f, last=last: v.tensor_scalar_mul(bl[:, f, :], last, 1.0 / 16),
                         reads=[f'L{f}'], writes=['bl'])
                    S.op('act', lambda a, f=f: a.activation(out=dec[:, f, :], in_=bl[:, f, :], func=AF.Exp),
                         reads=['bl'], writes=['dec'])
                    for nm, src, sgn in (('qin', 'qT', 1.0), ('kin', 'kT', -1.0)):
                        S.op('act', lambda a, f=f, sgn=sgn: a.activation(out=T[f'tmp{f}'], in_=T[f'L{f}'], func=AF.Exp,
                                                                         scale=sgn / 16), reads=[f'L{f}'], writes=[f'tmp{f}'])
                        S.op('dve', lambda v, f=f, nm=nm, src=src: v.tensor_tensor(
                            out=rv(T[f'{nm}{f}']), in0=T[f'{src}{f}'], in1=T[f'tmp{f}'], op=ALU.mult),
                            reads=[f'{src}{f}', f'tmp{f}'], writes=[f'{nm}{f}'])
                    for c in range(2):
                        S.op('act', lambda a, f=f, c=c: a.activation(
                            out=T[f'tmp{f}'][:, c * 128:(c + 1) * 128], in_=T[f'L{f}'][:, c * 128:(c + 1) * 128],
                            func=AF.Exp, scale=-1.0 / 16, bias=bl[:, f, c:c + 1]),
                            reads=[f'L{f}', 'bl'], writes=[f'tmp{f}'])
                    S.op('dve', lambda v, f=f: v.tensor_tensor(out=T[f'kst{f}'], in0=T[f'kT{f}'], in1=T[f'tmp{f}'],
                                                               op=ALU.mult), reads=[f'kT{f}', f'tmp{f}'], writes=[f'kst{f}'])
                for c in range(2):
                    cs = slice(c * 128, (c + 1) * 128)
                    for nm, nf, dst, dk_ in (('vT', 4, vtok, 'vtok'), ('kst', 2, ksttok, 'ksttok')):
                        for f in range(nf):
                            S.op('pe', lambda p, nm=nm, f=f, cs=cs: p.transpose(p_s[f % 2][:, 0:128], T[f'{nm}{f}'][:, cs], ident[:]),
                                 reads=[f'{nm}{f}', 'ident'], writes=[f'p_s{f % 2}'])
                            S.op('act', lambda a, f=f, c=c, dst=dst: a.activation(
                                out=rv(dst[:, c, f * 128:(f + 1) * 128]), in_=p_s[f % 2][:, 0:128], func=AF.Identity),
                                reads=[f'p_s{f % 2}'], writes=[dk_])
                def gla_chain(c, h):
                    cs = slice(c * 128, (c + 1) * 128)
                    f, r0 = h // 2, (h % 2) * 64
                    rows = slice(r0, r0 + 64)
                    w_ = h // 2
                    Pm_ = HTf[:, o0 + 1664 * w_:o0 + 1664 * w_ + 128] if w_ else Pm
                    sq_ = zsqf[:, 8 * NQ + 256 * w_:8 * NQ + 256 * w_ + 128]
                    rs_ = zsqf[:, 8 * NQ + 256 * w_ + 128:8 * NQ + 256 * w_ + 256]
                    kP, kS, kR = f'Pm{w_}', f'sqt{w_}', f'rst{w_}'
                    (bS, kbS), (bO, kbO), (bT, kbT), (bU, kbU) = ((p_y[0], 'p_y0'), (p_y[1], 'p_y1'), (p_s[0], 'p_s0'), (p_s[1], 'p_s1')) if w_ == 0 else \
                                                                 ((p_g[0], 'p_g0'), (p_g[1], 'p_g1'), (p_u[0], 'p_u0'), (p_u[1], 'p_u1'))
                    S.op('pe', lambda p: p.matmul(bS[:, 0:128], rv(T[f'kin{f}'][rows, cs]), rv(T[f'qin{f}'][rows, cs]), start=True, stop=True),
                         reads=[f'kin{f}', f'qin{f}'], writes=[kbS])
                    yield
                    S.op('dve', lambda v: v.tensor_tensor(out=rv(Pm_), in0=bS[:, 0:128], in1=mask2[:, 128:256], op=ALU.mult),
                         reads=[kbS, 'mask2'], writes=[kP])
                    yield
                    S.op('pe', lambda p: p.matmul(bO[:, 0:128], rv(vtok[:, c, h * 128:(h + 1) * 128]), rv(Pm_), start=True, stop=False),
                         reads=['vtok', kP], writes=[kbO])
                    S.op('pe', lambda p: p.matmul(bO[:, 0:128], rv(Sst[f][rows, :]), rv(T[f'qin{f}'][rows, cs]), start=False, stop=True),
                         reads=[f'Sst{f}', f'qin{f}'], writes=[kbO])
                    S.op('pe', lambda p: p.matmul(bU[:, 0:128], rv(ksttok[:, c, f * 128:(f + 1) * 128]), rv(vtok[:, c, h * 128:(h + 1) * 128]),
                                                  start=True, stop=True), reads=['ksttok', 'vtok'], writes=[kbU])
                    yield
                    S.op('act', lambda a: a.activation(out=sq_, in_=bO[:, 0:128], func=AF.Square), reads=[kbO], writes=[kS])
                    S.op('dve', lambda v: v.scalar_tensor_tensor(out=rv(Sst[f][rows, :]), in0=Sst[f][rows, :], scalar=dec[rows, f, c:c + 1],
                                                                 in1=bU[rows, 0:128], op0=ALU.mult, op1=ALU.add),
                         reads=[f'Sst{f}', 'dec', kbU, kbO], writes=[f'Sst{f}'])
                    yield
                    S.op('pe', lambda p: p.matmul(bT[:, 0:128], ones[:], sq_, start=True, stop=True), reads=['ones', kS], writes=[kbT])
                    yield
                    S.op('dve', lambda v: v.tensor_scalar(out=rs_, in0=bT[:, 0:128], scalar1=1.0 / 128, scalar2=1e-5, op0=ALU.mult, op1=ALU.add),
                         reads=[kbT], writes=[kR])
                    yield
                    S.op('act', lambda a: a.activation(out=rs_, in_=rs_, func=AF.Sqrt), reads=[kR], writes=[kR])
                    yield
                    S.op('dve', lambda v: v.reciprocal(rs_, rs_), reads=[kR], writes=[kR])
                    S.op('dve', lambda v: v.scalar_tensor_tensor(out=sq_, in0=bO[:, 0:128], scalar=gng[:, 0:1], in1=rs_, op0=ALU.mult, op1=ALU.mult),
                         reads=[kbO, 'gng', kR, kS], writes=[kS])
                    S.op('dve', lambda v: v.tensor_tensor(out=rv(T[f'og{h}'][:, cs]), in0=sq_, in1=T[f'gT{h}'][:, cs], op=ALU.mult),
                         reads=[kS, f'gT{h}'], writes=[f'og{h}'])
                    yield

                def run_il(gens):
                    gens = list(gens)
                    while gens:
                        for g_ in list(gens):
                            try:
                                next(g_)
                            except StopIteration:
                                gens.remove(g_)
                for c in range(2):
                    run_il([gla_chain(c, 0), gla_chain(c, 2)])
                    run_il([gla_chain(c, 1), gla_chain(c, 3)])
                fence(light=True)
                for dk in range(KC):
                    pa, pk = proj(w_out_d, dk, lambda k: rv(T[f'og{k}']) if k < 4 else rv(U5[:, k - 4, t0:t0 + NQ]), KC, NQ,
                                  [f'og{k}' for k in range(4)] + ['U'])
                    S.op('act', lambda a, pa=pa, dk=dk: a.activation(out=zsq[:, dk, 0:NQ], in_=pa, func=AF.Identity),
                         reads=[pk, 'vtok', 'ksttok'], writes=['zsq_y', 'zsq'])
                post_norm_tile_from_sbuf(l, 1, t0, 1.0, NQ)
                fence(light=True)
            gla_sample(Us5)
            for f in range(2):
                S.dma('sp', 'st', lambda q, f=f: q.dma_start(
                    out=st_out['gla_p'][0:1, f * 16384:(f + 1) * 16384].rearrange("o (r v) -> (o r) v", v=128),
                    in_=Sst[f]), reads=[f'Sst{f}'])
            fence()


        def projp(parts, N, rkeys):
            sl = wslot[0] % 4; wslot[0] += 1
            buf = wgu[sl // 2][:, sl % 2]
            key = f"wslot{sl}"
            for i, (src, _) in enumerate(parts):
                kr, nc_ = src.shape
                S.dma('pool', key, lambda q, i=i, src=src, kr=kr, nc_=nc_: q.dma_start(out=rv(buf[0:kr, i, 0:nc_]), in_=src),
                      writes=[key])
            pt, pk = PBK[pslot[0] % 4]; pslot[0] += 1
            for i, (src, rhs) in enumerate(parts):
                kr = src.shape[0]
                S.op('pe', lambda p, i=i, kr=kr, rhs=rhs: p.matmul(pt[:, 0:N], rv(buf[0:kr, i, :]), rhs,
                                                                   start=(i == 0), stop=(i == len(parts) - 1)),
                     reads=[key] + rkeys, writes=[pk])
            return pt[:, 0:N], pk

        def mix1_sublayer():
            l = 1
            hinf = hin[:].rearrange("p a b -> p (a b)")
            XV = HTf[:, 0:4 * KC * NQ].rearrange("p (v k n) -> p v k n", v=4, k=KC)
            o1 = 4 * KC * NQ
            R1 = ['lw', 'la', 'lg0', 'lg1', 'Bt', 'Kt']
            TR = {n: HTf[:, o1 + i * NQ:o1 + (i + 1) * NQ] for i, n in enumerate(R1)}
            o2 = o1 + len(R1) * NQ
            AR = HTf[:, o2:o2 + 2 * NQ].rearrange("p (a n) -> p a n", a=2)
            o3 = o2 + 2 * NQ
            Vtok, Bhat, Khat = (HTf[:, o3 + i * 128:o3 + (i + 1) * 128] for i in range(3))
            o4 = o3 + 384
            AabAbr, AakAkr = HTf[:, o4:o4 + 256], HTf[:, o4 + 256:o4 + 512]
            Xm, XTm, Tinv = (HTf[:, o4 + 512 + i * 128:o4 + 512 + (i + 1) * 128] for i in range(3))
            RH, UT = HTf[:, o4 + 896:o4 + 960], HTf[:, o4 + 960:o4 + 1024]
            assert o4 + 4096 <= HTf.shape[1]
            og = [hin[:, f, 256:512] for f in range(KC)]
            PA = ['rT', 'kT', 'vT', 'lgw', 'Lw', 'asg', 'kk', 'km', 'gt', 'tmp']
            TP = {n: ztf[:, i * NQ:(i + 1) * NQ] for i, n in enumerate(PA)}
            xx = zsqf[:, 0:KC * NQ].rearrange("p (k n) -> p k n", k=KC)
            tm2, ytok = zsqf[:, KC * NQ:KC * NQ + NQ], zsqf[:, KC * NQ + NQ:KC * NQ + 2 * NQ]
            KEYS1 = R1 + PA + ['XV', 'AR', 'Vtok', 'Bhat', 'Khat', 'og', 'xx', 'tm2', 'ytok', 'Swk0', 'Swk1'] + \
                    [f'{n}{c_}{hb}' for c_ in range(2) for hb in range(2) for n in ('Aab', 'Aak', 'X', 'XT', 'T', 'RH', 'UT')]
            ALIAS.extend(k for k in KEYS1 if k not in ALIAS)
            fence()
            for f in range(KC):
                S.op('act', lambda a, f=f: a.activation(out=r32(Swk)[:, f, :], in_=ones[:, 0:64], func=AF.Identity, scale=0.0),
                     reads=['ones'], writes=['Swk', 'Swk0', 'Swk1'])
            S.op('pool', lambda g: g.memset(hprev[:], 0.0), writes=['hprev'])
            for k in range(KC):
                S.op('act', lambda a, k=k: a.activation(out=shp[:, k:k + 1], in_=xT[:, k, NP - 1:NP], func=AF.Identity,
                                                        scale=mods[l][:, 4 * KC + k, 16:17], bias=mods[l][:, 3 * KC + k, 16:17]),
                     reads=['xT', f"mods{l}"], writes=['shp'])
            vec = lambda i, f: rvec[:, i, f:f + 1]
            EXPM05 = float(np.exp(-0.5))
            for sc in range(NP // NQ):
                t0 = sc * NQ
                modulate('act', hin, 0, l, 1, t0, NQ, [], 'hin')
                S.op('dve', lambda v: v.tensor_tensor(out=xx[:, :, 1:NQ], in0=hin[:, :, 0:NQ - 1], in1=hin[:, :, 1:NQ],
                                                      op=ALU.subtract), reads=['hin'], writes=['xx'])
                S.op('dve', lambda v: v.tensor_tensor(out=xx[:, :, 0:1], in0=hprev[:].unsqueeze(2), in1=hin[:, :, 0:1],
                                                      op=ALU.subtract), reads=['hin', 'hprev', 'xx'], writes=['xx'])
                S.op('dve', lambda v: v.tensor_copy(hprev[:].unsqueeze(2), hin[:, :, NQ - 1:NQ]), reads=['hin', 'xx'],
                     writes=['hprev'])

                def xvar(slot, i):
                    for k in range(KC):
                        S.op('dve', lambda v, k=k: v.scalar_tensor_tensor(
                            out=rv(XV[:, slot, k, :]), in0=xx[:, k, :], scalar=rmu[:, i, k:k + 1], in1=hin[:, k, 0:NQ],
                            op0=ALU.mult, op1=ALU.add), reads=['xx', 'hin', 'rmu'], writes=['XV'])
                xs = lambda slot: (lambda k: rv(XV[:, slot, k, :]))
                xvar(0, 0); xvar(1, 2); xvar(2, 3)
                xvar(3, 1)
                pa, pk = proj(rw['w1T'], 0, xs(3), KC, NQ, ['XV'])
                S.op('act', lambda a, pa=pa: a.activation(out=rv(TR['lw'][0:64, :]), in_=pa[0:64, :], func=AF.Tanh),
                     reads=[pk], writes=['lw'])
                xvar(3, 4)
                pa, pk = proj(rw['a1T'], 0, xs(3), KC, NQ, ['XV'])
                S.op('act', lambda a, pa=pa: a.activation(out=rv(TR['la'][0:64, :]), in_=pa[0:64, :], func=AF.Identity),
                     reads=[pk], writes=['la'])
                xvar(3, 5)
                pa, pk = proj(rw['g1T'], 0, xs(3), KC, NQ, ['XV'])
                S.op('act', lambda a, pa=pa: a.activation(out=rv(TR['lg0']), in_=pa, func=AF.Sigmoid), reads=[pk], writes=['lg0'])
                pa, pk = proj(rw['g1T'], 1, xs(3), KC, NQ, ['XV'])
                S.op('act', lambda a, pa=pa: a.activation(out=rv(TR['lg1'][0:32, :]), in_=pa[0:32, :], func=AF.Sigmoid),
                     reads=[pk], writes=['lg1'])
                for f in range(KC):
                    fc = f * 128
                    for nm, wn, sl_ in (('rT', 'w_r', 0), ('kT', 'w_k', 1), ('vT', 'w_v', 2)):
                        pa, pk = proj(rw[wn + 'T'], f, xs(sl_), KC, NQ, ['XV'])
                        S.op('act', lambda a, pa=pa, nm=nm: a.activation(out=TP[nm], in_=pa, func=AF.Identity),
                             reads=[pk], writes=[nm])
                    pa, pk = projp([(rw['w2'][0:64, fc:fc + 128], rv(TR['lw'][0:64, :]))], NQ, ['lw'])
                    S.op('act', lambda a, pa=pa, f=f: a.activation(out=TP['lgw'], in_=pa, func=AF.Sigmoid, bias=vec(0, f), scale=1.0),
                         reads=[pk, 'rvec'], writes=['lgw'])
                    S.op('dve', lambda v: v.tensor_scalar_mul(TP['lgw'], TP['lgw'], -EXPM05), reads=['lgw'], writes=['lgw'])
                    S.op('dve', lambda v: v.tensor_tensor_scan(out=TP['Lw'], data0=rstm[:], data1=TP['lgw'], initial=0.0,
                                                               op0=ALU.mult, op1=ALU.add), reads=['rstm', 'lgw'], writes=['Lw'])
                    S.op('dve', lambda v: v.tensor_copy(llast[:], TP['Lw'].rearrange("p (c t) -> p c t", c=2)[:, :, 127]),
                         reads=['Lw'], writes=['llast'])
                    S.op('act', lambda a: a.activation(out=pcl[:], in_=llast[:], func=AF.Exp), reads=['llast'], writes=['pcl'])
                    pa, pk = projp([(rw['a2'][0:64, fc:fc + 128], rv(TR['la'][0:64, :]))], NQ, ['la'])
                    S.op('act', lambda a, pa=pa, f=f: a.activation(out=TP['asg'], in_=pa, func=AF.Sigmoid, bias=vec(1, f), scale=1.0),
                         reads=[pk, 'rvec'], writes=['asg'])
                    pa, pk = projp([(rw['g2'][0:128, fc:fc + 128], rv(TR['lg0'])), (rw['g2'][128:160, fc:fc + 128], rv(TR['lg1'][0:32, :]))],
                                   NQ, ['lg0', 'lg1'])
                    S.op('act', lambda a, pa=pa: a.activation(out=TP['gt'], in_=pa, func=AF.Identity), reads=[pk], writes=['gt'])
                    S.op('dve', lambda v, f=f: v.tensor_scalar_mul(TP['kk'], TP['kT'], vec(2, f)), reads=['kT', 'rvec'], writes=['kk'])
                    S.op('act', lambda a: a.activation(out=TP['tmp'], in_=TP['kk'], func=AF.Square), reads=['kk'], writes=['tmp'])
                    S.op('pe', lambda p: p.matmul(p_s[0][:, 0:NQ], bones[:], TP['tmp'], start=True, stop=True),
                         reads=['bones', 'tmp'], writes=['p_s0'])
                    S.op('act', lambda a: a.activation(out=TP['tmp'], in_=p_s[0][:, 0:NQ], func=AF.Sqrt), reads=['p_s0'], writes=['tmp'])
                    S.op('dve', lambda v: v.tensor_scalar_max(TP['tmp'], TP['tmp'], 1e-12), reads=['tmp'], writes=['tmp'])
                    S.op('dve', lambda v: v.reciprocal(TP['tmp'], TP['tmp']), reads=['tmp'], writes=['tmp'])
                    S.op('dve', lambda v: v.tensor_tensor(out=TP['kk'], in0=TP['kk'], in1=TP['tmp'], op=ALU.mult),
                         reads=['kk', 'tmp'], writes=['kk'])
                    S.op('dve', lambda v, f=f: v.tensor_scalar(out=TP['km'], in0=TP['asg'], scalar1=-1.0, scalar2=vec(3, f),
                                                               op0=ALU.add, op1=ALU.mult), reads=['asg', 'rvec'], writes=['km'])
                    S.op('dve', lambda v: v.scalar_tensor_tensor(out=TP['km'], in0=TP['km'], scalar=1.0, in1=TP['kT'],
                                                                 op0=ALU.add, op1=ALU.mult), reads=['km', 'kT'], writes=['km'])
                    S.op('dve', lambda v: v.tensor_tensor(out=TP['tmp'], in0=TP['Lw'], in1=TP['lgw'], op=ALU.subtract),
                         reads=['Lw', 'lgw'], writes=['tmp'])
                    S.op('act', lambda a: a.activation(out=TP['tmp'], in_=TP['tmp'], func=AF.Exp), reads=['tmp'], writes=['tmp'])
                    S.op('dve', lambda v: v.scalar_tensor_tensor(out=rv(AR[:, 0, :]), in0=TP['kk'], scalar=-1.0, in1=TP['tmp'],
                                                                 op0=ALU.mult, op1=ALU.mult), reads=['kk', 'tmp'], writes=['AR'])
                    S.op('act', lambda a: a.activation(out=TP['tmp'], in_=TP['Lw'], func=AF.Exp), reads=['Lw', 'AR'], writes=['tmp'])
                    S.op('dve', lambda v: v.tensor_tensor(out=rv(AR[:, 1, :]), in0=TP['rT'], in1=TP['tmp'], op=ALU.mult),
                         reads=['rT', 'tmp'], writes=['AR'])
                    S.op('act', lambda a: a.activation(out=TP['tmp'], in_=TP['Lw'], func=AF.Exp, scale=-1.0), reads=['Lw', 'AR'], writes=['tmp'])
                    S.op('dve', lambda v: v.tensor_tensor(out=tm2, in0=TP['kk'], in1=TP['asg'], op=ALU.mult),
                         reads=['kk', 'asg'], writes=['tm2'])
                    S.op('dve', lambda v: v.tensor_tensor(out=rv(TR['Bt']), in0=tm2, in1=TP['tmp'], op=ALU.mult),
                         reads=['tm2', 'tmp'], writes=['Bt'])
                    S.op('dve', lambda v: v.tensor_tensor(out=rv(TR['Kt']), in0=TP['km'], in1=TP['tmp'], op=ALU.mult),
                         reads=['km', 'tmp'], writes=['Kt'])
                    S.op('dve', lambda v, f=f: v.scalar_tensor_tensor(out=TP['asg'], in0=TP['rT'], scalar=vec(4, f), in1=TP['km'],
                                                                      op0=ALU.mult, op1=ALU.mult),
                         reads=['rT', 'km', 'rvec', 'tm2', 'asg'], writes=['asg'])
                    S.op('pe', lambda p: p.matmul(p_s[1][:, 0:NQ], bones[:], TP['asg'], start=True, stop=True),
                         reads=['bones', 'asg'], writes=['p_s1'])
                    S.op('dve', lambda v: v.tensor_tensor(out=TP['rT'], in0=p_s[1][:, 0:NQ], in1=TP['vT'], op=ALU.mult),
                         reads=['p_s1', 'vT', 'AR', 'asg'], writes=['rT'])
                    INVB = {(0, 0): ((p_y[0], 'p_y0'), (p_y[1], 'p_y1')), (0, 1): ((p_g[0], 'p_g0'), (p_g[1], 'p_g1')),
                            (1, 0): ((p_u[0], 'p_u0'), (p_u[1], 'p_u1')), (1, 1): ((p_s[0], 'p_s0'), (p_s[1], 'p_s1'))}

                    def scratch(c, hb):
                        ob = o4 + (2 * c + hb) * 1024
                        d_ = dict(Aab=HTf[:, ob:ob + 256], Aak=HTf[:, ob + 256:ob + 512], RH=HTf[:, ob + 896:ob + 960], UT=HTf[:, ob + 960:ob + 1024])
                        d_['X'], d_['XT'], d_['T'] = (HTf[:, ob + 512 + i * 128:ob + 512 + (i + 1) * 128] for i in range(3))
                        d_['k'] = {n: f'{n}{c}{hb}' for n in ('Aab', 'Aak', 'X', 'XT', 'T', 'RH', 'UT')}
                        return d_

                    def inv_chain(c, hb, f=f):
                        cs = slice(c * 128, (c + 1) * 128)
                        rows = slice(hb * 64, (hb + 1) * 64)
                        arc = AR[rows, :, cs]
                        sc_ = scratch(c, hb); K_ = sc_['k']
                        (bA, kA), (bB, kB) = INVB[(c, hb)]
                        for lhs, lk, dst, dk_ in ((TR['Bt'], 'Bt', sc_['Aab'], K_['Aab']), (TR['Kt'], 'Kt', sc_['Aak'], K_['Aak'])):
                            S.op('pe', lambda p, lhs=lhs: p.matmul(bA[:, 0:256], rv(lhs[rows, cs]), rv(arc), start=True, stop=True),
                                 reads=[lk, 'AR'], writes=[kA])
                            S.op('dve', lambda v, dst=dst: v.tensor_tensor(out=rv(dst), in0=bA[:, 0:256], in1=mask2[:], op=ALU.mult),
                                 reads=[kA, 'mask2'], writes=[dk_])
                            yield
                        S.op('pe', lambda p: p.matmul(bB[:, 0:128], rv(AR[rows, 0, cs]), rv(TR['Bt'][rows, cs]), start=True, stop=True),
                             reads=['AR', 'Bt'], writes=[kB])
                        S.op('dve', lambda v: v.tensor_tensor(out=rv(sc_['XT']), in0=bB[:, 0:128], in1=maskLT[:], op=ALU.mult),
                             reads=[kB, 'maskLT'], writes=[K_['XT']])
                        S.op('act', lambda a: a.activation(out=rv(sc_['X']), in_=sc_['Aab'][:, 0:128], func=AF.Identity), reads=[K_['Aab']], writes=[K_['X']])
                        S.op('dve', lambda v: v.tensor_tensor(out=rv(sc_['T']), in0=sc_['Aab'][:, 0:128], in1=ident[:], op=ALU.add),
                             reads=[K_['Aab'], 'ident'], writes=[K_['T']])
                        yield
                        for lv in range(6):
                            S.op('pe', lambda p: p.matmul(bA[:, 0:128], rv(sc_['XT']), rv(sc_['X']), start=True, stop=True), reads=[K_['XT'], K_['X']], writes=[kA])
                            S.op('pe', lambda p: p.matmul(bB[:, 0:128], rv(sc_['X']), rv(sc_['XT']), start=True, stop=True), reads=[K_['XT'], K_['X']], writes=[kB])
                            yield
                            S.op('act', lambda a: a.activation(out=rv(sc_['X']), in_=bA[:, 0:128], func=AF.Identity), reads=[kA], writes=[K_['X']])
                            S.op('dve', lambda v: v.tensor_copy(rv(sc_['XT']), bB[:, 0:128]), reads=[kB], writes=[K_['XT']])
                            yield
                            S.op('pe', lambda p: p.matmul(bA[:, 0:128], rv(sc_['XT']), rv(sc_['T']), start=True, stop=True), reads=[K_['XT'], K_['T']], writes=[kA])
                            yield
                            S.op('dve', lambda v: v.tensor_tensor(out=rv(sc_['T']), in0=bA[:, 0:128], in1=sc_['T'], op=ALU.add), reads=[kA, K_['T']], writes=[K_['T']])
                            yield

                    def run_interleaved(gens):
                        gens = list(gens)
                        while gens:
                            for g_ in list(gens):
                                try:
                                    next(g_)
                                except StopIteration:
                                    gens.remove(g_)
                    run_interleaved([inv_chain(c_, hb_) for c_ in range(2) for hb_ in range(2)])
                    for c in range(2):
                        cs = slice(c * 128, (c + 1) * 128)
                        S.op('act', lambda a, c=c, cs=cs: a.activation(out=TP['tmp'][:, cs], in_=TP['Lw'][:, cs], func=AF.Exp,
                                                                       scale=-1.0, bias=llast[:, c:c + 1]),
                             reads=['Lw', 'llast', 'Bt', 'Kt'], writes=['tmp'])
                        S.op('dve', lambda v, cs=cs: v.tensor_tensor(out=tm2[:, cs], in0=tm2[:, cs], in1=TP['tmp'][:, cs], op=ALU.mult),
                             reads=['tm2', 'tmp'], writes=['tm2'])
                        S.op('dve', lambda v, cs=cs: v.tensor_tensor(out=TP['tmp'][:, cs], in0=TP['km'][:, cs], in1=TP['tmp'][:, cs],
                                                                     op=ALU.mult), reads=['km', 'tmp'], writes=['tmp'])
                        for src, skey, dst, dkey in ((TP['vT'], 'vT', Vtok, 'Vtok'), (tm2, 'tm2', Bhat, 'Bhat'), (TP['tmp'], 'tmp', Khat, 'Khat')):
                            S.op('pe', lambda p, src=src, cs=cs: p.transpose(p_s[0][:, 0:128], src[:, cs], ident[:]),
                                 reads=[skey, 'ident'], writes=['p_s0'])
                            S.op('act', lambda a, dst=dst: a.activation(out=rv(dst), in_=p_s[0][:, 0:128], func=AF.Identity),
                                 reads=['p_s0'], writes=[dkey])
                        def head_chain(hb, cs=cs, c=c, f=f):
                            rows = slice(hb * 64, (hb + 1) * 64)
                            sc_ = scratch(c, hb); K_ = sc_['k']
                            AabAbr_, AakAkr_, Tinv_, RH_, UT_ = sc_['Aab'], sc_['Aak'], sc_['T'], sc_['RH'], sc_['UT']
                            kAab, kAak, kT, kRH, kUT = K_['Aab'], K_['Aak'], K_['T'], K_['RH'], K_['UT']
                            (bA, kA), (bB, kB), (bC, kC) = ((p_y[0], 'p_y0'), (p_y[1], 'p_y1'), (p_s[1], 'p_s1')) if hb == 0 else \
                                                           ((p_g[0], 'p_g0'), (p_g[1], 'p_g1'), (p_u[0], 'p_u0'))
                            Sh = Swk[rows, f, :]
                            Vh = Vtok[:, hb * 64:(hb + 1) * 64]
                            S.op('pe', lambda p: p.matmul(bA[:, 0:64], rv(AR[rows, 0, cs]), rv(Sh), start=True, stop=False), reads=['AR', f'Swk{hb}'], writes=[kA])
                            S.op('pe', lambda p: p.matmul(bA[:, 0:64], rv(AakAkr_[:, 0:128]), rv(Vh), start=False, stop=True), reads=[kAak, 'Vtok'], writes=[kA])
                            yield
                            S.op('act', lambda a: a.activation(out=rv(RH_), in_=bA[:, 0:64], func=AF.Identity), reads=[kA], writes=[kRH])
                            yield
                            S.op('pe', lambda p: p.matmul(bB[:, 0:64], rv(Tinv_), rv(RH_), start=True, stop=True), reads=[kT, kRH], writes=[kB])
                            yield
                            S.op('act', lambda a: a.activation(out=rv(UT_), in_=bB[:, 0:64], func=AF.Identity), reads=[kB], writes=[kUT])
                            yield
                            S.op('pe', lambda p: p.matmul(bA[:, 0:64], rv(AR[rows, 1, cs]), rv(Sh), start=True, stop=False), reads=['AR', f'Swk{hb}'], writes=[kA])
                            S.op('pe', lambda p: p.matmul(bA[:, 0:64], rv(AabAbr_[:, 128:256]), rv(UT_), start=False, stop=False), reads=[kAab, kUT], writes=[kA])
                            S.op('pe', lambda p: p.matmul(bA[:, 0:64], rv(AakAkr_[:, 128:256]), rv(Vh), start=False, stop=True), reads=[kAak, 'Vtok'], writes=[kA])
                            S.op('pe', lambda p: p.matmul(bC[:, 0:64], rv(Bhat), rv(UT_), start=True, stop=False), reads=['Bhat', kUT], writes=[kC])
                            S.op('pe', lambda p: p.matmul(bC[:, 0:64], rv(Khat), rv(Vh), start=False, stop=True), reads=['Khat', 'Vtok'], writes=[kC])
                            yield
                            S.op('dve', lambda v: v.tensor_copy(ytok[:, c * 128 + hb * 64:c * 128 + (hb + 1) * 64], bA[:, 0:64]), reads=[kA], writes=['ytok'])
                            S.op('dve', lambda v: v.scalar_tensor_tensor(out=r32(Swk)[rows, f, :], in0=Swk[rows, f, :], scalar=pcl[rows, c:c + 1],
                                                                         in1=bC[rows, 0:64], op0=ALU.mult, op1=ALU.add),
                                 reads=[f'Swk{hb}', 'pcl', kC], writes=[f'Swk{hb}'])
                            yield
                        run_interleaved([head_chain(0), head_chain(1)])
                        yv = ytok[:, cs].rearrange("p (h i) -> p h i", h=2)
                        S.op('dve', lambda v, yv=yv: v.tensor_reduce(out=st2[:, 0:2], in_=yv, op=ALU.add, axis=mybir.AxisListType.X),
                             reads=['ytok'], writes=['st2'])
                        S.op('dve', lambda v: v.tensor_scalar_mul(st2[:, 0:2], st2[:, 0:2], 1.0 / 64), reads=['st2'], writes=['st2'])
                        S.op('dve', lambda v, yv=yv: v.tensor_tensor(out=yv, in0=yv, in1=st2[:, 0:2].unsqueeze(2).to_broadcast([128, 2, 64]),
                                                                     op=ALU.subtract), reads=['ytok', 'st2'], writes=['ytok'])
                        tv = TP['tmp'][:, cs].rearrange("p (h i) -> p h i", h=2)
                        S.op('dve', lambda v, yv=yv, tv=tv: v.tensor_tensor(out=tv, in0=yv, in1=yv, op=ALU.mult),
                             reads=['ytok', 'Khat'], writes=['tmp'])
                        S.op('dve', lambda v, tv=tv: v.tensor_reduce(out=st2[:, 2:4], in_=tv, op=ALU.add, axis=mybir.AxisListType.X),
                             reads=['tmp'], writes=['st2'])
                        S.op('dve', lambda v: v.tensor_scalar(out=st2[:, 2:4], in0=st2[:, 2:4], scalar1=1.0 / 64, scalar2=64e-5,
                                                              op0=ALU.mult, op1=ALU.add), reads=['st2'], writes=['st2'])
                        S.op('act', lambda a: a.activation(out=st2[:, 2:4], in_=st2[:, 2:4], func=AF.Sqrt), reads=['st2'], writes=['st2'])
                        S.op('dve', lambda v: v.reciprocal(st2[:, 2:4], st2[:, 2:4]), reads=['st2'], writes=['st2'])
                        S.op('dve', lambda v, yv=yv: v.tensor_tensor(out=yv, in0=yv, in1=st2[:, 2:4].unsqueeze(2).to_broadcast([128, 2, 64]),
                                                                     op=ALU.mult), reads=['ytok', 'st2'], writes=['ytok'])
                        S.op('pe', lambda p, cs=cs: p.transpose(p_s[0][:, 0:128], ytok[:, cs], ident[:]), reads=['ytok', 'ident'], writes=['p_s0'])
                        S.op('act', lambda a, f=f, cs=cs: a.activation(out=TP['kk'][:, cs], in_=p_s[0][:, 0:128], func=AF.Identity,
                                                                       scale=vec(5, f), bias=vec(6, f)), reads=['p_s0', 'rvec', 'Bt'], writes=['kk'])
                    S.op('dve', lambda v: v.tensor_tensor(out=TP['kk'], in0=TP['kk'], in1=TP['rT'], op=ALU.add), reads=['kk', 'rT'], writes=['kk'])
                    S.op('dve', lambda v, f=f: v.tensor_tensor(out=rv(og[f]), in0=TP['kk'], in1=TP['gt'], op=ALU.mult),
                         reads=['kk', 'gt'], writes=['og'])
                fence(light=True)
                for dk in range(KC):
                    pa, pk = proj(rw['w_oT'], dk, lambda k: rv(og[k]), KC, NQ, ['og'])
                    S.op('act', lambda a, pa=pa, dk=dk: a.activation(out=zsq[:, dk, 0:NQ], in_=pa, func=AF.Identity),
                         reads=[pk], writes=['zsq_y', 'zsq'])
                post_norm_tile_from_sbuf(l, 1, t0, 1.0, NQ)
                fence(light=True)
            rwkv_sample()
            S.dma('sp', 'st', lambda q: q.dma_start(out=st_out['shift_p'].rearrange("o (k p) -> p (o k)", p=128), in_=shp[:],
                                                    allow_slow_non_contiguous=True),
                  reads=['shp'])
            for f in range(KC):
                S.op('pe', lambda p, f=f: p.transpose(p_s[f % 2][0:64, 0:128], Swk[:, f, :], ident[:]),
                     reads=['Swk', 'ident'], writes=[f'p_s{f % 2}'])
                S.op('dve', lambda v, f=f: v.tensor_copy(ztf[0:64, f * 128:(f + 1) * 128], p_s[f % 2][0:64, 0:128]),
                     reads=[f'p_s{f % 2}'], writes=['zt'])
                for hb in range(2):
                    h = 2 * f + hb
                    S.dma('sp', 'st', lambda q, f=f, hb=hb, h=h: q.dma_start(
                        out=st_out['wkv_p'][0:1, h * 4096:(h + 1) * 4096].rearrange("o (i j) -> (o i) j", j=64),
                        in_=ztf[0:64, f * 128 + hb * 64:f * 128 + (hb + 1) * 64]), reads=['zt'])
            fence()


        def rwkv_sample():
            l = 1
            N16 = ['rS', 'kS', 'vS', 'lgw', 'asg', 'kk', 'km', 'gt', 'tmp', 'wS', 'aS', 'bS', 'bon', 'yf']
            P = {n: ztf[:, i * NS:(i + 1) * NS] for i, n in enumerate(N16)}
            TK = ['B_tok', 'K_tok', 'V_tok', 'SA_tok', 'Y_tok', 'Bm', 'Km2']
            Tk = {n: ztf[0:NS, 256 + i * 128:256 + (i + 1) * 128] for i, n in enumerate(TK)}
            s16 = ztf[0:NS, 1280:1288]
            xxS = ztf[:, 1296:1424].rearrange("p (k b) -> p k b", k=KC)
            shS = ztf[:, 1424:1552].rearrange("p (k b) -> p k b", k=KC)
            shT = ztf[:, 2064:2192].rearrange("p (k b) -> p k b", k=KC)
            Am = ztf[:, 1552:1808].rearrange("p (b c) -> p b c", b=NS)
            Rm = ztf[:, 1808:2064].rearrange("p (b c) -> p b c", b=NS)
            Sw = zsqf[:, 0:1024].rearrange("p (b i) -> p b i", b=NS)
            XVs = HTf[:, 0:512].rearrange("p (v k b) -> p v k b", v=4, k=KC)
            L16 = {n: HTf[:, 512 + i * NS:512 + (i + 1) * NS] for i, n in enumerate(['lwS', 'laS', 'lg0S', 'lg1S'])}
            ogS = HTf[:, 576:704].rearrange("p (k b) -> p k b", k=KC)
            KS = N16 + TK + ['s16', 'xxS', 'shS', 'shT', 'Am', 'Rm', 'Sw', 'XVs', 'ogS'] + list(L16) + \
                 [f'{n}{hb}' for hb in range(2) for n in ('Bm', 'Km2', 'SA_tok', 'Y_tok', 'Sw')]
            ALIAS.extend(k for k in KS if k not in ALIAS)
            fence()
            vec = lambda i, f: rvec[:, i, f:f + 1]
            EXPM05 = float(np.exp(-0.5))
            S.dma('sp', 'ldr', lambda q: q.dma_start(out=shT, in_=rshift_d), writes=['shT'])
            for k in range(KC):
                S.op('dve', lambda v, k=k: v.tensor_tensor(out=shS[:, k, :], in0=xT[:, k, NP:NT], in1=mods[l][:, 4 * KC + k, 0:NS], op=ALU.mult),
                     reads=['xT', f"mods{l}"], writes=['shS'])
                S.op('dve', lambda v, k=k: v.tensor_tensor(out=shS[:, k, :], in0=shS[:, k, :], in1=mods[l][:, 3 * KC + k, 0:NS], op=ALU.add),
                     reads=['shS', f"mods{l}"], writes=['shS'])
            S.dma('sp', 'st', lambda q: q.dma_start(out=shifts_d, in_=shS), reads=['shS'])
            S.op('dve', lambda v: v.tensor_tensor(out=xxS, in0=shT, in1=shS, op=ALU.subtract), reads=['shT', 'shS'], writes=['xxS'])

            def xvar(slot, i):
                for k in range(KC):
                    S.op('dve', lambda v, k=k: v.scalar_tensor_tensor(out=rv(XVs[:, slot, k, :]), in0=xxS[:, k, :], scalar=rmu[:, i, k:k + 1],
                                                                      in1=shS[:, k, :], op0=ALU.mult, op1=ALU.add),
                         reads=['xxS', 'shS', 'rmu'], writes=['XVs'])
            xs = lambda slot: (lambda k: rv(XVs[:, slot, k, :]))
            xvar(0, 0); xvar(1, 2); xvar(2, 3)
            xvar(3, 1)
            pa, pk = proj(rw['w1T'], 0, xs(3), KC, NS, ['XVs'])
            S.op('act', lambda a, pa=pa: a.activation(out=rv(L16['lwS'][0:64, :]), in_=pa[0:64, :], func=AF.Tanh), reads=[pk], writes=['lwS'])
            xvar(3, 4)
            pa, pk = proj(rw['a1T'], 0, xs(3), KC, NS, ['XVs'])
            S.op('act', lambda a, pa=pa: a.activation(out=rv(L16['laS'][0:64, :]), in_=pa[0:64, :], func=AF.Identity), reads=[pk], writes=['laS'])
            xvar(3, 5)
            pa, pk = proj(rw['g1T'], 0, xs(3), KC, NS, ['XVs'])
            S.op('act', lambda a, pa=pa: a.activation(out=rv(L16['lg0S']), in_=pa, func=AF.Sigmoid), reads=[pk], writes=['lg0S'])
            pa, pk = proj(rw['g1T'], 1, xs(3), KC, NS, ['XVs'])
            S.op('act', lambda a, pa=pa: a.activation(out=rv(L16['lg1S'][0:32, :]), in_=pa[0:32, :], func=AF.Sigmoid), reads=[pk], writes=['lg1S'])
            for f in range(KC):
                fc = f * 128
                for nm, wn, sl_ in (('rS', 'w_r', 0), ('kS', 'w_k', 1), ('vS', 'w_v', 2)):
                    pa, pk = proj(rw[wn + 'T'], f, xs(sl_), KC, NS, ['XVs'])
                    S.op('act', lambda a, pa=pa, nm=nm: a.activation(out=P[nm], in_=pa, func=AF.Identity), reads=[pk], writes=[nm])
                pa, pk = projp([(rw['w2'][0:64, fc:fc + 128], rv(L16['lwS'][0:64, :]))], NS, ['lwS'])
                S.op('act', lambda a, pa=pa, f=f: a.activation(out=P['lgw'], in_=pa, func=AF.Sigmoid, bias=vec(0, f), scale=1.0),
                     reads=[pk, 'rvec'], writes=['lgw'])
                S.op('act', lambda a: a.activation(out=P['wS'], in_=P['lgw'], func=AF.Exp, scale=-EXPM05), reads=['lgw'], writes=['wS'])
                pa, pk = projp([(rw['a2'][0:64, fc:fc + 128], rv(L16['laS'][0:64, :]))], NS, ['laS'])
                S.op('act', lambda a, pa=pa, f=f: a.activation(out=P['asg'], in_=pa, func=AF.Sigmoid, bias=vec(1, f), scale=1.0),
                     reads=[pk, 'rvec'], writes=['asg'])
                pa, pk = projp([(rw['g2'][0:128, fc:fc + 128], rv(L16['lg0S'])), (rw['g2'][128:160, fc:fc + 128], rv(L16['lg1S'][0:32, :]))],
                               NS, ['lg0S', 'lg1S'])
                S.op('act', lambda a, pa=pa: a.activation(out=P['gt'], in_=pa, func=AF.Identity), reads=[pk], writes=['gt'])
                S.op('dve', lambda v, f=f: v.tensor_scalar_mul(P['kk'], P['kS'], vec(2, f)), reads=['kS', 'rvec'], writes=['kk'])
                S.op('act', lambda a: a.activation(out=P['tmp'], in_=P['kk'], func=AF.Square), reads=['kk'], writes=['tmp'])
                S.op('pe', lambda p: p.matmul(p_s[0][:, 0:NS], bones[:], P['tmp'], start=True, stop=True), reads=['bones', 'tmp'], writes=['p_s0'])
                S.op('act', lambda a: a.activation(out=P['tmp'], in_=p_s[0][:, 0:NS], func=AF.Sqrt), reads=['p_s0'], writes=['tmp'])
                S.op('dve', lambda v: v.tensor_scalar_max(P['tmp'], P['tmp'], 1e-12), reads=['tmp'], writes=['tmp'])
                S.op('dve', lambda v: v.reciprocal(P['tmp'], P['tmp']), reads=['tmp'], writes=['tmp'])
                S.op('dve', lambda v: v.tensor_tensor(out=P['kk'], in0=P['kk'], in1=P['tmp'], op=ALU.mult), reads=['kk', 'tmp'], writes=['kk'])
                S.op('dve', lambda v, f=f: v.tensor_scalar(out=P['km'], in0=P['asg'], scalar1=-1.0, scalar2=vec(3, f), op0=ALU.add, op1=ALU.mult),
                     reads=['asg', 'rvec'], writes=['km'])
                S.op('dve', lambda v: v.scalar_tensor_tensor(out=P['km'], in0=P['km'], scalar=1.0, in1=P['kS'], op0=ALU.add, op1=ALU.mult),
                     reads=['km', 'kS'], writes=['km'])
                S.op('dve', lambda v: v.tensor_scalar_mul(P['aS'], P['kk'], -1.0), reads=['kk'], writes=['aS'])
                S.op('dve', lambda v: v.tensor_tensor(out=P['bS'], in0=P['kk'], in1=P['asg'], op=ALU.mult), reads=['kk', 'asg'], writes=['bS'])
                S.op('dve', lambda v, f=f: v.scalar_tensor_tensor(out=P['tmp'], in0=P['rS'], scalar=vec(4, f), in1=P['km'], op0=ALU.mult, op1=ALU.mult),
                     reads=['rS', 'km', 'rvec'], writes=['tmp'])
                S.op('pe', lambda p: p.matmul(p_s[1][:, 0:NS], bones[:], P['tmp'], start=True, stop=True), reads=['bones', 'tmp'], writes=['p_s1'])
                S.op('dve', lambda v: v.tensor_tensor(out=P['bon'], in0=p_s[1][:, 0:NS], in1=P['vS'], op=ALU.mult), reads=['p_s1', 'vS'], writes=['bon'])
                for src, dst in (('bS', 'B_tok'), ('km', 'K_tok'), ('vS', 'V_tok')):
                    S.op('pe', lambda p, src=src: p.transpose(p_s[0][0:NS, 0:128], P[src], ident[:]), reads=[src, 'ident'], writes=['p_s0'])
                    S.op('act', lambda a, dst=dst: a.activation(out=Tk[dst], in_=p_s[0][0:NS, 0:128], func=AF.Identity), reads=['p_s0'], writes=[dst])
                S.op('pool', lambda g: g.memset(Am, 0.0), writes=['Am'])
                S.op('pool', lambda g: g.memset(Rm, 0.0), writes=['Rm'])
                for b in range(NS):
                    S.op('dve', lambda v, b=b: v.tensor_copy(Am[:, b, b:b + 1], P['aS'][:, b:b + 1]), reads=['aS', 'Am'], writes=['Am'])
                    S.op('dve', lambda v, b=b: v.tensor_copy(Rm[:, b, b:b + 1], P['rS'][:, b:b + 1]), reads=['rS', 'Rm'], writes=['Rm'])
                def smp_chain(bt, hb, f=f):
                    rows = slice(hb * 64, (hb + 1) * 64)
                    hc = slice(hb * 64, (hb + 1) * 64)
                    Bm_ = Tk['Bm'] if hb == 0 else ztf[0:NS, 2192:2320]
                    Km_ = Tk['Km2'] if hb == 0 else ztf[0:NS, 2320:2448]
                    kBm, kKm, kSA, kY, kSw = f'Bm{hb}', f'Km2{hb}', f'SA_tok{hb}', f'Y_tok{hb}', f'Sw{hb}'
                    (bA, kA), (bB, kB), (bC, kC) = ((p_y[0], 'p_y0'), (p_y[1], 'p_y1'), (p_s[0], 'p_s0')) if hb == 0 else \
                                                   ((p_g[0], 'p_g0'), (p_g[1], 'p_g1'), (p_u[0], 'p_u0'))
                    for bi in range(4):
                        b = 4 * bt + bi
                        S.op('pe', lambda p, b=b, bi=bi: p.matmul(bA[0:NS, 0:64], Am[rows, b, :], Sw[rows, b, :], start=(bi == 0), stop=(bi == 3)),
                             reads=['Am', kSw], writes=[kA])
                    yield
                    S.op('act', lambda a: a.activation(out=Tk['SA_tok'][:, hc], in_=bA[0:NS, 0:64], func=AF.Identity), reads=[kA], writes=[kSA])
                    yield
                    for bi in range(4):
                        b = 4 * bt + bi
                        S.op('dve', lambda v, b=b: v.tensor_scalar_mul(Bm_, Tk['B_tok'], ident[0:NS, b:b + 1]), reads=['B_tok', 'ident'], writes=[kBm])
                        S.op('dve', lambda v, b=b: v.tensor_scalar_mul(Km_, Tk['K_tok'], ident[0:NS, b:b + 1]), reads=['K_tok', 'ident'], writes=[kKm])
                        yield
                        S.op('pe', lambda p: p.matmul(bB[:, 0:64], Bm_, Tk['SA_tok'][:, hc], start=True, stop=False), reads=[kBm, kSA], writes=[kB])
                        S.op('pe', lambda p: p.matmul(bB[:, 0:64], Km_, Tk['V_tok'][:, hc], start=False, stop=True), reads=[kKm, 'V_tok'], writes=[kB])
                        yield
                        S.op('dve', lambda v, b=b, bi=bi: v.scalar_tensor_tensor(
                            out=Sw[rows, b, :], in0=Sw[rows, b, :], scalar=P['wS'][rows, b:b + 1], in1=bB[rows, 0:64],
                            op0=ALU.mult, op1=ALU.add), reads=[kSw, 'wS', kB], writes=[kSw])
                        yield
                        S.op('pe', lambda p, b=b, bi=bi: p.matmul(bC[0:NS, 0:64], Rm[rows, b, :], Sw[rows, b, :], start=(bi == 0), stop=(bi == 3)),
                             reads=['Rm', kSw], writes=[kC])
                        yield
                    if bt == 0:
                        S.op('dve', lambda v: v.tensor_copy(Tk['Y_tok'][:, hc], bC[0:NS, 0:64]), reads=[kC], writes=[kY, 'Y_tok'])
                    else:
                        S.op('dve', lambda v: v.tensor_tensor(out=Tk['Y_tok'][:, hc], in0=Tk['Y_tok'][:, hc], in1=bC[0:NS, 0:64], op=ALU.add),
                             reads=[kC, kY], writes=[kY, 'Y_tok'])
                    yield

                def run_il2(gens):
                    gens = list(gens)
                    while gens:
                        for g_ in list(gens):
                            try:
                                next(g_)
                            except StopIteration:
                                gens.remove(g_)
                S.dma('sp', 'ldw', lambda q, f=f: q.dma_start(out=Sw, in_=wkvs0_d[:, :, f, :]), writes=['Sw', 'Sw0', 'Sw1'])
                for bt in range(4):
                    run_il2([smp_chain(bt, 0), smp_chain(bt, 1)])
                S.dma('sp', 'st', lambda q, f=f: q.dma_start(out=wkvs_d[:, :, f, :], in_=Sw), reads=['Sw', 'Sw0', 'Sw1'])
                yv = Tk['Y_tok'].rearrange("p (h i) -> p h i", h=2)
                tv = Tk['Bm'].rearrange("p (h i) -> p h i", h=2)
                S.op('dve', lambda v: v.tensor_reduce(out=s16[:, 0:2], in_=yv, op=ALU.add, axis=mybir.AxisListType.X),
                     reads=['Y_tok', 'Y_tok0', 'Y_tok1'], writes=['s16', 'Y_tok'])
                S.op('dve', lambda v: v.tensor_scalar_mul(s16[:, 0:2], s16[:, 0:2], 1.0 / 64), reads=['s16'], writes=['s16'])
                S.op('dve', lambda v: v.tensor_tensor(out=yv, in0=yv, in1=s16[:, 0:2].unsqueeze(2).to_broadcast([NS, 2, 64]), op=ALU.subtract),
                     reads=['Y_tok', 's16'], writes=['Y_tok'])
                S.op('dve', lambda v: v.tensor_tensor(out=tv, in0=yv, in1=yv, op=ALU.mult), reads=['Y_tok'], writes=['Bm', 'Bm0'])
                S.op('dve', lambda v: v.tensor_reduce(out=s16[:, 2:4], in_=tv, op=ALU.add, axis=mybir.AxisListType.X), reads=['Bm', 'Bm0'], writes=['s16'])
                S.op('dve', lambda v: v.tensor_scalar(out=s16[:, 2:4], in0=s16[:, 2:4], scalar1=1.0 / 64, scalar2=64e-5, op0=ALU.mult, op1=ALU.add),
                     reads=['s16'], writes=['s16'])
                S.op('act', lambda a: a.activation(out=s16[:, 2:4], in_=s16[:, 2:4], func=AF.Sqrt), reads=['s16'], writes=['s16'])
                S.op('dve', lambda v: v.reciprocal(s16[:, 2:4], s16[:, 2:4]), reads=['s16'], writes=['s16'])
                S.op('dve', lambda v: v.tensor_tensor(out=yv, in0=yv, in1=s16[:, 2:4].unsqueeze(2).to_broadcast([NS, 2, 64]), op=ALU.mult),
                     reads=['Y_tok', 's16'], writes=['Y_tok'])
                S.op('pe', lambda p: p.transpose(p_s[1][:, 0:NS], Tk['Y_tok'], ident[0:NS, 0:NS]), reads=['Y_tok', 'ident'], writes=['p_s1'])
                S.op('act', lambda a, f=f: a.activation(out=P['yf'], in_=p_s[1][:, 0:NS], func=AF.Identity, scale=vec(5, f), bias=vec(6, f)),
                     reads=['p_s1', 'rvec'], writes=['yf'])
                S.op('dve', lambda v: v.tensor_tensor(out=P['yf'], in0=P['yf'], in1=P['bon'], op=ALU.add), reads=['yf', 'bon'], writes=['yf'])
                S.op('dve', lambda v, f=f: v.tensor_tensor(out=rv(ogS[:, f, :]), in0=P['yf'], in1=P['gt'], op=ALU.mult), reads=['yf', 'gt'], writes=['ogS'])
            fence()
            for dk in range(KC):
                pa, pk = proj(rw['w_oT'], dk, lambda k: rv(ogS[:, k, :]), KC, NS, ['ogS'])
                S.op('act', lambda a, pa=pa, dk=dk: a.activation(out=zsq[:, dk, 0:NS], in_=pa, func=AF.Identity), reads=[pk], writes=['zsq_y', 'zsq'])
            post_norm_tile_from_sbuf(l, 1, NP, 1.0, NS)
            fence()

        def mixer_stub_sublayer(l):
            for t in range(NTILE):
                c0 = t * TW
                for k in range(KC):
                    S.op('act', lambda a, k=k, c0=c0: a.activation(out=zt[:, k, :], in_=xT[:, k, c0:c0 + TW],
                                                                   func=AF.Identity, scale=DN_ALPHA),
                         reads=['xT'], writes=['zt'])
                    S.op('act', lambda a, k=k: a.activation(out=zsq[:, k, :], in_=zt[:, k, :], func=AF.Square),
                         reads=['zt', 'zsq_y'], writes=['zsq', 'zsq_y'])
                finish_norm(l, 1, c0)

        for l in range(DEPTH):
            if only == 's5':
                if l == 0:
                    U5, Us5 = s5_phase()
                    S.dma('sp', 'st', lambda q: q.dma_start(out=yT_d[:, 0:4, 0:NP], in_=U5), reads=['U'])
                    S.dma('sp', 'st', lambda q: q.dma_start(out=yT_d[:, 0:4, NP:NT], in_=Us5), reads=['Us'])
                continue
            ffn_sublayer(l, 0, tuple(w for w in ffw["ffn1"]))
            mix0_sublayer() if l == 0 else mix1_sublayer()
            ffn_sublayer(l, 2, tuple(w for w in ffw["ffn2"]))

        if only is None:
            S.dma('sp', 'st', lambda q: q.dma_start(out=yT_d, in_=xT[:]), reads=['xT'])
        S.op('pool', lambda g: g.memset(zsq[:, 0, :], 0.0), writes=['zsq', 'zsq_y'])
        for name, ap in st_out.items():
            if name in ('gla_p', 'shift_p', 'wkv_p'):
                continue
            nb, width = ap.shape
            for c in range(0, width, TW):
                wdt = min(TW, width - c)
                S.dma('sp', 'st', lambda q, ap=ap, nb=nb, c=c, wdt=wdt: q.dma_start(
                    out=ap[:, c:c + wdt], in_=zsq[0:nb, 0, 0:wdt]), reads=['zsq'])
        S.finish('sp')
        S.emit_all()
    return nc


def _featmajor(v, ncols):
    return np.ascontiguousarray(np.swapaxes(v.reshape(v.shape[:-1] + (ncols, 128)), -1, -2))


def _tile_w(w, nk):
    C = w.shape[1]
    nb = (C + 127) // 128
    wp = np.zeros((nk * 128, nb * 128), np.float32)
    wp[:, :C] = w
    return np.ascontiguousarray(wp.reshape(nk, 128, nb, 128).transpose(2, 1, 0, 3))


def prep_inputs(inp):
    f = lambda k: np.ascontiguousarray(np.asarray(inp[k], dtype=np.float32))
    xp, xs, cp, cs = f("x_prompt"), f("x_sample"), f("c_prompt"), f("c_sample")
    shared = {
        "ada_w": f("ada_w"),
        "ada_bT": _featmajor(f("ada_b"), NMOD * KC),
        "ln_gT": _featmajor(f("ln_g"), KC),
        "ln_bT": _featmajor(f("ln_b"), KC),
        "w_inT": _tile_w(f("w_in")[:, 0:1664], KC), "w_inuT": _tile_w(f("w_in")[:, 1552:2064], KC),
        "w_outT": _tile_w(f("w_out"), KC), "gla_w_gk": f("gla_w_gk"),
        "gla_b_gkT": _featmajor(f("gla_b_gk"), 2),
        "gla_norm_gT": np.ascontiguousarray(f("gla_norm_g").reshape(128, 1)),
        "s5_aT": np.ascontiguousarray(np.tile(np.stack([f("s5_a_re").T, f("s5_a_im").T], 1), (2, 1, 1))),
        "s5_lsT": np.ascontiguousarray(np.tile(f("s5_log_step")[None, :], (128, 1))),
        "s5_BA": np.ascontiguousarray(np.concatenate([f("s5_b_re").transpose(1, 0, 2), f("s5_b_im").transpose(1, 0, 2)], 0)),
        "s5_BB": np.ascontiguousarray(np.concatenate([f("s5_b_im").transpose(1, 0, 2), f("s5_b_re").transpose(1, 0, 2)], 0)),
        "s5_CT": np.ascontiguousarray(np.concatenate([f("s5_c_re").transpose(2, 0, 1), f("s5_c_im").transpose(2, 0, 1)], 0)),
        "s5_dT": _featmajor(f("s5_d").reshape(-1), 4), "s5_b_gluT": _featmajor(f("s5_b_glu"), 4), "s5_w_gluT": _tile_w(f("s5_w_glu"), 4),
        "rwkv_muT": np.ascontiguousarray(_featmajor(f("rwkv_mu"), KC).transpose(1, 0, 2)),
        "rwkv_vecT": np.ascontiguousarray(_featmajor(np.stack([f("rwkv_w0"), f("rwkv_a0"), f("rwkv_k_k"), f("rwkv_k_a"),
                                                                f("rwkv_r_k").reshape(-1), f("rwkv_lnx_g"), f("rwkv_lnx_b")]), KC).transpose(1, 0, 2)),
    }
    for nm in ("ffn1", "ffn2"):
        shared[nm + "_wd"] = f(nm + "_wd")
        wg, wu = f(nm + "_wg"), f(nm + "_wu")
        shared[nm + "_wguT"] = np.ascontiguousarray(np.stack(
            [np.stack([_tile_w(wg[l], KC), _tile_w(wu[l], KC)], axis=2) for l in range(DEPTH)]))
    for nm in ("w2", "a2", "g2"):
        shared["rwkv_" + nm] = f("rwkv_" + nm)
    for nm in ("w_r", "w_k", "w_v", "w_o", "w1", "a1", "g1"):
        shared["rwkv_" + nm + "T"] = _tile_w(f("rwkv_" + nm), KC)
    in_maps = []
    for i in range(8):
        tok = np.concatenate([xp[i], xs[16 * i:16 * i + 16, 0, :]], axis=0)
        xT = np.ascontiguousarray(tok.T.reshape(KC, 128, NT).transpose(1, 0, 2))
        cc = np.concatenate([cs[16 * i:16 * i + 16], cp[i:i + 1], np.zeros((1, D), np.float32)], axis=0)
        cT = np.ascontiguousarray(cc.T.reshape(KC, 128, NCC).transpose(1, 0, 2))
        h0 = np.concatenate([f('state_s5_re')[16 * i:16 * i + 16].transpose(2, 1, 0), f('state_s5_im')[16 * i:16 * i + 16].transpose(2, 1, 0)], 0)
        g0 = f('state_gla')[16 * i:16 * i + 16].reshape(16, 2, 2, 64, 128).transpose(2, 3, 0, 1, 4).reshape(128, 16, 2, 128)
        sh = f('state_rwkv_shift')[16 * i:16 * i + 16]
        shT = sh.T.reshape(KC, 128, 16).transpose(1, 0, 2)
        w0 = f('state_rwkv_wkv')[16 * i:16 * i + 16].reshape(16, KC, 2, 64, 64).transpose(2, 4, 0, 1, 3).reshape(128, 16, KC, 64)
        in_maps.append(dict(shared, xT=xT, cT=cT, s5_h0T=np.ascontiguousarray(h0), gla_s0T=np.ascontiguousarray(g0),
                            rwkv_shiftT=np.ascontiguousarray(shT), wkv_s0T=np.ascontiguousarray(w0)))
    return in_maps


def kernel(**inp):
    in_maps = prep_inputs(inp)
    nc = build_nc()
    res = run_bass_kernel_spmd(nc, in_maps, core_ids=list(range(8))).results

    def tokens(r):
        return r["yT"].transpose(2, 1, 0).reshape(NT, D)
    y_prompt = np.stack([tokens(r)[:NP] for r in res]).astype(np.float32)
    y_sample = np.concatenate([tokens(r)[NP:] for r in res])[:, None, :].astype(np.float32)
    outs = [y_prompt, y_sample]
    for grp in ("p", "s"):
        cat = lambda k: np.concatenate([r[k + "_" + grp] for r in res], axis=0)
        if grp == "p":
            s5 = np.stack([r["s5_hp"] for r in res])
            s5re, s5im = s5[:, 0:64].transpose(0, 2, 1), s5[:, 64:128].transpose(0, 2, 1)
        else:
            s5 = np.concatenate([r["s5_hs"].transpose(2, 1, 0) for r in res], 0)
            s5re, s5im = s5[:, :, 0:64], s5[:, :, 64:128]
        if grp == "p":
            gla = cat("gla").reshape((-1,) + GLA_SHAPE)
        else:
            gla = np.concatenate([r["gla_sT"].reshape(2, 64, 16, 2, 128).transpose(2, 3, 0, 1, 4).reshape(16, 4, 64, 128) for r in res], 0)
        if grp == "p":
            shift, wkv = cat("shift"), cat("wkv").reshape((-1,) + WKV_SHAPE)
        else:
            shift = np.concatenate([r["shift_sT"].transpose(2, 1, 0).reshape(16, D) for r in res], 0)
            wkv = np.concatenate([r["wkv_sT"].reshape(2, 64, 16, KC, 64).transpose(2, 3, 0, 4, 1).reshape(16, 16, 64, 64) for r in res], 0)
        outs += [gla, s5re, s5im, shift, wkv]
    return tuple(np.ascontiguousarray(o, dtype=np.float32) for o in outs)
```
